# Optimizing a Trainium2 kernel written in Bass

```python
import math
import jax
import jax.numpy as jnp
from jax import lax
import numpy as np

D_MODEL = 1024
BATCH = 4
SEQ = 4096
DEPTH = 4

NORM_EPS = 1e-6

HG_WIDTH = D_MODEL
HG_EXPAND = 128
HG_HEADS = HG_WIDTH // HG_EXPAND
HG_DK = HG_EXPAND
HG_DV = HG_WIDTH // HG_HEADS
HG_CHUNK = 64

RW_WIDTH = D_MODEL
RW_HEAD = 64
RW_HEADS = RW_WIDTH // RW_HEAD
RW_DECAY_LORA = max(32, int(round(1.8 * D_MODEL ** 0.5 / 32)) * 32)
RW_AAA_LORA = max(32, int(round(1.8 * D_MODEL ** 0.5 / 32)) * 32)
RW_MV_LORA = max(32, int(round(1.3 * D_MODEL ** 0.5 / 32)) * 32)
RW_GATE_LORA = max(32, int(round(0.6 * D_MODEL ** 0.8 / 32)) * 32)
RW_GN_EPS = 64e-5

S5_WIDTH = D_MODEL
S5_GROUP = 16
S5_GROUPS = S5_WIDTH // S5_GROUP
S5_STATE = 64
S5_DT_MIN = 1e-3
S5_DT_MAX = 1e-1

FFN_HIDDEN = ((8 * D_MODEL + 3 * 256 - 1) // (3 * 256)) * 256

N_BRANCH = 3
HG_IN = 4 * HG_WIDTH
RW_IN = 3 * RW_WIDTH + RW_DECAY_LORA + RW_AAA_LORA + RW_GATE_LORA
OFF_HG = 0
OFF_RW = OFF_HG + HG_IN
OFF_S5 = OFF_RW + RW_IN
OFF_GATE = OFF_S5 + S5_WIDTH
IN_WIDTH = OFF_GATE + N_BRANCH * D_MODEL
MIX_WIDTH = HG_WIDTH + RW_WIDTH + S5_WIDTH

kernel_name = "hgrn2_rwkv7_s5_gated_hybrid"


def rmsnorm(x, g):
    xf = x.astype(jnp.float32)
    y = xf * lax.rsqrt(jnp.mean(xf * xf, axis=-1, keepdims=True) + NORM_EPS)
    return (y * g.astype(jnp.float32)).astype(x.dtype)


def token_shift(z):
    return jnp.pad(z, ((0, 0), (1, 0), (0, 0)))[:, :-1]


def hgrn2_mixer(q, f_logit, i, og, lb, onorm):
    B, S, _ = q.shape
    f32 = jnp.float32
    q = jax.nn.silu(q.astype(f32))
    lb = jnp.maximum(lb.astype(f32), 0.0)
    log_f = jnp.logaddexp(jnp.log(lb), jnp.log1p(-lb) + jax.nn.log_sigmoid(f_logit.astype(f32)))
    k = -jnp.expm1(log_f)
    v = i.astype(f32)
    nc = S // HG_CHUNK

    def to_chunks(z, d):
        return z.reshape(B, nc, HG_CHUNK, HG_HEADS, d).transpose(1, 0, 3, 2, 4)

    qc, kc, gc = to_chunks(q, HG_DK), to_chunks(k, HG_DK), to_chunks(log_f, HG_DK)
    vc = to_chunks(v, HG_DV)
    causal = jnp.tril(jnp.ones((HG_CHUNK, HG_CHUNK), dtype=bool))

    def step(state, inp):
        qb, kb, vb, gb = inp
        b = jnp.cumsum(gb, axis=2)
        diff = b[:, :, :, None, :] - b[:, :, None, :, :]
        decay = jnp.exp(jnp.where(causal[:, :, None], diff, -jnp.inf))
        scores = jnp.einsum('bhtk,bhsk,bhtsk->bhts', qb, kb, decay)
        o = jnp.einsum('bhts,bhsv->bhtv', scores, vb) + jnp.einsum('bhtk,bhkv->bhtv', qb * jnp.exp(b), state)
        b_last = b[:, :, -1:, :]
        state = jnp.exp(b_last[:, :, 0, :])[..., None] * state + jnp.einsum('bhsk,bhsv->bhkv', kb * jnp.exp(b_last - b), vb)
        return state, o

    s0 = jnp.zeros((B, HG_HEADS, HG_DK, HG_DV), f32)
    _, oc = lax.scan(step, s0, (qc, kc, vc, gc))
    o = oc.transpose(1, 0, 3, 2, 4).reshape(B, S, HG_HEADS, HG_DV)
    o = o * lax.rsqrt(jnp.mean(o * o, axis=-1, keepdims=True) + NORM_EPS)
    return o.reshape(B, S, HG_WIDTH) * onorm * jax.nn.silu(og.astype(f32))


def rwkv7_mixer(feats, v_first, vres, mu, w0, w2, a0, a2, g2, k_k, k_a, r_k, ln_w, ln_b):
    B, S, _ = feats.shape
    f32 = jnp.float32
    W = RW_WIDTH
    z = feats.astype(f32)
    z = z + mu * (token_shift(z) - z)
    r, k, v = z[..., :W], z[..., W:2 * W], z[..., 2 * W:3 * W]
    o1 = 3 * W
    o2 = o1 + RW_DECAY_LORA
    o3 = o2 + RW_AAA_LORA
    wd, ad, gd = z[..., o1:o2], z[..., o2:o3], z[..., o3:o3 + RW_GATE_LORA]
    w_log = -jax.nn.softplus(-(w0 + jnp.tanh(wd) @ w2)) - 0.5
    decay = jnp.exp(-jnp.exp(w_log))
    a = jax.nn.sigmoid(a0 + ad @ a2)
    g = jax.nn.sigmoid(gd) @ g2
    if vres is None:
        v_first = v
    else:
        v0, v1, v2 = vres
        v = v + (v_first - v) * jax.nn.sigmoid(v0 + (v @ v1) @ v2)

    def heads(t):
        return t.reshape(B, S, RW_HEADS, RW_HEAD)

    kk = heads(k * k_k)
    kk = kk * lax.rsqrt(jnp.maximum(jnp.sum(kk * kk, axis=-1, keepdims=True), 1e-24))
    k = k * (1.0 + (a - 1.0) * k_a)
    rh, wh, kh, vh, ah = heads(r), heads(decay), heads(k), heads(v), heads(a)
    a_vec = -kk
    b_vec = kk * ah

    def step(state, inp):
        r_t, w_t, k_t, v_t, a_t, b_t = inp
        sa = jnp.einsum('bhvk,bhk->bhv', state, a_t)
        state = state * w_t[:, :, None, :] + sa[..., None] * b_t[:, :, None, :] + v_t[..., None] * k_t[:, :, None, :]
        return state, jnp.einsum('bhvk,bhk->bhv', state, r_t)

    xs = tuple(jnp.moveaxis(t, 1, 0) for t in (rh, wh, kh, vh, a_vec, b_vec))
    s0 = jnp.zeros((B, RW_HEADS, RW_HEAD, RW_HEAD), f32)
    _, y = lax.scan(step, s0, xs)
    y = jnp.moveaxis(y, 0, 1)
    mean = jnp.mean(y, axis=-1, keepdims=True)
    var = jnp.mean(jnp.square(y - mean), axis=-1, keepdims=True)
    y = ((y - mean) * lax.rsqrt(var + RW_GN_EPS)).reshape(B, S, W) * ln_w + ln_b
    bonus = jnp.sum(rh * kh * r_k, axis=-1, keepdims=True) * vh
    out = (y + bonus.reshape(B, S, W)) * g
    return out, v_first


def s5_mixer(u, lam_re, lam_im, log_step, b_re, b_im, c_re, c_im, d, glu_w, glu_b):
    B, S, _ = u.shape
    f32 = jnp.float32
    u = u.astype(f32).reshape(B, S, S5_GROUPS, S5_GROUP)
    lam_re = lam_re.astype(f32)
    lam_im = lam_im.astype(f32)
    dt = jnp.exp(log_step.astype(f32))[:, None]
    mag = jnp.exp(lam_re * dt)
    ab_re = mag * jnp.cos(lam_im * dt)
    ab_im = mag * jnp.sin(lam_im * dt)
    den = lam_re * lam_re + lam_im * lam_im
    nr, ni = ab_re - 1.0, ab_im
    coef_re = (nr * lam_re + ni * lam_im) / den
    coef_im = (ni * lam_re - nr * lam_im) / den
    bb_re = coef_re[..., None] * b_re - coef_im[..., None] * b_im
    bb_im = coef_re[..., None] * b_im + coef_im[..., None] * b_re
    bu_re = jnp.einsum('bsgh,gph->bsgp', u, bb_re)
    bu_im = jnp.einsum('bsgh,gph->bsgp', u, bb_im)
    a_re = jnp.broadcast_to(ab_re, (1, S, S5_GROUPS, S5_STATE))
    a_im = jnp.broadcast_to(ab_im, (1, S, S5_GROUPS, S5_STATE))

    def combine(left, right):
        ar_l, ai_l, br_l, bi_l = left
        ar_r, ai_r, br_r, bi_r = right
        return (ar_r * ar_l - ai_r * ai_l,
                ar_r * ai_l + ai_r * ar_l,
                ar_r * br_l - ai_r * bi_l + br_r,
                ar_r * bi_l + ai_r * br_l + bi_r)

    _, _, xs_re, xs_im = lax.associative_scan(combine, (a_re, a_im, bu_re, bu_im), axis=1)
    y = (jnp.einsum('gjp,bsgp->bsgj', c_re, xs_re) - jnp.einsum('gjp,bsgp->bsgj', c_im, xs_im)
         + d.reshape(S5_GROUPS, S5_GROUP) * u)
    zg = jax.nn.gelu(y)
    out = zg * jax.nn.sigmoid(jnp.einsum('bsgj,gjk->bsgk', zg, glu_w) + glu_b.reshape(S5_GROUPS, S5_GROUP))
    return out.reshape(B, S, S5_WIDTH)


def setup_inputs(seed: int = 0) -> dict:
    key = jax.random.key(seed)
    ks = iter(jax.random.split(key, 48))
    f32 = jnp.float32
    L = DEPTH

    def nrm(shape, scale):
        return scale * jax.random.normal(next(ks), shape, f32)

    return {
        "x": nrm((BATCH, SEQ, D_MODEL), 1.0),
        "mix_norm": 1.0 + nrm((L, D_MODEL), 0.02),
        "w_in": nrm((L, D_MODEL, IN_WIDTH), D_MODEL ** -0.5),
        "hg_lb_logits": nrm((L, HG_WIDTH), 0.1),
        "hg_onorm": 1.0 + nrm((L, HG_WIDTH), 0.02),
        "rw_shift_mu": jax.random.uniform(next(ks), (L, RW_IN), f32),
        "rw_w0": jnp.linspace(-6.0, -1.0, RW_WIDTH, dtype=f32)[None, :] + nrm((L, RW_WIDTH), 0.1),
        "rw_w2": nrm((L, RW_DECAY_LORA, RW_WIDTH), 0.5 * RW_DECAY_LORA ** -0.5),
        "rw_a0": nrm((L, RW_WIDTH), 0.1),
        "rw_a2": nrm((L, RW_AAA_LORA, RW_WIDTH), 0.5 * RW_AAA_LORA ** -0.5),
        "rw_g2": nrm((L, RW_GATE_LORA, RW_WIDTH), RW_GATE_LORA ** -0.5),
        "rw_v0": 0.5 + nrm((L - 1, RW_WIDTH), 0.1),
        "rw_v1": nrm((L - 1, RW_WIDTH, RW_MV_LORA), RW_WIDTH ** -0.5),
        "rw_v2": nrm((L - 1, RW_MV_LORA, RW_WIDTH), 0.5 * RW_MV_LORA ** -0.5),
        "rw_k_k": 0.85 + nrm((L, RW_WIDTH), 0.02),
        "rw_k_a": 1.0 + nrm((L, RW_WIDTH), 0.02),
        "rw_r_k": nrm((L, RW_HEADS, RW_HEAD), 0.1),
        "rw_ln_w": 1.0 + nrm((L, RW_WIDTH), 0.02),
        "rw_ln_b": nrm((L, RW_WIDTH), 0.02),
        "s5_lambda_re": -0.5 + nrm((L, S5_GROUPS, S5_STATE), 0.01),
        "s5_lambda_im": jnp.pi * jnp.arange(S5_STATE, dtype=f32)[None, None, :] + nrm((L, S5_GROUPS, S5_STATE), 0.01),
        "s5_log_step": jax.random.uniform(next(ks), (L, S5_GROUPS), f32, minval=math.log(S5_DT_MIN), maxval=math.log(S5_DT_MAX)),
        "s5_b_re": nrm((L, S5_GROUPS, S5_STATE, S5_GROUP), (2 * S5_GROUP) ** -0.5),
        "s5_b_im": nrm((L, S5_GROUPS, S5_STATE, S5_GROUP), (2 * S5_GROUP) ** -0.5),
        "s5_c_re": nrm((L, S5_GROUPS, S5_GROUP, S5_STATE), (2 * S5_STATE) ** -0.5),
        "s5_c_im": nrm((L, S5_GROUPS, S5_GROUP, S5_STATE), (2 * S5_STATE) ** -0.5),
        "s5_d": nrm((L, S5_WIDTH), 1.0),
        "s5_glu_w": nrm((L, S5_GROUPS, S5_GROUP, S5_GROUP), S5_GROUP ** -0.5),
        "s5_glu_b": nrm((L, S5_WIDTH), 0.02),
        "w_branch": nrm((L, MIX_WIDTH, D_MODEL), D_MODEL ** -0.5),
        "w_out": nrm((L, D_MODEL, D_MODEL), D_MODEL ** -0.5),
        "ffn_norm": 1.0 + nrm((L, D_MODEL), 0.02),
        "ffn_w_gate": nrm((L, D_MODEL, FFN_HIDDEN), D_MODEL ** -0.5),
        "ffn_w_up": nrm((L, D_MODEL, FFN_HIDDEN), D_MODEL ** -0.5),
        "ffn_w_down": nrm((L, FFN_HIDDEN, D_MODEL), FFN_HIDDEN ** -0.5),
        "final_norm": 1.0 + nrm((D_MODEL,), 0.02),
    }


def reference(x, mix_norm, w_in, hg_lb_logits, hg_onorm, rw_shift_mu, rw_w0, rw_w2, rw_a0, rw_a2,
              rw_g2, rw_v0, rw_v1, rw_v2, rw_k_k, rw_k_a, rw_r_k, rw_ln_w, rw_ln_b,
              s5_lambda_re, s5_lambda_im, s5_log_step, s5_b_re, s5_b_im, s5_c_re, s5_c_im, s5_d,
              s5_glu_w, s5_glu_b, w_branch, w_out, ffn_norm, ffn_w_gate, ffn_w_up, ffn_w_down,
              final_norm):
    B, S, D = x.shape
    lb_sm = jax.nn.softmax(hg_lb_logits.astype(jnp.float32), axis=0)
    lb_all = jnp.cumsum(lb_sm, axis=0) - lb_sm[0:1]
    h = x
    v_first = None
    for l in range(DEPTH):
        xn = rmsnorm(h, mix_norm[l])
        proj = xn @ w_in[l]
        hg = proj[..., OFF_HG:OFF_RW]
        o_hg = hgrn2_mixer(hg[..., :HG_WIDTH], hg[..., HG_WIDTH:2 * HG_WIDTH],
                           hg[..., 2 * HG_WIDTH:3 * HG_WIDTH], hg[..., 3 * HG_WIDTH:],
                           lb_all[l], hg_onorm[l])
        vres = None if l == 0 else (rw_v0[l - 1], rw_v1[l - 1], rw_v2[l - 1])
        o_rw, v_first = rwkv7_mixer(proj[..., OFF_RW:OFF_S5], v_first, vres, rw_shift_mu[l],
                                    rw_w0[l], rw_w2[l], rw_a0[l], rw_a2[l], rw_g2[l],
                                    rw_k_k[l], rw_k_a[l], rw_r_k[l], rw_ln_w[l], rw_ln_b[l])
        o_s5 = s5_mixer(proj[..., OFF_S5:OFF_GATE], s5_lambda_re[l], s5_lambda_im[l], s5_log_step[l],
                        s5_b_re[l], s5_b_im[l], s5_c_re[l], s5_c_im[l], s5_d[l], s5_glu_w[l], s5_glu_b[l])
        gates = jax.nn.sigmoid(proj[..., OFF_GATE:].astype(jnp.float32)).reshape(B, S, N_BRANCH, D)
        wb = w_branch[l]
        merged = (gates[:, :, 0] * (o_hg @ wb[:HG_WIDTH])
                  + gates[:, :, 1] * (o_rw @ wb[HG_WIDTH:HG_WIDTH + RW_WIDTH])
                  + gates[:, :, 2] * (o_s5 @ wb[HG_WIDTH + RW_WIDTH:]))
        h = h + (merged.astype(h.dtype) @ w_out[l]).astype(h.dtype)
        hn = rmsnorm(h, ffn_norm[l])
        h = h + ((jax.nn.silu(hn @ ffn_w_gate[l]) * (hn @ ffn_w_up[l])) @ ffn_w_down[l]).astype(h.dtype)
    return rmsnorm(h, final_norm)
```

```python
import numpy as np
import concourse.bass as bass
import concourse.mybir as mybir

F32 = mybir.dt.float32
BF16 = mybir.dt.bfloat16
I32 = mybir.dt.int32
AF = mybir.ActivationFunctionType
ALU = mybir.AluOpType
AX = mybir.AxisListType


class Buf:
    __slots__ = ("t", "lw", "rd", "name", "pe_rg")

    def __init__(self, t, name=""):
        self.t = t
        self.lw = None
        self.rd = []
        self.name = name
        self.pe_rg = None

    def v(self):
        return View((self,), self.t.ap())

    def __getitem__(self, idx):
        return View((self,), self.t.ap()[idx])


class View:
    __slots__ = ("bufs", "ap")

    def __init__(self, bufs, ap):
        self.bufs = bufs
        self.ap = ap

    def __getitem__(self, idx):
        return View(self.bufs, self.ap[idx])

    def rr(self, pat, **kw):
        return View(self.bufs, self.ap.rearrange(pat, **kw))

    def bc(self, shape):
        return View(self.bufs, self.ap.to_broadcast(list(shape)))

    def bitcast(self, dt):
        return View(self.bufs, self.ap.bitcast(dt))

    @property
    def shape(self):
        return tuple(self.ap.shape)


def _bufs(*xs):
    out = []
    for x in xs:
        if isinstance(x, View):
            for b in x.bufs:
                if b not in out:
                    out.append(b)
    return out


def _ap(x):
    return x.ap if isinstance(x, View) else x


class Sched:
    ND = 8

    def __init__(self, nc):
        self.nc = nc
        self.eng = dict(pe=nc.tensor, dve=nc.vector, act=nc.scalar, pool=nc.gpsimd, sp=nc.sync)
        self.csem = {}
        self.cnt = {}
        for e in ("pe", "dve", "act", "pool"):
            self.csem[e] = nc.alloc_semaphore("cs_" + e)
            self.cnt[e] = 0
        self.dsem = {}
        self.dcnt = {}
        self.dk = {}
        for q in ("sp", "pool"):
            self.dsem[q] = [nc.alloc_semaphore("ds_%s%d" % (q, i)) for i in range(self.ND)]
            self.dcnt[q] = [0] * self.ND
            self.dk[q] = 0
        self.seen = {e: {} for e in self.eng}
        import os as _os
        self.nowait_same = set(_os.environ.get("NOWAIT", "pe").split(","))
        self.ninst = 0
        self.nwait = 0
        self.per = {e: 0 for e in self.eng}

    def sb(self, name, shape, dt=F32):
        return Buf(self.nc.alloc_sbuf_tensor(name, list(shape), dt), name)

    def ps(self, name, shape, dt=F32):
        return Buf(self.nc.alloc_psum_tensor(name, list(shape), dt), name)

    def dram(self, name, shape, dt=F32, kind="Internal"):
        return Buf(self.nc.dram_tensor(name, list(shape), dt, kind=kind), name)

    def _wait(self, e, tok, force=False):
        if tok is None:
            return
        sem, val, key, owner = tok
        if owner == e and e in self.nowait_same and not force:
            return
        if self.seen[e].get(key, 0) >= val:
            return
        self.eng[e].wait_ge(sem, val)
        self.seen[e][key] = val
        self.nwait += 1

    def _deps(self, e, reads, writes):
        for b in reads:
            self._wait(e, b.lw)
        for b in writes:
            self._wait(e, b.lw)
            for r in b.rd:
                self._wait(e, r)

    def _commit(self, tok, reads, writes):
        for b in reads:
            if b in writes:
                continue
            b.rd.append(tok)
            if len(b.rd) > 24:
                latest = {}
                for t in b.rd:
                    if t[2] not in latest or latest[t[2]][1] < t[1]:
                        latest[t[2]] = t
                b.rd = list(latest.values())
        for b in writes:
            b.lw = tok
            b.rd = []

    def op(self, e, fn, reads=(), writes=()):
        self._deps(e, reads, writes)
        inst = fn(self.eng[e])
        self.cnt[e] += 1
        inst.then_inc(self.csem[e], 1)
        tok = (self.csem[e], self.cnt[e], "c" + e, e)
        self._commit(tok, reads, writes)
        self.ninst += 1
        self.per[e] += 1
        return tok

    def dma(self, out, in_, q="sp", **kw):
        reads = _bufs(in_)
        writes = _bufs(out)
        k = self.dk[q]
        self.dk[q] += 1
        i = k % self.ND
        sem = self.dsem[q][i]
        key = "d%s%d" % (q, i)
        prev = self.dcnt[q][i]
        if prev > 0 and self.seen[q].get(key, 0) < 16 * prev:
            self.eng[q].wait_ge(sem, 16 * prev)
            self.seen[q][key] = 16 * prev
        self._deps(q, reads, writes)
        inst = self.eng[q].dma_start(out=_ap(out), in_=_ap(in_), **kw)
        self.dcnt[q][i] += 1
        inst.then_inc(sem, 16)
        tok = (sem, 16 * self.dcnt[q][i], key, "dma" + q)
        self._commit(tok, reads, writes)
        self.ninst += 1
        self.per[q] += 1
        return tok

    def finish(self, bufs):
        for b in bufs:
            self._wait("sp", b.lw)
        for e in ("pe", "dve", "act", "pool"):
            if self.cnt[e] > 0:
                self._wait("sp", (self.csem[e], self.cnt[e], "c" + e, e))
        for q in ("sp", "pool"):
            for i in range(self.ND):
                if self.dcnt[q][i] > 0:
                    self._wait("sp", (self.dsem[q][i], 16 * self.dcnt[q][i], "d%s%d" % (q, i), "dma" + q))

    def act(self, out, in_, func, bias=None, scale=None, accum=None, e="act"):
        kw = {}
        if bias is not None:
            kw["bias"] = _ap(bias)
        if scale is not None:
            kw["scale"] = _ap(scale)
        if accum is not None:
            kw["accum_out"] = _ap(accum)
        return self.op(e, lambda g: g.activation(out=_ap(out), in_=_ap(in_), func=func, **kw),
                       _bufs(in_, bias, scale), _bufs(out, accum))

    def tt(self, out, a, b, op, e="dve"):
        return self.op(e, lambda g: g.tensor_tensor(out=_ap(out), in0=_ap(a), in1=_ap(b), op=op),
                       _bufs(a, b), _bufs(out))

    def ts(self, out, a, s1, op0, s2=None, op1=None, e="dve"):
        if op1 is None:
            return self.op(e, lambda g: g.tensor_scalar(out=_ap(out), in0=_ap(a), scalar1=_ap(s1), scalar2=None, op0=op0),
                           _bufs(a, s1), _bufs(out))
        return self.op(e, lambda g: g.tensor_scalar(out=_ap(out), in0=_ap(a), scalar1=_ap(s1), scalar2=_ap(s2), op0=op0, op1=op1),
                       _bufs(a, s1, s2), _bufs(out))

    def stt(self, out, a, s, b, op0, op1):
        return self.op("dve", lambda g: g.scalar_tensor_tensor(out=_ap(out), in0=_ap(a), scalar=_ap(s), in1=_ap(b), op0=op0, op1=op1),
                       _bufs(a, s, b), _bufs(out))

    def copy(self, out, in_, e="dve"):
        if e == "act":
            return self.act(out, in_, AF.Copy)
        return self.op(e, lambda g: g.tensor_copy(out=_ap(out), in_=_ap(in_)), _bufs(in_), _bufs(out))

    def memset(self, out, val, e="pool"):
        return self.op(e, lambda g: g.memset(_ap(out), val), [], _bufs(out))

    def recip(self, out, in_, e="dve"):
        return self.op(e, lambda g: g.reciprocal(out=_ap(out), in_=_ap(in_)), _bufs(in_), _bufs(out))

    def scan(self, out, d0, d1, init, op0=ALU.mult, op1=ALU.add):
        return self.op("dve", lambda g: g.tensor_tensor_scan(out=_ap(out), data0=_ap(d0), data1=_ap(d1), initial=_ap(init), op0=op0, op1=op1),
                       _bufs(d0, d1, init), _bufs(out))

    def mm(self, out, lhsT, rhs, start=True, stop=True):
        la = _ap(lhsT)
        rg = (la.base_partition(), la.partition_size())
        for b in _bufs(out):
            if b.pe_rg is not None and b.pe_rg != rg and b.lw is not None and b.lw[3] == "pe":
                self._wait("pe", b.lw, force=True)
            b.pe_rg = rg
        return self.op("pe", lambda g: g.matmul(_ap(out), lhsT=_ap(lhsT), rhs=_ap(rhs), start=start, stop=stop),
                       _bufs(lhsT, rhs), _bufs(out))

    def tr(self, out, in_, ident):
        return self.op("pe", lambda g: g.transpose(out=_ap(out), in_=_ap(in_), identity=_ap(ident)),
                       _bufs(in_, ident), _bufs(out))

    def asel(self, out, in_, pattern, cmp, fill, base, cm):
        return self.op("pool", lambda g: g.affine_select(out=_ap(out), in_=_ap(in_), pattern=pattern, compare_op=cmp, fill=fill, base=base, channel_multiplier=cm),
                       _bufs(in_), _bufs(out))


class Ring:
    def __init__(self, S, name, n, shape, dt=F32):
        self.tiles = [S.sb("%s%d" % (name, i), shape, dt) for i in range(n)]
        self.free = list(self.tiles)
        self.name = name

    def get(self):
        assert self.free, "ring %s exhausted" % self.name
        return self.free.pop(0)

    def put(self, *ts):
        for t in ts:
            assert t not in self.free
            self.free.append(t)

import numpy as np

D = 1024
NTOK = 512
FH = 2816
NHT = FH // 128
INW = 11552
EPS = 1e-6

OFF_A = 0
OFF_B = 1024
OFF_CV = OFF_B + 8 * 384
OFF_CP = OFF_CV + 1024
OFF_CL = OFF_CP + 8 * 256
OFF_D = OFF_CL + 288
OFF_E = OFF_D + 1024
assert OFF_E + 3072 == INW


def perm_cols():
    p = []
    HG = 0
    p += list(range(HG + 2048, HG + 3072))
    for h in range(8):
        p += list(range(HG + h * 128, HG + (h + 1) * 128))
        p += list(range(HG + 1024 + h * 128, HG + 1024 + (h + 1) * 128))
        p += list(range(HG + 3072 + h * 128, HG + 3072 + (h + 1) * 128))
    RW = 4096
    p += list(range(RW + 2048, RW + 3072))
    for q in range(8):
        p += list(range(RW + q * 128, RW + (q + 1) * 128))
        p += list(range(RW + 1024 + q * 128, RW + 1024 + (q + 1) * 128))
    p += list(range(RW + 3072, RW + 3360))
    S5 = RW + 3360
    p += list(range(S5, S5 + 1024))
    p += list(range(S5 + 1024, S5 + 1024 + 3072))
    p = np.array(p, dtype=np.int64)
    assert p.shape[0] == INW and len(set(p.tolist())) == INW
    return p


VEC_NAMES = ["mix_norm", "ffn_norm", "hg_lb_logits", "hg_onorm", "rw_w0", "rw_a0", "rw_v0", "rw_k_k", "rw_k_a",
             "rw_r_k", "rw_ln_w", "rw_ln_b", "s5_d", "s5_glu_b", "final_norm"]
NV = len(VEC_NAMES)
VI = {n: i for i, n in enumerate(VEC_NAMES)}


class K:
    pass


def build(L, T, dbg=(), stub=()):
    NTT = T // NTOK
    nc = bass.Bass("TRN2", target_bir_lowering=False)
    S = Sched(nc)
    k = K()
    k.S = S
    k.nc = nc
    k.L = L
    k.T = T

    def ext(name, shape):
        return Buf(nc.dram_tensor(name, list(shape), F32, kind="ExternalInput"), name)

    xT = ext("xT", [D, T])
    w_in = ext("w_in", [L, D, INW])
    w_branch = ext("w_branch", [L, 3 * D, D])
    w_out = ext("w_out", [L, D, D])
    w_gate = ext("ffn_w_gate", [L, D, FH])
    w_up = ext("ffn_w_up", [L, D, FH])
    w_down = ext("ffn_w_down", [L, FH, D])
    vecs = ext("vecs", [L, 128, NV * 8])
    out = Buf(nc.dram_tensor("out", [D, T], F32, kind="ExternalOutput"), "out")
    dbg_out = {}
    for name in dbg:
        dbg_out[name] = Buf(nc.dram_tensor("dbg_" + name, [D, T], F32, kind="ExternalOutput"), "dbg_" + name)

    class WT:
        def __init__(self, name, src, blocks):
            self.name = name
            self.src = src
            self.blocks = {}
            off = 0
            for (r0, kc, c0, n) in blocks:
                self.blocks[(r0, c0)] = (off, kc, n)
                off += kc * n
            self.total = off
            self.scr = S.dram(name + "_t", [L, 128, off], BF16)

        def __getitem__(self, idx):
            l, rs, cs = idx
            r0 = 0 if rs.start is None else rs.start
            return ("wt", self, l, r0, cs.start)

    in_blocks = [(0, 8, 0, 512), (0, 8, 512, 512)] + [(0, 8, OFF_B + 384 * h, 384) for h in range(8)] \
        + [(0, 8, OFF_CV + 512 * i, 512) for i in range(2)] + [(0, 8, OFF_CP + 512 * i, 512) for i in range(4)] \
        + [(0, 8, OFF_CL, 288)] + [(0, 8, OFF_D + 512 * i, 512) for i in range(2)] + [(0, 8, OFF_E + 512 * i, 512) for i in range(6)]
    w_in_b = WT("w_in_b", w_in, in_blocks)
    w_branch_b = WT("w_branch_b", w_branch, [(br * D, 8, c0, 512) for br in range(3) for c0 in (0, 512)])
    w_out_b = WT("w_out_b", w_out, [(0, 8, 0, 512), (0, 8, 512, 512)])
    fblocks = [(0, 8, 512 * i, 512) for i in range(5)] + [(0, 8, 2560, 256)]
    w_gate_b = WT("w_gate_b", w_gate, fblocks)
    w_up_b = WT("w_up_b", w_up, fblocks)
    w_down_b = WT("w_down_b", w_down, [(0, NHT, 128 * i, 128) for i in range(8)])
    hT = S.dram("hT", [D, T], F32)

    def conv_items(l):
        items = []
        for wt in (w_in_b, w_branch_b, w_out_b, w_gate_b, w_up_b, w_down_b):
            for (r0, c0), (off, kc, n) in wt.blocks.items():
                for c in range(kc):
                    items.append((wt, l, r0 + c * 128, c0, n, off + c * n))
        return items

    def do_conv(items):
        for it in items:
            (wt, l, r, c0, n, off) = it
            i = k.cvi % 3
            k.cvi += 1
            S.dma(cst32[i][:, 0:n], wt.src[l, r:r + 128, c0:c0 + n])
            S.copy(cst16[i][:, 0:n], cst32[i][:, 0:n], e="pool")
            k.cvpend.append((wt.scr[l, :, off:off + n], cst16[i][:, 0:n]))
            if len(k.cvpend) > 1:
                d, sv = k.cvpend.pop(0)
                S.dma(d, sv)

    def flush_conv():
        while k.cvpend:
            d, sv = k.cvpend.pop(0)
            S.dma(d, sv)

    k.cvpend = []
    k.cvi = 0
    cst32 = [S.sb("cst32_%d" % i, [128, 512]) for i in range(3)]
    cst16 = [S.sb("cst16_%d" % i, [128, 512], BF16) for i in range(3)]

    ident = S.sb("ident", [128, 128])
    identb = S.sb("identb", [128, 128], BF16)
    onesb = S.sb("onesb", [128, 128], BF16)
    onesf = S.sb("onesf", [128, 128])
    vec = S.sb("vec", [128, NV, 8])
    xn = S.sb("xn", [128, 8, NTOK], BF16)
    merged = S.sb("merged", [128, 8, NTOK])
    ocur = S.sb("ocur", [128, 8, NTOK], BF16)
    wbufs = [S.sb("wbuf%d" % i, [128, 8 * 512], BF16) for i in range(3)]
    k.wi = 0
    R32 = Ring(S, "r32_", 14, [128, NTOK])
    R16 = Ring(S, "r16_", 8, [128, NTOK], BF16)
    BIG = [S.sb("big%d" % i, [128, NTOK]) for i in range(20)]
    psb = [S.ps("pb%d" % i, [128, 512]) for i in range(8)]
    k.pi = 0

    def ps():
        b = psb[k.pi % 7]
        k.pi += 1
        return b
    psheld = psb[7]
    k.ppi = {"a": 0, "b": 0}

    def ps_pool(name):
        if name == "a":
            b = psb[k.ppi["a"] % 4]
        else:
            b = psb[4 + k.ppi["b"] % 3]
        k.ppi[name] += 1
        return b

    def wbuf():
        b = wbufs[k.wi % 3]
        k.wi += 1
        return b

    def load_w(desc, kc, ncols):
        _, wt, l, r0, c0 = desc
        off, kc_, n_ = wt.blocks[(r0, c0)]
        assert kc_ == kc and n_ == ncols, (wt.name, r0, c0, kc, ncols, kc_, n_)
        b = wbuf()
        v = b.v()[:, 0:kc * ncols]
        S.dma(v, wt.scr[l, :, off:off + kc * ncols])
        return v.rr("p (c n) -> p c n", c=kc)

    S.memset(onesf.v(), 1.0)
    S.memset(onesb.v(), 1.0)
    S.asel(ident.v(), onesf.v(), [[-1, 128]], ALU.is_equal, 0.0, 0, 1)
    S.copy(identb.v(), ident.v(), e="pool")

    vecs_all = [vecs[l] for l in range(L)] + [vecs[L - 1]] * (4 - L)
    k.__dict__.update(locals())

    do_conv(conv_items(0))
    flush_conv()

    for c in range(8):
        S.dma(hT[c * 128:(c + 1) * 128, :], xT[c * 128:(c + 1) * 128, :])

    def rmsnorm_tile(src_dram, l, j, gidx):
        t0 = j * NTOK
        hts = []
        pss = ps()
        for c in range(8):
            ht = R32.get()
            S.dma(ht.v(), src_dram[c * 128:(c + 1) * 128, t0:t0 + NTOK])
            sq = R16.get()
            S.act(sq.v(), ht.v(), AF.Square)
            S.mm(pss.v(), onesb.v(), sq.v(), start=(c == 0), stop=(c == 7))
            R16.put(sq)
            hts.append(ht)
        rstd = R32.get()
        S.act(rstd.v(), pss.v(), AF.Sqrt, bias=k.epsb[:, 0:1], scale=1.0 / D)
        S.recip(rstd.v(), rstd.v())
        for c in range(8):
            S.stt(xn[:, c, :], hts[c].v(), vec[:, gidx, c:c + 1], rstd.v(), ALU.mult, ALU.mult)
            R32.put(hts[c])
        R32.put(rstd)

    k.rmsnorm_tile = rmsnorm_tile
    epsb = S.sb("epsb", [128, 4])
    S.memset(epsb[:, 0:1], EPS)
    S.memset(epsb[:, 1:2], 0.0)
    k.epsb = epsb

    def ffn_tile(l, j):
        t0 = j * NTOK
        rmsnorm_tile(hT, l, j, VI["ffn_norm"])
        def act_tile(ht):
            b = BIG[ht // 2]
            return b.v().bitcast(BF16)[:, (ht % 2) * NTOK:(ht % 2 + 1) * NTOK]
        for blk in range(6):
            c0 = blk * 512
            ncol = min(512, FH - c0)
            wg = load_w(w_gate_b[l, :, c0:c0 + ncol], 8, ncol)
            wu = load_w(w_up_b[l, :, c0:c0 + ncol], 8, ncol)
            for s in range(ncol // 128):
                ht = (c0 // 128) + s
                pg = ps()
                pu = ps()
                for c in range(8):
                    S.mm(pg.v(), wg[:, c, s * 128:(s + 1) * 128], xn[:, c, :], start=(c == 0), stop=(c == 7))
                for c in range(8):
                    S.mm(pu.v(), wu[:, c, s * 128:(s + 1) * 128], xn[:, c, :], start=(c == 0), stop=(c == 7))
                sg = R32.get()
                S.act(sg.v(), pg.v(), AF.Silu)
                S.tt(act_tile(ht), sg.v(), pu.v(), ALU.mult)
                R32.put(sg)
        for dt_ in range(8):
            wd = load_w(w_down_b[l, :, dt_ * 128:(dt_ + 1) * 128], NHT, 128)
            po = ps()
            for ht in range(NHT):
                S.mm(po.v(), wd[:, ht, :], act_tile(ht), start=(ht == 0), stop=(ht == NHT - 1))
            hres = R32.get()
            S.dma(hres.v(), hT[dt_ * 128:(dt_ + 1) * 128, t0:t0 + NTOK])
            S.tt(hres.v(), hres.v(), po.v(), ALU.add)
            S.dma(hT[dt_ * 128:(dt_ + 1) * 128, t0:t0 + NTOK], hres.v())
            R32.put(hres)

    k.ffn_tile = ffn_tile

    def merge_branch(l, j, br, first):
        for half in range(2):
            wb = load_w(w_branch_b[l, br * D:(br + 1) * D, half * 512:(half + 1) * 512], 8, 512)
            wg = load_w(w_in_b[l, :, OFF_E + br * D + half * 512: OFF_E + br * D + (half + 1) * 512], 8, 512)
            for s in range(4):
                dt_ = half * 4 + s
                pg = ps()
                pb = ps()
                for c in range(8):
                    S.mm(pg.v(), wg[:, c, s * 128:(s + 1) * 128], xn[:, c, :], start=(c == 0), stop=(c == 7))
                for c in range(8):
                    S.mm(pb.v(), wb[:, c, s * 128:(s + 1) * 128], ocur[:, c, :], start=(c == 0), stop=(c == 7))
                sg = R32.get()
                S.act(sg.v(), pg.v(), AF.Sigmoid)
                if first:
                    S.tt(merged[:, dt_, :], sg.v(), pb.v(), ALU.mult)
                else:
                    S.tt(sg.v(), sg.v(), pb.v(), ALU.mult)
                    S.tt(merged[:, dt_, :], merged[:, dt_, :], sg.v(), ALU.add, e="pool")
                R32.put(sg)

    k.merge_branch = merge_branch

    def wout_tile(l, j):
        t0 = j * NTOK
        for c in range(8):
            S.copy(ocur[:, c, :], merged[:, c, :], e=("act" if c % 2 else "dve"))
        for half in range(2):
            wo = load_w(w_out_b[l, :, half * 512:(half + 1) * 512], 8, 512)
            for s in range(4):
                dt_ = half * 4 + s
                po = ps()
                for c in range(8):
                    S.mm(po.v(), wo[:, c, s * 128:(s + 1) * 128], ocur[:, c, :], start=(c == 0), stop=(c == 7))
                hres = R32.get()
                S.dma(hres.v(), hT[dt_ * 128:(dt_ + 1) * 128, t0:t0 + NTOK])
                S.tt(hres.v(), hres.v(), po.v(), ALU.add)
                S.dma(hT[dt_ * 128:(dt_ + 1) * 128, t0:t0 + NTOK], hres.v())
                R32.put(hres)

    k.wout_tile = wout_tile
    return k


def finalize(k):
    S = k.S
    L, T = k.L, k.T
    for j in range(T // NTOK):
        t0 = j * NTOK
        hts = []
        pss = k.ps()
        for c in range(8):
            ht = k.R32.get()
            S.dma(ht.v(), k.hT[c * 128:(c + 1) * 128, t0:t0 + NTOK])
            sq = k.R16.get()
            S.act(sq.v(), ht.v(), AF.Square)
            S.mm(pss.v(), k.onesb.v(), sq.v(), start=(c == 0), stop=(c == 7))
            k.R16.put(sq)
            hts.append(ht)
        rstd = k.R32.get()
        S.act(rstd.v(), pss.v(), AF.Sqrt, bias=k.epsb[:, 0:1], scale=1.0 / D)
        S.recip(rstd.v(), rstd.v())
        for c in range(8):
            S.stt(hts[c].v(), hts[c].v(), k.vec[:, VI["final_norm"], c:c + 1], rstd.v(), ALU.mult, ALU.mult)
            S.dma(k.out[c * 128:(c + 1) * 128, t0:t0 + NTOK], hts[c].v())
            k.R32.put(hts[c])
        k.R32.put(rstd)
    S.finish([k.out] + list(k.dbg_out.values()))


def stub_mixer(k, l, j, col0):
    S = k.S
    for half in range(2):
        w = k.load_w(k.w_in_b[l, :, col0 + half * 512: col0 + (half + 1) * 512], 8, 512)
        for s in range(4):
            p = k.ps()
            for c in range(8):
                S.mm(p.v(), w[:, c, s * 128:(s + 1) * 128], k.xn[:, c, :], start=(c == 0), stop=(c == 7))
            S.copy(k.ocur[:, half * 4 + s, :], p.v(), e="act")


def run_layers(k, mixers):
    S = k.S
    for l in range(k.L):
        S.dma(k.vec.v().rr("p v c -> p (v c)"), k.vecs[l])
        nt = k.T // NTOK
        items = k.conv_items(l + 1) if l + 1 < k.L else []
        per = (len(items) + nt - 1) // nt
        for j in range(nt):
            k.do_conv(items[j * per:(j + 1) * per])
            k.rmsnorm_tile(k.hT, l, j, VI["mix_norm"])
            for bi, mx in enumerate(mixers):
                mx(k, l, j)
                k.merge_branch(l, j, bi, bi == 0)
            k.wout_tile(l, j)
            k.ffn_tile(l, j)
    finalize(k)


def hg_setup(k):
    S = k.S
    L = k.L
    k.resetmask = S.sb("resetmask", [128, NTOK])
    S.memset(k.resetmask.v(), 1.0)
    S.memset(k.resetmask.v().rr("p (c t) -> p c t", t=64)[:, :, 0:1], 0.0)
    k.maskincl = S.sb("maskincl", [128, 64])
    k.maskstr = S.sb("maskstr", [128, 64])
    for half in range(2):
        sl = slice(half * 64, half * 64 + 64)
        S.asel(k.maskincl[sl, :], k.onesf[sl, 0:64], [[1, 64]], ALU.is_ge, 0.0, 0, -1)
        S.asel(k.maskstr[sl, :], k.onesf[sl, 0:64], [[1, 64]], ALU.is_gt, 0.0, 0, -1)
    k.lball = S.sb("lball", [128, 4, 8])
    k.omlall = S.sb("omlall", [128, 4, 8])
    E = S.sb("lbE", [128, 4, 8])
    sm = S.sb("lbsum", [128, 8])
    c0 = VI["hg_lb_logits"] * 8
    S.memset(E.v(), 0.0)
    for l in range(4):
        S.dma(E[:, l, :], k.vecs_all[min(l, L - 1) if False else l][:, c0:c0 + 8])
    S.act(E.v(), E.v(), AF.Exp)
    S.tt(sm.v(), E[:, 0, :], E[:, 1, :], ALU.add)
    S.tt(sm.v(), sm.v(), E[:, 2, :], ALU.add)
    S.tt(sm.v(), sm.v(), E[:, 3, :], ALU.add)
    S.recip(sm.v(), sm.v())
    for l in range(4):
        S.tt(E[:, l, :], E[:, l, :], sm.v(), ALU.mult)
    S.memset(k.lball[:, 0, :], 0.0, e="dve")
    for l in range(1, 4):
        S.tt(k.lball[:, l, :], k.lball[:, l - 1, :], E[:, l, :], ALU.add)
    S.ts(k.lball.v(), k.lball.v(), 0.0, ALU.max)
    S.ts(k.omlall.v(), k.lball.v(), -1.0, ALU.mult, 1.0, ALU.add)
    k.hgS = S.sb("hgS", [128, 8, 128])
    k.hgSb = S.sb("hgSb", [128, 8, 128], BF16)
    k.sct = [S.sb("hgsct%d" % i, [128, 64], BF16) for i in range(4)]
    k.scti = 0


def hg_layer_init(k, l):
    k.S.memset(k.hgS.v(), 0.0)
    k.S.memset(k.hgSb.v(), 0.0)


def hgrn2_tile(k, l, j):
    S = k.S
    R32, R16, BIG, xn, ps = k.R32, k.R16, k.BIG, k.xn, k.ps
    wib = k.w_in_b
    vtm = [BIG[tb].v().bitcast(BF16) for tb in range(4)]
    for half in range(2):
        w = k.load_w(wib[l, :, OFF_A + half * 512: OFF_A + (half + 1) * 512], 8, 512)
        for tb in range(4):
            p = ps()
            for c in range(8):
                S.mm(p.v(), xn[:, c, tb * 128:(tb + 1) * 128], w[:, c, :], start=(c == 0), stop=(c == 7))
            S.copy(vtm[tb][:, half * 512:(half + 1) * 512], p.v(), e="act")
    for h in range(8):
        w = k.load_w(wib[l, :, OFF_B + h * 384: OFF_B + (h + 1) * 384], 8, 384)
        lbv = k.lball[:, l, h:h + 1]
        omlv = k.omlall[:, l, h:h + 1]

        def proj(col0):
            p = ps()
            for c in range(8):
                S.mm(p.v(), w[:, c, col0:col0 + 128], xn[:, c, :], start=(c == 0), stop=(c == 7))
            return p
        pq = proj(0)
        q = R32.get()
        S.act(q.v(), pq.v(), AF.Silu)
        pf = proj(128)
        f = R32.get()
        S.act(f.v(), pf.v(), AF.Sigmoid)
        S.ts(f.v(), f.v(), omlv, ALU.mult, lbv, ALU.add)
        lf = R32.get()
        S.act(lf.v(), f.v(), AF.Ln)
        b = R32.get()
        S.scan(b.v(), k.resetmask.v(), lf.v(), 0.0)
        R32.put(lf)
        kk = R32.get()
        S.ts(kk.v(), f.v(), -1.0, ALU.mult, 1.0, ALU.add, e="pool")
        R32.put(f)
        eb = R32.get()
        S.act(eb.v(), b.v(), AF.Exp)
        enb = R32.get()
        S.act(enb.v(), b.v(), AF.Exp, scale=-1.0)
        R32.put(b)
        qt = R16.get()
        S.tt(qt.v(), q.v(), eb.v(), ALU.mult)
        R32.put(q)
        ktf = R32.get()
        S.tt(ktf.v(), kk.v(), enb.v(), ALU.mult, e="pool")
        R32.put(kk, enb)
        ktb = R16.get()
        S.copy(ktb.v(), ktf.v(), e="pool")
        kdT = R16.get()
        eb3 = eb.v().rr("p (c t) -> p c t", t=64)
        S.tt(kdT.v().rr("p (c t) -> p c t", t=64), ktf.v().rr("p (c t) -> p c t", t=64),
             eb3[:, :, 63:64].bc([128, 8, 64]), ALU.mult)
        R32.put(ktf)
        ptr = ps()
        ptb = ptr.v().bitcast(BF16)
        for blk in range(4):
            S.tr(ptb[:, blk * 128:(blk + 1) * 128], kdT[:, blk * 128:(blk + 1) * 128], k.identb.v())
        kdtm = R16.get()
        S.copy(kdtm.v(), ptb[:, 0:512], e="act")
        R16.put(kdT)
        pog = proj(256)
        ogs = R32.get()
        S.act(ogs.v(), pog.v(), AF.Silu)
        osb = R32.get()
        for c in range(8):
            cs = slice(c * 64, (c + 1) * 64)
            pb = (c % 2) * 64
            rows = slice(pb, pb + 64)
            kdc = kdtm.v().rr("p (b n) -> p b n", b=4)[rows, c // 2, :]
            vch = vtm[c // 2][rows, h * 128:(h + 1) * 128]
            p1 = ps()
            S.mm(p1[rows, 0:64], ktb[:, cs], qt[:, cs])
            sct = k.sct[k.scti % 4]
            k.scti += 1
            S.tt(sct[rows, :], p1[rows, 0:64], k.maskincl[rows, :], ALU.mult)
            p2 = ps()
            S.mm(p2[:, 0:64], vch, sct[rows, :], start=True, stop=False)
            S.mm(p2[:, 0:64], k.hgSb[:, h, :], qt[:, cs], start=False, stop=True)
            S.copy(osb[:, cs], p2[:, 0:64], e="act")
            p3 = ps()
            S.mm(p3[:, 0:128], kdc, vch)
            S.stt(k.hgS[:, h, :], k.hgS[:, h, :], eb[:, c * 64 + 63:c * 64 + 64], p3[:, 0:128], ALU.mult, ALU.add)
            S.copy(k.hgSb[:, h, :], k.hgS[:, h, :], e="pool")
        R32.put(eb)
        R16.put(qt, ktb, kdtm)
        sq = R16.get()
        S.act(sq.v(), osb.v(), AF.Square)
        pn = ps()
        S.mm(pn.v(), k.onesb.v(), sq.v())
        R16.put(sq)
        rstd = R32.get()
        S.act(rstd.v(), pn.v(), AF.Sqrt, bias=k.epsb[:, 0:1], scale=1.0 / 128)
        S.recip(rstd.v(), rstd.v())
        S.stt(osb.v(), osb.v(), k.vec[:, VI["hg_onorm"], h:h + 1], rstd.v(), ALU.mult, ALU.mult)
        S.tt(k.ocur[:, h, :], osb.v(), ogs.v(), ALU.mult, e="pool")
        R32.put(rstd, ogs, osb)
    if "hg" in k.dbg_out:
        t0 = j * NTOK
        for h in range(8):
            tmp = R32.get()
            S.copy(tmp.v(), k.ocur[:, h, :], e="pool")
            S.dma(k.dbg_out["hg"][h * 128:(h + 1) * 128, t0:t0 + NTOK], tmp.v())
            R32.put(tmp)


C0 = 0.6065306597126334
GN_EPS = 64e-5


def rw_setup(k):
    S = k.S
    nc = k.nc
    L, T = k.L, k.T

    def ext(name, shape):
        return Buf(nc.dram_tensor(name, list(shape), F32, kind="ExternalInput"), name)
    k.rw_w2 = ext("rw_w2", [L, 64, D])
    k.rw_a2 = ext("rw_a2", [L, 64, D])
    k.rw_g2 = ext("rw_g2", [L, 160, D])
    k.rw_v1 = ext("rw_v1", [L, D, 32])
    k.rw_v2 = ext("rw_v2", [L, 32, D])
    k.rwmu_d = ext("rwmu", [L, 128, 27])
    k.vfirst = S.dram("vfirst", [D, T])
    k.vdram = S.dram("vdram", [D, NTOK])
    k.w2b = S.sb("wa2b", [128, D], BF16)
    k.a2b = k.w2b
    k.g2b = S.sb("g2b", [128, 2, D], BF16)
    k.v1b = S.sb("v1b", [128, 8, 32], BF16)
    k.rwmu = S.sb("rwmu_s", [128, 27])
    k.omka = S.sb("omka", [128, 8])
    k.rwP = S.sb("rwP", [128, 8, 64])
    k.rwPb = S.sb("rwPb", [128, 8, 64], BF16)
    k.rwcarry = S.sb("rwcarry", [128, 27])
    k.zx = [S.sb("zx%d" % i, [128, NTOK + 8]) for i in range(2)]
    k.zxi = 0
    k.blockb = S.sb("blockb", [128, 128], BF16)
    k.blockf = S.sb("blockf", [128, 128])
    for t in (k.blockb, k.blockf):
        S.memset(t.v(), 0.0)
        S.memset(t[0:64, 0:64], 1.0)
        S.memset(t[64:128, 64:128], 1.0)
    k.mask4 = S.sb("mask4", [128, 128])
    S.copy(k.mask4[:, 0:64], k.maskstr.v(), e="pool")
    S.copy(k.mask4[:, 64:128], k.maskincl.v(), e="pool")
    k.msb = [S.sb("msb%d" % i, [128, 128], BF16) for i in range(32)]
    k.iws = [S.sb("iws%d" % i, [128, 128], BF16) for i in range(40)]
    for t in k.iws:
        S.memset(t.v(), 0.0)
    k.ttb = [S.sb("ttb%d" % i, [128, 128], BF16) for i in range(16)]
    k.zsb = [S.sb("zsb%d" % i, [128, 128], BF16) for i in range(4)]
    k.gC = [S.sb("gC%d" % i, [128, 8]) for i in range(4)]
    k.tiny = S.sb("rwtiny", [128, 1])
    S.memset(k.tiny.v(), 1e-24)
    k.gneps = S.sb("gneps", [128, 1])
    S.memset(k.gneps.v(), GN_EPS)


def rw_layer_init(k, l):
    S = k.S
    st = k.cst32

    def ld(dst_view, src_view, rows, n, i, rbase=0):
        for h0 in range(0, n, 512):
            S.dma(st[i][rbase:rbase + rows, 0:512], src_view[:, h0:h0 + 512], q="pool")
            S.copy(dst_view[:, h0:h0 + 512], st[i][rbase:rbase + rows, 0:512], e="pool")
    ld(k.w2b[0:64, :], k.rw_w2[l], 64, D, 0)
    ld(k.a2b[64:128, :], k.rw_a2[l], 64, D, 1, rbase=64)
    ld(k.g2b[:, 0, :], k.rw_g2[l, 0:128, :], 128, D, 0)
    ld(k.g2b[0:32, 1, :], k.rw_g2[l, 128:160, :], 32, D, 1)
    ld(k.g2b[32:64, 1, :], k.rw_v2[l], 32, D, 0, rbase=32)
    S.dma(st[1][:, 0:256].rr("p (c n) -> p c n", c=8), k.rw_v1[l].rr("(c p) n -> p c n", p=128))
    S.copy(k.v1b.v(), st[1][:, 0:256].rr("p (c n) -> p c n", c=8), e="pool")
    S.dma(k.rwmu.v(), k.rwmu_d[l])
    S.memset(k.rwP.v(), 0.0)
    S.memset(k.rwPb.v(), 0.0)
    S.memset(k.rwcarry.v(), 0.0)


def rw_layer_vecs(k, l):
    k.S.ts(k.omka.v(), k.vec[:, VI["rw_k_a"], :], -1.0, ALU.mult, 1.0, ALU.add)


def rwkv_tile(k, l, j):
    S = k.S
    R32, R16, BIG, xn, ps = k.R32, k.R16, k.BIG, k.xn, k.ps
    wib = k.w_in_b
    t0 = j * NTOK
    V = lambda name, c: k.vec[:, VI[name], c:c + 1]

    def lerp(pview, ti, n=128):
        zx = k.zx[k.zxi % 2]
        k.zxi += 1
        S.copy(zx[0:n, 1:NTOK + 1], pview, e="act")
        S.copy(zx[0:n, 0:1], k.rwcarry[0:n, ti:ti + 1], e="pool")
        S.copy(k.rwcarry[0:n, ti:ti + 1], zx[0:n, NTOK:NTOK + 1], e="pool")
        d = R32.get()
        S.tt(d[0:n, :], zx[0:n, 0:NTOK], zx[0:n, 1:NTOK + 1], ALU.subtract)
        S.stt(d[0:n, :], d[0:n, :], k.rwmu[0:n, ti:ti + 1], zx[0:n, 1:NTOK + 1], ALU.mult, ALU.add)
        return d

    def projw(w, col0, n=128):
        p = ps()
        for c in range(8):
            S.mm(p[0:n, :], w[:, c, col0:col0 + n], xn[:, c, :], start=(c == 0), stop=(c == 7))
        return p

    w = k.load_w(wib[l, :, OFF_CL:OFF_CL + 288], 8, 288)
    z = lerp(projw(w, 0).v(), 24)
    lora1 = R16.get()
    S.act(lora1[0:64, :], z[0:64, :], AF.Tanh)
    S.copy(lora1[64:128, :], z[64:128, :], e="pool")
    R32.put(z)
    z = lerp(projw(w, 128).v(), 25)
    gsb0 = R16.get()
    S.act(gsb0.v(), z.v(), AF.Sigmoid)
    R32.put(z)
    z = lerp(projw(w, 256, 32)[0:32, :], 26, 32)
    gsb1 = R16.get()
    S.act(gsb1[0:32, :], z[0:32, :], AF.Sigmoid)
    R32.put(z)

    if k.stage == "a":
        R16.put(lora1, gsb0, gsb1)
        return
    vls = []
    for half in range(2):
        w = k.load_w(wib[l, :, OFF_CV + half * 512:OFF_CV + (half + 1) * 512], 8, 512)
        for s in range(4):
            vt = half * 4 + s
            vl = lerp(projw(w, s * 128).v(), vt)
            if l == 0:
                S.dma(k.vfirst[vt * 128:(vt + 1) * 128, t0:t0 + NTOK], vl.v())
                S.dma(k.vdram[vt * 128:(vt + 1) * 128, :], vl.v())
                R32.put(vl)
            else:
                vb = R16.get()
                S.copy(vb.v(), vl.v(), e="pool")
                S.mm(k.psheld[32:64, :], k.v1b[:, vt, :], vb.v(), start=(vt == 0), stop=(vt == 7))
                R16.put(vb)
                vls.append(vl)
    if l > 0:
        vv1 = R16.get()
        S.copy(vv1[32:64, :], k.psheld[32:64, :], e="act")
        for vt in range(8):
            vl = vls[vt]
            p2 = ps()
            S.mm(p2.v(), k.g2b[32:64, 1, vt * 128:(vt + 1) * 128], vv1[32:64, :])
            sv = R32.get()
            S.act(sv.v(), p2.v(), AF.Sigmoid, bias=V("rw_v0", vt))
            vf = R32.get()
            S.dma(vf.v(), k.vfirst[vt * 128:(vt + 1) * 128, t0:t0 + NTOK])
            S.tt(vf.v(), vf.v(), vl.v(), ALU.subtract)
            S.tt(vf.v(), vf.v(), sv.v(), ALU.mult, e="pool")
            S.tt(vl.v(), vl.v(), vf.v(), ALU.add)
            S.dma(k.vdram[vt * 128:(vt + 1) * 128, :], vl.v())
            R32.put(sv, vf, vl)
        R16.put(vv1)

    if k.stage == "b":
        R16.put(lora1, gsb0, gsb1)
        return
    for grp in range(2):
        prbs = []
        yTs = []
        for gi in range(4):
            p = grp * 4 + gi
            AR = BIG[gi].v().bitcast(BF16).rr("p (c s t) -> p c s t", c=8, s=2)
            BK = BIG[4 + gi].v().bitcast(BF16).rr("p (c s t) -> p c s t", c=8, s=2)
            BKtm = BIG[8 + gi].v().bitcast(BF16).rr("p (c n) -> p c n", c=8)
            VU = BIG[12 + gi].v().bitcast(BF16).rr("p (c h v) -> p c h v", c=8, h=2)
            yTs.append(BIG[16 + gi])
            if gi % 2 == 0:
                wp = k.load_w(wib[l, :, OFF_CP + (p // 2) * 512:OFF_CP + (p // 2 + 1) * 512], 8, 512)
            cb = (p % 2) * 256
            rl = lerp(projw(wp, cb).v(), 8 + 2 * p)
            kl = lerp(projw(wp, cb + 128).v(), 9 + 2 * p)
            pw = ps()
            S.mm(pw.v(), k.w2b[0:64, p * 128:(p + 1) * 128], lora1[0:64, :])
            sg = R32.get()
            S.act(sg.v(), pw.v(), AF.Sigmoid, bias=V("rw_w0", p))
            pa = ps()
            S.mm(pa.v(), k.a2b[64:128, p * 128:(p + 1) * 128], lora1[64:128, :])
            a = R32.get()
            S.act(a.v(), pa.v(), AF.Sigmoid, bias=V("rw_a0", p))
            kks = R16.get()
            S.act(kks.v(), kl.v(), AF.Square, scale=V("rw_k_k", p))
            pk = ps()
            S.mm(pk.v(), k.blockb.v(), kks.v())
            R16.put(kks)
            rn = R32.get()
            S.act(rn.v(), pk.v(), AF.Sqrt, bias=k.tiny[:, 0:1])
            S.recip(rn.v(), rn.v())
            kkn = R32.get()
            S.stt(kkn.v(), kl.v(), V("rw_k_k", p), rn.v(), ALU.mult, ALU.mult)
            R32.put(rn)
            kmod = R32.get()
            S.ts(kmod.v(), a.v(), V("rw_k_a", p), ALU.mult, k.omka[:, p:p + 1], ALU.add, e="pool")
            S.tt(kmod.v(), kmod.v(), kl.v(), ALU.mult, e="pool")
            R32.put(kl)
            prb = R16.get()
            S.stt(prb.v(), rl.v(), V("rw_r_k", p), kmod.v(), ALU.mult, ALU.mult)
            prbs.append(prb)
            cs = R32.get()
            S.scan(cs.v(), k.resetmask.v(), sg.v(), 0.0)
            csm = R32.get()
            S.tt(csm.v(), cs.v(), sg.v(), ALU.subtract, e="pool")
            R32.put(sg)
            G = R32.get()
            S.act(G.v(), cs.v(), AF.Exp, scale=-C0)
            G1 = csm
            S.act(G1.v(), csm.v(), AF.Exp, scale=-C0)
            Gi = cs
            S.act(Gi.v(), cs.v(), AF.Exp, scale=C0)
            S.copy(k.gC[gi].v(), G.v().rr("p (c t) -> p c t", t=64)[:, :, 63], e="pool")
            c3 = lambda t: t.v().rr("p (c t) -> p c t", t=64)
            S.stt(AR[:, :, 0, :], c3(kkn), -1.0, c3(G1), ALU.mult, ALU.mult)
            S.tt(AR[:, :, 1, :], c3(rl), c3(G), ALU.mult, e="pool")
            R32.put(rl, G, G1)
            bt = R32.get()
            S.tt(bt.v(), kkn.v(), a.v(), ALU.mult, e="pool")
            S.tt(bt.v(), bt.v(), Gi.v(), ALU.mult)
            kt = kmod
            S.tt(kt.v(), kmod.v(), Gi.v(), ALU.mult, e="pool")
            R32.put(kkn, a, Gi)
            BK2 = BK.rr("p (cb par) s t -> p cb par s t", par=2)
            bt4 = bt.v().rr("p (cb par t) -> p cb par t", par=2, t=64)
            kt4 = kt.v().rr("p (cb par t) -> p cb par t", par=2, t=64)
            S.copy(BK2[:, :, 0, 0, :], kt4[:, :, 0, :], e="act")
            S.copy(BK2[:, :, 0, 1, :], bt4[:, :, 0, :], e="dve")
            S.copy(BK2[:, :, 1, 0, :], bt4[:, :, 1, :], e="act")
            S.copy(BK2[:, :, 1, 1, :], kt4[:, :, 1, :], e="dve")
            R32.put(bt, kt)
            for half in range(2):
                pt = ps()
                ptb = pt.v().bitcast(BF16)
                for cc in range(4):
                    c = half * 4 + cc
                    S.tr(ptb[:, cc * 128:(cc + 1) * 128], BK[:, c, :, :].rr("p s t -> p (s t)"), k.identb.v())
                S.copy(BKtm[:, half * 4:(half + 1) * 4, :].rr("p c n -> p (c n)"), ptb[:, 0:512], e=("act" if half else "dve"))
            vl = R32.get()
            S.dma(vl.v(), k.vdram[p * 128:(p + 1) * 128, :])
            vb = R16.get()
            S.copy(vb.v(), vl.v(), e="pool")
            R32.put(vl)
            pt = ps()
            ptb = pt.v().bitcast(BF16)
            for blk in range(4):
                S.tr(ptb[:, blk * 128:(blk + 1) * 128], vb[:, blk * 128:(blk + 1) * 128], k.identb.v())
            R16.put(vb)
            VU2 = VU.rr("p (cb par) h v -> p cb par (h v)", par=2)
            pt4 = ptb[:, 0:512].rr("p (cb n) -> p cb n", cb=4)
            S.copy(VU2[0:64, :, 0, :], pt4[0:64, :, :], e="act")
            S.copy(VU2[64:128, :, 1, :], pt4[64:128, :, :], e="dve")
            k.rw_pair = getattr(k, "rw_pair", {})
            k.rw_pair[gi] = (AR, BK, BKtm, VU)

        def scores(b):
            msb = {}
            mi = (b % 2) * 16
            for gi in range(4):
                AR, BK, BKtm, VU = k.rw_pair[gi]
                for hh in range(2):
                    hr = slice(hh * 64, hh * 64 + 64)
                    for par in range(2):
                        c = 2 * b + par
                        pS = k.ps_pool("b")
                        S.mm(pS[:, 0:128], BK[hr, c, :, :].rr("p s t -> p (s t)"), AR[hr, c, :, :].rr("p s t -> p (s t)"))
                        m = k.msb[mi]
                        mi += 1
                        S.tt(m.v(), pS[:, 0:128], k.mask4.v(), ALU.mult)
                        msb[(gi, hh, par)] = m
            return msb

        def inverse_gen(b, msb):
            units = [(gi, hh) for gi in range(4) for hh in range(2)]
            ws = {}
            for ui, (gi, hh) in enumerate(units):
                Nt, Lt, N2, L2, Rt = k.iws[ui * 5:(ui + 1) * 5]
                S.copy(Nt[0:64, 0:64], msb[(gi, hh, 1)][0:64, 0:64], e="pool")
                S.copy(Nt[64:128, 64:128], msb[(gi, hh, 0)][64:128, 0:64], e="pool")
                ws[ui] = [Nt, Lt, N2, L2, Rt]
            yield
            for ui in range(8):
                Nt, Lt, N2, L2, Rt = ws[ui]
                pt = k.ps_pool("b")
                ptb = pt.v().bitcast(BF16)
                S.tr(ptb[:, 0:128], Nt.v(), k.identb.v())
                S.copy(Lt.v(), ptb[:, 0:128], e="act")
                S.tt(Rt.v(), Nt.v(), k.identb.v(), ALU.add, e="pool")
            yield
            for lev in range(5):
                last = (lev == 4)
                for half in range(2):
                    for ui in range(half * 4, half * 4 + 4):
                        Nt, Lt, N2, L2, Rt = ws[ui]
                        pl = k.ps_pool("b")
                        S.mm(pl[:, 0:128], Nt.v(), Lt.v())
                        S.copy(L2.v(), pl[:, 0:128], e="act")
                        if not last:
                            pn = k.ps_pool("b")
                            S.mm(pn[:, 0:128], Lt.v(), Nt.v())
                            S.copy(N2.v(), pn[:, 0:128], e="dve")
                    yield
                for half in range(2):
                    for ui in range(half * 4, half * 4 + 4):
                        Nt, Lt, N2, L2, Rt = ws[ui]
                        pr = k.ps_pool("b")
                        S.mm(pr[:, 0:128], L2.v(), Rt.v())
                        dst = k.ttb[(b % 2) * 8 + ui] if last else Rt
                        S.tt(dst.v(), pr[:, 0:128], Rt.v(), ALU.add)
                        ws[ui] = [N2, L2, Nt, Lt, Rt]
                    yield

        def steps_gen(b, msb):
            for par in range(2):
                c = 2 * b + par
                Vr = slice(0, 64) if par == 0 else slice(64, 128)
                Ur = slice(64, 128) if par == 0 else slice(0, 64)
                pZs, pUs, pYs, pPs = {}, {}, {}, {}
                for gi in range(4):
                    p = grp * 4 + gi
                    AR, BK, BKtm, VU = k.rw_pair[gi]
                    pZ = k.ps_pool("a")
                    for hh in range(2):
                        hr = slice(hh * 64, hh * 64 + 64)
                        S.mm(pZ[Ur, hh * 64:(hh + 1) * 64], AR[hr, c, 0, :], k.rwPb[hr, p, :], start=True, stop=False)
                        S.mm(pZ[Ur, hh * 64:(hh + 1) * 64], msb[(gi, hh, par)][Vr, 0:64], VU[Vr, c, hh, :], start=False, stop=True)
                    pZs[gi] = pZ
                for gi in range(4):
                    S.copy(k.zsb[gi][Ur, :], pZs[gi][Ur, 0:128], e="act")
                yield
                for gi in range(4):
                    pU = k.ps_pool("a")
                    for hh in range(2):
                        S.mm(pU[Ur, hh * 64:(hh + 1) * 64], k.ttb[(b % 2) * 8 + gi * 2 + hh][Ur, Ur], k.zsb[gi][Ur, hh * 64:(hh + 1) * 64])
                    pUs[gi] = pU
                for gi in range(4):
                    AR, BK, BKtm, VU = k.rw_pair[gi]
                    S.copy(VU[Ur, c, :, :].rr("p h v -> p (h v)"), pUs[gi][Ur, 0:128], e="dve")
                yield
                for gi in range(4):
                    p = grp * 4 + gi
                    AR, BK, BKtm, VU = k.rw_pair[gi]
                    pY = k.ps_pool("a")
                    for hh in range(2):
                        hr = slice(hh * 64, hh * 64 + 64)
                        S.mm(pY[hr, 0:64], k.rwPb[hr, p, :], AR[hr, c, 1, :], start=True, stop=False)
                        S.mm(pY[hr, 0:64], VU[:, c, hh, :], msb[(gi, hh, par)][:, 64:128], start=False, stop=True)
                    pYs[gi] = pY
                for gi in range(4):
                    S.copy(yTs[gi][:, c * 64:(c + 1) * 64], pYs[gi][:, 0:64], e="act")
                yield
                for gi in range(4):
                    p = grp * 4 + gi
                    AR, BK, BKtm, VU = k.rw_pair[gi]
                    pP = k.ps_pool("a")
                    for hh in range(2):
                        hr = slice(hh * 64, hh * 64 + 64)
                        S.mm(pP[hr, 0:64], BKtm[:, c, hr], VU[:, c, hh, :])
                    pPs[gi] = pP
                    S.ts(k.rwP[:, p, :], k.rwP[:, p, :], k.gC[gi][:, c:c + 1], ALU.mult, e="pool")
                for gi in range(4):
                    p = grp * 4 + gi
                    S.stt(k.rwP[:, p, :], pPs[gi][:, 0:64], k.gC[gi][:, c:c + 1], k.rwP[:, p, :], ALU.mult, ALU.add)
                    S.copy(k.rwPb[:, p, :], k.rwP[:, p, :], e="act")
                yield

        msbs = {0: scores(0)}
        for _ in inverse_gen(0, msbs[0]):
            pass
        for b in range(4):
            g1 = steps_gen(b, msbs[b])
            g2 = None
            if b < 3:
                msbs[b + 1] = scores(b + 1)
                g2 = inverse_gen(b + 1, msbs[b + 1])
            d1 = d2 = False
            while not (d1 and (d2 or g2 is None)):
                if not d1:
                    try:
                        next(g1)
                    except StopIteration:
                        d1 = True
                if g2 is not None and not d2:
                    try:
                        next(g2)
                        next(g2)
                    except StopIteration:
                        d2 = True

        for gi in range(4):
            p = grp * 4 + gi
            y = yTs[gi]
            pm = ps()
            S.mm(pm.v(), k.blockf.v(), y.v())
            ysq = R32.get()
            S.act(ysq.v(), y.v(), AF.Square)
            pq = ps()
            S.mm(pq.v(), k.blockf.v(), ysq.v())
            m = R32.get()
            S.act(m.v(), pm.v(), AF.Copy, scale=1.0 / 64)
            S.act(ysq.v(), m.v(), AF.Square)
            var = R32.get()
            S.stt(var.v(), pq.v(), 1.0 / 64, ysq.v(), ALU.mult, ALU.subtract)
            S.act(var.v(), var.v(), AF.Sqrt, bias=k.gneps[:, 0:1])
            S.recip(var.v(), var.v())
            S.tt(y.v(), y.v(), m.v(), ALU.subtract, e="pool")
            S.tt(y.v(), y.v(), var.v(), ALU.mult)
            S.ts(y.v(), y.v(), V("rw_ln_w", p), ALU.mult, V("rw_ln_b", p), ALU.add, e="pool")
            R32.put(ysq, m, var)
            pbn = ps()
            S.mm(pbn.v(), k.blockb.v(), prbs[gi].v())
            R16.put(prbs[gi])
            vl = R32.get()
            S.dma(vl.v(), k.vdram[p * 128:(p + 1) * 128, :])
            S.tt(vl.v(), vl.v(), pbn.v(), ALU.mult)
            S.tt(y.v(), y.v(), vl.v(), ALU.add, e="pool")
            R32.put(vl)
            pg = ps()
            S.mm(pg.v(), k.g2b[:, 0, p * 128:(p + 1) * 128], gsb0.v(), start=True, stop=False)
            S.mm(pg.v(), k.g2b[0:32, 1, p * 128:(p + 1) * 128], gsb1[0:32, :], start=False, stop=True)
            S.tt(k.ocur[:, p, :], y.v(), pg.v(), ALU.mult)
    R16.put(lora1, gsb0, gsb1)
    if "rw" in k.dbg_out:
        for h in range(8):
            tmp = R32.get()
            S.copy(tmp.v(), k.ocur[:, h, :], e="pool")
            S.dma(k.dbg_out["rw"][h * 128:(h + 1) * 128, t0:t0 + NTOK], tmp.v())
            R32.put(tmp)

import math

TWO_PI = 2.0 * math.pi


def s5_setup(k):
    S = k.S
    nc = k.nc
    L, T = k.L, k.T

    def ext(name, shape):
        return Buf(nc.dram_tensor(name, list(shape), F32, kind="ExternalInput"), name)
    k.s5lam = ext("s5lam", [L, 128, 3, 32])
    k.s5b = ext("s5b", [L, 2, 128, 32 * 16])
    k.s5c = ext("s5c", [L, 2, 128, 32 * 16])
    k.s5cpad = ext("s5cpad", [L, 2, 8, 128, 4 * 128])
    k.s5glu = ext("s5glu", [L, 8, 128, 128])
    k.s5P = S.dram("s5P", [8, 128, 8 * 2 * 128], BF16)
    k.s5Q = S.dram("s5Q", [8, 128, 8 * 2 * 4 * 32], BF16)
    k.s5BD = S.dram("s5BD", [8, 128, 8 * 128], BF16)
    k.s5D = S.dram("s5D", [8, 128, 4 * 2 * 64], F32)
    k.s5P3 = S.dram("s5P3", [8, 128, 8 * 2 * 128], BF16)
    k.s5Q3 = S.dram("s5Q3", [8, 128, 8 * 2 * 128], BF16)
    k.s5pad3 = [S.sb("s5pad3_%d" % i, [128, 128]) for i in range(2)]
    for t in k.s5pad3:
        S.memset(t.v(), 0.0)
    k.s5small = S.sb("s5small", [128, 24, 32])
    k.s5pw = S.sb("s5pw", [128, 2, 9, 32])
    k.s5glub = S.sb("s5glub", [128, 8, 128], BF16)
    k.s5car = S.sb("s5car", [128, 2, 32])
    k.s5rho = S.sb("s5rho", [128, 32])
    k.s5pad = [S.sb("s5pad%d" % i, [128, 4 * 32]) for i in range(3)]
    for t in k.s5pad:
        S.memset(t.v(), 0.0)
    k.s5t = [S.sb("s5t%d" % i, [128, 72]) for i in range(12)]
    k.s5ti = 0
    k.s5x = [[S.sb("s5x%d_%d" % (i, c), [128, 64], BF16) for c in range(2)] for i in range(4)]


def s5_layer_init(k, l):
    S = k.S
    R32, BIG, ps = k.R32, k.BIG, k.ps
    sm = k.s5small

    def s(i):
        return sm[:, i, :]
    LR, LI, LS, DT, MAG, TH, R_, RF, M1, COS, SIN, ABR, ABI, DEN, NR, T1, T2, CRE, CIM, RH, RI = range(21)
    lamt = R32.get()
    lv = lamt.v()[:, 0:96].rr("p (a q) -> p a q", a=3)
    S.dma(lv, k.s5lam[l])
    S.copy(s(LR), lv[:, 0, :], e="pool")
    S.copy(s(LI), lv[:, 1, :], e="pool")
    S.act(s(DT), lv[:, 2, :], AF.Exp)
    R32.put(lamt)
    S.tt(s(T1), s(LR), s(DT), ALU.mult)
    S.act(s(MAG), s(T1), AF.Exp)
    S.act(s(RH), s(T1), AF.Exp, scale=8.0)
    S.copy(k.s5rho.v(), s(RH), e="pool")
    S.tt(s(TH), s(LI), s(DT), ALU.mult)

    def sincos(dst, shift):
        S.ts(s(R_), s(TH), 1.0 / TWO_PI, ALU.mult, shift, ALU.add)
        ri = sm[:, 23, :].bitcast(I32)
        S.copy(ri, s(R_), e="dve")
        S.copy(s(RF), ri, e="dve")
        S.tt(s(R_), s(R_), s(RF), ALU.subtract)
        S.ts(s(M1), s(R_), 0.5, ALU.is_gt)
        S.tt(s(R_), s(R_), s(M1), ALU.subtract)
        S.ts(s(M1), s(R_), -0.5, ALU.is_lt)
        S.tt(s(R_), s(R_), s(M1), ALU.add)
        S.act(dst, s(R_), AF.Sin, scale=6.28318)
    sincos(s(SIN), 0.0)
    sincos(s(COS), 0.25)
    S.tt(s(ABR), s(MAG), s(COS), ALU.mult)
    S.tt(s(ABI), s(MAG), s(SIN), ALU.mult)
    S.tt(s(DEN), s(LR), s(LR), ALU.mult)
    S.tt(s(T1), s(LI), s(LI), ALU.mult)
    S.tt(s(DEN), s(DEN), s(T1), ALU.add)
    S.recip(s(DEN), s(DEN))
    S.ts(s(NR), s(ABR), -1.0, ALU.add)
    S.tt(s(T1), s(NR), s(LR), ALU.mult)
    S.tt(s(T2), s(ABI), s(LI), ALU.mult)
    S.tt(s(T1), s(T1), s(T2), ALU.add)
    S.tt(s(CRE), s(T1), s(DEN), ALU.mult)
    S.tt(s(T1), s(ABI), s(LR), ALU.mult)
    S.tt(s(T2), s(NR), s(LI), ALU.mult)
    S.tt(s(T1), s(T1), s(T2), ALU.subtract)
    S.tt(s(CIM), s(T1), s(DEN), ALU.mult)
    pw = k.s5pw
    S.memset(pw[:, 0, 0, :], 1.0)
    S.memset(pw[:, 1, 0, :], 0.0)
    for d in range(8):
        S.tt(s(T1), pw[:, 0, d, :], s(ABR), ALU.mult)
        S.tt(s(T2), pw[:, 1, d, :], s(ABI), ALU.mult)
        S.tt(pw[:, 0, d + 1, :], s(T1), s(T2), ALU.subtract)
        S.tt(s(T1), pw[:, 0, d, :], s(ABI), ALU.mult)
        S.tt(s(T2), pw[:, 1, d, :], s(ABR), ALU.mult)
        S.tt(pw[:, 1, d + 1, :], s(T1), s(T2), ALU.add)
    S.recip(s(RI), s(RH))
    D1R, D1I = 21, 22
    S.tt(s(D1R), pw[:, 0, 8, :], s(RI), ALU.mult)
    S.tt(s(D1I), pw[:, 1, 8, :], s(RI), ALU.mult)
    S.ts(s(D1I), s(D1I), -1.0, ALU.mult)
    Dre = [BIG[i].v().rr("p (q n) -> p q n", n=64) for i in range(4)]
    Dim = [BIG[4 + i].v().rr("p (q n) -> p q n", n=64) for i in range(4)]
    for i in range(4):
        qs = slice(i * 8, (i + 1) * 8)
        tr1 = R32.get()
        tr2 = R32.get()
        S.copy(Dre[i][:, :, 0], s(D1R)[:, qs], e="pool")
        S.copy(Dim[i][:, :, 0], s(D1I)[:, qs], e="pool")
        m = 1
        while m < 64:
            t1 = tr1.v()[:, 0:8 * m].rr("p (q n) -> p q n", n=m)
            t2 = tr2.v()[:, 0:8 * m].rr("p (q n) -> p q n", n=m)
            br = Dre[i][:, :, m - 1:m].bc([128, 8, m])
            bi = Dim[i][:, :, m - 1:m].bc([128, 8, m])
            ar = Dre[i][:, :, 0:m]
            ai = Dim[i][:, :, 0:m]
            S.tt(t1, ar, br, ALU.mult)
            S.tt(t2, ai, bi, ALU.mult, e="pool")
            S.tt(Dre[i][:, :, m:2 * m], t1, t2, ALU.subtract)
            S.tt(t1, ar, bi, ALU.mult)
            S.tt(t2, ai, br, ALU.mult, e="pool")
            S.tt(Dim[i][:, :, m:2 * m], t1, t2, ALU.add)
            m *= 2
        R32.put(tr1, tr2)
    for gt in range(8):
        i, o = gt // 2, (gt % 2) * 4
        dv = k.s5D[gt].rr("p (k c n) -> p k c n", k=4, c=2)
        S.dma(dv[:, :, 0, :], Dre[i][:, o:o + 4, :])
        S.dma(dv[:, :, 1, :], Dim[i][:, o:o + 4, :])
    bre = R32.get()
    bim = R32.get()
    S.dma(bre.v(), k.s5b[l, 0])
    S.dma(bim.v(), k.s5b[l, 1])
    v3 = lambda t: t.v().rr("p (q h) -> p q h", h=16)
    bcq = lambda view: view.rr("p (q o) -> p q o", o=1).bc([128, 32, 16])
    t1 = R32.get()
    t2 = R32.get()
    abr = R32.get()
    abi = R32.get()

    def cmul_bc(ore, oim, are, aim, sre, sim):
        S.tt(v3(t1), v3(are), bcq(sre), ALU.mult)
        S.tt(v3(t2), v3(aim), bcq(sim), ALU.mult, e="pool")
        S.tt(v3(t1), v3(t1), v3(t2), ALU.subtract)
        S.tt(v3(t2), v3(are), bcq(sim), ALU.mult, e="pool")
        S.tt(v3(oim), v3(aim), bcq(sre), ALU.mult)
        S.tt(v3(oim), v3(oim), v3(t2), ALU.add)
        S.copy(v3(ore), v3(t1), e="pool")
    cmul_bc(abr, abi, bre, bim, s(CRE), s(CIM))
    R32.put(bre, bim)
    cre_t = R32.get()
    cim_t = R32.get()
    S.dma(cre_t.v(), k.s5c[l, 0])
    S.dma(cim_t.v(), k.s5c[l, 1])
    pre, pim, pimn = k.s5pad
    for d in range(8):
        tau = 7 - d
        for gt in range(8):
            for (dst, src, sc) in ((pre, abr, None), (pim, abi, None), (pimn, abi, -1.0)):
                dv = dst.v().rr("p (k g h) -> p k g h", k=4, g=2)
                sv = v3(src)[:, gt * 4:(gt + 1) * 4, :]
                if sc is None:
                    S.copy(dv[0:64, :, 0, :], sv[0:64], e="pool")
                    S.copy(dv[64:128, :, 1, :], sv[64:128], e="pool")
                else:
                    S.ts(dv[0:64, :, 0, :], sv[0:64], sc, ALU.mult)
                    S.ts(dv[64:128, :, 1, :], sv[64:128], sc, ALU.mult)
            S.copy(k.s5pad3[0][:, 96:128], pre[:, 96:128], e="pool")
            S.copy(k.s5pad3[1][:, 96:128], pim[:, 96:128], e="pool")
            pt = ps()
            S.tr(pt[:, 0:128], pre.v(), k.ident.v())
            S.tr(pt[:, 128:256], pim.v(), k.ident.v())
            S.tr(pt[:, 256:384], k.s5pad3[0].v(), k.ident.v())
            S.tr(pt[:, 384:512], k.s5pad3[1].v(), k.ident.v())
            pb = k.R16.get()
            S.copy(pb.v(), pt.v(), e="act")
            S.dma(k.s5P[gt].rr("p (t c n) -> p t c n", t=8, c=2)[:, tau, :, :], pb[:, 0:256].rr("p (c n) -> p c n", c=2))
            S.dma(k.s5P3[gt].rr("p (t c n) -> p t c n", t=8, c=2)[:, tau, :, :], pb[:, 256:512].rr("p (c n) -> p c n", c=2))
            k.R16.put(pb)
            cp = R32.get()
            cpi = R32.get()
            S.dma(cp.v(), k.s5cpad[l, 0, gt])
            S.dma(cpi.v(), k.s5cpad[l, 1, gt])
            pbd = ps()
            for kk in range(4):
                S.mm(pbd[:, 32 * kk:32 * kk + 32], cp[:, 128 * kk:128 * kk + 128], pre[:, 32 * kk:32 * kk + 32], start=True, stop=False)
                S.mm(pbd[:, 32 * kk:32 * kk + 32], cpi[:, 128 * kk:128 * kk + 128], pimn[:, 32 * kk:32 * kk + 32], start=False, stop=True)
            R32.put(cp, cpi)
            bdT = R32.get()
            if d == 0:
                S.stt(bdT[:, 0:128], k.ident.v(), k.vec[:, VI["s5_d"], gt:gt + 1], pbd[:, 0:128], ALU.mult, ALU.add)
            else:
                S.copy(bdT[:, 0:128], pbd[:, 0:128], e="act")
            pt2 = ps()
            S.tr(pt2[:, 0:128], bdT[:, 0:128], k.ident.v())
            R32.put(bdT)
            bdb = k.R16.get()
            S.copy(bdb[:, 0:128], pt2[:, 0:128], e="act")
            S.dma(k.s5BD[gt].rr("p (d n) -> p d n", d=8)[:, d, :], bdb[:, 0:128])
            k.R16.put(bdb)
        if d < 7:
            cmul_bc(abr, abi, abr, abi, s(ABR), s(ABI))
    R32.put(abr, abi)
    qre = R32.get()
    qim = R32.get()
    s5qpad = [BIG[8 + i].v().bitcast(BF16).rr("p (q n) -> p q n", q=32) for i in range(2)]
    s5q3 = [BIG[10 + i].v().bitcast(BF16).rr("p (g n) -> p g n", g=8) for i in range(2)]
    for i in range(4):
        S.memset(BIG[8 + i].v(), 0.0)
    for tp in range(8):
        cmul_bc(qre, qim, cre_t, cim_t, pw[:, 0, tp + 1, :], pw[:, 1, tp + 1, :])
        for ci, (src, sc) in enumerate(((qre, 1.0), (qim, -1.0))):
            qp = s5qpad[ci]
            S.ts(qp[0:64, :, 0:16], v3(src)[0:64], sc, ALU.mult)
            S.ts(qp[64:128, :, 16:32], v3(src)[64:128], sc, ALU.mult)
            q3 = s5q3[ci]
            S.copy(q3[:, :, 96:128], qp.rr("p (g k) n -> p g k n", k=4)[:, :, 3, :], e="pool")
            for gt in range(8):
                dv = k.s5Q[gt].rr("p (t c k n) -> p t c k n", t=8, c=2, k=4)
                S.dma(dv[:, tp, ci, :, :], qp[:, gt * 4:(gt + 1) * 4, :])
                dv3 = k.s5Q3[gt].rr("p (t c n) -> p t c n", t=8, c=2)
                S.dma(dv3[:, tp, ci, :], q3[:, gt, :])
    R32.put(qre, qim, cre_t, cim_t, t1, t2)
    for gt in range(8):
        g = R32.get()
        S.dma(g[:, 0:128], k.s5glu[l, gt])
        S.copy(k.s5glub[:, gt, :], g[:, 0:128], e="pool")
        R32.put(g)
    S.memset(k.s5car.v(), 0.0)


def s5_tile(k, l, j):
    S = k.S
    R32, R16, BIG, xn, ps = k.R32, k.R16, k.BIG, k.xn, k.ps
    wib = k.w_in_b
    t0 = j * NTOK

    def tmp():
        t = k.s5t[k.s5ti % 12]
        k.s5ti += 1
        return t
    for gt in range(8):
        base = (gt % 2) * 10
        BDs = BIG[base + 0].v().bitcast(BF16).rr("p (d n) -> p d n", d=8)
        Pv = [BIG[base + 1 + i].v().bitcast(BF16).rr("p (t c n) -> p t c n", t=4, c=2) for i in range(2)]
        Qv = [BIG[base + 3 + i].v().bitcast(BF16).rr("p (t c k n) -> p t c k n", t=4, c=2, k=4) for i in range(2)]
        Dv = BIG[base + 5].v().rr("p (k c n) -> p k c n", k=4, c=2)
        P3v = [BIG[base + 6 + i].v().bitcast(BF16).rr("p (t c n) -> p t c n", t=4, c=2) for i in range(2)]
        Q3v = [BIG[base + 8 + i].v().bitcast(BF16).rr("p (t c n) -> p t c n", t=4, c=2) for i in range(2)]
        S.dma(BIG[base + 0].v().bitcast(BF16), k.s5BD[gt])
        for i in range(2):
            S.dma(BIG[base + 1 + i].v().bitcast(BF16), k.s5P[gt][:, i * 1024:(i + 1) * 1024])
            S.dma(BIG[base + 3 + i].v().bitcast(BF16), k.s5Q[gt][:, i * 1024:(i + 1) * 1024])
            S.dma(BIG[base + 6 + i].v().bitcast(BF16), k.s5P3[gt][:, i * 1024:(i + 1) * 1024])
            S.dma(BIG[base + 8 + i].v().bitcast(BF16), k.s5Q3[gt][:, i * 1024:(i + 1) * 1024])
        S.dma(BIG[base + 5].v(), k.s5D[gt])
        if gt % 4 == 0:
            w = k.load_w(wib[l, :, OFF_D + (gt // 4) * 512:OFF_D + (gt // 4 + 1) * 512], 8, 512)
        pu = ps()
        for c in range(8):
            S.mm(pu.v(), w[:, c, (gt % 4) * 128:(gt % 4 + 1) * 128], xn[:, c, :], start=(c == 0), stop=(c == 7))
        Ut = R16.get()
        Utv = Ut.v().rr("p (t n) -> p t n", t=8)
        S.copy(Utv, pu.v().rr("p (n t) -> p t n", t=8), e="act")
        for kk in range(4):
            q = gt * 4 + kk
            rows = slice(32 * kk, 32 * kk + 32)
            pwr = ps()
            pwi = ps()
            for (pw_, ci) in ((pwr, 0), (pwi, 1)):
                for tau in range(8):
                    if kk < 3:
                        S.mm(pw_[:, 0:64], Pv[tau // 4][rows, tau % 4, ci, :], Utv[rows, tau, :], start=(tau == 0), stop=(tau == 7))
                    else:
                        S.mm(pw_[:, 0:64], P3v[tau // 4][:, tau % 4, ci, :], Utv[:, tau, :], start=(tau == 0), stop=(tau == 7))
            dre = Dv[:, kk, 0, :]
            dim = Dv[:, kk, 1, :]
            a1, a2, a3, a4 = tmp(), tmp(), tmp(), tmp()
            S.tt(a1[:, 0:64], dre, pwr[:, 0:64], ALU.mult)
            S.tt(a2[:, 0:64], dim, pwi[:, 0:64], ALU.mult)
            S.tt(a1[:, 0:64], a1[:, 0:64], a2[:, 0:64], ALU.subtract, e="pool")
            S.tt(a3[:, 0:64], dre, pwi[:, 0:64], ALU.mult)
            S.tt(a4[:, 0:64], dim, pwr[:, 0:64], ALU.mult)
            S.tt(a3[:, 0:64], a3[:, 0:64], a4[:, 0:64], ALU.add, e="pool")
            rho = k.s5rho[:, q:q + 1].bc([128, 64])
            wre, wim = tmp(), tmp()
            S.scan(wre[:, 0:64], rho, a1[:, 0:64], k.s5car[:, 0, q:q + 1])
            S.scan(wim[:, 0:64], rho, a3[:, 0:64], k.s5car[:, 1, q:q + 1])
            xre, xim = tmp(), tmp()
            S.copy(xre[:, 0:1], k.s5car[:, 0, q:q + 1], e="pool")
            S.copy(xim[:, 0:1], k.s5car[:, 1, q:q + 1], e="pool")
            S.tt(a1[:, 0:64], dre, wre[:, 0:64], ALU.mult)
            S.tt(a2[:, 0:64], dim, wim[:, 0:64], ALU.mult, e="pool")
            S.tt(xre[:, 1:65], a1[:, 0:64], a2[:, 0:64], ALU.add)
            S.tt(a3[:, 0:64], dre, wim[:, 0:64], ALU.mult, e="pool")
            S.tt(a4[:, 0:64], dim, wre[:, 0:64], ALU.mult)
            S.tt(xim[:, 1:65], a3[:, 0:64], a4[:, 0:64], ALU.subtract)
            S.copy(k.s5car[:, 0, q:q + 1], xre[:, 64:65], e="pool")
            S.copy(k.s5car[:, 1, q:q + 1], xim[:, 64:65], e="pool")
            S.copy(k.s5x[kk][0].v(), xre[:, 0:64], e="act")
            S.copy(k.s5x[kk][1].v(), xim[:, 0:64], e="act")
        ysb = R32.get()
        yv = ysb.v().rr("p (n t) -> p t n", t=8)
        for tp in range(8):
            py = ps()
            nmm = (tp + 1)
            for tau in range(tp + 1):
                S.mm(py[:, 0:64], BDs[:, tp - tau, :], Utv[:, tau, :], start=(tau == 0), stop=False)
            for kk in range(3):
                S.mm(py[32 * kk:32 * kk + 32, 0:64], Qv[tp // 4][:, tp % 4, 0, kk, :], k.s5x[kk][0].v(), start=False, stop=False)
                S.mm(py[32 * kk:32 * kk + 32, 0:64], Qv[tp // 4][:, tp % 4, 1, kk, :], k.s5x[kk][1].v(), start=False, stop=False)
            S.mm(py[:, 0:64], Q3v[tp // 4][:, tp % 4, 0, :], k.s5x[3][0].v(), start=False, stop=False)
            S.mm(py[:, 0:64], Q3v[tp // 4][:, tp % 4, 1, :], k.s5x[3][1].v(), start=False, stop=True)
            S.copy(yv[:, tp, :], py[:, 0:64], e="act")
        R16.put(Ut)
        x2 = R32.get()
        S.act(x2.v(), ysb.v(), AF.Square)
        S.ts(x2.v(), x2.v(), 0.044715, ALU.mult, 1.0, ALU.add, e="pool")
        S.tt(x2.v(), x2.v(), ysb.v(), ALU.mult)
        S.act(x2.v(), x2.v(), AF.Tanh, scale=0.7978845608028654)
        S.stt(x2.v(), x2.v(), 1.0, ysb.v(), ALU.add, ALU.mult)
        zb = R16.get()
        S.act(zb.v(), x2.v(), AF.Copy, scale=0.5)
        pg = ps()
        S.mm(pg.v(), k.s5glub[:, gt, :], zb.v())
        R16.put(zb)
        sgl = ysb
        S.act(sgl.v(), pg.v(), AF.Sigmoid, bias=k.vec[:, VI["s5_glu_b"], gt:gt + 1])
        S.stt(k.ocur[:, gt, :], x2.v(), 0.5, sgl.v(), ALU.mult, ALU.mult)
        R32.put(x2, ysb)
    if "s5" in k.dbg_out:
        for h in range(8):
            tmp_ = R32.get()
            S.copy(tmp_.v(), k.ocur[:, h, :], e="pool")
            S.dma(k.dbg_out["s5"][h * 128:(h + 1) * 128, t0:t0 + NTOK], tmp_.v())
            R32.put(tmp_)


def build_full(L, T, dbg=()):
    k = build(L, T, dbg=dbg)
    k.stage = "z"
    hg_setup(k)
    rw_setup(k)
    s5_setup(k)
    S = k.S
    nt = T // NTOK
    for l in range(L):
        S.dma(k.vec.v().rr("p v c -> p (v c)"), k.vecs[l])
        hg_layer_init(k, l)
        rw_layer_init(k, l)
        rw_layer_vecs(k, l)
        s5_layer_init(k, l)
        items = k.conv_items(l + 1) if l + 1 < L else []
        per = (len(items) + nt - 1) // nt
        for j in range(nt):
            k.do_conv(items[j * per:(j + 1) * per])
            if j == nt - 1:
                k.flush_conv()
            k.rmsnorm_tile(k.hT, l, j, VI["mix_norm"])
            hgrn2_tile(k, l, j)
            k.merge_branch(l, j, 0, True)
            rwkv_tile(k, l, j)
            k.merge_branch(l, j, 1, False)
            s5_tile(k, l, j)
            k.merge_branch(l, j, 2, False)
            k.wout_tile(l, j)
            k.ffn_tile(l, j)
    finalize(k)
    return k

from concourse.bass_utils import run_bass_kernel_spmd

L_FULL = 4
T_FULL = 4096


def _pack_vecs(inp, L):
    v = np.zeros((L, 128, NV, 8), np.float32)
    for n, i in VI.items():
        a = np.asarray(inp[n], np.float32)
        if n == "final_norm":
            a = np.broadcast_to(a[None], (4, D))
        elif n == "rw_v0":
            a = np.concatenate([np.zeros((1, D), np.float32), a], 0)
        elif n == "rw_r_k":
            a = a.reshape(a.shape[0], D)
        a = a[:L]
        v[:, :, i, :] = a.reshape(L, 8, 128).transpose(0, 2, 1)
    return v.reshape(L, 128, NV * 8)


def _rw_inmap(inp, L):
    mu_idx = list(range(2048, 3072))
    for p in range(8):
        mu_idx += list(range(p * 128, (p + 1) * 128)) + list(range(1024 + p * 128, 1024 + (p + 1) * 128))
    mu_idx += list(range(3072, 3360))
    mu = np.zeros((L, 27 * 128), np.float32)
    mu[:, :3360] = inp["rw_shift_mu"][:L][:, mu_idx]
    mu = np.ascontiguousarray(mu.reshape(L, 27, 128).transpose(0, 2, 1))
    v1 = np.concatenate([np.zeros((1, 1024, 32), np.float32), inp["rw_v1"]], 0)[:L]
    v2 = np.concatenate([np.zeros((1, 32, 1024), np.float32), inp["rw_v2"]], 0)[:L]
    return {"rw_w2": inp["rw_w2"][:L], "rw_a2": inp["rw_a2"][:L], "rw_g2": inp["rw_g2"][:L],
            "rw_v1": np.ascontiguousarray(v1), "rw_v2": np.ascontiguousarray(v2), "rwmu": mu}


def _s5_inmap(inp, L):
    def pairlay(a):
        Lh = a.shape[0]
        X = a.shape[3]
        return a.reshape(Lh, 32, 2, 64, X).transpose(0, 2, 3, 1, 4).reshape(Lh, 128, 32, X)
    lr = pairlay(inp["s5_lambda_re"][:L, :, :, None])[..., 0]
    li = pairlay(inp["s5_lambda_im"][:L, :, :, None])[..., 0]
    ls = pairlay(np.broadcast_to(inp["s5_log_step"][:L, :, None, None], (L, 64, 64, 1)))[..., 0]
    lam = np.ascontiguousarray(np.stack([lr, li, ls], 2))
    b = np.stack([pairlay(inp["s5_b_re"][:L]), pairlay(inp["s5_b_im"][:L])], 1).reshape(L, 2, 128, 512)
    cT = [inp["s5_c_re"][:L].transpose(0, 1, 3, 2), inp["s5_c_im"][:L].transpose(0, 1, 3, 2)]
    c = np.stack([pairlay(x) for x in cT], 1)
    cpad = np.zeros((L, 2, 8, 128, 4, 8, 16), np.float32)
    for gt in range(8):
        for kk in range(4):
            for g2 in range(2):
                cpad[:, :, gt, g2 * 64:(g2 + 1) * 64, kk, 2 * kk + g2, :] = c[:, :, g2 * 64:(g2 + 1) * 64, gt * 4 + kk, :]
    glu = np.zeros((L, 8, 128, 128), np.float32)
    for gt in range(8):
        for g8 in range(8):
            glu[:, gt, g8 * 16:(g8 + 1) * 16, g8 * 16:(g8 + 1) * 16] = inp["s5_glu_w"][:L, gt * 8 + g8]
    return {"s5lam": lam, "s5b": np.ascontiguousarray(b), "s5c": np.ascontiguousarray(c.reshape(L, 2, 128, 512)),
            "s5cpad": np.ascontiguousarray(cpad.reshape(L, 2, 8, 128, 512)), "s5glu": glu}


def kernel(**inputs):
    inp = {k_: np.asarray(v, dtype=np.float32) for k_, v in inputs.items()}
    L, T = L_FULL, T_FULL
    B = inp["x"].shape[0]
    perm = perm_cols()
    common = {"w_in": np.ascontiguousarray(inp["w_in"][:, :, perm]),
              "w_branch": inp["w_branch"], "w_out": inp["w_out"], "ffn_w_gate": inp["ffn_w_gate"],
              "ffn_w_up": inp["ffn_w_up"], "ffn_w_down": inp["ffn_w_down"], "vecs": _pack_vecs(inp, L)}
    common.update(_rw_inmap(inp, L))
    common.update(_s5_inmap(inp, L))
    k = build_full(L, T)
    in_maps = []
    for c in range(8):
        m = dict(common)
        m["xT"] = np.ascontiguousarray(inp["x"][c % B].T)
        in_maps.append(m)
    res = run_bass_kernel_spmd(k.nc, in_maps, core_ids=list(range(8)))
    out = np.stack([np.ascontiguousarray(res.results[b]["out"].T) for b in range(B)], 0)
    return out.astype(np.float32)
```

```python
import numpy as np
import concourse.bass as bass
import concourse.mybir as mybir

F32 = mybir.dt.float32
BF16 = mybir.dt.bfloat16
I32 = mybir.dt.int32
AF = mybir.ActivationFunctionType
ALU = mybir.AluOpType
AX = mybir.AxisListType


class Buf:
    __slots__ = ("t", "lw", "rd", "name", "pe_rg")

    def __init__(self, t, name=""):
        self.t = t
        self.lw = None
        self.rd = []
        self.name = name
        self.pe_rg = None

    def v(self):
        return View((self,), self.t.ap())

    def __getitem__(self, idx):
        return View((self,), self.t.ap()[idx])


class View:
    __slots__ = ("bufs", "ap")

    def __init__(self, bufs, ap):
        self.bufs = bufs
        self.ap = ap

    def __getitem__(self, idx):
        return View(self.bufs, self.ap[idx])

    def rr(self, pat, **kw):
        return View(self.bufs, self.ap.rearrange(pat, **kw))

    def bc(self, shape):
        return View(self.bufs, self.ap.to_broadcast(list(shape)))

    def bitcast(self, dt):
        return View(self.bufs, self.ap.bitcast(dt))

    @property
    def shape(self):
        return tuple(self.ap.shape)


def _bufs(*xs):
    out = []
    for x in xs:
        if isinstance(x, View):
            for b in x.bufs:
                if b not in out:
                    out.append(b)
    return out


def _ap(x):
    return x.ap if isinstance(x, View) else x


class Sched:
    ND = 8

    def __init__(self, nc):
        self.nc = nc
        self.eng = dict(pe=nc.tensor, dve=nc.vector, act=nc.scalar, pool=nc.gpsimd, sp=nc.sync)
        self.csem = {}
        self.cnt = {}
        for e in ("pe", "dve", "act", "pool"):
            self.csem[e] = nc.alloc_semaphore("cs_" + e)
            self.cnt[e] = 0
        self.dsem = {}
        self.dcnt = {}
        self.dk = {}
        for q in ("sp", "pool"):
            self.dsem[q] = [nc.alloc_semaphore("ds_%s%d" % (q, i)) for i in range(self.ND)]
            self.dcnt[q] = [0] * self.ND
            self.dk[q] = 0
        self.seen = {e: {} for e in self.eng}
        import os as _os
        self.nowait_same = set(_os.environ.get("NOWAIT", "pe").split(","))
        self.ninst = 0
        self.nwait = 0
        self.per = {e: 0 for e in self.eng}

    def sb(self, name, shape, dt=F32):
        return Buf(self.nc.alloc_sbuf_tensor(name, list(shape), dt), name)

    def ps(self, name, shape, dt=F32):
        return Buf(self.nc.alloc_psum_tensor(name, list(shape), dt), name)

    def dram(self, name, shape, dt=F32, kind="Internal"):
        return Buf(self.nc.dram_tensor(name, list(shape), dt, kind=kind), name)

    def _wait(self, e, tok, force=False):
        if tok is None:
            return
        sem, val, key, owner = tok
        if owner == e and e in self.nowait_same and not force:
            return
        if self.seen[e].get(key, 0) >= val:
            return
        self.eng[e].wait_ge(sem, val)
        self.seen[e][key] = val
        self.nwait += 1

    def _deps(self, e, reads, writes):
        for b in reads:
            self._wait(e, b.lw)
        for b in writes:
            self._wait(e, b.lw)
            for r in b.rd:
                self._wait(e, r)

    def _commit(self, tok, reads, writes):
        for b in reads:
            if b in writes:
                continue
            b.rd.append(tok)
            if len(b.rd) > 24:
                latest = {}
                for t in b.rd:
                    if t[2] not in latest or latest[t[2]][1] < t[1]:
                        latest[t[2]] = t
                b.rd = list(latest.values())
        for b in writes:
            b.lw = tok
            b.rd = []

    def op(self, e, fn, reads=(), writes=()):
        self._deps(e, reads, writes)
        inst = fn(self.eng[e])
        self.cnt[e] += 1
        inst.then_inc(self.csem[e], 1)
        tok = (self.csem[e], self.cnt[e], "c" + e, e)
        self._commit(tok, reads, writes)
        self.ninst += 1
        self.per[e] += 1
        return tok

    def dma(self, out, in_, q="sp", **kw):
        reads = _bufs(in_)
        writes = _bufs(out)
        k = self.dk[q]
        self.dk[q] += 1
        i = k % self.ND
        sem = self.dsem[q][i]
        key = "d%s%d" % (q, i)
        prev = self.dcnt[q][i]
        if prev > 0 and self.seen[q].get(key, 0) < 16 * prev:
            self.eng[q].wait_ge(sem, 16 * prev)
            self.seen[q][key] = 16 * prev
        self._deps(q, reads, writes)
        inst = self.eng[q].dma_start(out=_ap(out), in_=_ap(in_), **kw)
        self.dcnt[q][i] += 1
        inst.then_inc(sem, 16)
        tok = (sem, 16 * self.dcnt[q][i], key, "dma" + q)
        self._commit(tok, reads, writes)
        self.ninst += 1
        self.per[q] += 1
        return tok

    def finish(self, bufs):
        for b in bufs:
            self._wait("sp", b.lw)
        for e in ("pe", "dve", "act", "pool"):
            if self.cnt[e] > 0:
                self._wait("sp", (self.csem[e], self.cnt[e], "c" + e, e))
        for q in ("sp", "pool"):
            for i in range(self.ND):
                if self.dcnt[q][i] > 0:
                    self._wait("sp", (self.dsem[q][i], 16 * self.dcnt[q][i], "d%s%d" % (q, i), "dma" + q))

    def act(self, out, in_, func, bias=None, scale=None, accum=None, e="act"):
        kw = {}
        if bias is not None:
            kw["bias"] = _ap(bias)
        if scale is not None:
            kw["scale"] = _ap(scale)
        if accum is not None:
            kw["accum_out"] = _ap(accum)
        return self.op(e, lambda g: g.activation(out=_ap(out), in_=_ap(in_), func=func, **kw),
                       _bufs(in_, bias, scale), _bufs(out, accum))

    def tt(self, out, a, b, op, e="dve"):
        return self.op(e, lambda g: g.tensor_tensor(out=_ap(out), in0=_ap(a), in1=_ap(b), op=op),
                       _bufs(a, b), _bufs(out))

    def ts(self, out, a, s1, op0, s2=None, op1=None, e="dve"):
        if op1 is None:
            return self.op(e, lambda g: g.tensor_scalar(out=_ap(out), in0=_ap(a), scalar1=_ap(s1), scalar2=None, op0=op0),
                           _bufs(a, s1), _bufs(out))
        return self.op(e, lambda g: g.tensor_scalar(out=_ap(out), in0=_ap(a), scalar1=_ap(s1), scalar2=_ap(s2), op0=op0, op1=op1),
                       _bufs(a, s1, s2), _bufs(out))

    def stt(self, out, a, s, b, op0, op1):
        return self.op("dve", lambda g: g.scalar_tensor_tensor(out=_ap(out), in0=_ap(a), scalar=_ap(s), in1=_ap(b), op0=op0, op1=op1),
                       _bufs(a, s, b), _bufs(out))

    def copy(self, out, in_, e="dve"):
        if e == "act":
            return self.act(out, in_, AF.Copy)
        return self.op(e, lambda g: g.tensor_copy(out=_ap(out), in_=_ap(in_)), _bufs(in_), _bufs(out))

    def memset(self, out, val, e="pool"):
        return self.op(e, lambda g: g.memset(_ap(out), val), [], _bufs(out))

    def recip(self, out, in_, e="dve"):
        return self.op(e, lambda g: g.reciprocal(out=_ap(out), in_=_ap(in_)), _bufs(in_), _bufs(out))

    def scan(self, out, d0, d1, init, op0=ALU.mult, op1=ALU.add):
        return self.op("dve", lambda g: g.tensor_tensor_scan(out=_ap(out), data0=_ap(d0), data1=_ap(d1), initial=_ap(init), op0=op0, op1=op1),
                       _bufs(d0, d1, init), _bufs(out))

    def mm(self, out, lhsT, rhs, start=True, stop=True):
        la = _ap(lhsT)
        rg = (la.base_partition(), la.partition_size())
        for b in _bufs(out):
            if b.pe_rg is not None and b.pe_rg != rg and b.lw is not None and b.lw[3] == "pe":
                self._wait("pe", b.lw, force=True)
            b.pe_rg = rg
        return self.op("pe", lambda g: g.matmul(_ap(out), lhsT=_ap(lhsT), rhs=_ap(rhs), start=start, stop=stop),
                       _bufs(lhsT, rhs), _bufs(out))

    def tr(self, out, in_, ident):
        return self.op("pe", lambda g: g.transpose(out=_ap(out), in_=_ap(in_), identity=_ap(ident)),
                       _bufs(in_, ident), _bufs(out))

    def asel(self, out, in_, pattern, cmp, fill, base, cm):
        return self.op("pool", lambda g: g.affine_select(out=_ap(out), in_=_ap(in_), pattern=pattern, compare_op=cmp, fill=fill, base=base, channel_multiplier=cm),
                       _bufs(in_), _bufs(out))


class Ring:
    def __init__(self, S, name, n, shape, dt=F32):
        self.tiles = [S.sb("%s%d" % (name, i), shape, dt) for i in range(n)]
        self.free = list(self.tiles)
        self.name = name

    def get(self):
        assert self.free, "ring %s exhausted" % self.name
        return self.free.pop(0)

    def put(self, *ts):
        for t in ts:
            assert t not in self.free
            self.free.append(t)

import numpy as np

D = 1024
NTOK = 512
FH = 2816
NHT = FH // 128
INW = 11552
EPS = 1e-6

OFF_A = 0
OFF_B = 1024
OFF_CV = OFF_B + 8 * 384
OFF_CP = OFF_CV + 1024
OFF_CL = OFF_CP + 8 * 256
OFF_D = OFF_CL + 288
OFF_E = OFF_D + 1024
assert OFF_E + 3072 == INW


def perm_cols():
    p = []
    HG = 0
    p += list(range(HG + 2048, HG + 3072))
    for h in range(8):
        p += list(range(HG + h * 128, HG + (h + 1) * 128))
        p += list(range(HG + 1024 + h * 128, HG + 1024 + (h + 1) * 128))
        p += list(range(HG + 3072 + h * 128, HG + 3072 + (h + 1) * 128))
    RW = 4096
    p += list(range(RW + 2048, RW + 3072))
    for q in range(8):
        p += list(range(RW + q * 128, RW + (q + 1) * 128))
        p += list(range(RW + 1024 + q * 128, RW + 1024 + (q + 1) * 128))
    p += list(range(RW + 3072, RW + 3360))
    S5 = RW + 3360
    p += list(range(S5, S5 + 1024))
    p += list(range(S5 + 1024, S5 + 1024 + 3072))
    p = np.array(p, dtype=np.int64)
    assert p.shape[0] == INW and len(set(p.tolist())) == INW
    return p


VEC_NAMES = ["mix_norm", "ffn_norm", "hg_lb_logits", "hg_onorm", "rw_w0", "rw_a0", "rw_v0", "rw_k_k", "rw_k_a",
             "rw_r_k", "rw_ln_w", "rw_ln_b", "s5_d", "s5_glu_b", "final_norm"]
NV = len(VEC_NAMES)
VI = {n: i for i, n in enumerate(VEC_NAMES)}


class K:
    pass


def build(L, T, dbg=(), stub=()):
    NTT = T // NTOK
    nc = bass.Bass("TRN2", target_bir_lowering=False)
    S = Sched(nc)
    k = K()
    k.S = S
    k.nc = nc
    k.L = L
    k.T = T

    def ext(name, shape):
        return Buf(nc.dram_tensor(name, list(shape), F32, kind="ExternalInput"), name)

    xT = ext("xT", [D, T])
    w_in = ext("w_in", [L, D, INW])
    w_branch = ext("w_branch", [L, 3 * D, D])
    w_out = ext("w_out", [L, D, D])
    w_gate = ext("ffn_w_gate", [L, D, FH])
    w_up = ext("ffn_w_up", [L, D, FH])
    w_down = ext("ffn_w_down", [L, FH, D])
    vecs = ext("vecs", [L, 128, NV * 8])
    out = Buf(nc.dram_tensor("out", [D, T], F32, kind="ExternalOutput"), "out")
    dbg_out = {}
    for name in dbg:
        dbg_out[name] = Buf(nc.dram_tensor("dbg_" + name, [D, T], F32, kind="ExternalOutput"), "dbg_" + name)

    class WT:
        def __init__(self, name, src, blocks):
            self.name = name
            self.src = src
            self.blocks = {}
            off = 0
            for (r0, kc, c0, n) in blocks:
                self.blocks[(r0, c0)] = (off, kc, n)
                off += kc * n
            self.total = off
            self.scr = S.dram(name + "_t", [L, 128, off], BF16)

        def __getitem__(self, idx):
            l, rs, cs = idx
            r0 = 0 if rs.start is None else rs.start
            return ("wt", self, l, r0, cs.start)

    in_blocks = [(0, 8, 0, 512), (0, 8, 512, 512)] + [(0, 8, OFF_B + 384 * h, 384) for h in range(8)] \
        + [(0, 8, OFF_CV + 512 * i, 512) for i in range(2)] + [(0, 8, OFF_CP + 512 * i, 512) for i in range(4)] \
        + [(0, 8, OFF_CL, 288)] + [(0, 8, OFF_D + 512 * i, 512) for i in range(2)] + [(0, 8, OFF_E + 512 * i, 512) for i in range(6)]
    w_in_b = WT("w_in_b", w_in, in_blocks)
    w_branch_b = WT("w_branch_b", w_branch, [(br * D, 8, c0, 512) for br in range(3) for c0 in (0, 512)])
    w_out_b = WT("w_out_b", w_out, [(0, 8, 0, 512), (0, 8, 512, 512)])
    fblocks = [(0, 8, 512 * i, 512) for i in range(5)] + [(0, 8, 2560, 256)]
    w_gate_b = WT("w_gate_b", w_gate, fblocks)
    w_up_b = WT("w_up_b", w_up, fblocks)
    w_down_b = WT("w_down_b", w_down, [(0, NHT, 128 * i, 128) for i in range(8)])
    hT = S.dram("hT", [D, T], F32)

    def conv_items(l):
        items = []
        for wt in (w_in_b, w_branch_b, w_out_b, w_gate_b, w_up_b, w_down_b):
            for (r0, c0), (off, kc, n) in wt.blocks.items():
                for c in range(kc):
                    items.append((wt, l, r0 + c * 128, c0, n, off + c * n))
        return items

    def do_conv(items):
        for it in items:
            (wt, l, r, c0, n, off) = it
            i = k.cvi % 3
            k.cvi += 1
            S.dma(cst32[i][:, 0:n], wt.src[l, r:r + 128, c0:c0 + n])
            S.copy(cst16[i][:, 0:n], cst32[i][:, 0:n], e="pool")
            k.cvpend.append((wt.scr[l, :, off:off + n], cst16[i][:, 0:n]))
            if len(k.cvpend) > 1:
                d, sv = k.cvpend.pop(0)
                S.dma(d, sv)

    def flush_conv():
        while k.cvpend:
            d, sv = k.cvpend.pop(0)
            S.dma(d, sv)

    k.cvpend = []
    k.cvq = []

    def conv_tick(n=2):
        if k.cvq:
            do_conv(k.cvq[:n])
            del k.cvq[:n]
    k.conv_tick = conv_tick
    k.cvi = 0
    cst32 = [S.sb("cst32_%d" % i, [128, 512]) for i in range(3)]
    cst16 = [S.sb("cst16_%d" % i, [128, 512], BF16) for i in range(3)]

    ident = S.sb("ident", [128, 128])
    identb = S.sb("identb", [128, 128], BF16)
    onesb = S.sb("onesb", [128, 128], BF16)
    onesf = S.sb("onesf", [128, 128])
    vec = S.sb("vec", [128, NV, 8])
    xn = S.sb("xn", [128, 8, NTOK], BF16)
    merged = S.sb("merged", [128, 8, NTOK])
    ocur = S.sb("ocur", [128, 8, NTOK], BF16)
    wbufs = [S.sb("wbuf%d" % i, [128, 8 * 512], BF16) for i in range(3)]
    k.wi = 0
    R32 = Ring(S, "r32_", 14, [128, NTOK])
    R16 = Ring(S, "r16_", 8, [128, NTOK], BF16)
    BIG = [S.sb("big%d" % i, [128, NTOK]) for i in range(20)]
    psb = [S.ps("pb%d" % i, [128, 512]) for i in range(8)]
    k.pi = 0

    def ps():
        b = psb[k.pi % 7]
        k.pi += 1
        return b
    psheld = psb[7]
    k.ppi = {"a": 0, "b": 0}

    def ps_pool(name):
        if name == "a":
            b = psb[k.ppi["a"] % 4]
        else:
            b = psb[4 + k.ppi["b"] % 3]
        k.ppi[name] += 1
        return b

    def wbuf():
        b = wbufs[k.wi % 3]
        k.wi += 1
        return b

    def load_w(desc, kc, ncols):
        _, wt, l, r0, c0 = desc
        off, kc_, n_ = wt.blocks[(r0, c0)]
        assert kc_ == kc and n_ == ncols, (wt.name, r0, c0, kc, ncols, kc_, n_)
        b = wbuf()
        v = b.v()[:, 0:kc * ncols]
        S.dma(v, wt.scr[l, :, off:off + kc * ncols])
        return v.rr("p (c n) -> p c n", c=kc)

    S.memset(onesf.v(), 1.0)
    S.memset(onesb.v(), 1.0)
    S.asel(ident.v(), onesf.v(), [[-1, 128]], ALU.is_equal, 0.0, 0, 1)
    S.copy(identb.v(), ident.v(), e="pool")

    vecs_all = [vecs[l] for l in range(L)] + [vecs[L - 1]] * (4 - L)
    k.__dict__.update(locals())

    do_conv(conv_items(0))
    flush_conv()

    for c in range(8):
        S.dma(hT[c * 128:(c + 1) * 128, :], xT[c * 128:(c + 1) * 128, :])

    def rmsnorm_tile(src_dram, l, j, gidx):
        t0 = j * NTOK
        hts = []
        pss = ps()
        for c in range(8):
            ht = R32.get()
            S.dma(ht.v(), src_dram[c * 128:(c + 1) * 128, t0:t0 + NTOK])
            sq = R16.get()
            S.act(sq.v(), ht.v(), AF.Square)
            S.mm(pss.v(), onesb.v(), sq.v(), start=(c == 0), stop=(c == 7))
            R16.put(sq)
            hts.append(ht)
        rstd = R32.get()
        S.act(rstd.v(), pss.v(), AF.Sqrt, bias=k.epsb[:, 0:1], scale=1.0 / D)
        S.recip(rstd.v(), rstd.v())
        for c in range(8):
            S.stt(xn[:, c, :], hts[c].v(), vec[:, gidx, c:c + 1], rstd.v(), ALU.mult, ALU.mult)
            R32.put(hts[c])
        R32.put(rstd)

    k.rmsnorm_tile = rmsnorm_tile
    epsb = S.sb("epsb", [128, 4])
    S.memset(epsb[:, 0:1], EPS)
    S.memset(epsb[:, 1:2], 0.0)
    k.epsb = epsb

    def ffn_tile(l, j):
        t0 = j * NTOK
        rmsnorm_tile(hT, l, j, VI["ffn_norm"])
        def act_tile(ht):
            b = BIG[ht // 2]
            return b.v().bitcast(BF16)[:, (ht % 2) * NTOK:(ht % 2 + 1) * NTOK]
        for blk in range(6):
            c0 = blk * 512
            ncol = min(512, FH - c0)
            wg = load_w(w_gate_b[l, :, c0:c0 + ncol], 8, ncol)
            wu = load_w(w_up_b[l, :, c0:c0 + ncol], 8, ncol)
            conv_tick()
            for s in range(ncol // 128):
                ht = (c0 // 128) + s
                pg = ps()
                pu = ps()
                for c in range(8):
                    S.mm(pg.v(), wg[:, c, s * 128:(s + 1) * 128], xn[:, c, :], start=(c == 0), stop=(c == 7))
                for c in range(8):
                    S.mm(pu.v(), wu[:, c, s * 128:(s + 1) * 128], xn[:, c, :], start=(c == 0), stop=(c == 7))
                sg = R32.get()
                S.act(sg.v(), pg.v(), AF.Silu)
                S.tt(act_tile(ht), sg.v(), pu.v(), ALU.mult)
                R32.put(sg)
        for dt_ in range(8):
            wd = load_w(w_down_b[l, :, dt_ * 128:(dt_ + 1) * 128], NHT, 128)
            conv_tick()
            po = ps()
            for ht in range(NHT):
                S.mm(po.v(), wd[:, ht, :], act_tile(ht), start=(ht == 0), stop=(ht == NHT - 1))
            hres = R32.get()
            S.dma(hres.v(), hT[dt_ * 128:(dt_ + 1) * 128, t0:t0 + NTOK])
            S.tt(hres.v(), hres.v(), po.v(), ALU.add)
            S.dma(hT[dt_ * 128:(dt_ + 1) * 128, t0:t0 + NTOK], hres.v())
            R32.put(hres)

    k.ffn_tile = ffn_tile

    def merge_branch(l, j, br, first):
        for half in range(2):
            wb = load_w(w_branch_b[l, br * D:(br + 1) * D, half * 512:(half + 1) * 512], 8, 512)
            wg = load_w(w_in_b[l, :, OFF_E + br * D + half * 512: OFF_E + br * D + (half + 1) * 512], 8, 512)
            conv_tick()
            for s in range(4):
                dt_ = half * 4 + s
                pg = ps()
                pb = ps()
                for c in range(8):
                    S.mm(pg.v(), wg[:, c, s * 128:(s + 1) * 128], xn[:, c, :], start=(c == 0), stop=(c == 7))
                for c in range(8):
                    S.mm(pb.v(), wb[:, c, s * 128:(s + 1) * 128], ocur[:, c, :], start=(c == 0), stop=(c == 7))
                sg = R32.get()
                S.act(sg.v(), pg.v(), AF.Sigmoid)
                if first:
                    S.tt(merged[:, dt_, :], sg.v(), pb.v(), ALU.mult)
                else:
                    S.tt(sg.v(), sg.v(), pb.v(), ALU.mult)
                    S.tt(merged[:, dt_, :], merged[:, dt_, :], sg.v(), ALU.add, e="pool")
                R32.put(sg)

    k.merge_branch = merge_branch

    def wout_tile(l, j):
        t0 = j * NTOK
        for c in range(8):
            S.copy(ocur[:, c, :], merged[:, c, :], e=("act" if c % 2 else "dve"))
        for half in range(2):
            wo = load_w(w_out_b[l, :, half * 512:(half + 1) * 512], 8, 512)
            for s in range(4):
                dt_ = half * 4 + s
                po = ps()
                for c in range(8):
                    S.mm(po.v(), wo[:, c, s * 128:(s + 1) * 128], ocur[:, c, :], start=(c == 0), stop=(c == 7))
                hres = R32.get()
                S.dma(hres.v(), hT[dt_ * 128:(dt_ + 1) * 128, t0:t0 + NTOK])
                S.tt(hres.v(), hres.v(), po.v(), ALU.add)
                S.dma(hT[dt_ * 128:(dt_ + 1) * 128, t0:t0 + NTOK], hres.v())
                R32.put(hres)

    k.wout_tile = wout_tile
    return k


def finalize(k):
    S = k.S
    L, T = k.L, k.T
    for j in range(T // NTOK):
        t0 = j * NTOK
        hts = []
        pss = k.ps()
        for c in range(8):
            ht = k.R32.get()
            S.dma(ht.v(), k.hT[c * 128:(c + 1) * 128, t0:t0 + NTOK])
            sq = k.R16.get()
            S.act(sq.v(), ht.v(), AF.Square)
            S.mm(pss.v(), k.onesb.v(), sq.v(), start=(c == 0), stop=(c == 7))
            k.R16.put(sq)
            hts.append(ht)
        rstd = k.R32.get()
        S.act(rstd.v(), pss.v(), AF.Sqrt, bias=k.epsb[:, 0:1], scale=1.0 / D)
        S.recip(rstd.v(), rstd.v())
        for c in range(8):
            S.stt(hts[c].v(), hts[c].v(), k.vec[:, VI["final_norm"], c:c + 1], rstd.v(), ALU.mult, ALU.mult)
            S.dma(k.out[c * 128:(c + 1) * 128, t0:t0 + NTOK], hts[c].v())
            k.R32.put(hts[c])
        k.R32.put(rstd)
    S.finish([k.out] + list(k.dbg_out.values()))


def stub_mixer(k, l, j, col0):
    S = k.S
    for half in range(2):
        w = k.load_w(k.w_in_b[l, :, col0 + half * 512: col0 + (half + 1) * 512], 8, 512)
        for s in range(4):
            p = k.ps()
            for c in range(8):
                S.mm(p.v(), w[:, c, s * 128:(s + 1) * 128], k.xn[:, c, :], start=(c == 0), stop=(c == 7))
            S.copy(k.ocur[:, half * 4 + s, :], p.v(), e="act")


def run_layers(k, mixers):
    S = k.S
    for l in range(k.L):
        S.dma(k.vec.v().rr("p v c -> p (v c)"), k.vecs[l])
        nt = k.T // NTOK
        items = k.conv_items(l + 1) if l + 1 < k.L else []
        per = (len(items) + nt - 1) // nt
        for j in range(nt):
            k.do_conv(items[j * per:(j + 1) * per])
            k.rmsnorm_tile(k.hT, l, j, VI["mix_norm"])
            for bi, mx in enumerate(mixers):
                mx(k, l, j)
                k.merge_branch(l, j, bi, bi == 0)
            k.wout_tile(l, j)
            k.ffn_tile(l, j)
    finalize(k)


def hg_setup(k):
    S = k.S
    L = k.L
    k.resetmask = S.sb("resetmask", [128, NTOK])
    S.memset(k.resetmask.v(), 1.0)
    S.memset(k.resetmask.v().rr("p (c t) -> p c t", t=64)[:, :, 0:1], 0.0)
    k.maskincl = S.sb("maskincl", [128, 64])
    k.maskstr = S.sb("maskstr", [128, 64])
    for half in range(2):
        sl = slice(half * 64, half * 64 + 64)
        S.asel(k.maskincl[sl, :], k.onesf[sl, 0:64], [[1, 64]], ALU.is_ge, 0.0, 0, -1)
        S.asel(k.maskstr[sl, :], k.onesf[sl, 0:64], [[1, 64]], ALU.is_gt, 0.0, 0, -1)
    k.lball = S.sb("lball", [128, 4, 8])
    k.omlall = S.sb("omlall", [128, 4, 8])
    E = S.sb("lbE", [128, 4, 8])
    sm = S.sb("lbsum", [128, 8])
    c0 = VI["hg_lb_logits"] * 8
    S.memset(E.v(), 0.0)
    for l in range(4):
        S.dma(E[:, l, :], k.vecs_all[min(l, L - 1) if False else l][:, c0:c0 + 8])
    S.act(E.v(), E.v(), AF.Exp)
    S.tt(sm.v(), E[:, 0, :], E[:, 1, :], ALU.add)
    S.tt(sm.v(), sm.v(), E[:, 2, :], ALU.add)
    S.tt(sm.v(), sm.v(), E[:, 3, :], ALU.add)
    S.recip(sm.v(), sm.v())
    for l in range(4):
        S.tt(E[:, l, :], E[:, l, :], sm.v(), ALU.mult)
    S.memset(k.lball[:, 0, :], 0.0, e="dve")
    for l in range(1, 4):
        S.tt(k.lball[:, l, :], k.lball[:, l - 1, :], E[:, l, :], ALU.add)
    S.ts(k.lball.v(), k.lball.v(), 0.0, ALU.max)
    S.ts(k.omlall.v(), k.lball.v(), -1.0, ALU.mult, 1.0, ALU.add)
    k.hgS = S.sb("hgS", [128, 8, 128])
    k.hgSb = S.sb("hgSb", [128, 8, 128], BF16)
    k.sct = [S.sb("hgsct%d" % i, [128, 64], BF16) for i in range(4)]
    k.scti = 0


def hg_layer_init(k, l):
    k.S.memset(k.hgS.v(), 0.0)
    k.S.memset(k.hgSb.v(), 0.0)


def hgrn2_tile(k, l, j):
    S = k.S
    R32, R16, BIG, xn, ps = k.R32, k.R16, k.BIG, k.xn, k.ps
    wib = k.w_in_b
    vtm = [BIG[tb].v().bitcast(BF16) for tb in range(4)]
    for half in range(2):
        w = k.load_w(wib[l, :, OFF_A + half * 512: OFF_A + (half + 1) * 512], 8, 512)
        for tb in range(4):
            p = ps()
            for c in range(8):
                S.mm(p.v(), xn[:, c, tb * 128:(tb + 1) * 128], w[:, c, :], start=(c == 0), stop=(c == 7))
            S.copy(vtm[tb][:, half * 512:(half + 1) * 512], p.v(), e="act")
    for h in range(8):
        w = k.load_w(wib[l, :, OFF_B + h * 384: OFF_B + (h + 1) * 384], 8, 384)
        k.conv_tick()
        lbv = k.lball[:, l, h:h + 1]
        omlv = k.omlall[:, l, h:h + 1]

        def proj(col0):
            p = ps()
            for c in range(8):
                S.mm(p.v(), w[:, c, col0:col0 + 128], xn[:, c, :], start=(c == 0), stop=(c == 7))
            return p
        pq = proj(0)
        q = R32.get()
        S.act(q.v(), pq.v(), AF.Silu)
        pf = proj(128)
        f = R32.get()
        S.act(f.v(), pf.v(), AF.Sigmoid)
        S.ts(f.v(), f.v(), omlv, ALU.mult, lbv, ALU.add)
        lf = R32.get()
        S.act(lf.v(), f.v(), AF.Ln)
        b = R32.get()
        S.scan(b.v(), k.resetmask.v(), lf.v(), 0.0)
        R32.put(lf)
        kk = R32.get()
        S.ts(kk.v(), f.v(), -1.0, ALU.mult, 1.0, ALU.add, e="pool")
        R32.put(f)
        eb = R32.get()
        S.act(eb.v(), b.v(), AF.Exp)
        enb = R32.get()
        S.act(enb.v(), b.v(), AF.Exp, scale=-1.0)
        R32.put(b)
        qt = R16.get()
        S.tt(qt.v(), q.v(), eb.v(), ALU.mult)
        R32.put(q)
        ktf = R32.get()
        S.tt(ktf.v(), kk.v(), enb.v(), ALU.mult, e="pool")
        R32.put(kk, enb)
        ktb = R16.get()
        S.copy(ktb.v(), ktf.v(), e="pool")
        kdT = R16.get()
        eb3 = eb.v().rr("p (c t) -> p c t", t=64)
        S.tt(kdT.v().rr("p (c t) -> p c t", t=64), ktf.v().rr("p (c t) -> p c t", t=64),
             eb3[:, :, 63:64].bc([128, 8, 64]), ALU.mult)
        R32.put(ktf)
        ptr = ps()
        ptb = ptr.v().bitcast(BF16)
        for blk in range(4):
            S.tr(ptb[:, blk * 128:(blk + 1) * 128], kdT[:, blk * 128:(blk + 1) * 128], k.identb.v())
        kdtm = R16.get()
        S.copy(kdtm.v(), ptb[:, 0:512], e="act")
        R16.put(kdT)
        pog = proj(256)
        ogs = R32.get()
        S.act(ogs.v(), pog.v(), AF.Silu)
        osb = R32.get()
        for c in range(8):
            cs = slice(c * 64, (c + 1) * 64)
            pb = (c % 2) * 64
            rows = slice(pb, pb + 64)
            kdc = kdtm.v().rr("p (b n) -> p b n", b=4)[rows, c // 2, :]
            vch = vtm[c // 2][rows, h * 128:(h + 1) * 128]
            p1 = ps()
            S.mm(p1[rows, 0:64], ktb[:, cs], qt[:, cs])
            sct = k.sct[k.scti % 4]
            k.scti += 1
            S.tt(sct[rows, :], p1[rows, 0:64], k.maskincl[rows, :], ALU.mult)
            p2 = ps()
            S.mm(p2[:, 0:64], vch, sct[rows, :], start=True, stop=False)
            S.mm(p2[:, 0:64], k.hgSb[:, h, :], qt[:, cs], start=False, stop=True)
            S.copy(osb[:, cs], p2[:, 0:64], e="act")
            p3 = ps()
            S.mm(p3[:, 0:128], kdc, vch)
            S.stt(k.hgS[:, h, :], k.hgS[:, h, :], eb[:, c * 64 + 63:c * 64 + 64], p3[:, 0:128], ALU.mult, ALU.add)
            S.copy(k.hgSb[:, h, :], k.hgS[:, h, :], e="pool")
        R32.put(eb)
        R16.put(qt, ktb, kdtm)
        sq = R16.get()
        S.act(sq.v(), osb.v(), AF.Square)
        pn = ps()
        S.mm(pn.v(), k.onesb.v(), sq.v())
        R16.put(sq)
        rstd = R32.get()
        S.act(rstd.v(), pn.v(), AF.Sqrt, bias=k.epsb[:, 0:1], scale=1.0 / 128)
        S.recip(rstd.v(), rstd.v())
        S.stt(osb.v(), osb.v(), k.vec[:, VI["hg_onorm"], h:h + 1], rstd.v(), ALU.mult, ALU.mult)
        S.tt(k.ocur[:, h, :], osb.v(), ogs.v(), ALU.mult, e="pool")
        R32.put(rstd, ogs, osb)
    if "hg" in k.dbg_out:
        t0 = j * NTOK
        for h in range(8):
            tmp = R32.get()
            S.copy(tmp.v(), k.ocur[:, h, :], e="pool")
            S.dma(k.dbg_out["hg"][h * 128:(h + 1) * 128, t0:t0 + NTOK], tmp.v())
            R32.put(tmp)


C0 = 0.6065306597126334
GN_EPS = 64e-5


def rw_setup(k):
    S = k.S
    nc = k.nc
    L, T = k.L, k.T

    def ext(name, shape):
        return Buf(nc.dram_tensor(name, list(shape), F32, kind="ExternalInput"), name)
    k.rw_w2 = ext("rw_w2", [L, 64, D])
    k.rw_a2 = ext("rw_a2", [L, 64, D])
    k.rw_g2 = ext("rw_g2", [L, 160, D])
    k.rw_v1 = ext("rw_v1", [L, D, 32])
    k.rw_v2 = ext("rw_v2", [L, 32, D])
    k.rwmu_d = ext("rwmu", [L, 128, 27])
    k.vfirst = S.dram("vfirst", [D, T])
    k.vdram = S.dram("vdram", [D, NTOK])
    k.w2b = S.sb("wa2b", [128, D], BF16)
    k.a2b = k.w2b
    k.g2b = S.sb("g2b", [128, 2, D], BF16)
    k.v1b = S.sb("v1b", [128, 8, 32], BF16)
    k.rwmu = S.sb("rwmu_s", [128, 27])
    k.omka = S.sb("omka", [128, 8])
    k.rwP = S.sb("rwP", [128, 8, 64])
    k.rwPb = S.sb("rwPb", [128, 8, 64], BF16)
    k.rwcarry = S.sb("rwcarry", [128, 27])
    k.zx = [S.sb("zx%d" % i, [128, NTOK + 8]) for i in range(2)]
    k.zxi = 0
    k.blockb = S.sb("blockb", [128, 128], BF16)
    k.blockf = S.sb("blockf", [128, 128])
    for t in (k.blockb, k.blockf):
        S.memset(t.v(), 0.0)
        S.memset(t[0:64, 0:64], 1.0)
        S.memset(t[64:128, 64:128], 1.0)
    k.mask4 = S.sb("mask4", [128, 128])
    S.copy(k.mask4[:, 0:64], k.maskstr.v(), e="pool")
    S.copy(k.mask4[:, 64:128], k.maskincl.v(), e="pool")
    k.msb = [S.sb("msb%d" % i, [128, 128], BF16) for i in range(32)]
    k.iws = [S.sb("iws%d" % i, [128, 128], BF16) for i in range(40)]
    for t in k.iws:
        S.memset(t.v(), 0.0)
    k.ttb = [S.sb("ttb%d" % i, [128, 128], BF16) for i in range(16)]
    k.zsb = [S.sb("zsb%d" % i, [128, 128], BF16) for i in range(4)]
    k.gC = [S.sb("gC%d" % i, [128, 8]) for i in range(4)]
    k.tiny = S.sb("rwtiny", [128, 1])
    S.memset(k.tiny.v(), 1e-24)
    k.gneps = S.sb("gneps", [128, 1])
    S.memset(k.gneps.v(), GN_EPS)


def rw_layer_init(k, l):
    S = k.S
    st = k.cst32

    def ld(dst_view, src_view, rows, n, i, rbase=0):
        for h0 in range(0, n, 512):
            S.dma(st[i][rbase:rbase + rows, 0:512], src_view[:, h0:h0 + 512], q="pool")
            S.copy(dst_view[:, h0:h0 + 512], st[i][rbase:rbase + rows, 0:512], e="pool")
    ld(k.w2b[0:64, :], k.rw_w2[l], 64, D, 0)
    ld(k.a2b[64:128, :], k.rw_a2[l], 64, D, 1, rbase=64)
    ld(k.g2b[:, 0, :], k.rw_g2[l, 0:128, :], 128, D, 0)
    ld(k.g2b[0:32, 1, :], k.rw_g2[l, 128:160, :], 32, D, 1)
    ld(k.g2b[32:64, 1, :], k.rw_v2[l], 32, D, 0, rbase=32)
    S.dma(st[1][:, 0:256].rr("p (c n) -> p c n", c=8), k.rw_v1[l].rr("(c p) n -> p c n", p=128))
    S.copy(k.v1b.v(), st[1][:, 0:256].rr("p (c n) -> p c n", c=8), e="pool")
    S.dma(k.rwmu.v(), k.rwmu_d[l])
    S.memset(k.rwP.v(), 0.0)
    S.memset(k.rwPb.v(), 0.0)
    S.memset(k.rwcarry.v(), 0.0)


def rw_layer_vecs(k, l):
    k.S.ts(k.omka.v(), k.vec[:, VI["rw_k_a"], :], -1.0, ALU.mult, 1.0, ALU.add)


def rwkv_tile(k, l, j):
    S = k.S
    R32, R16, BIG, xn, ps = k.R32, k.R16, k.BIG, k.xn, k.ps
    wib = k.w_in_b
    t0 = j * NTOK
    V = lambda name, c: k.vec[:, VI[name], c:c + 1]

    def lerp(pview, ti, n=128):
        zx = k.zx[k.zxi % 2]
        k.zxi += 1
        S.copy(zx[0:n, 1:NTOK + 1], pview, e="act")
        S.copy(zx[0:n, 0:1], k.rwcarry[0:n, ti:ti + 1], e="act")
        S.copy(k.rwcarry[0:n, ti:ti + 1], zx[0:n, NTOK:NTOK + 1], e="act")
        d = R32.get()
        S.tt(d[0:n, :], zx[0:n, 0:NTOK], zx[0:n, 1:NTOK + 1], ALU.subtract)
        S.stt(d[0:n, :], d[0:n, :], k.rwmu[0:n, ti:ti + 1], zx[0:n, 1:NTOK + 1], ALU.mult, ALU.add)
        return d

    def projw(w, col0, n=128):
        p = ps()
        for c in range(8):
            S.mm(p[0:n, :], w[:, c, col0:col0 + n], xn[:, c, :], start=(c == 0), stop=(c == 7))
        return p

    w = k.load_w(wib[l, :, OFF_CL:OFF_CL + 288], 8, 288)
    z = lerp(projw(w, 0).v(), 24)
    lora1 = R16.get()
    S.act(lora1[0:64, :], z[0:64, :], AF.Tanh)
    S.copy(lora1[64:128, :], z[64:128, :], e="pool")
    R32.put(z)
    z = lerp(projw(w, 128).v(), 25)
    gsb0 = R16.get()
    S.act(gsb0.v(), z.v(), AF.Sigmoid)
    R32.put(z)
    z = lerp(projw(w, 256, 32)[0:32, :], 26, 32)
    gsb1 = R16.get()
    S.act(gsb1[0:32, :], z[0:32, :], AF.Sigmoid)
    R32.put(z)

    if k.stage == "a":
        R16.put(lora1, gsb0, gsb1)
        return
    vls = []
    for half in range(2):
        w = k.load_w(wib[l, :, OFF_CV + half * 512:OFF_CV + (half + 1) * 512], 8, 512)
        for s in range(4):
            vt = half * 4 + s
            vl = lerp(projw(w, s * 128).v(), vt)
            if l == 0:
                S.dma(k.vfirst[vt * 128:(vt + 1) * 128, t0:t0 + NTOK], vl.v())
                S.dma(k.vdram[vt * 128:(vt + 1) * 128, :], vl.v())
                R32.put(vl)
            else:
                vb = R16.get()
                S.copy(vb.v(), vl.v(), e="pool")
                S.mm(k.psheld[32:64, :], k.v1b[:, vt, :], vb.v(), start=(vt == 0), stop=(vt == 7))
                R16.put(vb)
                vls.append(vl)
    if l > 0:
        vv1 = R16.get()
        S.copy(vv1[32:64, :], k.psheld[32:64, :], e="act")
        for vt in range(8):
            vl = vls[vt]
            p2 = ps()
            S.mm(p2.v(), k.g2b[32:64, 1, vt * 128:(vt + 1) * 128], vv1[32:64, :])
            sv = R32.get()
            S.act(sv.v(), p2.v(), AF.Sigmoid, bias=V("rw_v0", vt))
            vf = R32.get()
            S.dma(vf.v(), k.vfirst[vt * 128:(vt + 1) * 128, t0:t0 + NTOK])
            S.tt(vf.v(), vf.v(), vl.v(), ALU.subtract)
            S.tt(vf.v(), vf.v(), sv.v(), ALU.mult, e="pool")
            S.tt(vl.v(), vl.v(), vf.v(), ALU.add)
            S.dma(k.vdram[vt * 128:(vt + 1) * 128, :], vl.v())
            R32.put(sv, vf, vl)
        R16.put(vv1)

    if k.stage == "b":
        R16.put(lora1, gsb0, gsb1)
        return
    for grp in range(2):
        prbs = []
        yTs = []
        for gi in range(4):
            p = grp * 4 + gi
            AR = BIG[gi].v().bitcast(BF16).rr("p (c s t) -> p c s t", c=8, s=2)
            BK = BIG[4 + gi].v().bitcast(BF16).rr("p (c s t) -> p c s t", c=8, s=2)
            BKtm = BIG[8 + gi].v().bitcast(BF16).rr("p (c n) -> p c n", c=8)
            VU = BIG[12 + gi].v().bitcast(BF16).rr("p (c h v) -> p c h v", c=8, h=2)
            yTs.append(BIG[16 + gi])
            if gi % 2 == 0:
                wp = k.load_w(wib[l, :, OFF_CP + (p // 2) * 512:OFF_CP + (p // 2 + 1) * 512], 8, 512)
            cb = (p % 2) * 256
            k.conv_tick()
            rl = lerp(projw(wp, cb).v(), 8 + 2 * p)
            kl = lerp(projw(wp, cb + 128).v(), 9 + 2 * p)
            pw = ps()
            S.mm(pw.v(), k.w2b[0:64, p * 128:(p + 1) * 128], lora1[0:64, :])
            sg = R32.get()
            S.act(sg.v(), pw.v(), AF.Sigmoid, bias=V("rw_w0", p))
            pa = ps()
            S.mm(pa.v(), k.a2b[64:128, p * 128:(p + 1) * 128], lora1[64:128, :])
            a = R32.get()
            S.act(a.v(), pa.v(), AF.Sigmoid, bias=V("rw_a0", p))
            kks = R16.get()
            S.act(kks.v(), kl.v(), AF.Square, scale=V("rw_k_k", p))
            pk = ps()
            S.mm(pk.v(), k.blockb.v(), kks.v())
            R16.put(kks)
            rn = R32.get()
            S.act(rn.v(), pk.v(), AF.Sqrt, bias=k.tiny[:, 0:1])
            S.recip(rn.v(), rn.v())
            kkn = R32.get()
            S.stt(kkn.v(), kl.v(), V("rw_k_k", p), rn.v(), ALU.mult, ALU.mult)
            R32.put(rn)
            kmod = R32.get()
            S.ts(kmod.v(), a.v(), V("rw_k_a", p), ALU.mult, k.omka[:, p:p + 1], ALU.add, e="pool")
            S.tt(kmod.v(), kmod.v(), kl.v(), ALU.mult, e="pool")
            R32.put(kl)
            prb = R16.get()
            S.stt(prb.v(), rl.v(), V("rw_r_k", p), kmod.v(), ALU.mult, ALU.mult)
            prbs.append(prb)
            cs = R32.get()
            S.scan(cs.v(), k.resetmask.v(), sg.v(), 0.0)
            csm = R32.get()
            S.tt(csm.v(), cs.v(), sg.v(), ALU.subtract, e="pool")
            R32.put(sg)
            G = R32.get()
            S.act(G.v(), cs.v(), AF.Exp, scale=-C0)
            G1 = csm
            S.act(G1.v(), csm.v(), AF.Exp, scale=-C0)
            Gi = cs
            S.act(Gi.v(), cs.v(), AF.Exp, scale=C0)
            S.copy(k.gC[gi].v(), G.v().rr("p (c t) -> p c t", t=64)[:, :, 63], e="pool")
            c3 = lambda t: t.v().rr("p (c t) -> p c t", t=64)
            S.stt(AR[:, :, 0, :], c3(kkn), -1.0, c3(G1), ALU.mult, ALU.mult)
            S.tt(AR[:, :, 1, :], c3(rl), c3(G), ALU.mult, e="pool")
            R32.put(rl, G, G1)
            bt = R32.get()
            S.tt(bt.v(), kkn.v(), a.v(), ALU.mult, e="pool")
            S.tt(bt.v(), bt.v(), Gi.v(), ALU.mult)
            kt = kmod
            S.tt(kt.v(), kmod.v(), Gi.v(), ALU.mult, e="pool")
            R32.put(kkn, a, Gi)
            BK2 = BK.rr("p (cb par) s t -> p cb par s t", par=2)
            bt4 = bt.v().rr("p (cb par t) -> p cb par t", par=2, t=64)
            kt4 = kt.v().rr("p (cb par t) -> p cb par t", par=2, t=64)
            S.copy(BK2[:, :, 0, 0, :], kt4[:, :, 0, :], e="act")
            S.copy(BK2[:, :, 0, 1, :], bt4[:, :, 0, :], e="dve")
            S.copy(BK2[:, :, 1, 0, :], bt4[:, :, 1, :], e="act")
            S.copy(BK2[:, :, 1, 1, :], kt4[:, :, 1, :], e="dve")
            R32.put(bt, kt)
            for half in range(2):
                pt = ps()
                ptb = pt.v().bitcast(BF16)
                for cc in range(4):
                    c = half * 4 + cc
                    S.tr(ptb[:, cc * 128:(cc + 1) * 128], BK[:, c, :, :].rr("p s t -> p (s t)"), k.identb.v())
                S.copy(BKtm[:, half * 4:(half + 1) * 4, :].rr("p c n -> p (c n)"), ptb[:, 0:512], e=("act" if half else "dve"))
            vl = R32.get()
            S.dma(vl.v(), k.vdram[p * 128:(p + 1) * 128, :])
            vb = R16.get()
            S.copy(vb.v(), vl.v(), e="pool")
            R32.put(vl)
            pt = ps()
            ptb = pt.v().bitcast(BF16)
            for blk in range(4):
                S.tr(ptb[:, blk * 128:(blk + 1) * 128], vb[:, blk * 128:(blk + 1) * 128], k.identb.v())
            R16.put(vb)
            VU2 = VU.rr("p (cb par) h v -> p cb par (h v)", par=2)
            pt4 = ptb[:, 0:512].rr("p (cb n) -> p cb n", cb=4)
            S.copy(VU2[0:64, :, 0, :], pt4[0:64, :, :], e="act")
            S.copy(VU2[64:128, :, 1, :], pt4[64:128, :, :], e="dve")
            k.rw_pair = getattr(k, "rw_pair", {})
            k.rw_pair[gi] = (AR, BK, BKtm, VU)

        def scores(b):
            msb = {}
            mi = (b % 2) * 16
            for gi in range(4):
                AR, BK, BKtm, VU = k.rw_pair[gi]
                for hh in range(2):
                    hr = slice(hh * 64, hh * 64 + 64)
                    for par in range(2):
                        c = 2 * b + par
                        pS = k.ps_pool("b")
                        S.mm(pS[:, 0:128], BK[hr, c, :, :].rr("p s t -> p (s t)"), AR[hr, c, :, :].rr("p s t -> p (s t)"))
                        m = k.msb[mi]
                        mi += 1
                        S.tt(m.v(), pS[:, 0:128], k.mask4.v(), ALU.mult)
                        msb[(gi, hh, par)] = m
            return msb

        def inverse_gen(b, msb):
            units = [(gi, hh) for gi in range(4) for hh in range(2)]
            ws = {}
            for ui, (gi, hh) in enumerate(units):
                Nt, Lt, N2, L2, Rt = k.iws[ui * 5:(ui + 1) * 5]
                S.copy(Nt[0:64, 0:64], msb[(gi, hh, 1)][0:64, 0:64], e="pool")
                S.copy(Nt[64:128, 64:128], msb[(gi, hh, 0)][64:128, 0:64], e="pool")
                ws[ui] = [Nt, Lt, N2, L2, Rt]
            yield
            for ui in range(8):
                Nt, Lt, N2, L2, Rt = ws[ui]
                pt = k.ps_pool("b")
                ptb = pt.v().bitcast(BF16)
                S.tr(ptb[:, 0:128], Nt.v(), k.identb.v())
                S.copy(Lt.v(), ptb[:, 0:128], e="act")
                S.tt(Rt.v(), Nt.v(), k.identb.v(), ALU.add, e="pool")
            yield
            for lev in range(5):
                last = (lev == 4)
                for half in range(2):
                    for ui in range(half * 4, half * 4 + 4):
                        Nt, Lt, N2, L2, Rt = ws[ui]
                        pl = k.ps_pool("b")
                        S.mm(pl[:, 0:128], Nt.v(), Lt.v())
                        S.copy(L2.v(), pl[:, 0:128], e="act")
                        if not last:
                            pn = k.ps_pool("b")
                            S.mm(pn[:, 0:128], Lt.v(), Nt.v())
                            S.copy(N2.v(), pn[:, 0:128], e="dve")
                    yield
                for half in range(2):
                    for ui in range(half * 4, half * 4 + 4):
                        Nt, Lt, N2, L2, Rt = ws[ui]
                        pr = k.ps_pool("b")
                        S.mm(pr[:, 0:128], L2.v(), Rt.v())
                        dst = k.ttb[(b % 2) * 8 + ui] if last else Rt
                        S.tt(dst.v(), pr[:, 0:128], Rt.v(), ALU.add)
                        ws[ui] = [N2, L2, Nt, Lt, Rt]
                    yield

        def steps_gen(b, msb):
            for par in range(2):
                c = 2 * b + par
                Vr = slice(0, 64) if par == 0 else slice(64, 128)
                Ur = slice(64, 128) if par == 0 else slice(0, 64)
                pZs, pUs, pYs, pPs = {}, {}, {}, {}
                for gi in range(4):
                    p = grp * 4 + gi
                    AR, BK, BKtm, VU = k.rw_pair[gi]
                    pZ = k.ps_pool("a")
                    for hh in range(2):
                        hr = slice(hh * 64, hh * 64 + 64)
                        S.mm(pZ[Ur, hh * 64:(hh + 1) * 64], AR[hr, c, 0, :], k.rwPb[hr, p, :], start=True, stop=False)
                        S.mm(pZ[Ur, hh * 64:(hh + 1) * 64], msb[(gi, hh, par)][Vr, 0:64], VU[Vr, c, hh, :], start=False, stop=True)
                    pZs[gi] = pZ
                for gi in range(4):
                    S.copy(k.zsb[gi][Ur, :], pZs[gi][Ur, 0:128], e="act")
                yield
                for gi in range(4):
                    pU = k.ps_pool("a")
                    for hh in range(2):
                        S.mm(pU[Ur, hh * 64:(hh + 1) * 64], k.ttb[(b % 2) * 8 + gi * 2 + hh][Ur, Ur], k.zsb[gi][Ur, hh * 64:(hh + 1) * 64])
                    pUs[gi] = pU
                for gi in range(4):
                    AR, BK, BKtm, VU = k.rw_pair[gi]
                    S.copy(VU[Ur, c, :, :].rr("p h v -> p (h v)"), pUs[gi][Ur, 0:128], e="dve")
                yield
                for gi in range(4):
                    p = grp * 4 + gi
                    AR, BK, BKtm, VU = k.rw_pair[gi]
                    pY = k.ps_pool("a")
                    for hh in range(2):
                        hr = slice(hh * 64, hh * 64 + 64)
                        S.mm(pY[hr, 0:64], k.rwPb[hr, p, :], AR[hr, c, 1, :], start=True, stop=False)
                        S.mm(pY[hr, 0:64], VU[:, c, hh, :], msb[(gi, hh, par)][:, 64:128], start=False, stop=True)
                    pYs[gi] = pY
                for gi in range(4):
                    S.copy(yTs[gi][:, c * 64:(c + 1) * 64], pYs[gi][:, 0:64], e="act")
                yield
                for gi in range(4):
                    p = grp * 4 + gi
                    AR, BK, BKtm, VU = k.rw_pair[gi]
                    pP = k.ps_pool("a")
                    for hh in range(2):
                        hr = slice(hh * 64, hh * 64 + 64)
                        S.mm(pP[hr, 0:64], BKtm[:, c, hr], VU[:, c, hh, :])
                    pPs[gi] = pP
                    S.ts(k.rwP[:, p, :], k.rwP[:, p, :], k.gC[gi][:, c:c + 1], ALU.mult, e="pool")
                for gi in range(4):
                    p = grp * 4 + gi
                    S.stt(k.rwP[:, p, :], pPs[gi][:, 0:64], k.gC[gi][:, c:c + 1], k.rwP[:, p, :], ALU.mult, ALU.add)
                    S.copy(k.rwPb[:, p, :], k.rwP[:, p, :], e="act")
                yield

        msbs = {0: scores(0)}
        for _ in inverse_gen(0, msbs[0]):
            pass
        for b in range(4):
            g1 = steps_gen(b, msbs[b])
            k.conv_tick()
            g2 = None
            if b < 3:
                msbs[b + 1] = scores(b + 1)
                g2 = inverse_gen(b + 1, msbs[b + 1])
            d1 = d2 = False
            while not (d1 and (d2 or g2 is None)):
                if not d1:
                    try:
                        next(g1)
                    except StopIteration:
                        d1 = True
                if g2 is not None and not d2:
                    try:
                        next(g2)
                        next(g2)
                    except StopIteration:
                        d2 = True

        for gi in range(4):
            p = grp * 4 + gi
            y = yTs[gi]
            pm = ps()
            S.mm(pm.v(), k.blockf.v(), y.v())
            ysq = R32.get()
            S.act(ysq.v(), y.v(), AF.Square)
            pq = ps()
            S.mm(pq.v(), k.blockf.v(), ysq.v())
            m = R32.get()
            S.act(m.v(), pm.v(), AF.Copy, scale=1.0 / 64)
            S.act(ysq.v(), m.v(), AF.Square)
            var = R32.get()
            S.stt(var.v(), pq.v(), 1.0 / 64, ysq.v(), ALU.mult, ALU.subtract)
            S.act(var.v(), var.v(), AF.Sqrt, bias=k.gneps[:, 0:1])
            S.recip(var.v(), var.v())
            S.tt(y.v(), y.v(), m.v(), ALU.subtract, e="pool")
            S.tt(y.v(), y.v(), var.v(), ALU.mult)
            S.ts(y.v(), y.v(), V("rw_ln_w", p), ALU.mult, V("rw_ln_b", p), ALU.add, e="pool")
            R32.put(ysq, m, var)
            pbn = ps()
            S.mm(pbn.v(), k.blockb.v(), prbs[gi].v())
            R16.put(prbs[gi])
            vl = R32.get()
            S.dma(vl.v(), k.vdram[p * 128:(p + 1) * 128, :])
            S.tt(vl.v(), vl.v(), pbn.v(), ALU.mult)
            S.tt(y.v(), y.v(), vl.v(), ALU.add, e="pool")
            R32.put(vl)
            pg = ps()
            S.mm(pg.v(), k.g2b[:, 0, p * 128:(p + 1) * 128], gsb0.v(), start=True, stop=False)
            S.mm(pg.v(), k.g2b[0:32, 1, p * 128:(p + 1) * 128], gsb1[0:32, :], start=False, stop=True)
            S.tt(k.ocur[:, p, :], y.v(), pg.v(), ALU.mult)
    R16.put(lora1, gsb0, gsb1)
    if "rw" in k.dbg_out:
        for h in range(8):
            tmp = R32.get()
            S.copy(tmp.v(), k.ocur[:, h, :], e="pool")
            S.dma(k.dbg_out["rw"][h * 128:(h + 1) * 128, t0:t0 + NTOK], tmp.v())
            R32.put(tmp)

import math

TWO_PI = 2.0 * math.pi


def s5_setup(k):
    S = k.S
    nc = k.nc
    L, T = k.L, k.T

    def ext(name, shape):
        return Buf(nc.dram_tensor(name, list(shape), F32, kind="ExternalInput"), name)
    k.s5lam = ext("s5lam", [L, 128, 3, 32])
    k.s5b = ext("s5b", [L, 2, 128, 32 * 16])
    k.s5c = ext("s5c", [L, 2, 128, 32 * 16])
    k.s5cpad = ext("s5cpad", [L, 2, 8, 128, 4 * 128])
    k.s5glu = ext("s5glu", [L, 8, 128, 128])
    k.s5P = S.dram("s5P", [8, 128, 8 * 2 * 128], BF16)
    k.s5Q = S.dram("s5Q", [8, 128, 8 * 2 * 4 * 32], BF16)
    k.s5BD = S.dram("s5BD", [8, 128, 8 * 128], BF16)
    k.s5D = S.dram("s5D", [8, 128, 4 * 2 * 64], F32)
    k.s5P3 = S.dram("s5P3", [8, 128, 8 * 2 * 128], BF16)
    k.s5Q3 = S.dram("s5Q3", [8, 128, 8 * 2 * 128], BF16)
    k.s5pad3 = [S.sb("s5pad3_%d" % i, [128, 128]) for i in range(2)]
    for t in k.s5pad3:
        S.memset(t.v(), 0.0)
    k.s5small = S.sb("s5small", [128, 24, 32])
    k.s5pw = S.sb("s5pw", [128, 2, 9, 32])
    k.s5glub = S.sb("s5glub", [128, 8, 128], BF16)
    k.s5car = S.sb("s5car", [128, 2, 32])
    k.s5rho = S.sb("s5rho", [128, 32])
    k.s5pad = [S.sb("s5pad%d" % i, [128, 4 * 32]) for i in range(3)]
    for t in k.s5pad:
        S.memset(t.v(), 0.0)
    k.s5t = [S.sb("s5t%d" % i, [128, 72]) for i in range(12)]
    k.s5ti = 0
    k.s5x = [[S.sb("s5x%d_%d" % (i, c), [128, 64], BF16) for c in range(2)] for i in range(4)]


def s5_layer_init(k, l):
    S = k.S
    R32, BIG, ps = k.R32, k.BIG, k.ps
    sm = k.s5small

    def s(i):
        return sm[:, i, :]
    LR, LI, LS, DT, MAG, TH, R_, RF, M1, COS, SIN, ABR, ABI, DEN, NR, T1, T2, CRE, CIM, RH, RI = range(21)
    lamt = R32.get()
    lv = lamt.v()[:, 0:96].rr("p (a q) -> p a q", a=3)
    S.dma(lv, k.s5lam[l])
    S.copy(s(LR), lv[:, 0, :], e="pool")
    S.copy(s(LI), lv[:, 1, :], e="pool")
    S.act(s(DT), lv[:, 2, :], AF.Exp)
    R32.put(lamt)
    S.tt(s(T1), s(LR), s(DT), ALU.mult)
    S.act(s(MAG), s(T1), AF.Exp)
    S.act(s(RH), s(T1), AF.Exp, scale=8.0)
    S.copy(k.s5rho.v(), s(RH), e="pool")
    S.tt(s(TH), s(LI), s(DT), ALU.mult)

    def sincos(dst, shift):
        S.ts(s(R_), s(TH), 1.0 / TWO_PI, ALU.mult, shift, ALU.add)
        ri = sm[:, 23, :].bitcast(I32)
        S.copy(ri, s(R_), e="dve")
        S.copy(s(RF), ri, e="dve")
        S.tt(s(R_), s(R_), s(RF), ALU.subtract)
        S.ts(s(M1), s(R_), 0.5, ALU.is_gt)
        S.tt(s(R_), s(R_), s(M1), ALU.subtract)
        S.ts(s(M1), s(R_), -0.5, ALU.is_lt)
        S.tt(s(R_), s(R_), s(M1), ALU.add)
        S.act(dst, s(R_), AF.Sin, scale=6.28318)
    sincos(s(SIN), 0.0)
    sincos(s(COS), 0.25)
    S.tt(s(ABR), s(MAG), s(COS), ALU.mult)
    S.tt(s(ABI), s(MAG), s(SIN), ALU.mult)
    S.tt(s(DEN), s(LR), s(LR), ALU.mult)
    S.tt(s(T1), s(LI), s(LI), ALU.mult)
    S.tt(s(DEN), s(DEN), s(T1), ALU.add)
    S.recip(s(DEN), s(DEN))
    S.ts(s(NR), s(ABR), -1.0, ALU.add)
    S.tt(s(T1), s(NR), s(LR), ALU.mult)
    S.tt(s(T2), s(ABI), s(LI), ALU.mult)
    S.tt(s(T1), s(T1), s(T2), ALU.add)
    S.tt(s(CRE), s(T1), s(DEN), ALU.mult)
    S.tt(s(T1), s(ABI), s(LR), ALU.mult)
    S.tt(s(T2), s(NR), s(LI), ALU.mult)
    S.tt(s(T1), s(T1), s(T2), ALU.subtract)
    S.tt(s(CIM), s(T1), s(DEN), ALU.mult)
    pw = k.s5pw
    S.memset(pw[:, 0, 0, :], 1.0)
    S.memset(pw[:, 1, 0, :], 0.0)
    for d in range(8):
        S.tt(s(T1), pw[:, 0, d, :], s(ABR), ALU.mult)
        S.tt(s(T2), pw[:, 1, d, :], s(ABI), ALU.mult)
        S.tt(pw[:, 0, d + 1, :], s(T1), s(T2), ALU.subtract)
        S.tt(s(T1), pw[:, 0, d, :], s(ABI), ALU.mult)
        S.tt(s(T2), pw[:, 1, d, :], s(ABR), ALU.mult)
        S.tt(pw[:, 1, d + 1, :], s(T1), s(T2), ALU.add)
    S.recip(s(RI), s(RH))
    D1R, D1I = 21, 22
    S.tt(s(D1R), pw[:, 0, 8, :], s(RI), ALU.mult)
    S.tt(s(D1I), pw[:, 1, 8, :], s(RI), ALU.mult)
    S.ts(s(D1I), s(D1I), -1.0, ALU.mult)
    Dre = [BIG[i].v().rr("p (q n) -> p q n", n=64) for i in range(4)]
    Dim = [BIG[4 + i].v().rr("p (q n) -> p q n", n=64) for i in range(4)]
    for i in range(4):
        qs = slice(i * 8, (i + 1) * 8)
        tr1 = R32.get()
        tr2 = R32.get()
        S.copy(Dre[i][:, :, 0], s(D1R)[:, qs], e="pool")
        S.copy(Dim[i][:, :, 0], s(D1I)[:, qs], e="pool")
        m = 1
        while m < 64:
            t1 = tr1.v()[:, 0:8 * m].rr("p (q n) -> p q n", n=m)
            t2 = tr2.v()[:, 0:8 * m].rr("p (q n) -> p q n", n=m)
            br = Dre[i][:, :, m - 1:m].bc([128, 8, m])
            bi = Dim[i][:, :, m - 1:m].bc([128, 8, m])
            ar = Dre[i][:, :, 0:m]
            ai = Dim[i][:, :, 0:m]
            S.tt(t1, ar, br, ALU.mult)
            S.tt(t2, ai, bi, ALU.mult, e="pool")
            S.tt(Dre[i][:, :, m:2 * m], t1, t2, ALU.subtract)
            S.tt(t1, ar, bi, ALU.mult)
            S.tt(t2, ai, br, ALU.mult, e="pool")
            S.tt(Dim[i][:, :, m:2 * m], t1, t2, ALU.add)
            m *= 2
        R32.put(tr1, tr2)
    for gt in range(8):
        i, o = gt // 2, (gt % 2) * 4
        dv = k.s5D[gt].rr("p (k c n) -> p k c n", k=4, c=2)
        S.dma(dv[:, :, 0, :], Dre[i][:, o:o + 4, :])
        S.dma(dv[:, :, 1, :], Dim[i][:, o:o + 4, :])
    bre = R32.get()
    bim = R32.get()
    S.dma(bre.v(), k.s5b[l, 0])
    S.dma(bim.v(), k.s5b[l, 1])
    v3 = lambda t: t.v().rr("p (q h) -> p q h", h=16)
    bcq = lambda view: view.rr("p (q o) -> p q o", o=1).bc([128, 32, 16])
    t1 = R32.get()
    t2 = R32.get()
    abr = R32.get()
    abi = R32.get()

    def cmul_bc(ore, oim, are, aim, sre, sim):
        S.tt(v3(t1), v3(are), bcq(sre), ALU.mult)
        S.tt(v3(t2), v3(aim), bcq(sim), ALU.mult, e="pool")
        S.tt(v3(t1), v3(t1), v3(t2), ALU.subtract)
        S.tt(v3(t2), v3(are), bcq(sim), ALU.mult, e="pool")
        S.tt(v3(oim), v3(aim), bcq(sre), ALU.mult)
        S.tt(v3(oim), v3(oim), v3(t2), ALU.add)
        S.copy(v3(ore), v3(t1), e="pool")
    cmul_bc(abr, abi, bre, bim, s(CRE), s(CIM))
    R32.put(bre, bim)
    cre_t = R32.get()
    cim_t = R32.get()
    S.dma(cre_t.v(), k.s5c[l, 0])
    S.dma(cim_t.v(), k.s5c[l, 1])
    pre, pim, pimn = k.s5pad
    for d in range(8):
        tau = 7 - d
        for gt in range(8):
            for (dst, src, sc) in ((pre, abr, None), (pim, abi, None), (pimn, abi, -1.0)):
                dv = dst.v().rr("p (k g h) -> p k g h", k=4, g=2)
                sv = v3(src)[:, gt * 4:(gt + 1) * 4, :]
                if sc is None:
                    S.copy(dv[0:64, :, 0, :], sv[0:64], e="pool")
                    S.copy(dv[64:128, :, 1, :], sv[64:128], e="pool")
                else:
                    S.ts(dv[0:64, :, 0, :], sv[0:64], sc, ALU.mult)
                    S.ts(dv[64:128, :, 1, :], sv[64:128], sc, ALU.mult)
            S.copy(k.s5pad3[0][:, 96:128], pre[:, 96:128], e="pool")
            S.copy(k.s5pad3[1][:, 96:128], pim[:, 96:128], e="pool")
            pt = ps()
            S.tr(pt[:, 0:128], pre.v(), k.ident.v())
            S.tr(pt[:, 128:256], pim.v(), k.ident.v())
            S.tr(pt[:, 256:384], k.s5pad3[0].v(), k.ident.v())
            S.tr(pt[:, 384:512], k.s5pad3[1].v(), k.ident.v())
            pb = k.R16.get()
            S.copy(pb.v(), pt.v(), e="act")
            S.dma(k.s5P[gt].rr("p (t c n) -> p t c n", t=8, c=2)[:, tau, :, :], pb[:, 0:256].rr("p (c n) -> p c n", c=2))
            S.dma(k.s5P3[gt].rr("p (t c n) -> p t c n", t=8, c=2)[:, tau, :, :], pb[:, 256:512].rr("p (c n) -> p c n", c=2))
            k.R16.put(pb)
            cp = R32.get()
            cpi = R32.get()
            S.dma(cp.v(), k.s5cpad[l, 0, gt])
            S.dma(cpi.v(), k.s5cpad[l, 1, gt])
            pbd = ps()
            for kk in range(4):
                S.mm(pbd[:, 32 * kk:32 * kk + 32], cp[:, 128 * kk:128 * kk + 128], pre[:, 32 * kk:32 * kk + 32], start=True, stop=False)
                S.mm(pbd[:, 32 * kk:32 * kk + 32], cpi[:, 128 * kk:128 * kk + 128], pimn[:, 32 * kk:32 * kk + 32], start=False, stop=True)
            R32.put(cp, cpi)
            bdT = R32.get()
            if d == 0:
                S.stt(bdT[:, 0:128], k.ident.v(), k.vec[:, VI["s5_d"], gt:gt + 1], pbd[:, 0:128], ALU.mult, ALU.add)
            else:
                S.copy(bdT[:, 0:128], pbd[:, 0:128], e="act")
            pt2 = ps()
            S.tr(pt2[:, 0:128], bdT[:, 0:128], k.ident.v())
            R32.put(bdT)
            bdb = k.R16.get()
            S.copy(bdb[:, 0:128], pt2[:, 0:128], e="act")
            S.dma(k.s5BD[gt].rr("p (d n) -> p d n", d=8)[:, d, :], bdb[:, 0:128])
            k.R16.put(bdb)
        if d < 7:
            cmul_bc(abr, abi, abr, abi, s(ABR), s(ABI))
    R32.put(abr, abi)
    qre = R32.get()
    qim = R32.get()
    s5qpad = [BIG[8 + i].v().bitcast(BF16).rr("p (q n) -> p q n", q=32) for i in range(2)]
    s5q3 = [BIG[10 + i].v().bitcast(BF16).rr("p (g n) -> p g n", g=8) for i in range(2)]
    for i in range(4):
        S.memset(BIG[8 + i].v(), 0.0)
    for tp in range(8):
        cmul_bc(qre, qim, cre_t, cim_t, pw[:, 0, tp + 1, :], pw[:, 1, tp + 1, :])
        for ci, (src, sc) in enumerate(((qre, 1.0), (qim, -1.0))):
            qp = s5qpad[ci]
            S.ts(qp[0:64, :, 0:16], v3(src)[0:64], sc, ALU.mult)
            S.ts(qp[64:128, :, 16:32], v3(src)[64:128], sc, ALU.mult)
            q3 = s5q3[ci]
            S.copy(q3[:, :, 96:128], qp.rr("p (g k) n -> p g k n", k=4)[:, :, 3, :], e="pool")
            for gt in range(8):
                dv = k.s5Q[gt].rr("p (t c k n) -> p t c k n", t=8, c=2, k=4)
                S.dma(dv[:, tp, ci, :, :], qp[:, gt * 4:(gt + 1) * 4, :])
                dv3 = k.s5Q3[gt].rr("p (t c n) -> p t c n", t=8, c=2)
                S.dma(dv3[:, tp, ci, :], q3[:, gt, :])
    R32.put(qre, qim, cre_t, cim_t, t1, t2)
    for gt in range(8):
        g = R32.get()
        S.dma(g[:, 0:128], k.s5glu[l, gt])
        S.copy(k.s5glub[:, gt, :], g[:, 0:128], e="pool")
        R32.put(g)
    S.memset(k.s5car.v(), 0.0)


def s5_tile(k, l, j):
    S = k.S
    R32, R16, BIG, xn, ps = k.R32, k.R16, k.BIG, k.xn, k.ps
    wib = k.w_in_b
    t0 = j * NTOK

    def tmp():
        t = k.s5t[k.s5ti % 12]
        k.s5ti += 1
        return t
    for gt in range(8):
        base = (gt % 2) * 10
        BDs = BIG[base + 0].v().bitcast(BF16).rr("p (d n) -> p d n", d=8)
        Pv = [BIG[base + 1 + i].v().bitcast(BF16).rr("p (t c n) -> p t c n", t=4, c=2) for i in range(2)]
        Qv = [BIG[base + 3 + i].v().bitcast(BF16).rr("p (t c k n) -> p t c k n", t=4, c=2, k=4) for i in range(2)]
        Dv = BIG[base + 5].v().rr("p (k c n) -> p k c n", k=4, c=2)
        P3v = [BIG[base + 6 + i].v().bitcast(BF16).rr("p (t c n) -> p t c n", t=4, c=2) for i in range(2)]
        Q3v = [BIG[base + 8 + i].v().bitcast(BF16).rr("p (t c n) -> p t c n", t=4, c=2) for i in range(2)]
        S.dma(BIG[base + 0].v().bitcast(BF16), k.s5BD[gt])
        for i in range(2):
            S.dma(BIG[base + 1 + i].v().bitcast(BF16), k.s5P[gt][:, i * 1024:(i + 1) * 1024])
            S.dma(BIG[base + 3 + i].v().bitcast(BF16), k.s5Q[gt][:, i * 1024:(i + 1) * 1024])
            S.dma(BIG[base + 6 + i].v().bitcast(BF16), k.s5P3[gt][:, i * 1024:(i + 1) * 1024])
            S.dma(BIG[base + 8 + i].v().bitcast(BF16), k.s5Q3[gt][:, i * 1024:(i + 1) * 1024])
        S.dma(BIG[base + 5].v(), k.s5D[gt])
        if gt % 4 == 0:
            w = k.load_w(wib[l, :, OFF_D + (gt // 4) * 512:OFF_D + (gt // 4 + 1) * 512], 8, 512)
        k.conv_tick()
        pu = ps()
        for c in range(8):
            S.mm(pu.v(), w[:, c, (gt % 4) * 128:(gt % 4 + 1) * 128], xn[:, c, :], start=(c == 0), stop=(c == 7))
        Ut = R16.get()
        Utv = Ut.v().rr("p (t n) -> p t n", t=8)
        S.copy(Utv, pu.v().rr("p (n t) -> p t n", t=8), e="act")
        for kk in range(4):
            q = gt * 4 + kk
            rows = slice(32 * kk, 32 * kk + 32)
            pwr = ps()
            pwi = ps()
            for (pw_, ci) in ((pwr, 0), (pwi, 1)):
                for tau in range(8):
                    if kk < 3:
                        S.mm(pw_[:, 0:64], Pv[tau // 4][rows, tau % 4, ci, :], Utv[rows, tau, :], start=(tau == 0), stop=(tau == 7))
                    else:
                        S.mm(pw_[:, 0:64], P3v[tau // 4][:, tau % 4, ci, :], Utv[:, tau, :], start=(tau == 0), stop=(tau == 7))
            dre = Dv[:, kk, 0, :]
            dim = Dv[:, kk, 1, :]
            a1, a2, a3, a4 = tmp(), tmp(), tmp(), tmp()
            S.tt(a1[:, 0:64], dre, pwr[:, 0:64], ALU.mult)
            S.tt(a2[:, 0:64], dim, pwi[:, 0:64], ALU.mult)
            S.tt(a1[:, 0:64], a1[:, 0:64], a2[:, 0:64], ALU.subtract, e="pool")
            S.tt(a3[:, 0:64], dre, pwi[:, 0:64], ALU.mult)
            S.tt(a4[:, 0:64], dim, pwr[:, 0:64], ALU.mult)
            S.tt(a3[:, 0:64], a3[:, 0:64], a4[:, 0:64], ALU.add, e="pool")
            rho = k.s5rho[:, q:q + 1].bc([128, 64])
            wre, wim = tmp(), tmp()
            S.scan(wre[:, 0:64], rho, a1[:, 0:64], k.s5car[:, 0, q:q + 1])
            S.scan(wim[:, 0:64], rho, a3[:, 0:64], k.s5car[:, 1, q:q + 1])
            xre, xim = tmp(), tmp()
            S.copy(xre[:, 0:1], k.s5car[:, 0, q:q + 1], e="pool")
            S.copy(xim[:, 0:1], k.s5car[:, 1, q:q + 1], e="pool")
            S.tt(a1[:, 0:64], dre, wre[:, 0:64], ALU.mult)
            S.tt(a2[:, 0:64], dim, wim[:, 0:64], ALU.mult, e="pool")
            S.tt(xre[:, 1:65], a1[:, 0:64], a2[:, 0:64], ALU.add)
            S.tt(a3[:, 0:64], dre, wim[:, 0:64], ALU.mult, e="pool")
            S.tt(a4[:, 0:64], dim, wre[:, 0:64], ALU.mult)
            S.tt(xim[:, 1:65], a3[:, 0:64], a4[:, 0:64], ALU.subtract)
            S.copy(k.s5car[:, 0, q:q + 1], xre[:, 64:65], e="pool")
            S.copy(k.s5car[:, 1, q:q + 1], xim[:, 64:65], e="pool")
            S.copy(k.s5x[kk][0].v(), xre[:, 0:64], e="act")
            S.copy(k.s5x[kk][1].v(), xim[:, 0:64], e="act")
        ysb = R32.get()
        yv = ysb.v().rr("p (n t) -> p t n", t=8)
        for tp in range(8):
            py = ps()
            nmm = (tp + 1)
            for tau in range(tp + 1):
                S.mm(py[:, 0:64], BDs[:, tp - tau, :], Utv[:, tau, :], start=(tau == 0), stop=False)
            for kk in range(3):
                S.mm(py[32 * kk:32 * kk + 32, 0:64], Qv[tp // 4][:, tp % 4, 0, kk, :], k.s5x[kk][0].v(), start=False, stop=False)
                S.mm(py[32 * kk:32 * kk + 32, 0:64], Qv[tp // 4][:, tp % 4, 1, kk, :], k.s5x[kk][1].v(), start=False, stop=False)
            S.mm(py[:, 0:64], Q3v[tp // 4][:, tp % 4, 0, :], k.s5x[3][0].v(), start=False, stop=False)
            S.mm(py[:, 0:64], Q3v[tp // 4][:, tp % 4, 1, :], k.s5x[3][1].v(), start=False, stop=True)
            S.copy(yv[:, tp, :], py[:, 0:64], e="act")
        R16.put(Ut)
        x2 = R32.get()
        S.act(x2.v(), ysb.v(), AF.Square)
        S.ts(x2.v(), x2.v(), 0.044715, ALU.mult, 1.0, ALU.add, e="pool")
        S.tt(x2.v(), x2.v(), ysb.v(), ALU.mult)
        S.act(x2.v(), x2.v(), AF.Tanh, scale=0.7978845608028654)
        S.stt(x2.v(), x2.v(), 1.0, ysb.v(), ALU.add, ALU.mult)
        zb = R16.get()
        S.act(zb.v(), x2.v(), AF.Copy, scale=0.5)
        pg = ps()
        S.mm(pg.v(), k.s5glub[:, gt, :], zb.v())
        R16.put(zb)
        sgl = ysb
        S.act(sgl.v(), pg.v(), AF.Sigmoid, bias=k.vec[:, VI["s5_glu_b"], gt:gt + 1])
        S.stt(k.ocur[:, gt, :], x2.v(), 0.5, sgl.v(), ALU.mult, ALU.mult)
        R32.put(x2, ysb)
    if "s5" in k.dbg_out:
        for h in range(8):
            tmp_ = R32.get()
            S.copy(tmp_.v(), k.ocur[:, h, :], e="pool")
            S.dma(k.dbg_out["s5"][h * 128:(h + 1) * 128, t0:t0 + NTOK], tmp_.v())
            R32.put(tmp_)


def build_full(L, T, dbg=()):
    k = build(L, T, dbg=dbg)
    k.stage = "z"
    hg_setup(k)
    rw_setup(k)
    s5_setup(k)
    S = k.S
    nt = T // NTOK
    for l in range(L):
        S.dma(k.vec.v().rr("p v c -> p (v c)"), k.vecs[l])
        hg_layer_init(k, l)
        rw_layer_init(k, l)
        rw_layer_vecs(k, l)
        s5_layer_init(k, l)
        items = k.conv_items(l + 1) if l + 1 < L else []
        per = (len(items) + nt - 1) // nt
        for j in range(nt):
            k.cvq = list(items[j * per:(j + 1) * per])
            k.rmsnorm_tile(k.hT, l, j, VI["mix_norm"])
            hgrn2_tile(k, l, j)
            k.merge_branch(l, j, 0, True)
            rwkv_tile(k, l, j)
            k.merge_branch(l, j, 1, False)
            s5_tile(k, l, j)
            k.merge_branch(l, j, 2, False)
            k.wout_tile(l, j)
            k.ffn_tile(l, j)
            k.do_conv(k.cvq)
            k.cvq = []
            if j == nt - 1:
                k.flush_conv()
    finalize(k)
    return k

from concourse.bass_utils import run_bass_kernel_spmd

L_FULL = 4
T_FULL = 4096


def _pack_vecs(inp, L):
    v = np.zeros((L, 128, NV, 8), np.float32)
    for n, i in VI.items():
        a = np.asarray(inp[n], np.float32)
        if n == "final_norm":
            a = np.broadcast_to(a[None], (4, D))
        elif n == "rw_v0":
            a = np.concatenate([np.zeros((1, D), np.float32), a], 0)
        elif n == "rw_r_k":
            a = a.reshape(a.shape[0], D)
        a = a[:L]
        v[:, :, i, :] = a.reshape(L, 8, 128).transpose(0, 2, 1)
    return v.reshape(L, 128, NV * 8)


def _rw_inmap(inp, L):
    mu_idx = list(range(2048, 3072))
    for p in range(8):
        mu_idx += list(range(p * 128, (p + 1) * 128)) + list(range(1024 + p * 128, 1024 + (p + 1) * 128))
    mu_idx += list(range(3072, 3360))
    mu = np.zeros((L, 27 * 128), np.float32)
    mu[:, :3360] = inp["rw_shift_mu"][:L][:, mu_idx]
    mu = np.ascontiguousarray(mu.reshape(L, 27, 128).transpose(0, 2, 1))
    v1 = np.concatenate([np.zeros((1, 1024, 32), np.float32), inp["rw_v1"]], 0)[:L]
    v2 = np.concatenate([np.zeros((1, 32, 1024), np.float32), inp["rw_v2"]], 0)[:L]
    return {"rw_w2": inp["rw_w2"][:L], "rw_a2": inp["rw_a2"][:L], "rw_g2": inp["rw_g2"][:L],
            "rw_v1": np.ascontiguousarray(v1), "rw_v2": np.ascontiguousarray(v2), "rwmu": mu}


def _s5_inmap(inp, L):
    def pairlay(a):
        Lh = a.shape[0]
        X = a.shape[3]
        return a.reshape(Lh, 32, 2, 64, X).transpose(0, 2, 3, 1, 4).reshape(Lh, 128, 32, X)
    lr = pairlay(inp["s5_lambda_re"][:L, :, :, None])[..., 0]
    li = pairlay(inp["s5_lambda_im"][:L, :, :, None])[..., 0]
    ls = pairlay(np.broadcast_to(inp["s5_log_step"][:L, :, None, None], (L, 64, 64, 1)))[..., 0]
    lam = np.ascontiguousarray(np.stack([lr, li, ls], 2))
    b = np.stack([pairlay(inp["s5_b_re"][:L]), pairlay(inp["s5_b_im"][:L])], 1).reshape(L, 2, 128, 512)
    cT = [inp["s5_c_re"][:L].transpose(0, 1, 3, 2), inp["s5_c_im"][:L].transpose(0, 1, 3, 2)]
    c = np.stack([pairlay(x) for x in cT], 1)
    cpad = np.zeros((L, 2, 8, 128, 4, 8, 16), np.float32)
    for gt in range(8):
        for kk in range(4):
            for g2 in range(2):
                cpad[:, :, gt, g2 * 64:(g2 + 1) * 64, kk, 2 * kk + g2, :] = c[:, :, g2 * 64:(g2 + 1) * 64, gt * 4 + kk, :]
    glu = np.zeros((L, 8, 128, 128), np.float32)
    for gt in range(8):
        for g8 in range(8):
            glu[:, gt, g8 * 16:(g8 + 1) * 16, g8 * 16:(g8 + 1) * 16] = inp["s5_glu_w"][:L, gt * 8 + g8]
    return {"s5lam": lam, "s5b": np.ascontiguousarray(b), "s5c": np.ascontiguousarray(c.reshape(L, 2, 128, 512)),
            "s5cpad": np.ascontiguousarray(cpad.reshape(L, 2, 8, 128, 512)), "s5glu": glu}


def kernel(**inputs):
    inp = {k_: np.asarray(v, dtype=np.float32) for k_, v in inputs.items()}
    L, T = L_FULL, T_FULL
    B = inp["x"].shape[0]
    perm = perm_cols()
    common = {"w_in": np.ascontiguousarray(inp["w_in"][:, :, perm]),
              "w_branch": inp["w_branch"], "w_out": inp["w_out"], "ffn_w_gate": inp["ffn_w_gate"],
              "ffn_w_up": inp["ffn_w_up"], "ffn_w_down": inp["ffn_w_down"], "vecs": _pack_vecs(inp, L)}
    common.update(_rw_inmap(inp, L))
    common.update(_s5_inmap(inp, L))
    k = build_full(L, T)
    in_maps = []
    for c in range(8):
        m = dict(common)
        m["xT"] = np.ascontiguousarray(inp["x"][c % B].T)
        in_maps.append(m)
    res = run_bass_kernel_spmd(k.nc, in_maps, core_ids=list(range(8)))
    out = np.stack([np.ascontiguousarray(res.results[b]["out"].T) for b in range(B)], 0)
    return out.astype(np.float32)
```

```python
import numpy as np
import concourse.bass as bass
import concourse.mybir as mybir

F32 = mybir.dt.float32
BF16 = mybir.dt.bfloat16
I32 = mybir.dt.int32
AF = mybir.ActivationFunctionType
ALU = mybir.AluOpType
AX = mybir.AxisListType


class Buf:
    __slots__ = ("t", "lw", "rd", "name", "pe_rg")

    def __init__(self, t, name=""):
        self.t = t
        self.lw = None
        self.rd = []
        self.name = name
        self.pe_rg = None

    def v(self):
        return View((self,), self.t.ap())

    def __getitem__(self, idx):
        return View((self,), self.t.ap()[idx])


class View:
    __slots__ = ("bufs", "ap")

    def __init__(self, bufs, ap):
        self.bufs = bufs
        self.ap = ap

    def __getitem__(self, idx):
        return View(self.bufs, self.ap[idx])

    def rr(self, pat, **kw):
        return View(self.bufs, self.ap.rearrange(pat, **kw))

    def bc(self, shape):
        return View(self.bufs, self.ap.to_broadcast(list(shape)))

    def bitcast(self, dt):
        return View(self.bufs, self.ap.bitcast(dt))

    @property
    def shape(self):
        return tuple(self.ap.shape)


def _bufs(*xs):
    out = []
    for x in xs:
        if isinstance(x, View):
            for b in x.bufs:
                if b not in out:
                    out.append(b)
    return out


def _ap(x):
    return x.ap if isinstance(x, View) else x


class Sched:
    ND = 8

    def __init__(self, nc):
        self.nc = nc
        self.eng = dict(pe=nc.tensor, dve=nc.vector, act=nc.scalar, pool=nc.gpsimd, sp=nc.sync)
        self.csem = {}
        self.cnt = {}
        for e in ("pe", "dve", "act", "pool"):
            self.csem[e] = nc.alloc_semaphore("cs_" + e)
            self.cnt[e] = 0
        self.dsem = {}
        self.dcnt = {}
        self.dk = {}
        for q in ("sp", "pool"):
            self.dsem[q] = [nc.alloc_semaphore("ds_%s%d" % (q, i)) for i in range(self.ND)]
            self.dcnt[q] = [0] * self.ND
            self.dk[q] = 0
        self.seen = {e: {} for e in self.eng}
        import os as _os
        self.nowait_same = set(_os.environ.get("NOWAIT", "pe").split(","))
        self.ninst = 0
        self.nwait = 0
        self.per = {e: 0 for e in self.eng}

    def sb(self, name, shape, dt=F32):
        return Buf(self.nc.alloc_sbuf_tensor(name, list(shape), dt), name)

    def ps(self, name, shape, dt=F32):
        return Buf(self.nc.alloc_psum_tensor(name, list(shape), dt), name)

    def dram(self, name, shape, dt=F32, kind="Internal"):
        return Buf(self.nc.dram_tensor(name, list(shape), dt, kind=kind), name)

    def _wait(self, e, tok, force=False):
        if tok is None:
            return
        sem, val, key, owner = tok
        if owner == e and e in self.nowait_same and not force:
            return
        if self.seen[e].get(key, 0) >= val:
            return
        self.eng[e].wait_ge(sem, val)
        self.seen[e][key] = val
        self.nwait += 1

    def _deps(self, e, reads, writes):
        for b in reads:
            self._wait(e, b.lw)
        for b in writes:
            self._wait(e, b.lw)
            for r in b.rd:
                self._wait(e, r)

    def _commit(self, tok, reads, writes):
        for b in reads:
            if b in writes:
                continue
            b.rd.append(tok)
            if len(b.rd) > 24:
                latest = {}
                for t in b.rd:
                    if t[2] not in latest or latest[t[2]][1] < t[1]:
                        latest[t[2]] = t
                b.rd = list(latest.values())
        for b in writes:
            b.lw = tok
            b.rd = []

    def op(self, e, fn, reads=(), writes=()):
        self._deps(e, reads, writes)
        inst = fn(self.eng[e])
        self.cnt[e] += 1
        inst.then_inc(self.csem[e], 1)
        tok = (self.csem[e], self.cnt[e], "c" + e, e)
        self._commit(tok, reads, writes)
        self.ninst += 1
        self.per[e] += 1
        return tok

    def dma(self, out, in_, q="sp", **kw):
        reads = _bufs(in_)
        writes = _bufs(out)
        k = self.dk[q]
        self.dk[q] += 1
        i = k % self.ND
        sem = self.dsem[q][i]
        key = "d%s%d" % (q, i)
        prev = self.dcnt[q][i]
        if prev > 0 and self.seen[q].get(key, 0) < 16 * prev:
            self.eng[q].wait_ge(sem, 16 * prev)
            self.seen[q][key] = 16 * prev
        self._deps(q, reads, writes)
        inst = self.eng[q].dma_start(out=_ap(out), in_=_ap(in_), **kw)
        self.dcnt[q][i] += 1
        inst.then_inc(sem, 16)
        tok = (sem, 16 * self.dcnt[q][i], key, "dma" + q)
        self._commit(tok, reads, writes)
        self.ninst += 1
        self.per[q] += 1
        return tok

    def finish(self, bufs):
        for b in bufs:
            self._wait("sp", b.lw)
        for e in ("pe", "dve", "act", "pool"):
            if self.cnt[e] > 0:
                self._wait("sp", (self.csem[e], self.cnt[e], "c" + e, e))
        for q in ("sp", "pool"):
            for i in range(self.ND):
                if self.dcnt[q][i] > 0:
                    self._wait("sp", (self.dsem[q][i], 16 * self.dcnt[q][i], "d%s%d" % (q, i), "dma" + q))

    def act(self, out, in_, func, bias=None, scale=None, accum=None, e="act"):
        kw = {}
        if bias is not None:
            kw["bias"] = _ap(bias)
        if scale is not None:
            kw["scale"] = _ap(scale)
        if accum is not None:
            kw["accum_out"] = _ap(accum)
        return self.op(e, lambda g: g.activation(out=_ap(out), in_=_ap(in_), func=func, **kw),
                       _bufs(in_, bias, scale), _bufs(out, accum))

    def tt(self, out, a, b, op, e="dve"):
        return self.op(e, lambda g: g.tensor_tensor(out=_ap(out), in0=_ap(a), in1=_ap(b), op=op),
                       _bufs(a, b), _bufs(out))

    def ts(self, out, a, s1, op0, s2=None, op1=None, e="dve"):
        if op1 is None:
            return self.op(e, lambda g: g.tensor_scalar(out=_ap(out), in0=_ap(a), scalar1=_ap(s1), scalar2=None, op0=op0),
                           _bufs(a, s1), _bufs(out))
        return self.op(e, lambda g: g.tensor_scalar(out=_ap(out), in0=_ap(a), scalar1=_ap(s1), scalar2=_ap(s2), op0=op0, op1=op1),
                       _bufs(a, s1, s2), _bufs(out))

    def stt(self, out, a, s, b, op0, op1):
        return self.op("dve", lambda g: g.scalar_tensor_tensor(out=_ap(out), in0=_ap(a), scalar=_ap(s), in1=_ap(b), op0=op0, op1=op1),
                       _bufs(a, s, b), _bufs(out))

    def copy(self, out, in_, e="dve"):
        if e == "act":
            return self.act(out, in_, AF.Copy)
        return self.op(e, lambda g: g.tensor_copy(out=_ap(out), in_=_ap(in_)), _bufs(in_), _bufs(out))

    def memset(self, out, val, e="pool"):
        return self.op(e, lambda g: g.memset(_ap(out), val), [], _bufs(out))

    def recip(self, out, in_, e="dve"):
        return self.op(e, lambda g: g.reciprocal(out=_ap(out), in_=_ap(in_)), _bufs(in_), _bufs(out))

    def scan(self, out, d0, d1, init, op0=ALU.mult, op1=ALU.add):
        return self.op("dve", lambda g: g.tensor_tensor_scan(out=_ap(out), data0=_ap(d0), data1=_ap(d1), initial=_ap(init), op0=op0, op1=op1),
                       _bufs(d0, d1, init), _bufs(out))

    def mm(self, out, lhsT, rhs, start=True, stop=True):
        la = _ap(lhsT)
        rg = (la.base_partition(), la.partition_size())
        for b in _bufs(out):
            if b.pe_rg is not None and b.pe_rg != rg and b.lw is not None and b.lw[3] == "pe":
                self._wait("pe", b.lw, force=True)
            b.pe_rg = rg
        return self.op("pe", lambda g: g.matmul(_ap(out), lhsT=_ap(lhsT), rhs=_ap(rhs), start=start, stop=stop),
                       _bufs(lhsT, rhs), _bufs(out))

    def tr(self, out, in_, ident):
        return self.op("pe", lambda g: g.transpose(out=_ap(out), in_=_ap(in_), identity=_ap(ident)),
                       _bufs(in_, ident), _bufs(out))

    def asel(self, out, in_, pattern, cmp, fill, base, cm):
        return self.op("pool", lambda g: g.affine_select(out=_ap(out), in_=_ap(in_), pattern=pattern, compare_op=cmp, fill=fill, base=base, channel_multiplier=cm),
                       _bufs(in_), _bufs(out))


class Ring:
    def __init__(self, S, name, n, shape, dt=F32):
        self.tiles = [S.sb("%s%d" % (name, i), shape, dt) for i in range(n)]
        self.free = list(self.tiles)
        self.name = name

    def get(self):
        assert self.free, "ring %s exhausted" % self.name
        return self.free.pop(0)

    def put(self, *ts):
        for t in ts:
            assert t not in self.free
            self.free.append(t)

import numpy as np

D = 1024
NTOK = 512
FH = 2816
NHT = FH // 128
INW = 11552
EPS = 1e-6

OFF_A = 0
OFF_B = 1024
OFF_CV = OFF_B + 8 * 384
OFF_CP = OFF_CV + 1024
OFF_CL = OFF_CP + 8 * 256
OFF_D = OFF_CL + 288
OFF_E = OFF_D + 1024
assert OFF_E + 3072 == INW


def perm_cols():
    p = []
    HG = 0
    p += list(range(HG + 2048, HG + 3072))
    for h in range(8):
        p += list(range(HG + h * 128, HG + (h + 1) * 128))
        p += list(range(HG + 1024 + h * 128, HG + 1024 + (h + 1) * 128))
        p += list(range(HG + 3072 + h * 128, HG + 3072 + (h + 1) * 128))
    RW = 4096
    p += list(range(RW + 2048, RW + 3072))
    for q in range(8):
        p += list(range(RW + q * 128, RW + (q + 1) * 128))
        p += list(range(RW + 1024 + q * 128, RW + 1024 + (q + 1) * 128))
    p += list(range(RW + 3072, RW + 3360))
    S5 = RW + 3360
    p += list(range(S5, S5 + 1024))
    p += list(range(S5 + 1024, S5 + 1024 + 3072))
    p = np.array(p, dtype=np.int64)
    assert p.shape[0] == INW and len(set(p.tolist())) == INW
    return p


VEC_NAMES = ["mix_norm", "ffn_norm", "hg_lb_logits", "hg_onorm", "rw_w0", "rw_a0", "rw_v0", "rw_k_k", "rw_k_a",
             "rw_r_k", "rw_ln_w", "rw_ln_b", "s5_d", "s5_glu_b", "final_norm"]
NV = len(VEC_NAMES)
VI = {n: i for i, n in enumerate(VEC_NAMES)}


class K:
    pass


def build(L, T, dbg=(), stub=()):
    NTT = T // NTOK
    nc = bass.Bass("TRN2", target_bir_lowering=False)
    S = Sched(nc)
    k = K()
    k.S = S
    k.nc = nc
    k.L = L
    k.T = T

    def ext(name, shape):
        return Buf(nc.dram_tensor(name, list(shape), F32, kind="ExternalInput"), name)

    xT = ext("xT", [D, T])
    w_in = ext("w_in", [L, D, INW])
    w_branch = ext("w_branch", [L, 3 * D, D])
    w_out = ext("w_out", [L, D, D])
    w_gate = ext("ffn_w_gate", [L, D, FH])
    w_up = ext("ffn_w_up", [L, D, FH])
    w_down = ext("ffn_w_down", [L, FH, D])
    vecs = ext("vecs", [L, 128, NV * 8])
    out = Buf(nc.dram_tensor("out", [D, T], F32, kind="ExternalOutput"), "out")
    dbg_out = {}
    for name in dbg:
        dbg_out[name] = Buf(nc.dram_tensor("dbg_" + name, [D, T], F32, kind="ExternalOutput"), "dbg_" + name)

    class WT:
        def __init__(self, name, src, blocks):
            self.name = name
            self.src = src
            self.blocks = {}
            off = 0
            for (r0, kc, c0, n) in blocks:
                self.blocks[(r0, c0)] = (off, kc, n)
                off += kc * n
            self.total = off
            self.scr = S.dram(name + "_t", [L, 128, off], BF16)

        def __getitem__(self, idx):
            l, rs, cs = idx
            r0 = 0 if rs.start is None else rs.start
            return ("wt", self, l, r0, cs.start)

    in_blocks = [(0, 8, 0, 512), (0, 8, 512, 512)] + [(0, 8, OFF_B + 384 * h, 384) for h in range(8)] \
        + [(0, 8, OFF_CV + 512 * i, 512) for i in range(2)] + [(0, 8, OFF_CP + 512 * i, 512) for i in range(4)] \
        + [(0, 8, OFF_CL, 288)] + [(0, 8, OFF_D + 512 * i, 512) for i in range(2)] + [(0, 8, OFF_E + 512 * i, 512) for i in range(6)]
    w_in_b = WT("w_in_b", w_in, in_blocks)
    w_branch_b = WT("w_branch_b", w_branch, [(br * D, 8, c0, 512) for br in range(3) for c0 in (0, 512)])
    w_out_b = WT("w_out_b", w_out, [(0, 8, 0, 512), (0, 8, 512, 512)])
    fblocks = [(0, 8, 512 * i, 512) for i in range(5)] + [(0, 8, 2560, 256)]
    w_gate_b = WT("w_gate_b", w_gate, fblocks)
    w_up_b = WT("w_up_b", w_up, fblocks)
    w_down_b = WT("w_down_b", w_down, [(0, NHT, 128 * i, 128) for i in range(8)])
    hT = S.dram("hT", [D, T], F32)

    def conv_items(l):
        items = []
        for wt in (w_in_b, w_branch_b, w_out_b, w_gate_b, w_up_b, w_down_b):
            for (r0, c0), (off, kc, n) in wt.blocks.items():
                for c in range(kc):
                    items.append((wt, l, r0 + c * 128, c0, n, off + c * n))
        return items

    def do_conv(items):
        for it in items:
            (wt, l, r, c0, n, off) = it
            i = k.cvi % 3
            k.cvi += 1
            S.dma(cst32[i][:, 0:n], wt.src[l, r:r + 128, c0:c0 + n])
            S.copy(cst16[i][:, 0:n], cst32[i][:, 0:n], e="pool")
            k.cvpend.append((wt.scr[l, :, off:off + n], cst16[i][:, 0:n]))
            if len(k.cvpend) > 1:
                d, sv = k.cvpend.pop(0)
                S.dma(d, sv)

    def flush_conv():
        while k.cvpend:
            d, sv = k.cvpend.pop(0)
            S.dma(d, sv)

    k.cvpend = []
    k.mark = lambda name: None
    k.cvq = []

    def conv_tick(n=2):
        if k.cvq:
            do_conv(k.cvq[:n])
            del k.cvq[:n]
    k.conv_tick = conv_tick
    k.cvi = 0
    cst32 = [S.sb("cst32_%d" % i, [128, 512]) for i in range(3)]
    cst16 = [S.sb("cst16_%d" % i, [128, 512], BF16) for i in range(3)]

    ident = S.sb("ident", [128, 128])
    identb = S.sb("identb", [128, 128], BF16)
    onesb = S.sb("onesb", [128, 128], BF16)
    onesf = S.sb("onesf", [128, 128])
    vec = S.sb("vec", [128, NV, 8])
    xn = S.sb("xn", [128, 8, NTOK], BF16)
    merged = S.sb("merged", [128, 8, NTOK])
    ocur = S.sb("ocur", [128, 8, NTOK], BF16)
    wbufs = [S.sb("wbuf%d" % i, [128, 8 * 512], BF16) for i in range(3)]
    k.wi = 0
    R32 = Ring(S, "r32_", 14, [128, NTOK])
    R16 = Ring(S, "r16_", 8, [128, NTOK], BF16)
    BIG = [S.sb("big%d" % i, [128, NTOK]) for i in range(20)]
    psb = [S.ps("pb%d" % i, [128, 512]) for i in range(8)]
    k.pi = 0

    def ps():
        b = psb[k.pi % 7]
        k.pi += 1
        return b
    psheld = psb[7]
    k.ppi = {"a": 0, "b": 0}

    def ps_pool(name):
        if name == "a":
            b = psb[k.ppi["a"] % 4]
        else:
            b = psb[4 + k.ppi["b"] % 3]
        k.ppi[name] += 1
        return b

    def wbuf():
        b = wbufs[k.wi % 3]
        k.wi += 1
        return b

    def load_w(desc, kc, ncols):
        _, wt, l, r0, c0 = desc
        off, kc_, n_ = wt.blocks[(r0, c0)]
        assert kc_ == kc and n_ == ncols, (wt.name, r0, c0, kc, ncols, kc_, n_)
        b = wbuf()
        v = b.v()[:, 0:kc * ncols]
        S.dma(v, wt.scr[l, :, off:off + kc * ncols])
        return v.rr("p (c n) -> p c n", c=kc)

    S.memset(onesf.v(), 1.0)
    S.memset(onesb.v(), 1.0)
    S.asel(ident.v(), onesf.v(), [[-1, 128]], ALU.is_equal, 0.0, 0, 1)
    S.copy(identb.v(), ident.v(), e="pool")

    vecs_all = [vecs[l] for l in range(L)] + [vecs[L - 1]] * (4 - L)
    k.__dict__.update(locals())

    do_conv(conv_items(0))
    flush_conv()

    for c in range(8):
        S.dma(hT[c * 128:(c + 1) * 128, :], xT[c * 128:(c + 1) * 128, :])

    def rmsnorm_tile(src_dram, l, j, gidx):
        t0 = j * NTOK
        hts = []
        pss = ps()
        for c in range(8):
            ht = R32.get()
            S.dma(ht.v(), src_dram[c * 128:(c + 1) * 128, t0:t0 + NTOK])
            sq = R16.get()
            S.act(sq.v(), ht.v(), AF.Square)
            S.mm(pss.v(), onesb.v(), sq.v(), start=(c == 0), stop=(c == 7))
            R16.put(sq)
            hts.append(ht)
        rstd = R32.get()
        S.act(rstd.v(), pss.v(), AF.Sqrt, bias=k.epsb[:, 0:1], scale=1.0 / D)
        S.recip(rstd.v(), rstd.v())
        for c in range(8):
            S.stt(xn[:, c, :], hts[c].v(), vec[:, gidx, c:c + 1], rstd.v(), ALU.mult, ALU.mult)
            R32.put(hts[c])
        R32.put(rstd)

    k.rmsnorm_tile = rmsnorm_tile
    epsb = S.sb("epsb", [128, 4])
    S.memset(epsb[:, 0:1], EPS)
    S.memset(epsb[:, 1:2], 0.0)
    k.epsb = epsb

    def ffn_tile(l, j):
        t0 = j * NTOK
        rmsnorm_tile(hT, l, j, VI["ffn_norm"])
        def act_tile(ht):
            b = BIG[ht // 2]
            return b.v().bitcast(BF16)[:, (ht % 2) * NTOK:(ht % 2 + 1) * NTOK]
        for blk in range(6):
            c0 = blk * 512
            ncol = min(512, FH - c0)
            wg = load_w(w_gate_b[l, :, c0:c0 + ncol], 8, ncol)
            wu = load_w(w_up_b[l, :, c0:c0 + ncol], 8, ncol)
            conv_tick()
            for s in range(ncol // 128):
                ht = (c0 // 128) + s
                pg = ps()
                pu = ps()
                for c in range(8):
                    S.mm(pg.v(), wg[:, c, s * 128:(s + 1) * 128], xn[:, c, :], start=(c == 0), stop=(c == 7))
                for c in range(8):
                    S.mm(pu.v(), wu[:, c, s * 128:(s + 1) * 128], xn[:, c, :], start=(c == 0), stop=(c == 7))
                sg = R32.get()
                S.act(sg.v(), pg.v(), AF.Silu)
                S.tt(act_tile(ht), sg.v(), pu.v(), ALU.mult)
                R32.put(sg)
        for dt_ in range(8):
            wd = load_w(w_down_b[l, :, dt_ * 128:(dt_ + 1) * 128], NHT, 128)
            conv_tick()
            po = ps()
            for ht in range(NHT):
                S.mm(po.v(), wd[:, ht, :], act_tile(ht), start=(ht == 0), stop=(ht == NHT - 1))
            hres = R32.get()
            S.dma(hres.v(), hT[dt_ * 128:(dt_ + 1) * 128, t0:t0 + NTOK])
            S.tt(hres.v(), hres.v(), po.v(), ALU.add)
            S.dma(hT[dt_ * 128:(dt_ + 1) * 128, t0:t0 + NTOK], hres.v())
            R32.put(hres)

    k.ffn_tile = ffn_tile

    def merge_branch(l, j, br, first):
        for half in range(2):
            wb = load_w(w_branch_b[l, br * D:(br + 1) * D, half * 512:(half + 1) * 512], 8, 512)
            wg = load_w(w_in_b[l, :, OFF_E + br * D + half * 512: OFF_E + br * D + (half + 1) * 512], 8, 512)
            conv_tick()
            for s in range(4):
                dt_ = half * 4 + s
                pg = ps()
                pb = ps()
                for c in range(8):
                    S.mm(pg.v(), wg[:, c, s * 128:(s + 1) * 128], xn[:, c, :], start=(c == 0), stop=(c == 7))
                for c in range(8):
                    S.mm(pb.v(), wb[:, c, s * 128:(s + 1) * 128], ocur[:, c, :], start=(c == 0), stop=(c == 7))
                sg = R32.get()
                S.act(sg.v(), pg.v(), AF.Sigmoid)
                if first:
                    S.tt(merged[:, dt_, :], sg.v(), pb.v(), ALU.mult)
                else:
                    S.tt(sg.v(), sg.v(), pb.v(), ALU.mult)
                    S.tt(merged[:, dt_, :], merged[:, dt_, :], sg.v(), ALU.add, e="pool")
                R32.put(sg)

    k.merge_branch = merge_branch

    def wout_tile(l, j):
        t0 = j * NTOK
        for c in range(8):
            S.copy(ocur[:, c, :], merged[:, c, :], e=("act" if c % 2 else "dve"))
        for half in range(2):
            wo = load_w(w_out_b[l, :, half * 512:(half + 1) * 512], 8, 512)
            for s in range(4):
                dt_ = half * 4 + s
                po = ps()
                for c in range(8):
                    S.mm(po.v(), wo[:, c, s * 128:(s + 1) * 128], ocur[:, c, :], start=(c == 0), stop=(c == 7))
                hres = R32.get()
                S.dma(hres.v(), hT[dt_ * 128:(dt_ + 1) * 128, t0:t0 + NTOK])
                S.tt(hres.v(), hres.v(), po.v(), ALU.add)
                S.dma(hT[dt_ * 128:(dt_ + 1) * 128, t0:t0 + NTOK], hres.v())
                R32.put(hres)

    k.wout_tile = wout_tile
    return k


def finalize(k):
    S = k.S
    L, T = k.L, k.T
    for j in range(T // NTOK):
        t0 = j * NTOK
        hts = []
        pss = k.ps()
        for c in range(8):
            ht = k.R32.get()
            S.dma(ht.v(), k.hT[c * 128:(c + 1) * 128, t0:t0 + NTOK])
            sq = k.R16.get()
            S.act(sq.v(), ht.v(), AF.Square)
            S.mm(pss.v(), k.onesb.v(), sq.v(), start=(c == 0), stop=(c == 7))
            k.R16.put(sq)
            hts.append(ht)
        rstd = k.R32.get()
        S.act(rstd.v(), pss.v(), AF.Sqrt, bias=k.epsb[:, 0:1], scale=1.0 / D)
        S.recip(rstd.v(), rstd.v())
        for c in range(8):
            S.stt(hts[c].v(), hts[c].v(), k.vec[:, VI["final_norm"], c:c + 1], rstd.v(), ALU.mult, ALU.mult)
            S.dma(k.out[c * 128:(c + 1) * 128, t0:t0 + NTOK], hts[c].v())
            k.R32.put(hts[c])
        k.R32.put(rstd)
    S.finish([k.out] + list(k.dbg_out.values()))


def stub_mixer(k, l, j, col0):
    S = k.S
    for half in range(2):
        w = k.load_w(k.w_in_b[l, :, col0 + half * 512: col0 + (half + 1) * 512], 8, 512)
        for s in range(4):
            p = k.ps()
            for c in range(8):
                S.mm(p.v(), w[:, c, s * 128:(s + 1) * 128], k.xn[:, c, :], start=(c == 0), stop=(c == 7))
            S.copy(k.ocur[:, half * 4 + s, :], p.v(), e="act")


def run_layers(k, mixers):
    S = k.S
    for l in range(k.L):
        S.dma(k.vec.v().rr("p v c -> p (v c)"), k.vecs[l])
        nt = k.T // NTOK
        items = k.conv_items(l + 1) if l + 1 < k.L else []
        per = (len(items) + nt - 1) // nt
        for j in range(nt):
            k.do_conv(items[j * per:(j + 1) * per])
            k.rmsnorm_tile(k.hT, l, j, VI["mix_norm"])
            for bi, mx in enumerate(mixers):
                mx(k, l, j)
                k.merge_branch(l, j, bi, bi == 0)
            k.wout_tile(l, j)
            k.ffn_tile(l, j)
    finalize(k)


def hg_setup(k):
    S = k.S
    L = k.L
    k.resetmask = S.sb("resetmask", [128, NTOK])
    S.memset(k.resetmask.v(), 1.0)
    S.memset(k.resetmask.v().rr("p (c t) -> p c t", t=64)[:, :, 0:1], 0.0)
    k.maskincl = S.sb("maskincl", [128, 64])
    k.maskstr = S.sb("maskstr", [128, 64])
    for half in range(2):
        sl = slice(half * 64, half * 64 + 64)
        S.asel(k.maskincl[sl, :], k.onesf[sl, 0:64], [[1, 64]], ALU.is_ge, 0.0, 0, -1)
        S.asel(k.maskstr[sl, :], k.onesf[sl, 0:64], [[1, 64]], ALU.is_gt, 0.0, 0, -1)
    k.lball = S.sb("lball", [128, 4, 8])
    k.omlall = S.sb("omlall", [128, 4, 8])
    E = S.sb("lbE", [128, 4, 8])
    sm = S.sb("lbsum", [128, 8])
    c0 = VI["hg_lb_logits"] * 8
    S.memset(E.v(), 0.0)
    for l in range(4):
        S.dma(E[:, l, :], k.vecs_all[min(l, L - 1) if False else l][:, c0:c0 + 8])
    S.act(E.v(), E.v(), AF.Exp)
    S.tt(sm.v(), E[:, 0, :], E[:, 1, :], ALU.add)
    S.tt(sm.v(), sm.v(), E[:, 2, :], ALU.add)
    S.tt(sm.v(), sm.v(), E[:, 3, :], ALU.add)
    S.recip(sm.v(), sm.v())
    for l in range(4):
        S.tt(E[:, l, :], E[:, l, :], sm.v(), ALU.mult)
    S.memset(k.lball[:, 0, :], 0.0, e="dve")
    for l in range(1, 4):
        S.tt(k.lball[:, l, :], k.lball[:, l - 1, :], E[:, l, :], ALU.add)
    S.ts(k.lball.v(), k.lball.v(), 0.0, ALU.max)
    S.ts(k.omlall.v(), k.lball.v(), -1.0, ALU.mult, 1.0, ALU.add)
    k.hgS = S.sb("hgS", [128, 8, 128])
    k.hgSb = S.sb("hgSb", [128, 8, 128], BF16)
    k.sct = [S.sb("hgsct%d" % i, [128, 64], BF16) for i in range(4)]
    k.scti = 0


def hg_layer_init(k, l):
    k.S.memset(k.hgS.v(), 0.0)
    k.S.memset(k.hgSb.v(), 0.0)


def hgrn2_tile(k, l, j):
    S = k.S
    R32, R16, BIG, xn, ps = k.R32, k.R16, k.BIG, k.xn, k.ps
    wib = k.w_in_b
    vtm = [BIG[tb].v().bitcast(BF16) for tb in range(4)]
    for half in range(2):
        w = k.load_w(wib[l, :, OFF_A + half * 512: OFF_A + (half + 1) * 512], 8, 512)
        for tb in range(4):
            p = ps()
            for c in range(8):
                S.mm(p.v(), xn[:, c, tb * 128:(tb + 1) * 128], w[:, c, :], start=(c == 0), stop=(c == 7))
            S.copy(vtm[tb][:, half * 512:(half + 1) * 512], p.v(), e="act")
    for h in range(8):
        w = k.load_w(wib[l, :, OFF_B + h * 384: OFF_B + (h + 1) * 384], 8, 384)
        k.conv_tick()
        lbv = k.lball[:, l, h:h + 1]
        omlv = k.omlall[:, l, h:h + 1]

        def proj(col0):
            p = ps()
            for c in range(8):
                S.mm(p.v(), w[:, c, col0:col0 + 128], xn[:, c, :], start=(c == 0), stop=(c == 7))
            return p
        pq = proj(0)
        q = R32.get()
        S.act(q.v(), pq.v(), AF.Silu)
        pf = proj(128)
        f = R32.get()
        S.act(f.v(), pf.v(), AF.Sigmoid)
        S.ts(f.v(), f.v(), omlv, ALU.mult, lbv, ALU.add)
        lf = R32.get()
        S.act(lf.v(), f.v(), AF.Ln)
        b = R32.get()
        S.scan(b.v(), k.resetmask.v(), lf.v(), 0.0)
        R32.put(lf)
        kk = R32.get()
        S.ts(kk.v(), f.v(), -1.0, ALU.mult, 1.0, ALU.add, e="pool")
        R32.put(f)
        eb = R32.get()
        S.act(eb.v(), b.v(), AF.Exp)
        enb = R32.get()
        S.act(enb.v(), b.v(), AF.Exp, scale=-1.0)
        R32.put(b)
        qt = R16.get()
        S.tt(qt.v(), q.v(), eb.v(), ALU.mult)
        R32.put(q)
        ktf = R32.get()
        S.tt(ktf.v(), kk.v(), enb.v(), ALU.mult, e="pool")
        R32.put(kk, enb)
        ktb = R16.get()
        S.copy(ktb.v(), ktf.v(), e="act")
        kdT = R16.get()
        eb3 = eb.v().rr("p (c t) -> p c t", t=64)
        S.tt(kdT.v().rr("p (c t) -> p c t", t=64), ktf.v().rr("p (c t) -> p c t", t=64),
             eb3[:, :, 63:64].bc([128, 8, 64]), ALU.mult)
        R32.put(ktf)
        ptr = ps()
        ptb = ptr.v().bitcast(BF16)
        for blk in range(4):
            S.tr(ptb[:, blk * 128:(blk + 1) * 128], kdT[:, blk * 128:(blk + 1) * 128], k.identb.v())
        kdtm = R16.get()
        S.copy(kdtm.v(), ptb[:, 0:512], e="act")
        R16.put(kdT)
        pog = proj(256)
        ogs = R32.get()
        S.act(ogs.v(), pog.v(), AF.Silu)
        osb = R32.get()
        for c in range(8):
            cs = slice(c * 64, (c + 1) * 64)
            pb = (c % 2) * 64
            rows = slice(pb, pb + 64)
            kdc = kdtm.v().rr("p (b n) -> p b n", b=4)[rows, c // 2, :]
            vch = vtm[c // 2][rows, h * 128:(h + 1) * 128]
            p1 = ps()
            S.mm(p1[rows, 0:64], ktb[:, cs], qt[:, cs])
            sct = k.sct[k.scti % 4]
            k.scti += 1
            S.tt(sct[rows, :], p1[rows, 0:64], k.maskincl[rows, :], ALU.mult)
            p2 = ps()
            S.mm(p2[:, 0:64], vch, sct[rows, :], start=True, stop=False)
            S.mm(p2[:, 0:64], k.hgSb[:, h, :], qt[:, cs], start=False, stop=True)
            S.copy(osb[:, cs], p2[:, 0:64], e="act")
            p3 = ps()
            S.mm(p3[:, 0:128], kdc, vch)
            S.stt(k.hgSb[:, h, :], k.hgS[:, h, :], eb[:, c * 64 + 63:c * 64 + 64], p3[:, 0:128], ALU.mult, ALU.add)
            S.stt(k.hgS[:, h, :], k.hgS[:, h, :], eb[:, c * 64 + 63:c * 64 + 64], p3[:, 0:128], ALU.mult, ALU.add)
        R32.put(eb)
        R16.put(qt, ktb, kdtm)
        sq = R16.get()
        S.act(sq.v(), osb.v(), AF.Square)
        pn = ps()
        S.mm(pn.v(), k.onesb.v(), sq.v())
        R16.put(sq)
        rstd = R32.get()
        S.act(rstd.v(), pn.v(), AF.Sqrt, bias=k.epsb[:, 0:1], scale=1.0 / 128)
        S.recip(rstd.v(), rstd.v())
        S.stt(osb.v(), osb.v(), k.vec[:, VI["hg_onorm"], h:h + 1], rstd.v(), ALU.mult, ALU.mult)
        S.tt(k.ocur[:, h, :], osb.v(), ogs.v(), ALU.mult, e="pool")
        R32.put(rstd, ogs, osb)
    if "hg" in k.dbg_out:
        t0 = j * NTOK
        for h in range(8):
            tmp = R32.get()
            S.copy(tmp.v(), k.ocur[:, h, :], e="pool")
            S.dma(k.dbg_out["hg"][h * 128:(h + 1) * 128, t0:t0 + NTOK], tmp.v())
            R32.put(tmp)


C0 = 0.6065306597126334
GN_EPS = 64e-5


def rw_setup(k):
    S = k.S
    nc = k.nc
    L, T = k.L, k.T

    def ext(name, shape):
        return Buf(nc.dram_tensor(name, list(shape), F32, kind="ExternalInput"), name)
    k.rw_w2 = ext("rw_w2", [L, 64, D])
    k.rw_a2 = ext("rw_a2", [L, 64, D])
    k.rw_g2 = ext("rw_g2", [L, 160, D])
    k.rw_v1 = ext("rw_v1", [L, D, 32])
    k.rw_v2 = ext("rw_v2", [L, 32, D])
    k.rwmu_d = ext("rwmu", [L, 128, 27])
    k.vfirst = S.dram("vfirst", [D, T])
    k.vdram = S.dram("vdram", [D, NTOK])
    k.w2b = S.sb("wa2b", [128, D], BF16)
    k.a2b = k.w2b
    k.g2b = S.sb("g2b", [128, 2, D], BF16)
    k.v1b = S.sb("v1b", [128, 8, 32], BF16)
    k.rwmu = S.sb("rwmu_s", [128, 27])
    k.omka = S.sb("omka", [128, 8])
    k.rwP = S.sb("rwP", [128, 8, 64])
    k.rwPb = S.sb("rwPb", [128, 8, 64], BF16)
    k.rwcarry = S.sb("rwcarry", [128, 27])
    k.zx = [S.sb("zx%d" % i, [128, NTOK + 8]) for i in range(2)]
    k.zxi = 0
    k.blockb = S.sb("blockb", [128, 128], BF16)
    k.blockf = S.sb("blockf", [128, 128])
    for t in (k.blockb, k.blockf):
        S.memset(t.v(), 0.0)
        S.memset(t[0:64, 0:64], 1.0)
        S.memset(t[64:128, 64:128], 1.0)
    k.mask4 = S.sb("mask4", [128, 128])
    S.copy(k.mask4[:, 0:64], k.maskstr.v(), e="pool")
    S.copy(k.mask4[:, 64:128], k.maskincl.v(), e="pool")
    k.msb = [S.sb("msb%d" % i, [128, 128], BF16) for i in range(32)]
    k.iwnr = [S.sb("iwnr%d" % i, [128, 256], BF16) for i in range(16)]
    k.iwl = [S.sb("iwl%d" % i, [128, 128], BF16) for i in range(16)]
    for t in k.iwnr + k.iwl:
        S.memset(t.v(), 0.0)
    k.ttb = [S.sb("ttb%d" % i, [128, 128], BF16) for i in range(16)]
    k.zsb = [S.sb("zsb%d" % i, [128, 128], BF16) for i in range(4)]
    k.gC = [S.sb("gC%d" % i, [128, 8]) for i in range(4)]
    k.tiny = S.sb("rwtiny", [128, 1])
    S.memset(k.tiny.v(), 1e-24)
    k.gneps = S.sb("gneps", [128, 1])
    S.memset(k.gneps.v(), GN_EPS)


def rw_layer_init(k, l):
    S = k.S
    st = k.cst32

    def ld(dst_view, src_view, rows, n, i, rbase=0):
        for h0 in range(0, n, 512):
            S.dma(st[i][rbase:rbase + rows, 0:512], src_view[:, h0:h0 + 512], q="pool")
            S.copy(dst_view[:, h0:h0 + 512], st[i][rbase:rbase + rows, 0:512], e="pool")
    ld(k.w2b[0:64, :], k.rw_w2[l], 64, D, 0)
    ld(k.a2b[64:128, :], k.rw_a2[l], 64, D, 1, rbase=64)
    ld(k.g2b[:, 0, :], k.rw_g2[l, 0:128, :], 128, D, 0)
    ld(k.g2b[0:32, 1, :], k.rw_g2[l, 128:160, :], 32, D, 1)
    ld(k.g2b[32:64, 1, :], k.rw_v2[l], 32, D, 0, rbase=32)
    S.dma(st[1][:, 0:256].rr("p (c n) -> p c n", c=8), k.rw_v1[l].rr("(c p) n -> p c n", p=128))
    S.copy(k.v1b.v(), st[1][:, 0:256].rr("p (c n) -> p c n", c=8), e="pool")
    S.dma(k.rwmu.v(), k.rwmu_d[l])
    S.memset(k.rwP.v(), 0.0)
    S.memset(k.rwPb.v(), 0.0)
    S.memset(k.rwcarry.v(), 0.0)


def rw_layer_vecs(k, l):
    k.S.ts(k.omka.v(), k.vec[:, VI["rw_k_a"], :], -1.0, ALU.mult, 1.0, ALU.add)


def rwkv_tile(k, l, j):
    S = k.S
    R32, R16, BIG, xn, ps = k.R32, k.R16, k.BIG, k.xn, k.ps
    wib = k.w_in_b
    t0 = j * NTOK
    V = lambda name, c: k.vec[:, VI[name], c:c + 1]

    def lerp(pview, ti, n=128):
        zx = k.zx[k.zxi % 2]
        k.zxi += 1
        S.copy(zx[0:n, 1:NTOK + 1], pview, e="act")
        S.copy(zx[0:n, 0:1], k.rwcarry[0:n, ti:ti + 1], e="act")
        S.copy(k.rwcarry[0:n, ti:ti + 1], zx[0:n, NTOK:NTOK + 1], e="act")
        d = R32.get()
        S.tt(d[0:n, :], zx[0:n, 0:NTOK], zx[0:n, 1:NTOK + 1], ALU.subtract)
        S.stt(d[0:n, :], d[0:n, :], k.rwmu[0:n, ti:ti + 1], zx[0:n, 1:NTOK + 1], ALU.mult, ALU.add)
        return d

    def projw(w, col0, n=128):
        p = ps()
        for c in range(8):
            S.mm(p[0:n, :], w[:, c, col0:col0 + n], xn[:, c, :], start=(c == 0), stop=(c == 7))
        return p

    w = k.load_w(wib[l, :, OFF_CL:OFF_CL + 288], 8, 288)
    z = lerp(projw(w, 0).v(), 24)
    lora1 = R16.get()
    S.act(lora1[0:64, :], z[0:64, :], AF.Tanh)
    S.copy(lora1[64:128, :], z[64:128, :], e="pool")
    R32.put(z)
    z = lerp(projw(w, 128).v(), 25)
    gsb0 = R16.get()
    S.act(gsb0.v(), z.v(), AF.Sigmoid)
    R32.put(z)
    z = lerp(projw(w, 256, 32)[0:32, :], 26, 32)
    gsb1 = R16.get()
    S.act(gsb1[0:32, :], z[0:32, :], AF.Sigmoid)
    R32.put(z)

    if k.stage == "a":
        R16.put(lora1, gsb0, gsb1)
        return
    k.mark('rw_a')
    vls = []
    for half in range(2):
        w = k.load_w(wib[l, :, OFF_CV + half * 512:OFF_CV + (half + 1) * 512], 8, 512)
        for s in range(4):
            vt = half * 4 + s
            vl = lerp(projw(w, s * 128).v(), vt)
            if l == 0:
                S.dma(k.vfirst[vt * 128:(vt + 1) * 128, t0:t0 + NTOK], vl.v())
                S.dma(k.vdram[vt * 128:(vt + 1) * 128, :], vl.v())
                R32.put(vl)
            else:
                vb = R16.get()
                S.copy(vb.v(), vl.v(), e="pool")
                S.mm(k.psheld[32:64, :], k.v1b[:, vt, :], vb.v(), start=(vt == 0), stop=(vt == 7))
                R16.put(vb)
                vls.append(vl)
    if l > 0:
        vv1 = R16.get()
        S.copy(vv1[32:64, :], k.psheld[32:64, :], e="act")
        for vt in range(8):
            vl = vls[vt]
            p2 = ps()
            S.mm(p2.v(), k.g2b[32:64, 1, vt * 128:(vt + 1) * 128], vv1[32:64, :])
            sv = R32.get()
            S.act(sv.v(), p2.v(), AF.Sigmoid, bias=V("rw_v0", vt))
            vf = R32.get()
            S.dma(vf.v(), k.vfirst[vt * 128:(vt + 1) * 128, t0:t0 + NTOK])
            S.tt(vf.v(), vf.v(), vl.v(), ALU.subtract)
            S.tt(vf.v(), vf.v(), sv.v(), ALU.mult, e="pool")
            S.tt(vl.v(), vl.v(), vf.v(), ALU.add)
            S.dma(k.vdram[vt * 128:(vt + 1) * 128, :], vl.v())
            R32.put(sv, vf, vl)
        R16.put(vv1)

    if k.stage == "b":
        R16.put(lora1, gsb0, gsb1)
        return
    k.mark('rw_b')
    for grp in range(2):
        prbs = []
        yTs = []
        for gi in range(4):
            p = grp * 4 + gi
            AR = BIG[gi].v().bitcast(BF16).rr("p (c s t) -> p c s t", c=8, s=2)
            BK = BIG[4 + gi].v().bitcast(BF16).rr("p (c s t) -> p c s t", c=8, s=2)
            BKtm = BIG[8 + gi].v().bitcast(BF16).rr("p (c n) -> p c n", c=8)
            VU = BIG[12 + gi].v().bitcast(BF16).rr("p (c h v) -> p c h v", c=8, h=2)
            yTs.append(BIG[16 + gi])
            if gi % 2 == 0:
                wp = k.load_w(wib[l, :, OFF_CP + (p // 2) * 512:OFF_CP + (p // 2 + 1) * 512], 8, 512)
            cb = (p % 2) * 256
            k.conv_tick()
            rl = lerp(projw(wp, cb).v(), 8 + 2 * p)
            kl = lerp(projw(wp, cb + 128).v(), 9 + 2 * p)
            pw = ps()
            S.mm(pw.v(), k.w2b[0:64, p * 128:(p + 1) * 128], lora1[0:64, :])
            sg = R32.get()
            S.act(sg.v(), pw.v(), AF.Sigmoid, bias=V("rw_w0", p))
            pa = ps()
            S.mm(pa.v(), k.a2b[64:128, p * 128:(p + 1) * 128], lora1[64:128, :])
            a = R32.get()
            S.act(a.v(), pa.v(), AF.Sigmoid, bias=V("rw_a0", p))
            kks = R16.get()
            S.act(kks.v(), kl.v(), AF.Square, scale=V("rw_k_k", p))
            pk = ps()
            S.mm(pk.v(), k.blockb.v(), kks.v())
            R16.put(kks)
            rn = R32.get()
            S.act(rn.v(), pk.v(), AF.Sqrt, bias=k.tiny[:, 0:1])
            S.recip(rn.v(), rn.v())
            kkn = R32.get()
            S.stt(kkn.v(), kl.v(), V("rw_k_k", p), rn.v(), ALU.mult, ALU.mult)
            R32.put(rn)
            kmod = R32.get()
            S.ts(kmod.v(), a.v(), V("rw_k_a", p), ALU.mult, k.omka[:, p:p + 1], ALU.add, e="pool")
            S.tt(kmod.v(), kmod.v(), kl.v(), ALU.mult, e="pool")
            R32.put(kl)
            prb = R16.get()
            S.stt(prb.v(), rl.v(), V("rw_r_k", p), kmod.v(), ALU.mult, ALU.mult)
            prbs.append(prb)
            cs = R32.get()
            S.scan(cs.v(), k.resetmask.v(), sg.v(), 0.0)
            csm = R32.get()
            S.tt(csm.v(), cs.v(), sg.v(), ALU.subtract, e="pool")
            R32.put(sg)
            G = R32.get()
            S.act(G.v(), cs.v(), AF.Exp, scale=-C0)
            G1 = csm
            S.act(G1.v(), csm.v(), AF.Exp, scale=-C0)
            Gi = cs
            S.act(Gi.v(), cs.v(), AF.Exp, scale=C0)
            S.copy(k.gC[gi].v(), G.v().rr("p (c t) -> p c t", t=64)[:, :, 63], e="pool")
            c3 = lambda t: t.v().rr("p (c t) -> p c t", t=64)
            S.stt(AR[:, :, 0, :], c3(kkn), -1.0, c3(G1), ALU.mult, ALU.mult)
            S.tt(AR[:, :, 1, :], c3(rl), c3(G), ALU.mult, e="pool")
            R32.put(rl, G, G1)
            bt = R32.get()
            S.tt(bt.v(), kkn.v(), a.v(), ALU.mult, e="pool")
            S.tt(bt.v(), bt.v(), Gi.v(), ALU.mult)
            kt = kmod
            S.tt(kt.v(), kmod.v(), Gi.v(), ALU.mult, e="pool")
            R32.put(kkn, a, Gi)
            BK2 = BK.rr("p (cb par) s t -> p cb par s t", par=2)
            bt4 = bt.v().rr("p (cb par t) -> p cb par t", par=2, t=64)
            kt4 = kt.v().rr("p (cb par t) -> p cb par t", par=2, t=64)
            S.copy(BK2[:, :, 0, 0, :], kt4[:, :, 0, :], e="act")
            S.copy(BK2[:, :, 0, 1, :], bt4[:, :, 0, :], e="dve")
            S.copy(BK2[:, :, 1, 0, :], bt4[:, :, 1, :], e="act")
            S.copy(BK2[:, :, 1, 1, :], kt4[:, :, 1, :], e="dve")
            R32.put(bt, kt)
            for half in range(2):
                pt = ps()
                ptb = pt.v().bitcast(BF16)
                for cc in range(4):
                    c = half * 4 + cc
                    S.tr(ptb[:, cc * 128:(cc + 1) * 128], BK[:, c, :, :].rr("p s t -> p (s t)"), k.identb.v())
                S.copy(BKtm[:, half * 4:(half + 1) * 4, :].rr("p c n -> p (c n)"), ptb[:, 0:512], e=("act" if half else "dve"))
            vl = R32.get()
            S.dma(vl.v(), k.vdram[p * 128:(p + 1) * 128, :])
            vb = R16.get()
            S.copy(vb.v(), vl.v(), e="pool")
            R32.put(vl)
            pt = ps()
            ptb = pt.v().bitcast(BF16)
            for blk in range(4):
                S.tr(ptb[:, blk * 128:(blk + 1) * 128], vb[:, blk * 128:(blk + 1) * 128], k.identb.v())
            R16.put(vb)
            VU2 = VU.rr("p (cb par) h v -> p cb par (h v)", par=2)
            pt4 = ptb[:, 0:512].rr("p (cb n) -> p cb n", cb=4)
            S.copy(VU2[0:64, :, 0, :], pt4[0:64, :, :], e="act")
            S.copy(VU2[64:128, :, 1, :], pt4[64:128, :, :], e="dve")
            k.rw_pair = getattr(k, "rw_pair", {})
            k.rw_pair[gi] = (AR, BK, BKtm, VU)

        k.mark('rw_pre%d' % grp)
        def scores(b):
            msb = {}
            mi = (b % 2) * 16
            for gi in range(4):
                AR, BK, BKtm, VU = k.rw_pair[gi]
                for hh in range(2):
                    hr = slice(hh * 64, hh * 64 + 64)
                    for par in range(2):
                        c = 2 * b + par
                        pS = k.ps_pool("b")
                        S.mm(pS[:, 0:128], BK[hr, c, :, :].rr("p s t -> p (s t)"), AR[hr, c, :, :].rr("p s t -> p (s t)"))
                        m = k.msb[mi]
                        mi += 1
                        S.tt(m.v(), pS[:, 0:128], k.mask4.v(), ALU.mult)
                        msb[(gi, hh, par)] = m
            return msb

        def inverse_gen(b, msb):
            units = [(gi, hh) for gi in range(4) for hh in range(2)]
            ws = {}
            for ui, (gi, hh) in enumerate(units):
                NRa, NRb = k.iwnr[ui * 2:ui * 2 + 2]
                La, Lb = k.iwl[ui * 2:ui * 2 + 2]
                S.copy(NRa[0:64, 0:64], msb[(gi, hh, 1)][0:64, 0:64], e="pool")
                S.copy(NRa[64:128, 64:128], msb[(gi, hh, 0)][64:128, 0:64], e="pool")
                ws[ui] = [NRa, NRb, La, Lb]
            yield
            for ui in range(8):
                NRa, NRb, La, Lb = ws[ui]
                pt = k.ps_pool("b")
                ptb = pt.v().bitcast(BF16)
                S.tr(ptb[:, 0:128], NRa[:, 0:128], k.identb.v())
                S.copy(La.v(), ptb[:, 0:128], e="act")
                S.tt(NRa[:, 128:256], NRa[:, 0:128], k.identb.v(), ALU.add, e="pool")
            yield
            for lev in range(1, 6):
                for half in range(2):
                    for ui in range(half * 4, half * 4 + 4):
                        NRa, NRb, La, Lb = ws[ui]
                        pc = k.ps_pool("b")
                        if lev > 1:
                            S.mm(pc[:, 0:256], La.v(), NRa[:, 0:256])
                        else:
                            S.mm(pc[:, 0:128], La.v(), NRa[:, 0:128])
                        pl = k.ps_pool("b")
                        S.mm(pl[:, 0:128], NRa[:, 0:128], La.v())
                        S.copy(Lb.v(), pl[:, 0:128], e="act")
                        S.copy(NRb[:, 0:128], pc[:, 0:128], e="dve")
                        if lev > 1:
                            S.tt(NRb[:, 128:256], pc[:, 128:256], NRa[:, 128:256], ALU.add)
                        else:
                            S.copy(NRb[:, 128:256], NRa[:, 128:256], e="pool")
                        ws[ui] = [NRb, NRa, Lb, La]
                    yield
            for half in range(2):
                for ui in range(half * 4, half * 4 + 4):
                    NRa, NRb, La, Lb = ws[ui]
                    pr = k.ps_pool("b")
                    S.mm(pr[:, 0:128], La.v(), NRa[:, 128:256])
                    S.tt(k.ttb[(b % 2) * 8 + ui].v(), pr[:, 0:128], NRa[:, 128:256], ALU.add)
                yield

        def steps_gen(b, msb):
            for par in range(2):
                c = 2 * b + par
                Vr = slice(0, 64) if par == 0 else slice(64, 128)
                Ur = slice(64, 128) if par == 0 else slice(0, 64)
                pZs, pUs, pYs, pPs = {}, {}, {}, {}
                for gi in range(4):
                    p = grp * 4 + gi
                    AR, BK, BKtm, VU = k.rw_pair[gi]
                    pZ = k.ps_pool("a")
                    for hh in range(2):
                        hr = slice(hh * 64, hh * 64 + 64)
                        S.mm(pZ[Ur, hh * 64:(hh + 1) * 64], AR[hr, c, 0, :], k.rwPb[hr, p, :], start=True, stop=False)
                        S.mm(pZ[Ur, hh * 64:(hh + 1) * 64], msb[(gi, hh, par)][Vr, 0:64], VU[Vr, c, hh, :], start=False, stop=True)
                    pZs[gi] = pZ
                for gi in range(4):
                    S.copy(k.zsb[gi][Ur, :], pZs[gi][Ur, 0:128], e="act")
                yield
                for gi in range(4):
                    pU = k.ps_pool("a")
                    for hh in range(2):
                        S.mm(pU[Ur, hh * 64:(hh + 1) * 64], k.ttb[(b % 2) * 8 + gi * 2 + hh][Ur, Ur], k.zsb[gi][Ur, hh * 64:(hh + 1) * 64])
                    pUs[gi] = pU
                for gi in range(4):
                    AR, BK, BKtm, VU = k.rw_pair[gi]
                    S.copy(VU[Ur, c, :, :].rr("p h v -> p (h v)"), pUs[gi][Ur, 0:128], e="dve")
                yield
                for gi in range(4):
                    p = grp * 4 + gi
                    AR, BK, BKtm, VU = k.rw_pair[gi]
                    pY = k.ps_pool("a")
                    for hh in range(2):
                        hr = slice(hh * 64, hh * 64 + 64)
                        S.mm(pY[hr, 0:64], k.rwPb[hr, p, :], AR[hr, c, 1, :], start=True, stop=False)
                        S.mm(pY[hr, 0:64], VU[:, c, hh, :], msb[(gi, hh, par)][:, 64:128], start=False, stop=True)
                    pYs[gi] = pY
                for gi in range(4):
                    S.copy(yTs[gi][:, c * 64:(c + 1) * 64], pYs[gi][:, 0:64], e="act")
                yield
                for gi in range(4):
                    p = grp * 4 + gi
                    AR, BK, BKtm, VU = k.rw_pair[gi]
                    pP = k.ps_pool("a")
                    for hh in range(2):
                        hr = slice(hh * 64, hh * 64 + 64)
                        S.mm(pP[hr, 0:64], BKtm[:, c, hr], VU[:, c, hh, :])
                    pPs[gi] = pP
                    S.ts(k.rwP[:, p, :], k.rwP[:, p, :], k.gC[gi][:, c:c + 1], ALU.mult, e="pool")
                for gi in range(4):
                    p = grp * 4 + gi
                    S.stt(k.rwPb[:, p, :], pPs[gi][:, 0:64], k.gC[gi][:, c:c + 1], k.rwP[:, p, :], ALU.mult, ALU.add)
                    S.stt(k.rwP[:, p, :], pPs[gi][:, 0:64], k.gC[gi][:, c:c + 1], k.rwP[:, p, :], ALU.mult, ALU.add)
                yield

        import os as _os
        if _os.environ.get("RWIL", "1") == "1":
            msbs = {0: scores(0)}
            for _ in inverse_gen(0, msbs[0]):
                pass
            for b in range(4):
                g1 = steps_gen(b, msbs[b])
                k.conv_tick()
                g2 = None
                if b < 3:
                    msbs[b + 1] = scores(b + 1)
                    g2 = inverse_gen(b + 1, msbs[b + 1])
                d1 = d2 = False
                while not (d1 and (d2 or g2 is None)):
                    if not d1:
                        try:
                            next(g1)
                        except StopIteration:
                            d1 = True
                    if g2 is not None and not d2:
                        try:
                            next(g2)
                            next(g2)
                        except StopIteration:
                            d2 = True

        else:
            for b in range(4):
                msb_ = scores(b)
                for _ in inverse_gen(b, msb_):
                    pass
                for _ in steps_gen(b, msb_):
                    pass

        k.mark('rw_chunks%d' % grp)
        for gi in range(4):
            p = grp * 4 + gi
            y = yTs[gi]
            pm = ps()
            S.mm(pm.v(), k.blockf.v(), y.v())
            ysq = R32.get()
            S.act(ysq.v(), y.v(), AF.Square)
            pq = ps()
            S.mm(pq.v(), k.blockf.v(), ysq.v())
            m = R32.get()
            S.act(m.v(), pm.v(), AF.Copy, scale=1.0 / 64)
            S.act(ysq.v(), m.v(), AF.Square)
            var = R32.get()
            S.stt(var.v(), pq.v(), 1.0 / 64, ysq.v(), ALU.mult, ALU.subtract)
            S.act(var.v(), var.v(), AF.Sqrt, bias=k.gneps[:, 0:1])
            S.recip(var.v(), var.v())
            S.tt(y.v(), y.v(), m.v(), ALU.subtract, e="pool")
            S.tt(y.v(), y.v(), var.v(), ALU.mult)
            S.ts(y.v(), y.v(), V("rw_ln_w", p), ALU.mult, V("rw_ln_b", p), ALU.add, e="pool")
            R32.put(ysq, m, var)
            pbn = ps()
            S.mm(pbn.v(), k.blockb.v(), prbs[gi].v())
            R16.put(prbs[gi])
            vl = R32.get()
            S.dma(vl.v(), k.vdram[p * 128:(p + 1) * 128, :])
            S.tt(vl.v(), vl.v(), pbn.v(), ALU.mult)
            S.tt(y.v(), y.v(), vl.v(), ALU.add, e="pool")
            R32.put(vl)
            pg = ps()
            S.mm(pg.v(), k.g2b[:, 0, p * 128:(p + 1) * 128], gsb0.v(), start=True, stop=False)
            S.mm(pg.v(), k.g2b[0:32, 1, p * 128:(p + 1) * 128], gsb1[0:32, :], start=False, stop=True)
            S.tt(k.ocur[:, p, :], y.v(), pg.v(), ALU.mult)
    k.mark('rw_out')
    R16.put(lora1, gsb0, gsb1)
    if "rw" in k.dbg_out:
        for h in range(8):
            tmp = R32.get()
            S.copy(tmp.v(), k.ocur[:, h, :], e="pool")
            S.dma(k.dbg_out["rw"][h * 128:(h + 1) * 128, t0:t0 + NTOK], tmp.v())
            R32.put(tmp)

import math

TWO_PI = 2.0 * math.pi


def s5_setup(k):
    S = k.S
    nc = k.nc
    L, T = k.L, k.T

    def ext(name, shape):
        return Buf(nc.dram_tensor(name, list(shape), F32, kind="ExternalInput"), name)
    k.s5lam = ext("s5lam", [L, 128, 3, 32])
    k.s5b = ext("s5b", [L, 2, 128, 32 * 16])
    k.s5c = ext("s5c", [L, 2, 128, 32 * 16])
    k.s5cpad = ext("s5cpad", [L, 2, 8, 128, 4 * 128])
    k.s5glu = ext("s5glu", [L, 8, 128, 128])
    k.s5P = S.dram("s5P", [8, 128, 8 * 2 * 128], BF16)
    k.s5Q = S.dram("s5Q", [8, 128, 8 * 2 * 4 * 32], BF16)
    k.s5BD = S.dram("s5BD", [8, 128, 8 * 128], BF16)
    k.s5D = S.dram("s5D", [8, 128, 4 * 2 * 64], F32)
    k.s5P3 = S.dram("s5P3", [8, 128, 8 * 2 * 128], BF16)
    k.s5Q3 = S.dram("s5Q3", [8, 128, 8 * 2 * 128], BF16)
    k.s5pad3 = [S.sb("s5pad3_%d" % i, [128, 128]) for i in range(2)]
    for t in k.s5pad3:
        S.memset(t.v(), 0.0)
    k.s5small = S.sb("s5small", [128, 24, 32])
    k.s5pw = S.sb("s5pw", [128, 2, 9, 32])
    k.s5glub = S.sb("s5glub", [128, 8, 128], BF16)
    k.s5car = S.sb("s5car", [128, 2, 32])
    k.s5rho = S.sb("s5rho", [128, 32])
    k.s5pad = [S.sb("s5pad%d" % i, [128, 4 * 32]) for i in range(3)]
    for t in k.s5pad:
        S.memset(t.v(), 0.0)
    k.s5t = [S.sb("s5t%d" % i, [128, 72]) for i in range(12)]
    k.s5ti = 0
    k.s5x = [[S.sb("s5x%d_%d" % (i, c), [128, 64], BF16) for c in range(2)] for i in range(4)]


def s5_layer_init(k, l):
    S = k.S
    R32, BIG, ps = k.R32, k.BIG, k.ps
    sm = k.s5small

    def s(i):
        return sm[:, i, :]
    LR, LI, LS, DT, MAG, TH, R_, RF, M1, COS, SIN, ABR, ABI, DEN, NR, T1, T2, CRE, CIM, RH, RI = range(21)
    lamt = R32.get()
    lv = lamt.v()[:, 0:96].rr("p (a q) -> p a q", a=3)
    S.dma(lv, k.s5lam[l])
    S.copy(s(LR), lv[:, 0, :], e="pool")
    S.copy(s(LI), lv[:, 1, :], e="pool")
    S.act(s(DT), lv[:, 2, :], AF.Exp)
    R32.put(lamt)
    S.tt(s(T1), s(LR), s(DT), ALU.mult)
    S.act(s(MAG), s(T1), AF.Exp)
    S.act(s(RH), s(T1), AF.Exp, scale=8.0)
    S.copy(k.s5rho.v(), s(RH), e="pool")
    S.tt(s(TH), s(LI), s(DT), ALU.mult)

    def sincos(dst, shift):
        S.ts(s(R_), s(TH), 1.0 / TWO_PI, ALU.mult, shift, ALU.add)
        ri = sm[:, 23, :].bitcast(I32)
        S.copy(ri, s(R_), e="dve")
        S.copy(s(RF), ri, e="dve")
        S.tt(s(R_), s(R_), s(RF), ALU.subtract)
        S.ts(s(M1), s(R_), 0.5, ALU.is_gt)
        S.tt(s(R_), s(R_), s(M1), ALU.subtract)
        S.ts(s(M1), s(R_), -0.5, ALU.is_lt)
        S.tt(s(R_), s(R_), s(M1), ALU.add)
        S.act(dst, s(R_), AF.Sin, scale=6.28318)
    sincos(s(SIN), 0.0)
    sincos(s(COS), 0.25)
    S.tt(s(ABR), s(MAG), s(COS), ALU.mult)
    S.tt(s(ABI), s(MAG), s(SIN), ALU.mult)
    S.tt(s(DEN), s(LR), s(LR), ALU.mult)
    S.tt(s(T1), s(LI), s(LI), ALU.mult)
    S.tt(s(DEN), s(DEN), s(T1), ALU.add)
    S.recip(s(DEN), s(DEN))
    S.ts(s(NR), s(ABR), -1.0, ALU.add)
    S.tt(s(T1), s(NR), s(LR), ALU.mult)
    S.tt(s(T2), s(ABI), s(LI), ALU.mult)
    S.tt(s(T1), s(T1), s(T2), ALU.add)
    S.tt(s(CRE), s(T1), s(DEN), ALU.mult)
    S.tt(s(T1), s(ABI), s(LR), ALU.mult)
    S.tt(s(T2), s(NR), s(LI), ALU.mult)
    S.tt(s(T1), s(T1), s(T2), ALU.subtract)
    S.tt(s(CIM), s(T1), s(DEN), ALU.mult)
    pw = k.s5pw
    S.memset(pw[:, 0, 0, :], 1.0)
    S.memset(pw[:, 1, 0, :], 0.0)
    for d in range(8):
        S.tt(s(T1), pw[:, 0, d, :], s(ABR), ALU.mult)
        S.tt(s(T2), pw[:, 1, d, :], s(ABI), ALU.mult)
        S.tt(pw[:, 0, d + 1, :], s(T1), s(T2), ALU.subtract)
        S.tt(s(T1), pw[:, 0, d, :], s(ABI), ALU.mult)
        S.tt(s(T2), pw[:, 1, d, :], s(ABR), ALU.mult)
        S.tt(pw[:, 1, d + 1, :], s(T1), s(T2), ALU.add)
    S.recip(s(RI), s(RH))
    D1R, D1I = 21, 22
    S.tt(s(D1R), pw[:, 0, 8, :], s(RI), ALU.mult)
    S.tt(s(D1I), pw[:, 1, 8, :], s(RI), ALU.mult)
    S.ts(s(D1I), s(D1I), -1.0, ALU.mult)
    Dre = [BIG[i].v().rr("p (q n) -> p q n", n=64) for i in range(4)]
    Dim = [BIG[4 + i].v().rr("p (q n) -> p q n", n=64) for i in range(4)]
    for i in range(4):
        qs = slice(i * 8, (i + 1) * 8)
        tr1 = R32.get()
        tr2 = R32.get()
        S.copy(Dre[i][:, :, 0], s(D1R)[:, qs], e="pool")
        S.copy(Dim[i][:, :, 0], s(D1I)[:, qs], e="pool")
        m = 1
        while m < 64:
            t1 = tr1.v()[:, 0:8 * m].rr("p (q n) -> p q n", n=m)
            t2 = tr2.v()[:, 0:8 * m].rr("p (q n) -> p q n", n=m)
            br = Dre[i][:, :, m - 1:m].bc([128, 8, m])
            bi = Dim[i][:, :, m - 1:m].bc([128, 8, m])
            ar = Dre[i][:, :, 0:m]
            ai = Dim[i][:, :, 0:m]
            S.tt(t1, ar, br, ALU.mult)
            S.tt(t2, ai, bi, ALU.mult, e="pool")
            S.tt(Dre[i][:, :, m:2 * m], t1, t2, ALU.subtract)
            S.tt(t1, ar, bi, ALU.mult)
            S.tt(t2, ai, br, ALU.mult, e="pool")
            S.tt(Dim[i][:, :, m:2 * m], t1, t2, ALU.add)
            m *= 2
        R32.put(tr1, tr2)
    for gt in range(8):
        i, o = gt // 2, (gt % 2) * 4
        dv = k.s5D[gt].rr("p (k c n) -> p k c n", k=4, c=2)
        S.dma(dv[:, :, 0, :], Dre[i][:, o:o + 4, :])
        S.dma(dv[:, :, 1, :], Dim[i][:, o:o + 4, :])
    bre = R32.get()
    bim = R32.get()
    S.dma(bre.v(), k.s5b[l, 0])
    S.dma(bim.v(), k.s5b[l, 1])
    v3 = lambda t: t.v().rr("p (q h) -> p q h", h=16)
    bcq = lambda view: view.rr("p (q o) -> p q o", o=1).bc([128, 32, 16])
    t1 = R32.get()
    t2 = R32.get()
    abr = R32.get()
    abi = R32.get()

    def cmul_bc(ore, oim, are, aim, sre, sim):
        S.tt(v3(t1), v3(are), bcq(sre), ALU.mult)
        S.tt(v3(t2), v3(aim), bcq(sim), ALU.mult, e="pool")
        S.tt(v3(t1), v3(t1), v3(t2), ALU.subtract)
        S.tt(v3(t2), v3(are), bcq(sim), ALU.mult, e="pool")
        S.tt(v3(oim), v3(aim), bcq(sre), ALU.mult)
        S.tt(v3(oim), v3(oim), v3(t2), ALU.add)
        S.copy(v3(ore), v3(t1), e="pool")
    cmul_bc(abr, abi, bre, bim, s(CRE), s(CIM))
    R32.put(bre, bim)
    cre_t = R32.get()
    cim_t = R32.get()
    S.dma(cre_t.v(), k.s5c[l, 0])
    S.dma(cim_t.v(), k.s5c[l, 1])
    pre, pim, pimn = k.s5pad
    for d in range(8):
        tau = 7 - d
        for gt in range(8):
            for (dst, src, sc) in ((pre, abr, None), (pim, abi, None), (pimn, abi, -1.0)):
                dv = dst.v().rr("p (k g h) -> p k g h", k=4, g=2)
                sv = v3(src)[:, gt * 4:(gt + 1) * 4, :]
                if sc is None:
                    S.copy(dv[0:64, :, 0, :], sv[0:64], e="pool")
                    S.copy(dv[64:128, :, 1, :], sv[64:128], e="pool")
                else:
                    S.ts(dv[0:64, :, 0, :], sv[0:64], sc, ALU.mult)
                    S.ts(dv[64:128, :, 1, :], sv[64:128], sc, ALU.mult)
            S.copy(k.s5pad3[0][:, 96:128], pre[:, 96:128], e="pool")
            S.copy(k.s5pad3[1][:, 96:128], pim[:, 96:128], e="pool")
            pt = ps()
            S.tr(pt[:, 0:128], pre.v(), k.ident.v())
            S.tr(pt[:, 128:256], pim.v(), k.ident.v())
            S.tr(pt[:, 256:384], k.s5pad3[0].v(), k.ident.v())
            S.tr(pt[:, 384:512], k.s5pad3[1].v(), k.ident.v())
            pb = k.R16.get()
            S.copy(pb.v(), pt.v(), e="act")
            S.dma(k.s5P[gt].rr("p (t c n) -> p t c n", t=8, c=2)[:, tau, :, :], pb[:, 0:256].rr("p (c n) -> p c n", c=2))
            S.dma(k.s5P3[gt].rr("p (t c n) -> p t c n", t=8, c=2)[:, tau, :, :], pb[:, 256:512].rr("p (c n) -> p c n", c=2))
            k.R16.put(pb)
            cp = R32.get()
            cpi = R32.get()
            S.dma(cp.v(), k.s5cpad[l, 0, gt])
            S.dma(cpi.v(), k.s5cpad[l, 1, gt])
            pbd = ps()
            for kk in range(4):
                S.mm(pbd[:, 32 * kk:32 * kk + 32], cp[:, 128 * kk:128 * kk + 128], pre[:, 32 * kk:32 * kk + 32], start=True, stop=False)
                S.mm(pbd[:, 32 * kk:32 * kk + 32], cpi[:, 128 * kk:128 * kk + 128], pimn[:, 32 * kk:32 * kk + 32], start=False, stop=True)
            R32.put(cp, cpi)
            bdT = R32.get()
            if d == 0:
                S.stt(bdT[:, 0:128], k.ident.v(), k.vec[:, VI["s5_d"], gt:gt + 1], pbd[:, 0:128], ALU.mult, ALU.add)
            else:
                S.copy(bdT[:, 0:128], pbd[:, 0:128], e="act")
            pt2 = ps()
            S.tr(pt2[:, 0:128], bdT[:, 0:128], k.ident.v())
            R32.put(bdT)
            bdb = k.R16.get()
            S.copy(bdb[:, 0:128], pt2[:, 0:128], e="act")
            S.dma(k.s5BD[gt].rr("p (d n) -> p d n", d=8)[:, d, :], bdb[:, 0:128])
            k.R16.put(bdb)
        if d < 7:
            cmul_bc(abr, abi, abr, abi, s(ABR), s(ABI))
    R32.put(abr, abi)
    qre = R32.get()
    qim = R32.get()
    s5qpad = [BIG[8 + i].v().bitcast(BF16).rr("p (q n) -> p q n", q=32) for i in range(2)]
    s5q3 = [BIG[10 + i].v().bitcast(BF16).rr("p (g n) -> p g n", g=8) for i in range(2)]
    for i in range(4):
        S.memset(BIG[8 + i].v(), 0.0)
    for tp in range(8):
        cmul_bc(qre, qim, cre_t, cim_t, pw[:, 0, tp + 1, :], pw[:, 1, tp + 1, :])
        for ci, (src, sc) in enumerate(((qre, 1.0), (qim, -1.0))):
            qp = s5qpad[ci]
            S.ts(qp[0:64, :, 0:16], v3(src)[0:64], sc, ALU.mult)
            S.ts(qp[64:128, :, 16:32], v3(src)[64:128], sc, ALU.mult)
            q3 = s5q3[ci]
            S.copy(q3[:, :, 96:128], qp.rr("p (g k) n -> p g k n", k=4)[:, :, 3, :], e="pool")
            for gt in range(8):
                dv = k.s5Q[gt].rr("p (t c k n) -> p t c k n", t=8, c=2, k=4)
                S.dma(dv[:, tp, ci, :, :], qp[:, gt * 4:(gt + 1) * 4, :])
                dv3 = k.s5Q3[gt].rr("p (t c n) -> p t c n", t=8, c=2)
                S.dma(dv3[:, tp, ci, :], q3[:, gt, :])
    R32.put(qre, qim, cre_t, cim_t, t1, t2)
    for gt in range(8):
        g = R32.get()
        S.dma(g[:, 0:128], k.s5glu[l, gt])
        S.copy(k.s5glub[:, gt, :], g[:, 0:128], e="pool")
        R32.put(g)
    S.memset(k.s5car.v(), 0.0)


def s5_tile(k, l, j):
    S = k.S
    R32, R16, BIG, xn, ps = k.R32, k.R16, k.BIG, k.xn, k.ps
    wib = k.w_in_b
    t0 = j * NTOK

    def tmp():
        t = k.s5t[k.s5ti % 12]
        k.s5ti += 1
        return t
    for gt in range(8):
        base = (gt % 2) * 10
        BDs = BIG[base + 0].v().bitcast(BF16).rr("p (d n) -> p d n", d=8)
        Pv = [BIG[base + 1 + i].v().bitcast(BF16).rr("p (t c n) -> p t c n", t=4, c=2) for i in range(2)]
        Qv = [BIG[base + 3 + i].v().bitcast(BF16).rr("p (t c k n) -> p t c k n", t=4, c=2, k=4) for i in range(2)]
        Dv = BIG[base + 5].v().rr("p (k c n) -> p k c n", k=4, c=2)
        P3v = [BIG[base + 6 + i].v().bitcast(BF16).rr("p (t c n) -> p t c n", t=4, c=2) for i in range(2)]
        Q3v = [BIG[base + 8 + i].v().bitcast(BF16).rr("p (t c n) -> p t c n", t=4, c=2) for i in range(2)]
        S.dma(BIG[base + 0].v().bitcast(BF16), k.s5BD[gt])
        for i in range(2):
            S.dma(BIG[base + 1 + i].v().bitcast(BF16), k.s5P[gt][:, i * 1024:(i + 1) * 1024])
            S.dma(BIG[base + 3 + i].v().bitcast(BF16), k.s5Q[gt][:, i * 1024:(i + 1) * 1024])
            S.dma(BIG[base + 6 + i].v().bitcast(BF16), k.s5P3[gt][:, i * 1024:(i + 1) * 1024])
            S.dma(BIG[base + 8 + i].v().bitcast(BF16), k.s5Q3[gt][:, i * 1024:(i + 1) * 1024])
        S.dma(BIG[base + 5].v(), k.s5D[gt])
        if gt % 4 == 0:
            w = k.load_w(wib[l, :, OFF_D + (gt // 4) * 512:OFF_D + (gt // 4 + 1) * 512], 8, 512)
        k.conv_tick()
        pu = ps()
        for c in range(8):
            S.mm(pu.v(), w[:, c, (gt % 4) * 128:(gt % 4 + 1) * 128], xn[:, c, :], start=(c == 0), stop=(c == 7))
        Ut = R16.get()
        Utv = Ut.v().rr("p (t n) -> p t n", t=8)
        S.copy(Utv, pu.v().rr("p (n t) -> p t n", t=8), e="act")
        for kk in range(4):
            q = gt * 4 + kk
            rows = slice(32 * kk, 32 * kk + 32)
            pwr = ps()
            pwi = ps()
            for (pw_, ci) in ((pwr, 0), (pwi, 1)):
                for tau in range(8):
                    if kk < 3:
                        S.mm(pw_[:, 0:64], Pv[tau // 4][rows, tau % 4, ci, :], Utv[rows, tau, :], start=(tau == 0), stop=(tau == 7))
                    else:
                        S.mm(pw_[:, 0:64], P3v[tau // 4][:, tau % 4, ci, :], Utv[:, tau, :], start=(tau == 0), stop=(tau == 7))
            dre = Dv[:, kk, 0, :]
            dim = Dv[:, kk, 1, :]
            a1, a2, a3, a4 = tmp(), tmp(), tmp(), tmp()
            S.tt(a1[:, 0:64], dre, pwr[:, 0:64], ALU.mult)
            S.tt(a2[:, 0:64], dim, pwi[:, 0:64], ALU.mult)
            S.tt(a1[:, 0:64], a1[:, 0:64], a2[:, 0:64], ALU.subtract, e="pool")
            S.tt(a3[:, 0:64], dre, pwi[:, 0:64], ALU.mult)
            S.tt(a4[:, 0:64], dim, pwr[:, 0:64], ALU.mult)
            S.tt(a3[:, 0:64], a3[:, 0:64], a4[:, 0:64], ALU.add, e="pool")
            rho = k.s5rho[:, q:q + 1].bc([128, 64])
            wre, wim = tmp(), tmp()
            S.scan(wre[:, 0:64], rho, a1[:, 0:64], k.s5car[:, 0, q:q + 1])
            S.scan(wim[:, 0:64], rho, a3[:, 0:64], k.s5car[:, 1, q:q + 1])
            xre, xim = tmp(), tmp()
            S.copy(xre[:, 0:1], k.s5car[:, 0, q:q + 1], e="pool")
            S.copy(xim[:, 0:1], k.s5car[:, 1, q:q + 1], e="pool")
            S.tt(a1[:, 0:64], dre, wre[:, 0:64], ALU.mult)
            S.tt(a2[:, 0:64], dim, wim[:, 0:64], ALU.mult, e="pool")
            S.tt(xre[:, 1:65], a1[:, 0:64], a2[:, 0:64], ALU.add)
            S.tt(a3[:, 0:64], dre, wim[:, 0:64], ALU.mult, e="pool")
            S.tt(a4[:, 0:64], dim, wre[:, 0:64], ALU.mult)
            S.tt(xim[:, 1:65], a3[:, 0:64], a4[:, 0:64], ALU.subtract)
            S.copy(k.s5car[:, 0, q:q + 1], xre[:, 64:65], e="pool")
            S.copy(k.s5car[:, 1, q:q + 1], xim[:, 64:65], e="pool")
            S.copy(k.s5x[kk][0].v(), xre[:, 0:64], e="act")
            S.copy(k.s5x[kk][1].v(), xim[:, 0:64], e="act")
        ysb = R32.get()
        yv = ysb.v().rr("p (n t) -> p t n", t=8)
        for tp in range(8):
            py = ps()
            nmm = (tp + 1)
            for tau in range(tp + 1):
                S.mm(py[:, 0:64], BDs[:, tp - tau, :], Utv[:, tau, :], start=(tau == 0), stop=False)
            for kk in range(3):
                S.mm(py[32 * kk:32 * kk + 32, 0:64], Qv[tp // 4][:, tp % 4, 0, kk, :], k.s5x[kk][0].v(), start=False, stop=False)
                S.mm(py[32 * kk:32 * kk + 32, 0:64], Qv[tp // 4][:, tp % 4, 1, kk, :], k.s5x[kk][1].v(), start=False, stop=False)
            S.mm(py[:, 0:64], Q3v[tp // 4][:, tp % 4, 0, :], k.s5x[3][0].v(), start=False, stop=False)
            S.mm(py[:, 0:64], Q3v[tp // 4][:, tp % 4, 1, :], k.s5x[3][1].v(), start=False, stop=True)
            S.copy(yv[:, tp, :], py[:, 0:64], e="act")
        R16.put(Ut)
        x2 = R32.get()
        S.act(x2.v(), ysb.v(), AF.Square)
        S.ts(x2.v(), x2.v(), 0.044715, ALU.mult, 1.0, ALU.add, e="pool")
        S.tt(x2.v(), x2.v(), ysb.v(), ALU.mult)
        S.act(x2.v(), x2.v(), AF.Tanh, scale=0.7978845608028654)
        S.stt(x2.v(), x2.v(), 1.0, ysb.v(), ALU.add, ALU.mult)
        zb = R16.get()
        S.act(zb.v(), x2.v(), AF.Copy, scale=0.5)
        pg = ps()
        S.mm(pg.v(), k.s5glub[:, gt, :], zb.v())
        R16.put(zb)
        sgl = ysb
        S.act(sgl.v(), pg.v(), AF.Sigmoid, bias=k.vec[:, VI["s5_glu_b"], gt:gt + 1])
        S.stt(k.ocur[:, gt, :], x2.v(), 0.5, sgl.v(), ALU.mult, ALU.mult)
        R32.put(x2, ysb)
    if "s5" in k.dbg_out:
        for h in range(8):
            tmp_ = R32.get()
            S.copy(tmp_.v(), k.ocur[:, h, :], e="pool")
            S.dma(k.dbg_out["s5"][h * 128:(h + 1) * 128, t0:t0 + NTOK], tmp_.v())
            R32.put(tmp_)


def build_full(L, T, dbg=()):
    k = build(L, T, dbg=dbg)
    k.stage = "z"
    hg_setup(k)
    rw_setup(k)
    s5_setup(k)
    S = k.S
    nt = T // NTOK
    for l in range(L):
        S.dma(k.vec.v().rr("p v c -> p (v c)"), k.vecs[l])
        hg_layer_init(k, l)
        rw_layer_init(k, l)
        rw_layer_vecs(k, l)
        s5_layer_init(k, l)
        items = k.conv_items(l + 1) if l + 1 < L else []
        per = (len(items) + nt - 1) // nt
        for j in range(nt):
            k.cvq = list(items[j * per:(j + 1) * per])
            k.rmsnorm_tile(k.hT, l, j, VI["mix_norm"])
            hgrn2_tile(k, l, j)
            k.merge_branch(l, j, 0, True)
            rwkv_tile(k, l, j)
            k.merge_branch(l, j, 1, False)
            s5_tile(k, l, j)
            k.merge_branch(l, j, 2, False)
            k.wout_tile(l, j)
            k.ffn_tile(l, j)
            k.do_conv(k.cvq)
            k.cvq = []
            if j == nt - 1:
                k.flush_conv()
    finalize(k)
    return k

from concourse.bass_utils import run_bass_kernel_spmd

L_FULL = 4
T_FULL = 4096


def _pack_vecs(inp, L):
    v = np.zeros((L, 128, NV, 8), np.float32)
    for n, i in VI.items():
        a = np.asarray(inp[n], np.float32)
        if n == "final_norm":
            a = np.broadcast_to(a[None], (4, D))
        elif n == "rw_v0":
            a = np.concatenate([np.zeros((1, D), np.float32), a], 0)
        elif n == "rw_r_k":
            a = a.reshape(a.shape[0], D)
        a = a[:L]
        v[:, :, i, :] = a.reshape(L, 8, 128).transpose(0, 2, 1)
    return v.reshape(L, 128, NV * 8)


def _rw_inmap(inp, L):
    mu_idx = list(range(2048, 3072))
    for p in range(8):
        mu_idx += list(range(p * 128, (p + 1) * 128)) + list(range(1024 + p * 128, 1024 + (p + 1) * 128))
    mu_idx += list(range(3072, 3360))
    mu = np.zeros((L, 27 * 128), np.float32)
    mu[:, :3360] = inp["rw_shift_mu"][:L][:, mu_idx]
    mu = np.ascontiguousarray(mu.reshape(L, 27, 128).transpose(0, 2, 1))
    v1 = np.concatenate([np.zeros((1, 1024, 32), np.float32), inp["rw_v1"]], 0)[:L]
    v2 = np.concatenate([np.zeros((1, 32, 1024), np.float32), inp["rw_v2"]], 0)[:L]
    return {"rw_w2": inp["rw_w2"][:L], "rw_a2": inp["rw_a2"][:L], "rw_g2": inp["rw_g2"][:L],
            "rw_v1": np.ascontiguousarray(v1), "rw_v2": np.ascontiguousarray(v2), "rwmu": mu}


def _s5_inmap(inp, L):
    def pairlay(a):
        Lh = a.shape[0]
        X = a.shape[3]
        return a.reshape(Lh, 32, 2, 64, X).transpose(0, 2, 3, 1, 4).reshape(Lh, 128, 32, X)
    lr = pairlay(inp["s5_lambda_re"][:L, :, :, None])[..., 0]
    li = pairlay(inp["s5_lambda_im"][:L, :, :, None])[..., 0]
    ls = pairlay(np.broadcast_to(inp["s5_log_step"][:L, :, None, None], (L, 64, 64, 1)))[..., 0]
    lam = np.ascontiguousarray(np.stack([lr, li, ls], 2))
    b = np.stack([pairlay(inp["s5_b_re"][:L]), pairlay(inp["s5_b_im"][:L])], 1).reshape(L, 2, 128, 512)
    cT = [inp["s5_c_re"][:L].transpose(0, 1, 3, 2), inp["s5_c_im"][:L].transpose(0, 1, 3, 2)]
    c = np.stack([pairlay(x) for x in cT], 1)
    cpad = np.zeros((L, 2, 8, 128, 4, 8, 16), np.float32)
    for gt in range(8):
        for kk in range(4):
            for g2 in range(2):
                cpad[:, :, gt, g2 * 64:(g2 + 1) * 64, kk, 2 * kk + g2, :] = c[:, :, g2 * 64:(g2 + 1) * 64, gt * 4 + kk, :]
    glu = np.zeros((L, 8, 128, 128), np.float32)
    for gt in range(8):
        for g8 in range(8):
            glu[:, gt, g8 * 16:(g8 + 1) * 16, g8 * 16:(g8 + 1) * 16] = inp["s5_glu_w"][:L, gt * 8 + g8]
    return {"s5lam": lam, "s5b": np.ascontiguousarray(b), "s5c": np.ascontiguousarray(c.reshape(L, 2, 128, 512)),
            "s5cpad": np.ascontiguousarray(cpad.reshape(L, 2, 8, 128, 512)), "s5glu": glu}


def kernel(**inputs):
    inp = {k_: np.asarray(v, dtype=np.float32) for k_, v in inputs.items()}
    L, T = L_FULL, T_FULL
    B = inp["x"].shape[0]
    perm = perm_cols()
    common = {"w_in": np.ascontiguousarray(inp["w_in"][:, :, perm]),
              "w_branch": inp["w_branch"], "w_out": inp["w_out"], "ffn_w_gate": inp["ffn_w_gate"],
              "ffn_w_up": inp["ffn_w_up"], "ffn_w_down": inp["ffn_w_down"], "vecs": _pack_vecs(inp, L)}
    common.update(_rw_inmap(inp, L))
    common.update(_s5_inmap(inp, L))
    k = build_full(L, T)
    in_maps = []
    for c in range(8):
        m = dict(common)
        m["xT"] = np.ascontiguousarray(inp["x"][c % B].T)
        in_maps.append(m)
    res = run_bass_kernel_spmd(k.nc, in_maps, core_ids=list(range(8)))
    out = np.stack([np.ascontiguousarray(res.results[b]["out"].T) for b in range(B)], 0)
    return out.astype(np.float32)
```

```python
import numpy as np
import concourse.bass as bass
import concourse.mybir as mybir

F32 = mybir.dt.float32
BF16 = mybir.dt.bfloat16
I32 = mybir.dt.int32
AF = mybir.ActivationFunctionType
ALU = mybir.AluOpType
AX = mybir.AxisListType


class Buf:
    __slots__ = ("t", "lw", "rd", "name", "pe_rg")

    def __init__(self, t, name=""):
        self.t = t
        self.lw = None
        self.rd = []
        self.name = name
        self.pe_rg = None

    def v(self):
        return View((self,), self.t.ap())

    def __getitem__(self, idx):
        return View((self,), self.t.ap()[idx])


class View:
    __slots__ = ("bufs", "ap")

    def __init__(self, bufs, ap):
        self.bufs = bufs
        self.ap = ap

    def __getitem__(self, idx):
        return View(self.bufs, self.ap[idx])

    def rr(self, pat, **kw):
        return View(self.bufs, self.ap.rearrange(pat, **kw))

    def bc(self, shape):
        return View(self.bufs, self.ap.to_broadcast(list(shape)))

    def bitcast(self, dt):
        return View(self.bufs, self.ap.bitcast(dt))

    @property
    def shape(self):
        return tuple(self.ap.shape)


def _bufs(*xs):
    out = []
    for x in xs:
        if isinstance(x, View):
            for b in x.bufs:
                if b not in out:
                    out.append(b)
    return out


def _ap(x):
    return x.ap if isinstance(x, View) else x


class Sched:
    ND = 8

    def __init__(self, nc):
        self.nc = nc
        self.eng = dict(pe=nc.tensor, dve=nc.vector, act=nc.scalar, pool=nc.gpsimd, sp=nc.sync)
        self.csem = {}
        self.cnt = {}
        for e in ("pe", "dve", "act", "pool"):
            self.csem[e] = nc.alloc_semaphore("cs_" + e)
            self.cnt[e] = 0
        self.dsem = {}
        self.dcnt = {}
        self.dk = {}
        for q in ("sp", "pool"):
            self.dsem[q] = [nc.alloc_semaphore("ds_%s%d" % (q, i)) for i in range(self.ND)]
            self.dcnt[q] = [0] * self.ND
            self.dk[q] = 0
        self.seen = {e: {} for e in self.eng}
        import os as _os
        self.nowait_same = set(_os.environ.get("NOWAIT", "pe").split(","))
        self.ninst = 0
        self.nwait = 0
        self.per = {e: 0 for e in self.eng}

    def sb(self, name, shape, dt=F32):
        return Buf(self.nc.alloc_sbuf_tensor(name, list(shape), dt), name)

    def ps(self, name, shape, dt=F32):
        return Buf(self.nc.alloc_psum_tensor(name, list(shape), dt), name)

    def dram(self, name, shape, dt=F32, kind="Internal"):
        return Buf(self.nc.dram_tensor(name, list(shape), dt, kind=kind), name)

    def _wait(self, e, tok, force=False):
        if tok is None:
            return
        sem, val, key, owner = tok
        if owner == e and e in self.nowait_same and not force:
            return
        if self.seen[e].get(key, 0) >= val:
            return
        self.eng[e].wait_ge(sem, val)
        self.seen[e][key] = val
        self.nwait += 1

    def _deps(self, e, reads, writes):
        for b in reads:
            self._wait(e, b.lw)
        for b in writes:
            self._wait(e, b.lw)
            for r in b.rd:
                self._wait(e, r)

    def _commit(self, tok, reads, writes):
        for b in reads:
            if b in writes:
                continue
            b.rd.append(tok)
            if len(b.rd) > 24:
                latest = {}
                for t in b.rd:
                    if t[2] not in latest or latest[t[2]][1] < t[1]:
                        latest[t[2]] = t
                b.rd = list(latest.values())
        for b in writes:
            b.lw = tok
            b.rd = []

    def op(self, e, fn, reads=(), writes=()):
        self._deps(e, reads, writes)
        inst = fn(self.eng[e])
        self.cnt[e] += 1
        inst.then_inc(self.csem[e], 1)
        tok = (self.csem[e], self.cnt[e], "c" + e, e)
        self._commit(tok, reads, writes)
        self.ninst += 1
        self.per[e] += 1
        return tok

    def dma(self, out, in_, q="sp", **kw):
        reads = _bufs(in_)
        writes = _bufs(out)
        k = self.dk[q]
        self.dk[q] += 1
        i = k % self.ND
        sem = self.dsem[q][i]
        key = "d%s%d" % (q, i)
        prev = self.dcnt[q][i]
        if prev > 0 and self.seen[q].get(key, 0) < 16 * prev:
            self.eng[q].wait_ge(sem, 16 * prev)
            self.seen[q][key] = 16 * prev
        self._deps(q, reads, writes)
        inst = self.eng[q].dma_start(out=_ap(out), in_=_ap(in_), **kw)
        self.dcnt[q][i] += 1
        inst.then_inc(sem, 16)
        tok = (sem, 16 * self.dcnt[q][i], key, "dma" + q)
        self._commit(tok, reads, writes)
        self.ninst += 1
        self.per[q] += 1
        return tok

    def finish(self, bufs):
        for b in bufs:
            self._wait("sp", b.lw)
        for e in ("pe", "dve", "act", "pool"):
            if self.cnt[e] > 0:
                self._wait("sp", (self.csem[e], self.cnt[e], "c" + e, e))
        for q in ("sp", "pool"):
            for i in range(self.ND):
                if self.dcnt[q][i] > 0:
                    self._wait("sp", (self.dsem[q][i], 16 * self.dcnt[q][i], "d%s%d" % (q, i), "dma" + q))

    def act(self, out, in_, func, bias=None, scale=None, accum=None, e="act"):
        kw = {}
        if bias is not None:
            kw["bias"] = _ap(bias)
        if scale is not None:
            kw["scale"] = _ap(scale)
        if accum is not None:
            kw["accum_out"] = _ap(accum)
        return self.op(e, lambda g: g.activation(out=_ap(out), in_=_ap(in_), func=func, **kw),
                       _bufs(in_, bias, scale), _bufs(out, accum))

    def tt(self, out, a, b, op, e="dve"):
        return self.op(e, lambda g: g.tensor_tensor(out=_ap(out), in0=_ap(a), in1=_ap(b), op=op),
                       _bufs(a, b), _bufs(out))

    def ts(self, out, a, s1, op0, s2=None, op1=None, e="dve"):
        if op1 is None:
            return self.op(e, lambda g: g.tensor_scalar(out=_ap(out), in0=_ap(a), scalar1=_ap(s1), scalar2=None, op0=op0),
                           _bufs(a, s1), _bufs(out))
        return self.op(e, lambda g: g.tensor_scalar(out=_ap(out), in0=_ap(a), scalar1=_ap(s1), scalar2=_ap(s2), op0=op0, op1=op1),
                       _bufs(a, s1, s2), _bufs(out))

    def stt(self, out, a, s, b, op0, op1):
        return self.op("dve", lambda g: g.scalar_tensor_tensor(out=_ap(out), in0=_ap(a), scalar=_ap(s), in1=_ap(b), op0=op0, op1=op1),
                       _bufs(a, s, b), _bufs(out))

    def copy(self, out, in_, e="dve"):
        if e == "act":
            return self.act(out, in_, AF.Copy)
        return self.op(e, lambda g: g.tensor_copy(out=_ap(out), in_=_ap(in_)), _bufs(in_), _bufs(out))

    def memset(self, out, val, e="pool"):
        return self.op(e, lambda g: g.memset(_ap(out), val), [], _bufs(out))

    def recip(self, out, in_, e="dve"):
        return self.op(e, lambda g: g.reciprocal(out=_ap(out), in_=_ap(in_)), _bufs(in_), _bufs(out))

    def scan(self, out, d0, d1, init, op0=ALU.mult, op1=ALU.add):
        return self.op("dve", lambda g: g.tensor_tensor_scan(out=_ap(out), data0=_ap(d0), data1=_ap(d1), initial=_ap(init), op0=op0, op1=op1),
                       _bufs(d0, d1, init), _bufs(out))

    def mm(self, out, lhsT, rhs, start=True, stop=True):
        la = _ap(lhsT)
        rg = (la.base_partition(), la.partition_size())
        for b in _bufs(out):
            if b.pe_rg is not None and b.pe_rg != rg and b.lw is not None and b.lw[3] == "pe":
                self._wait("pe", b.lw, force=True)
            b.pe_rg = rg
        return self.op("pe", lambda g: g.matmul(_ap(out), lhsT=_ap(lhsT), rhs=_ap(rhs), start=start, stop=stop),
                       _bufs(lhsT, rhs), _bufs(out))

    def tr(self, out, in_, ident):
        return self.op("pe", lambda g: g.transpose(out=_ap(out), in_=_ap(in_), identity=_ap(ident)),
                       _bufs(in_, ident), _bufs(out))

    def asel(self, out, in_, pattern, cmp, fill, base, cm):
        return self.op("pool", lambda g: g.affine_select(out=_ap(out), in_=_ap(in_), pattern=pattern, compare_op=cmp, fill=fill, base=base, channel_multiplier=cm),
                       _bufs(in_), _bufs(out))


class Ring:
    def __init__(self, S, name, n, shape, dt=F32):
        self.tiles = [S.sb("%s%d" % (name, i), shape, dt) for i in range(n)]
        self.free = list(self.tiles)
        self.name = name

    def get(self):
        assert self.free, "ring %s exhausted" % self.name
        return self.free.pop(0)

    def put(self, *ts):
        for t in ts:
            assert t not in self.free
            self.free.append(t)

import numpy as np

D = 1024
NTOK = 512
FH = 2816
NHT = FH // 128
INW = 11552
EPS = 1e-6

OFF_A = 0
OFF_B = 1024
OFF_CV = OFF_B + 8 * 384
OFF_CP = OFF_CV + 1024
OFF_CL = OFF_CP + 8 * 256
OFF_D = OFF_CL + 288
OFF_E = OFF_D + 1024
assert OFF_E + 3072 == INW


def perm_cols():
    p = []
    HG = 0
    p += list(range(HG + 2048, HG + 3072))
    for h in range(8):
        p += list(range(HG + h * 128, HG + (h + 1) * 128))
        p += list(range(HG + 1024 + h * 128, HG + 1024 + (h + 1) * 128))
        p += list(range(HG + 3072 + h * 128, HG + 3072 + (h + 1) * 128))
    RW = 4096
    p += list(range(RW + 2048, RW + 3072))
    for q in range(8):
        p += list(range(RW + q * 128, RW + (q + 1) * 128))
        p += list(range(RW + 1024 + q * 128, RW + 1024 + (q + 1) * 128))
    p += list(range(RW + 3072, RW + 3360))
    S5 = RW + 3360
    p += list(range(S5, S5 + 1024))
    p += list(range(S5 + 1024, S5 + 1024 + 3072))
    p = np.array(p, dtype=np.int64)
    assert p.shape[0] == INW and len(set(p.tolist())) == INW
    return p


VEC_NAMES = ["mix_norm", "ffn_norm", "hg_lb_logits", "hg_onorm", "rw_w0", "rw_a0", "rw_v0", "rw_k_k", "rw_k_a",
             "rw_r_k", "rw_ln_w", "rw_ln_b", "s5_d", "s5_glu_b", "final_norm"]
NV = len(VEC_NAMES)
VI = {n: i for i, n in enumerate(VEC_NAMES)}


class K:
    pass


def build(L, T, dbg=(), stub=()):
    NTT = T // NTOK
    nc = bass.Bass("TRN2", target_bir_lowering=False)
    S = Sched(nc)
    k = K()
    k.S = S
    k.nc = nc
    k.L = L
    k.T = T

    def ext(name, shape):
        return Buf(nc.dram_tensor(name, list(shape), F32, kind="ExternalInput"), name)

    xT = ext("xT", [D, T])
    w_in = ext("w_in", [L, D, INW])
    w_branch = ext("w_branch", [L, 3 * D, D])
    w_out = ext("w_out", [L, D, D])
    w_gate = ext("ffn_w_gate", [L, D, FH])
    w_up = ext("ffn_w_up", [L, D, FH])
    w_down = ext("ffn_w_down", [L, FH, D])
    vecs = ext("vecs", [L, 128, NV * 8])
    out = Buf(nc.dram_tensor("out", [D, T], F32, kind="ExternalOutput"), "out")
    dbg_out = {}
    for name in dbg:
        dbg_out[name] = Buf(nc.dram_tensor("dbg_" + name, [D, T], F32, kind="ExternalOutput"), "dbg_" + name)

    class WT:
        def __init__(self, name, src, blocks):
            self.name = name
            self.src = src
            self.blocks = {}
            off = 0
            for (r0, kc, c0, n) in blocks:
                self.blocks[(r0, c0)] = (off, kc, n)
                off += kc * n
            self.total = off
            self.scr = S.dram(name + "_t", [L, 128, off], BF16)

        def __getitem__(self, idx):
            l, rs, cs = idx
            r0 = 0 if rs.start is None else rs.start
            return ("wt", self, l, r0, cs.start)

    in_blocks = [(0, 8, 0, 512), (0, 8, 512, 512)] + [(0, 8, OFF_B + 384 * h, 384) for h in range(8)] \
        + [(0, 8, OFF_CV + 512 * i, 512) for i in range(2)] + [(0, 8, OFF_CP + 512 * i, 512) for i in range(4)] \
        + [(0, 8, OFF_CL, 288)] + [(0, 8, OFF_D + 512 * i, 512) for i in range(2)] + [(0, 8, OFF_E + 512 * i, 512) for i in range(6)]
    w_in_b = WT("w_in_b", w_in, in_blocks)
    w_branch_b = WT("w_branch_b", w_branch, [(br * D, 8, c0, 512) for br in range(3) for c0 in (0, 512)])
    w_out_b = WT("w_out_b", w_out, [(0, 8, 0, 512), (0, 8, 512, 512)])
    fblocks = [(0, 8, 512 * i, 512) for i in range(5)] + [(0, 8, 2560, 256)]
    w_gate_b = WT("w_gate_b", w_gate, fblocks)
    w_up_b = WT("w_up_b", w_up, fblocks)
    w_down_b = WT("w_down_b", w_down, [(0, NHT, 128 * i, 128) for i in range(8)])
    hT = S.dram("hT", [D, T], F32)

    def conv_items(l):
        items = []
        for wt in (w_in_b, w_branch_b, w_out_b, w_gate_b, w_up_b, w_down_b):
            for (r0, c0), (off, kc, n) in wt.blocks.items():
                for c in range(kc):
                    items.append((wt, l, r0 + c * 128, c0, n, off + c * n))
        return items

    def do_conv(items):
        for it in items:
            (wt, l, r, c0, n, off) = it
            i = k.cvi % 3
            k.cvi += 1
            S.dma(cst32[i][:, 0:n], wt.src[l, r:r + 128, c0:c0 + n])
            S.copy(cst16[i][:, 0:n], cst32[i][:, 0:n], e="pool")
            k.cvpend.append((wt.scr[l, :, off:off + n], cst16[i][:, 0:n]))
            if len(k.cvpend) > 1:
                d, sv = k.cvpend.pop(0)
                S.dma(d, sv)

    def flush_conv():
        while k.cvpend:
            d, sv = k.cvpend.pop(0)
            S.dma(d, sv)

    k.cvpend = []
    k.mark = lambda name: None
    k.cvq = []

    def conv_tick(n=2):
        if k.cvq:
            do_conv(k.cvq[:n])
            del k.cvq[:n]
    k.conv_tick = conv_tick
    k.cvi = 0
    cst32 = [S.sb("cst32_%d" % i, [128, 512]) for i in range(3)]
    cst16 = [S.sb("cst16_%d" % i, [128, 512], BF16) for i in range(3)]

    ident = S.sb("ident", [128, 128])
    identb = S.sb("identb", [128, 128], BF16)
    onesb = S.sb("onesb", [128, 128], BF16)
    onesf = S.sb("onesf", [128, 128])
    vec = S.sb("vec", [128, NV, 8])
    xn = S.sb("xn", [128, 8, NTOK], BF16)
    merged = S.sb("merged", [128, 8, NTOK])
    ocur = S.sb("ocur", [128, 8, NTOK], BF16)
    wbufs = [S.sb("wbuf%d" % i, [128, 8 * 512], BF16) for i in range(3)]
    k.wi = 0
    R32 = Ring(S, "r32_", 14, [128, NTOK])
    R16 = Ring(S, "r16_", 8, [128, NTOK], BF16)
    BIG = [S.sb("big%d" % i, [128, NTOK]) for i in range(20)]
    psb = [S.ps("pb%d" % i, [128, 512]) for i in range(8)]
    k.pi = 0

    def ps():
        b = psb[k.pi % 7]
        k.pi += 1
        return b
    psheld = psb[7]
    k.ppi = {"a": 0, "b": 0}

    def ps_pool(name):
        if name == "a":
            b = psb[k.ppi["a"] % 4]
        else:
            b = psb[4 + k.ppi["b"] % 3]
        k.ppi[name] += 1
        return b

    def wbuf():
        b = wbufs[k.wi % 3]
        k.wi += 1
        return b

    def load_w(desc, kc, ncols):
        _, wt, l, r0, c0 = desc
        off, kc_, n_ = wt.blocks[(r0, c0)]
        assert kc_ == kc and n_ == ncols, (wt.name, r0, c0, kc, ncols, kc_, n_)
        b = wbuf()
        v = b.v()[:, 0:kc * ncols]
        S.dma(v, wt.scr[l, :, off:off + kc * ncols])
        return v.rr("p (c n) -> p c n", c=kc)

    S.memset(onesf.v(), 1.0)
    S.memset(onesb.v(), 1.0)
    S.asel(ident.v(), onesf.v(), [[-1, 128]], ALU.is_equal, 0.0, 0, 1)
    S.copy(identb.v(), ident.v(), e="pool")

    vecs_all = [vecs[l] for l in range(L)] + [vecs[L - 1]] * (4 - L)
    k.__dict__.update(locals())

    do_conv(conv_items(0))
    flush_conv()

    for c in range(8):
        S.dma(hT[c * 128:(c + 1) * 128, :], xT[c * 128:(c + 1) * 128, :])

    def rmsnorm_tile(src_dram, l, j, gidx):
        t0 = j * NTOK
        hts = []
        pss = ps()
        for c in range(8):
            ht = R32.get()
            S.dma(ht.v(), src_dram[c * 128:(c + 1) * 128, t0:t0 + NTOK])
            sq = R16.get()
            S.act(sq.v(), ht.v(), AF.Square)
            S.mm(pss.v(), onesb.v(), sq.v(), start=(c == 0), stop=(c == 7))
            R16.put(sq)
            hts.append(ht)
        rstd = R32.get()
        S.act(rstd.v(), pss.v(), AF.Sqrt, bias=k.epsb[:, 0:1], scale=1.0 / D)
        S.recip(rstd.v(), rstd.v())
        for c in range(8):
            S.stt(xn[:, c, :], hts[c].v(), vec[:, gidx, c:c + 1], rstd.v(), ALU.mult, ALU.mult)
            R32.put(hts[c])
        R32.put(rstd)

    k.rmsnorm_tile = rmsnorm_tile
    epsb = S.sb("epsb", [128, 4])
    S.memset(epsb[:, 0:1], EPS)
    S.memset(epsb[:, 1:2], 0.0)
    k.epsb = epsb

    def ffn_tile(l, j):
        t0 = j * NTOK
        rmsnorm_tile(hT, l, j, VI["ffn_norm"])
        def act_tile(ht):
            b = BIG[ht // 2]
            return b.v().bitcast(BF16)[:, (ht % 2) * NTOK:(ht % 2 + 1) * NTOK]
        for blk in range(6):
            c0 = blk * 512
            ncol = min(512, FH - c0)
            wg = load_w(w_gate_b[l, :, c0:c0 + ncol], 8, ncol)
            wu = load_w(w_up_b[l, :, c0:c0 + ncol], 8, ncol)
            conv_tick()
            for s in range(ncol // 128):
                ht = (c0 // 128) + s
                pg = ps()
                pu = ps()
                for c in range(8):
                    S.mm(pg.v(), wg[:, c, s * 128:(s + 1) * 128], xn[:, c, :], start=(c == 0), stop=(c == 7))
                for c in range(8):
                    S.mm(pu.v(), wu[:, c, s * 128:(s + 1) * 128], xn[:, c, :], start=(c == 0), stop=(c == 7))
                sg = R32.get()
                S.act(sg.v(), pg.v(), AF.Silu)
                S.tt(act_tile(ht), sg.v(), pu.v(), ALU.mult)
                R32.put(sg)
        for dt_ in range(8):
            wd = load_w(w_down_b[l, :, dt_ * 128:(dt_ + 1) * 128], NHT, 128)
            conv_tick()
            po = ps()
            for ht in range(NHT):
                S.mm(po.v(), wd[:, ht, :], act_tile(ht), start=(ht == 0), stop=(ht == NHT - 1))
            hres = R32.get()
            S.dma(hres.v(), hT[dt_ * 128:(dt_ + 1) * 128, t0:t0 + NTOK])
            S.tt(hres.v(), hres.v(), po.v(), ALU.add)
            S.dma(hT[dt_ * 128:(dt_ + 1) * 128, t0:t0 + NTOK], hres.v())
            R32.put(hres)

    k.ffn_tile = ffn_tile

    def merge_branch(l, j, br, first):
        for half in range(2):
            wb = load_w(w_branch_b[l, br * D:(br + 1) * D, half * 512:(half + 1) * 512], 8, 512)
            wg = load_w(w_in_b[l, :, OFF_E + br * D + half * 512: OFF_E + br * D + (half + 1) * 512], 8, 512)
            conv_tick()
            for s in range(4):
                dt_ = half * 4 + s
                pg = ps()
                pb = ps()
                for c in range(8):
                    S.mm(pg.v(), wg[:, c, s * 128:(s + 1) * 128], xn[:, c, :], start=(c == 0), stop=(c == 7))
                for c in range(8):
                    S.mm(pb.v(), wb[:, c, s * 128:(s + 1) * 128], ocur[:, c, :], start=(c == 0), stop=(c == 7))
                sg = R32.get()
                S.act(sg.v(), pg.v(), AF.Sigmoid)
                if first:
                    S.tt(merged[:, dt_, :], sg.v(), pb.v(), ALU.mult)
                else:
                    S.tt(sg.v(), sg.v(), pb.v(), ALU.mult)
                    S.tt(merged[:, dt_, :], merged[:, dt_, :], sg.v(), ALU.add, e="pool")
                R32.put(sg)

    k.merge_branch = merge_branch

    def wout_tile(l, j):
        t0 = j * NTOK
        for c in range(8):
            S.copy(ocur[:, c, :], merged[:, c, :], e=("act" if c % 2 else "dve"))
        for half in range(2):
            wo = load_w(w_out_b[l, :, half * 512:(half + 1) * 512], 8, 512)
            for s in range(4):
                dt_ = half * 4 + s
                po = ps()
                for c in range(8):
                    S.mm(po.v(), wo[:, c, s * 128:(s + 1) * 128], ocur[:, c, :], start=(c == 0), stop=(c == 7))
                hres = R32.get()
                S.dma(hres.v(), hT[dt_ * 128:(dt_ + 1) * 128, t0:t0 + NTOK])
                S.tt(hres.v(), hres.v(), po.v(), ALU.add)
                S.dma(hT[dt_ * 128:(dt_ + 1) * 128, t0:t0 + NTOK], hres.v())
                R32.put(hres)

    k.wout_tile = wout_tile
    return k


def finalize(k):
    S = k.S
    L, T = k.L, k.T
    for j in range(T // NTOK):
        t0 = j * NTOK
        hts = []
        pss = k.ps()
        for c in range(8):
            ht = k.R32.get()
            S.dma(ht.v(), k.hT[c * 128:(c + 1) * 128, t0:t0 + NTOK])
            sq = k.R16.get()
            S.act(sq.v(), ht.v(), AF.Square)
            S.mm(pss.v(), k.onesb.v(), sq.v(), start=(c == 0), stop=(c == 7))
            k.R16.put(sq)
            hts.append(ht)
        rstd = k.R32.get()
        S.act(rstd.v(), pss.v(), AF.Sqrt, bias=k.epsb[:, 0:1], scale=1.0 / D)
        S.recip(rstd.v(), rstd.v())
        for c in range(8):
            S.stt(hts[c].v(), hts[c].v(), k.vec[:, VI["final_norm"], c:c + 1], rstd.v(), ALU.mult, ALU.mult)
            S.dma(k.out[c * 128:(c + 1) * 128, t0:t0 + NTOK], hts[c].v())
            k.R32.put(hts[c])
        k.R32.put(rstd)
    S.finish([k.out] + list(k.dbg_out.values()))


def stub_mixer(k, l, j, col0):
    S = k.S
    for half in range(2):
        w = k.load_w(k.w_in_b[l, :, col0 + half * 512: col0 + (half + 1) * 512], 8, 512)
        for s in range(4):
            p = k.ps()
            for c in range(8):
                S.mm(p.v(), w[:, c, s * 128:(s + 1) * 128], k.xn[:, c, :], start=(c == 0), stop=(c == 7))
            S.copy(k.ocur[:, half * 4 + s, :], p.v(), e="act")


def run_layers(k, mixers):
    S = k.S
    for l in range(k.L):
        S.dma(k.vec.v().rr("p v c -> p (v c)"), k.vecs[l])
        nt = k.T // NTOK
        items = k.conv_items(l + 1) if l + 1 < k.L else []
        per = (len(items) + nt - 1) // nt
        for j in range(nt):
            k.do_conv(items[j * per:(j + 1) * per])
            k.rmsnorm_tile(k.hT, l, j, VI["mix_norm"])
            for bi, mx in enumerate(mixers):
                mx(k, l, j)
                k.merge_branch(l, j, bi, bi == 0)
            k.wout_tile(l, j)
            k.ffn_tile(l, j)
    finalize(k)


def hg_setup(k):
    S = k.S
    L = k.L
    k.resetmask = S.sb("resetmask", [128, NTOK])
    S.memset(k.resetmask.v(), 1.0)
    S.memset(k.resetmask.v().rr("p (c t) -> p c t", t=64)[:, :, 0:1], 0.0)
    k.maskincl = S.sb("maskincl", [128, 64])
    k.maskstr = S.sb("maskstr", [128, 64])
    for half in range(2):
        sl = slice(half * 64, half * 64 + 64)
        S.asel(k.maskincl[sl, :], k.onesf[sl, 0:64], [[1, 64]], ALU.is_ge, 0.0, 0, -1)
        S.asel(k.maskstr[sl, :], k.onesf[sl, 0:64], [[1, 64]], ALU.is_gt, 0.0, 0, -1)
    k.lball = S.sb("lball", [128, 4, 8])
    k.omlall = S.sb("omlall", [128, 4, 8])
    E = S.sb("lbE", [128, 4, 8])
    sm = S.sb("lbsum", [128, 8])
    c0 = VI["hg_lb_logits"] * 8
    S.memset(E.v(), 0.0)
    for l in range(4):
        S.dma(E[:, l, :], k.vecs_all[min(l, L - 1) if False else l][:, c0:c0 + 8])
    S.act(E.v(), E.v(), AF.Exp)
    S.tt(sm.v(), E[:, 0, :], E[:, 1, :], ALU.add)
    S.tt(sm.v(), sm.v(), E[:, 2, :], ALU.add)
    S.tt(sm.v(), sm.v(), E[:, 3, :], ALU.add)
    S.recip(sm.v(), sm.v())
    for l in range(4):
        S.tt(E[:, l, :], E[:, l, :], sm.v(), ALU.mult)
    S.memset(k.lball[:, 0, :], 0.0, e="dve")
    for l in range(1, 4):
        S.tt(k.lball[:, l, :], k.lball[:, l - 1, :], E[:, l, :], ALU.add)
    S.ts(k.lball.v(), k.lball.v(), 0.0, ALU.max)
    S.ts(k.omlall.v(), k.lball.v(), -1.0, ALU.mult, 1.0, ALU.add)
    k.hgS = S.sb("hgS", [128, 8, 128])
    k.hgSb = S.sb("hgSb", [128, 8, 128], BF16)
    k.sct = [S.sb("hgsct%d" % i, [128, 64], BF16) for i in range(4)]
    k.scti = 0


def hg_layer_init(k, l):
    k.S.memset(k.hgS.v(), 0.0)
    k.S.memset(k.hgSb.v(), 0.0)


def hgrn2_tile(k, l, j):
    S = k.S
    R32, R16, BIG, xn, ps = k.R32, k.R16, k.BIG, k.xn, k.ps
    wib = k.w_in_b
    vtm = [BIG[tb].v().bitcast(BF16) for tb in range(4)]
    for half in range(2):
        w = k.load_w(wib[l, :, OFF_A + half * 512: OFF_A + (half + 1) * 512], 8, 512)
        for tb in range(4):
            p = ps()
            for c in range(8):
                S.mm(p.v(), xn[:, c, tb * 128:(tb + 1) * 128], w[:, c, :], start=(c == 0), stop=(c == 7))
            S.copy(vtm[tb][:, half * 512:(half + 1) * 512], p.v(), e="act")
    for h in range(8):
        w = k.load_w(wib[l, :, OFF_B + h * 384: OFF_B + (h + 1) * 384], 8, 384)
        k.conv_tick()
        lbv = k.lball[:, l, h:h + 1]
        omlv = k.omlall[:, l, h:h + 1]

        def proj(col0):
            p = ps()
            for c in range(8):
                S.mm(p.v(), w[:, c, col0:col0 + 128], xn[:, c, :], start=(c == 0), stop=(c == 7))
            return p
        pq = proj(0)
        q = R32.get()
        S.act(q.v(), pq.v(), AF.Silu)
        pf = proj(128)
        f = R32.get()
        S.act(f.v(), pf.v(), AF.Sigmoid)
        S.ts(f.v(), f.v(), omlv, ALU.mult, lbv, ALU.add)
        lf = R32.get()
        S.act(lf.v(), f.v(), AF.Ln)
        b = R32.get()
        S.scan(b.v(), k.resetmask.v(), lf.v(), 0.0)
        R32.put(lf)
        kk = R32.get()
        S.ts(kk.v(), f.v(), -1.0, ALU.mult, 1.0, ALU.add, e="pool")
        R32.put(f)
        eb = R32.get()
        S.act(eb.v(), b.v(), AF.Exp)
        enb = R32.get()
        S.act(enb.v(), b.v(), AF.Exp, scale=-1.0)
        R32.put(b)
        qt = R16.get()
        S.tt(qt.v(), q.v(), eb.v(), ALU.mult)
        R32.put(q)
        ktf = R32.get()
        S.tt(ktf.v(), kk.v(), enb.v(), ALU.mult, e="pool")
        R32.put(kk, enb)
        ktb = R16.get()
        S.copy(ktb.v(), ktf.v(), e="act")
        kdT = R16.get()
        eb3 = eb.v().rr("p (c t) -> p c t", t=64)
        S.tt(kdT.v().rr("p (c t) -> p c t", t=64), ktf.v().rr("p (c t) -> p c t", t=64),
             eb3[:, :, 63:64].bc([128, 8, 64]), ALU.mult)
        R32.put(ktf)
        ptr = ps()
        ptb = ptr.v().bitcast(BF16)
        for blk in range(4):
            S.tr(ptb[:, blk * 128:(blk + 1) * 128], kdT[:, blk * 128:(blk + 1) * 128], k.identb.v())
        kdtm = R16.get()
        S.copy(kdtm.v(), ptb[:, 0:512], e="act")
        R16.put(kdT)
        pog = proj(256)
        ogs = R32.get()
        S.act(ogs.v(), pog.v(), AF.Silu)
        osb = R32.get()
        for c in range(8):
            cs = slice(c * 64, (c + 1) * 64)
            pb = (c % 2) * 64
            rows = slice(pb, pb + 64)
            kdc = kdtm.v().rr("p (b n) -> p b n", b=4)[rows, c // 2, :]
            vch = vtm[c // 2][rows, h * 128:(h + 1) * 128]
            p1 = ps()
            S.mm(p1[rows, 0:64], ktb[:, cs], qt[:, cs])
            sct = k.sct[k.scti % 4]
            k.scti += 1
            S.tt(sct[rows, :], p1[rows, 0:64], k.maskincl[rows, :], ALU.mult)
            p2 = ps()
            S.mm(p2[:, 0:64], vch, sct[rows, :], start=True, stop=False)
            S.mm(p2[:, 0:64], k.hgSb[:, h, :], qt[:, cs], start=False, stop=True)
            S.copy(osb[:, cs], p2[:, 0:64], e="act")
            p3 = ps()
            S.mm(p3[:, 0:128], kdc, vch)
            S.stt(k.hgSb[:, h, :], k.hgS[:, h, :], eb[:, c * 64 + 63:c * 64 + 64], p3[:, 0:128], ALU.mult, ALU.add)
            S.stt(k.hgS[:, h, :], k.hgS[:, h, :], eb[:, c * 64 + 63:c * 64 + 64], p3[:, 0:128], ALU.mult, ALU.add)
        R32.put(eb)
        R16.put(qt, ktb, kdtm)
        sq = R16.get()
        S.act(sq.v(), osb.v(), AF.Square)
        pn = ps()
        S.mm(pn.v(), k.onesb.v(), sq.v())
        R16.put(sq)
        rstd = R32.get()
        S.act(rstd.v(), pn.v(), AF.Sqrt, bias=k.epsb[:, 0:1], scale=1.0 / 128)
        S.recip(rstd.v(), rstd.v())
        S.stt(osb.v(), osb.v(), k.vec[:, VI["hg_onorm"], h:h + 1], rstd.v(), ALU.mult, ALU.mult)
        S.tt(k.ocur[:, h, :], osb.v(), ogs.v(), ALU.mult, e="pool")
        R32.put(rstd, ogs, osb)
    if "hg" in k.dbg_out:
        t0 = j * NTOK
        for h in range(8):
            tmp = R32.get()
            S.copy(tmp.v(), k.ocur[:, h, :], e="pool")
            S.dma(k.dbg_out["hg"][h * 128:(h + 1) * 128, t0:t0 + NTOK], tmp.v())
            R32.put(tmp)


C0 = 0.6065306597126334
GN_EPS = 64e-5


def rw_setup(k):
    S = k.S
    nc = k.nc
    L, T = k.L, k.T

    def ext(name, shape):
        return Buf(nc.dram_tensor(name, list(shape), F32, kind="ExternalInput"), name)
    k.rw_w2 = ext("rw_w2", [L, 64, D])
    k.rw_a2 = ext("rw_a2", [L, 64, D])
    k.rw_g2 = ext("rw_g2", [L, 160, D])
    k.rw_v1 = ext("rw_v1", [L, D, 32])
    k.rw_v2 = ext("rw_v2", [L, 32, D])
    k.rwmu_d = ext("rwmu", [L, 128, 27])
    k.vfirst = S.dram("vfirst", [D, T])
    k.vdram = S.dram("vdram", [D, NTOK])
    k.w2b = S.sb("wa2b", [128, D], BF16)
    k.a2b = k.w2b
    k.g2b = S.sb("g2b", [128, 2, D], BF16)
    k.v1b = S.sb("v1b", [128, 8, 32], BF16)
    k.rwmu = S.sb("rwmu_s", [128, 27])
    k.omka = S.sb("omka", [128, 8])
    k.rwP = S.sb("rwP", [128, 8, 64])
    k.rwPb = S.sb("rwPb", [128, 8, 64], BF16)
    k.rwcarry = S.sb("rwcarry", [128, 27])
    k.zx = [S.sb("zx%d" % i, [128, NTOK + 8]) for i in range(2)]
    k.zxi = 0
    k.blockb = S.sb("blockb", [128, 128], BF16)
    k.blockf = S.sb("blockf", [128, 128])
    for t in (k.blockb, k.blockf):
        S.memset(t.v(), 0.0)
        S.memset(t[0:64, 0:64], 1.0)
        S.memset(t[64:128, 64:128], 1.0)
    k.mask4 = S.sb("mask4", [128, 128])
    S.copy(k.mask4[:, 0:64], k.maskstr.v(), e="pool")
    S.copy(k.mask4[:, 64:128], k.maskincl.v(), e="pool")
    k.msb = [S.sb("msb%d" % i, [128, 128], BF16) for i in range(32)]
    k.iwnr = [S.sb("iwnr%d" % i, [128, 256], BF16) for i in range(16)]
    k.iwl = [S.sb("iwl%d" % i, [128, 128], BF16) for i in range(16)]
    for t in k.iwnr + k.iwl:
        S.memset(t.v(), 0.0)
    k.ttb = [S.sb("ttb%d" % i, [128, 128], BF16) for i in range(16)]
    k.zsb = [S.sb("zsb%d" % i, [128, 128], BF16) for i in range(4)]
    k.gC = [S.sb("gC%d" % i, [128, 8]) for i in range(4)]
    k.tiny = S.sb("rwtiny", [128, 1])
    S.memset(k.tiny.v(), 1e-24)
    k.gneps = S.sb("gneps", [128, 1])
    S.memset(k.gneps.v(), GN_EPS)


def rw_layer_init(k, l):
    S = k.S
    st = k.cst32

    def ld(dst_view, src_view, rows, n, i, rbase=0):
        for h0 in range(0, n, 512):
            S.dma(st[i][rbase:rbase + rows, 0:512], src_view[:, h0:h0 + 512], q="pool")
            S.copy(dst_view[:, h0:h0 + 512], st[i][rbase:rbase + rows, 0:512], e="pool")
    ld(k.w2b[0:64, :], k.rw_w2[l], 64, D, 0)
    ld(k.a2b[64:128, :], k.rw_a2[l], 64, D, 1, rbase=64)
    ld(k.g2b[:, 0, :], k.rw_g2[l, 0:128, :], 128, D, 0)
    ld(k.g2b[0:32, 1, :], k.rw_g2[l, 128:160, :], 32, D, 1)
    ld(k.g2b[32:64, 1, :], k.rw_v2[l], 32, D, 0, rbase=32)
    S.dma(st[1][:, 0:256].rr("p (c n) -> p c n", c=8), k.rw_v1[l].rr("(c p) n -> p c n", p=128))
    S.copy(k.v1b.v(), st[1][:, 0:256].rr("p (c n) -> p c n", c=8), e="pool")
    S.dma(k.rwmu.v(), k.rwmu_d[l])
    S.memset(k.rwP.v(), 0.0)
    S.memset(k.rwPb.v(), 0.0)
    S.memset(k.rwcarry.v(), 0.0)


def rw_layer_vecs(k, l):
    k.S.ts(k.omka.v(), k.vec[:, VI["rw_k_a"], :], -1.0, ALU.mult, 1.0, ALU.add)


def rwkv_tile(k, l, j):
    S = k.S
    R32, R16, BIG, xn, ps = k.R32, k.R16, k.BIG, k.xn, k.ps
    wib = k.w_in_b
    t0 = j * NTOK
    V = lambda name, c: k.vec[:, VI[name], c:c + 1]

    def lerp(pview, ti, n=128):
        zx = k.zx[k.zxi % 2]
        k.zxi += 1
        S.copy(zx[0:n, 1:NTOK + 1], pview, e="act")
        S.copy(zx[0:n, 0:1], k.rwcarry[0:n, ti:ti + 1], e="act")
        S.copy(k.rwcarry[0:n, ti:ti + 1], zx[0:n, NTOK:NTOK + 1], e="act")
        d = R32.get()
        S.tt(d[0:n, :], zx[0:n, 0:NTOK], zx[0:n, 1:NTOK + 1], ALU.subtract)
        S.stt(d[0:n, :], d[0:n, :], k.rwmu[0:n, ti:ti + 1], zx[0:n, 1:NTOK + 1], ALU.mult, ALU.add)
        return d

    def projw(w, col0, n=128):
        p = ps()
        for c in range(8):
            S.mm(p[0:n, :], w[:, c, col0:col0 + n], xn[:, c, :], start=(c == 0), stop=(c == 7))
        return p

    w = k.load_w(wib[l, :, OFF_CL:OFF_CL + 288], 8, 288)
    z = lerp(projw(w, 0).v(), 24)
    lora1 = R16.get()
    S.act(lora1[0:64, :], z[0:64, :], AF.Tanh)
    S.copy(lora1[64:128, :], z[64:128, :], e="pool")
    R32.put(z)
    z = lerp(projw(w, 128).v(), 25)
    gsb0 = R16.get()
    S.act(gsb0.v(), z.v(), AF.Sigmoid)
    R32.put(z)
    z = lerp(projw(w, 256, 32)[0:32, :], 26, 32)
    gsb1 = R16.get()
    S.act(gsb1[0:32, :], z[0:32, :], AF.Sigmoid)
    R32.put(z)

    if k.stage == "a":
        R16.put(lora1, gsb0, gsb1)
        return
    k.mark('rw_a')
    vls = []
    for half in range(2):
        w = k.load_w(wib[l, :, OFF_CV + half * 512:OFF_CV + (half + 1) * 512], 8, 512)
        for s in range(4):
            vt = half * 4 + s
            vl = lerp(projw(w, s * 128).v(), vt)
            if l == 0:
                S.dma(k.vfirst[vt * 128:(vt + 1) * 128, t0:t0 + NTOK], vl.v())
                S.dma(k.vdram[vt * 128:(vt + 1) * 128, :], vl.v())
                R32.put(vl)
            else:
                vb = R16.get()
                S.copy(vb.v(), vl.v(), e="pool")
                S.mm(k.psheld[32:64, :], k.v1b[:, vt, :], vb.v(), start=(vt == 0), stop=(vt == 7))
                R16.put(vb)
                vls.append(vl)
    if l > 0:
        vv1 = R16.get()
        S.copy(vv1[32:64, :], k.psheld[32:64, :], e="act")
        for vt in range(8):
            vl = vls[vt]
            p2 = ps()
            S.mm(p2.v(), k.g2b[32:64, 1, vt * 128:(vt + 1) * 128], vv1[32:64, :])
            sv = R32.get()
            S.act(sv.v(), p2.v(), AF.Sigmoid, bias=V("rw_v0", vt))
            vf = R32.get()
            S.dma(vf.v(), k.vfirst[vt * 128:(vt + 1) * 128, t0:t0 + NTOK])
            S.tt(vf.v(), vf.v(), vl.v(), ALU.subtract)
            S.tt(vf.v(), vf.v(), sv.v(), ALU.mult, e="pool")
            S.tt(vl.v(), vl.v(), vf.v(), ALU.add)
            S.dma(k.vdram[vt * 128:(vt + 1) * 128, :], vl.v())
            R32.put(sv, vf, vl)
        R16.put(vv1)

    if k.stage == "b":
        R16.put(lora1, gsb0, gsb1)
        return
    k.mark('rw_b')
    for grp in range(2):
        prbs = []
        yTs = []
        for gi in range(4):
            p = grp * 4 + gi
            AR = BIG[gi].v().bitcast(BF16).rr("p (c s t) -> p c s t", c=8, s=2)
            BK = BIG[4 + gi].v().bitcast(BF16).rr("p (c s t) -> p c s t", c=8, s=2)
            BKtm = BIG[8 + gi].v().bitcast(BF16).rr("p (c n) -> p c n", c=8)
            VU = BIG[12 + gi].v().bitcast(BF16).rr("p (c h v) -> p c h v", c=8, h=2)
            yTs.append(BIG[16 + gi])
            if gi % 2 == 0:
                wp = k.load_w(wib[l, :, OFF_CP + (p // 2) * 512:OFF_CP + (p // 2 + 1) * 512], 8, 512)
            cb = (p % 2) * 256
            k.conv_tick()
            rl = lerp(projw(wp, cb).v(), 8 + 2 * p)
            kl = lerp(projw(wp, cb + 128).v(), 9 + 2 * p)
            pw = ps()
            S.mm(pw.v(), k.w2b[0:64, p * 128:(p + 1) * 128], lora1[0:64, :])
            sg = R32.get()
            S.act(sg.v(), pw.v(), AF.Sigmoid, bias=V("rw_w0", p))
            pa = ps()
            S.mm(pa.v(), k.a2b[64:128, p * 128:(p + 1) * 128], lora1[64:128, :])
            a = R32.get()
            S.act(a.v(), pa.v(), AF.Sigmoid, bias=V("rw_a0", p))
            kks = R16.get()
            S.act(kks.v(), kl.v(), AF.Square, scale=V("rw_k_k", p))
            pk = ps()
            S.mm(pk.v(), k.blockb.v(), kks.v())
            R16.put(kks)
            rn = R32.get()
            S.act(rn.v(), pk.v(), AF.Sqrt, bias=k.tiny[:, 0:1])
            S.recip(rn.v(), rn.v())
            kkn = R32.get()
            S.stt(kkn.v(), kl.v(), V("rw_k_k", p), rn.v(), ALU.mult, ALU.mult)
            R32.put(rn)
            kmod = R32.get()
            S.ts(kmod.v(), a.v(), V("rw_k_a", p), ALU.mult, k.omka[:, p:p + 1], ALU.add, e="pool")
            S.tt(kmod.v(), kmod.v(), kl.v(), ALU.mult, e="pool")
            R32.put(kl)
            prb = R16.get()
            S.stt(prb.v(), rl.v(), V("rw_r_k", p), kmod.v(), ALU.mult, ALU.mult)
            prbs.append(prb)
            cs = R32.get()
            S.scan(cs.v(), k.resetmask.v(), sg.v(), 0.0)
            csm = R32.get()
            S.tt(csm.v(), cs.v(), sg.v(), ALU.subtract, e="pool")
            R32.put(sg)
            G = R32.get()
            S.act(G.v(), cs.v(), AF.Exp, scale=-C0)
            G1 = csm
            S.act(G1.v(), csm.v(), AF.Exp, scale=-C0)
            Gi = cs
            S.act(Gi.v(), cs.v(), AF.Exp, scale=C0)
            S.copy(k.gC[gi].v(), G.v().rr("p (c t) -> p c t", t=64)[:, :, 63], e="pool")
            c3 = lambda t: t.v().rr("p (c t) -> p c t", t=64)
            S.stt(AR[:, :, 0, :], c3(kkn), -1.0, c3(G1), ALU.mult, ALU.mult)
            S.tt(AR[:, :, 1, :], c3(rl), c3(G), ALU.mult, e="pool")
            R32.put(rl, G, G1)
            bt = R32.get()
            S.tt(bt.v(), kkn.v(), a.v(), ALU.mult, e="pool")
            S.tt(bt.v(), bt.v(), Gi.v(), ALU.mult)
            kt = kmod
            S.tt(kt.v(), kmod.v(), Gi.v(), ALU.mult, e="pool")
            R32.put(kkn, a, Gi)
            BK2 = BK.rr("p (cb par) s t -> p cb par s t", par=2)
            bt4 = bt.v().rr("p (cb par t) -> p cb par t", par=2, t=64)
            kt4 = kt.v().rr("p (cb par t) -> p cb par t", par=2, t=64)
            S.copy(BK2[:, :, 0, 0, :], kt4[:, :, 0, :], e="act")
            S.copy(BK2[:, :, 0, 1, :], bt4[:, :, 0, :], e="dve")
            S.copy(BK2[:, :, 1, 0, :], bt4[:, :, 1, :], e="act")
            S.copy(BK2[:, :, 1, 1, :], kt4[:, :, 1, :], e="dve")
            R32.put(bt, kt)
            for half in range(2):
                pt = ps()
                ptb = pt.v().bitcast(BF16)
                for cc in range(4):
                    c = half * 4 + cc
                    S.tr(ptb[:, cc * 128:(cc + 1) * 128], BK[:, c, :, :].rr("p s t -> p (s t)"), k.identb.v())
                S.copy(BKtm[:, half * 4:(half + 1) * 4, :].rr("p c n -> p (c n)"), ptb[:, 0:512], e=("act" if half else "dve"))
            vl = R32.get()
            S.dma(vl.v(), k.vdram[p * 128:(p + 1) * 128, :])
            vb = R16.get()
            S.copy(vb.v(), vl.v(), e="pool")
            R32.put(vl)
            pt = ps()
            ptb = pt.v().bitcast(BF16)
            for blk in range(4):
                S.tr(ptb[:, blk * 128:(blk + 1) * 128], vb[:, blk * 128:(blk + 1) * 128], k.identb.v())
            R16.put(vb)
            VU2 = VU.rr("p (cb par) h v -> p cb par (h v)", par=2)
            pt4 = ptb[:, 0:512].rr("p (cb n) -> p cb n", cb=4)
            S.copy(VU2[0:64, :, 0, :], pt4[0:64, :, :], e="act")
            S.copy(VU2[64:128, :, 1, :], pt4[64:128, :, :], e="dve")
            k.rw_pair = getattr(k, "rw_pair", {})
            k.rw_pair[gi] = (AR, BK, BKtm, VU)

        k.mark('rw_pre%d' % grp)
        def scores(b):
            msb = {}
            mi = (b % 2) * 16
            for gi in range(4):
                AR, BK, BKtm, VU = k.rw_pair[gi]
                for hh in range(2):
                    hr = slice(hh * 64, hh * 64 + 64)
                    for par in range(2):
                        c = 2 * b + par
                        pS = k.ps_pool("b")
                        S.mm(pS[:, 0:128], BK[hr, c, :, :].rr("p s t -> p (s t)"), AR[hr, c, :, :].rr("p s t -> p (s t)"))
                        m = k.msb[mi]
                        mi += 1
                        S.tt(m.v(), pS[:, 0:128], k.mask4.v(), ALU.mult)
                        msb[(gi, hh, par)] = m
            return msb

        def inverse_gen(b, msb):
            units = [(gi, hh) for gi in range(4) for hh in range(2)]
            ws = {}
            for ui, (gi, hh) in enumerate(units):
                NRa, NRb = k.iwnr[ui * 2:ui * 2 + 2]
                La, Lb = k.iwl[ui * 2:ui * 2 + 2]
                S.copy(NRa[0:64, 0:64], msb[(gi, hh, 1)][0:64, 0:64], e="pool")
                S.copy(NRa[64:128, 64:128], msb[(gi, hh, 0)][64:128, 0:64], e="pool")
                ws[ui] = [NRa, NRb, La, Lb]
            yield
            for ui in range(8):
                NRa, NRb, La, Lb = ws[ui]
                pt = k.ps_pool("b")
                ptb = pt.v().bitcast(BF16)
                S.tr(ptb[:, 0:128], NRa[:, 0:128], k.identb.v())
                S.copy(La.v(), ptb[:, 0:128], e="act")
                S.tt(NRa[:, 128:256], NRa[:, 0:128], k.identb.v(), ALU.add, e="pool")
            yield
            for lev in range(1, 6):
                for half in range(2):
                    for ui in range(half * 4, half * 4 + 4):
                        NRa, NRb, La, Lb = ws[ui]
                        pc = k.ps_pool("b")
                        if lev > 1:
                            S.mm(pc[:, 0:256], La.v(), NRa[:, 0:256])
                        else:
                            S.mm(pc[:, 0:128], La.v(), NRa[:, 0:128])
                        pl = k.ps_pool("b")
                        S.mm(pl[:, 0:128], NRa[:, 0:128], La.v())
                        S.copy(Lb.v(), pl[:, 0:128], e="act")
                        S.copy(NRb[:, 0:128], pc[:, 0:128], e="dve")
                        if lev > 1:
                            S.tt(NRb[:, 128:256], pc[:, 128:256], NRa[:, 128:256], ALU.add)
                        else:
                            S.copy(NRb[:, 128:256], NRa[:, 128:256], e="pool")
                        ws[ui] = [NRb, NRa, Lb, La]
                    yield
            for half in range(2):
                for ui in range(half * 4, half * 4 + 4):
                    NRa, NRb, La, Lb = ws[ui]
                    pr = k.ps_pool("b")
                    S.mm(pr[:, 0:128], La.v(), NRa[:, 128:256])
                    S.tt(k.ttb[(b % 2) * 8 + ui].v(), pr[:, 0:128], NRa[:, 128:256], ALU.add)
                yield

        def steps_gen(b, msb):
            for par in range(2):
                c = 2 * b + par
                Vr = slice(0, 64) if par == 0 else slice(64, 128)
                Ur = slice(64, 128) if par == 0 else slice(0, 64)
                pZs, pUs, pYs, pPs = {}, {}, {}, {}
                for gi in range(4):
                    p = grp * 4 + gi
                    AR, BK, BKtm, VU = k.rw_pair[gi]
                    pZ = k.ps_pool("a")
                    h_same = 0 if par == 0 else 1
                    h_oth = 1 - h_same
                    hs = slice(h_same * 64, h_same * 64 + 64)
                    ho = slice(h_oth * 64, h_oth * 64 + 64)
                    S.mm(pZ[Ur, h_same * 64:(h_same + 1) * 64], AR[hs, c, 0, :], k.rwPb[hs, p, :], start=True, stop=False)
                    S.mm(pZ[Ur, h_same * 64:(h_same + 1) * 64], msb[(gi, h_same, par)][Vr, 0:64], VU[Vr, c, h_same, :], start=False, stop=True)
                    S.mm(pZ[Ur, h_oth * 64:(h_oth + 1) * 64], msb[(gi, h_oth, par)][Vr, 0:64], VU[Vr, c, h_oth, :], start=True, stop=False)
                    S.mm(pZ[Ur, h_oth * 64:(h_oth + 1) * 64], AR[ho, c, 0, :], k.rwPb[ho, p, :], start=False, stop=True)
                    pZs[gi] = pZ
                for gi in range(4):
                    S.copy(k.zsb[gi][Ur, :], pZs[gi][Ur, 0:128], e="act")
                yield
                for gi in range(4):
                    pU = k.ps_pool("a")
                    for hh in range(2):
                        S.mm(pU[Ur, hh * 64:(hh + 1) * 64], k.ttb[(b % 2) * 8 + gi * 2 + hh][Ur, Ur], k.zsb[gi][Ur, hh * 64:(hh + 1) * 64])
                    pUs[gi] = pU
                for gi in range(4):
                    AR, BK, BKtm, VU = k.rw_pair[gi]
                    S.copy(VU[Ur, c, :, :].rr("p h v -> p (h v)"), pUs[gi][Ur, 0:128], e="dve")
                yield
                for gi in range(4):
                    p = grp * 4 + gi
                    AR, BK, BKtm, VU = k.rw_pair[gi]
                    pY = k.ps_pool("a")
                    for hh in range(2):
                        hr = slice(hh * 64, hh * 64 + 64)
                        S.mm(pY[hr, 0:64], VU[:, c, hh, :], msb[(gi, hh, par)][:, 64:128], start=True, stop=False)
                    for hh in range(2):
                        hr = slice(hh * 64, hh * 64 + 64)
                        S.mm(pY[hr, 0:64], k.rwPb[hr, p, :], AR[hr, c, 1, :], start=False, stop=True)
                    pYs[gi] = pY
                for gi in range(4):
                    S.copy(yTs[gi][:, c * 64:(c + 1) * 64], pYs[gi][:, 0:64], e="act")
                yield
                for gi in range(4):
                    p = grp * 4 + gi
                    AR, BK, BKtm, VU = k.rw_pair[gi]
                    pP = k.ps_pool("a")
                    for hh in range(2):
                        hr = slice(hh * 64, hh * 64 + 64)
                        S.mm(pP[hr, 0:64], BKtm[:, c, hr], VU[:, c, hh, :])
                    pPs[gi] = pP
                    S.ts(k.rwP[:, p, :], k.rwP[:, p, :], k.gC[gi][:, c:c + 1], ALU.mult, e="pool")
                for gi in range(4):
                    p = grp * 4 + gi
                    S.stt(k.rwPb[:, p, :], pPs[gi][:, 0:64], k.gC[gi][:, c:c + 1], k.rwP[:, p, :], ALU.mult, ALU.add)
                    S.stt(k.rwP[:, p, :], pPs[gi][:, 0:64], k.gC[gi][:, c:c + 1], k.rwP[:, p, :], ALU.mult, ALU.add)
                yield

        import os as _os
        if _os.environ.get("RWIL", "1") == "1":
            msbs = {0: scores(0)}
            for _ in inverse_gen(0, msbs[0]):
                pass
            for b in range(4):
                g1 = steps_gen(b, msbs[b])
                k.conv_tick()
                g2 = None
                if b < 3:
                    msbs[b + 1] = scores(b + 1)
                    g2 = inverse_gen(b + 1, msbs[b + 1])
                d1 = d2 = False
                while not (d1 and (d2 or g2 is None)):
                    if not d1:
                        try:
                            next(g1)
                        except StopIteration:
                            d1 = True
                    if g2 is not None and not d2:
                        try:
                            next(g2)
                            next(g2)
                        except StopIteration:
                            d2 = True

        else:
            for b in range(4):
                msb_ = scores(b)
                for _ in inverse_gen(b, msb_):
                    pass
                for _ in steps_gen(b, msb_):
                    pass

        k.mark('rw_chunks%d' % grp)
        for gi in range(4):
            p = grp * 4 + gi
            y = yTs[gi]
            pm = ps()
            S.mm(pm.v(), k.blockf.v(), y.v())
            ysq = R32.get()
            S.act(ysq.v(), y.v(), AF.Square)
            pq = ps()
            S.mm(pq.v(), k.blockf.v(), ysq.v())
            m = R32.get()
            S.act(m.v(), pm.v(), AF.Copy, scale=1.0 / 64)
            S.act(ysq.v(), m.v(), AF.Square)
            var = R32.get()
            S.stt(var.v(), pq.v(), 1.0 / 64, ysq.v(), ALU.mult, ALU.subtract)
            S.act(var.v(), var.v(), AF.Sqrt, bias=k.gneps[:, 0:1])
            S.recip(var.v(), var.v())
            S.tt(y.v(), y.v(), m.v(), ALU.subtract, e="pool")
            S.tt(y.v(), y.v(), var.v(), ALU.mult)
            S.ts(y.v(), y.v(), V("rw_ln_w", p), ALU.mult, V("rw_ln_b", p), ALU.add, e="pool")
            R32.put(ysq, m, var)
            pbn = ps()
            S.mm(pbn.v(), k.blockb.v(), prbs[gi].v())
            R16.put(prbs[gi])
            vl = R32.get()
            S.dma(vl.v(), k.vdram[p * 128:(p + 1) * 128, :])
            S.tt(vl.v(), vl.v(), pbn.v(), ALU.mult)
            S.tt(y.v(), y.v(), vl.v(), ALU.add, e="pool")
            R32.put(vl)
            pg = ps()
            S.mm(pg.v(), k.g2b[:, 0, p * 128:(p + 1) * 128], gsb0.v(), start=True, stop=False)
            S.mm(pg.v(), k.g2b[0:32, 1, p * 128:(p + 1) * 128], gsb1[0:32, :], start=False, stop=True)
            S.tt(k.ocur[:, p, :], y.v(), pg.v(), ALU.mult)
    k.mark('rw_out')
    R16.put(lora1, gsb0, gsb1)
    if "rw" in k.dbg_out:
        for h in range(8):
            tmp = R32.get()
            S.copy(tmp.v(), k.ocur[:, h, :], e="pool")
            S.dma(k.dbg_out["rw"][h * 128:(h + 1) * 128, t0:t0 + NTOK], tmp.v())
            R32.put(tmp)

import math

TWO_PI = 2.0 * math.pi


def s5_setup(k):
    S = k.S
    nc = k.nc
    L, T = k.L, k.T

    def ext(name, shape):
        return Buf(nc.dram_tensor(name, list(shape), F32, kind="ExternalInput"), name)
    k.s5lam = ext("s5lam", [L, 128, 3, 32])
    k.s5b = ext("s5b", [L, 2, 128, 32 * 16])
    k.s5c = ext("s5c", [L, 2, 128, 32 * 16])
    k.s5cpad = ext("s5cpad", [L, 2, 8, 128, 4 * 128])
    k.s5glu = ext("s5glu", [L, 8, 128, 128])
    k.s5P = S.dram("s5P", [8, 128, 8 * 2 * 128], BF16)
    k.s5Q = S.dram("s5Q", [8, 128, 8 * 2 * 4 * 32], BF16)
    k.s5BD = S.dram("s5BD", [8, 128, 8 * 128], BF16)
    k.s5D = S.dram("s5D", [8, 128, 4 * 2 * 64], F32)
    k.s5P3 = S.dram("s5P3", [8, 128, 8 * 2 * 128], BF16)
    k.s5Q3 = S.dram("s5Q3", [8, 128, 8 * 2 * 128], BF16)
    k.s5pad3 = [S.sb("s5pad3_%d" % i, [128, 128]) for i in range(2)]
    for t in k.s5pad3:
        S.memset(t.v(), 0.0)
    k.s5small = S.sb("s5small", [128, 24, 32])
    k.s5pw = S.sb("s5pw", [128, 2, 9, 32])
    k.s5glub = S.sb("s5glub", [128, 8, 128], BF16)
    k.s5car = S.sb("s5car", [128, 2, 32])
    k.s5rho = S.sb("s5rho", [128, 32])
    k.s5pad = [S.sb("s5pad%d" % i, [128, 4 * 32]) for i in range(3)]
    for t in k.s5pad:
        S.memset(t.v(), 0.0)
    k.s5t = [S.sb("s5t%d" % i, [128, 72]) for i in range(12)]
    k.s5ti = 0
    k.s5x = [[S.sb("s5x%d_%d" % (i, c), [128, 64], BF16) for c in range(2)] for i in range(4)]


def s5_layer_init(k, l):
    S = k.S
    R32, BIG, ps = k.R32, k.BIG, k.ps
    sm = k.s5small

    def s(i):
        return sm[:, i, :]
    LR, LI, LS, DT, MAG, TH, R_, RF, M1, COS, SIN, ABR, ABI, DEN, NR, T1, T2, CRE, CIM, RH, RI = range(21)
    lamt = R32.get()
    lv = lamt.v()[:, 0:96].rr("p (a q) -> p a q", a=3)
    S.dma(lv, k.s5lam[l])
    S.copy(s(LR), lv[:, 0, :], e="pool")
    S.copy(s(LI), lv[:, 1, :], e="pool")
    S.act(s(DT), lv[:, 2, :], AF.Exp)
    R32.put(lamt)
    S.tt(s(T1), s(LR), s(DT), ALU.mult)
    S.act(s(MAG), s(T1), AF.Exp)
    S.act(s(RH), s(T1), AF.Exp, scale=8.0)
    S.copy(k.s5rho.v(), s(RH), e="pool")
    S.tt(s(TH), s(LI), s(DT), ALU.mult)

    def sincos(dst, shift):
        S.ts(s(R_), s(TH), 1.0 / TWO_PI, ALU.mult, shift, ALU.add)
        ri = sm[:, 23, :].bitcast(I32)
        S.copy(ri, s(R_), e="dve")
        S.copy(s(RF), ri, e="dve")
        S.tt(s(R_), s(R_), s(RF), ALU.subtract)
        S.ts(s(M1), s(R_), 0.5, ALU.is_gt)
        S.tt(s(R_), s(R_), s(M1), ALU.subtract)
        S.ts(s(M1), s(R_), -0.5, ALU.is_lt)
        S.tt(s(R_), s(R_), s(M1), ALU.add)
        S.act(dst, s(R_), AF.Sin, scale=6.28318)
    sincos(s(SIN), 0.0)
    sincos(s(COS), 0.25)
    S.tt(s(ABR), s(MAG), s(COS), ALU.mult)
    S.tt(s(ABI), s(MAG), s(SIN), ALU.mult)
    S.tt(s(DEN), s(LR), s(LR), ALU.mult)
    S.tt(s(T1), s(LI), s(LI), ALU.mult)
    S.tt(s(DEN), s(DEN), s(T1), ALU.add)
    S.recip(s(DEN), s(DEN))
    S.ts(s(NR), s(ABR), -1.0, ALU.add)
    S.tt(s(T1), s(NR), s(LR), ALU.mult)
    S.tt(s(T2), s(ABI), s(LI), ALU.mult)
    S.tt(s(T1), s(T1), s(T2), ALU.add)
    S.tt(s(CRE), s(T1), s(DEN), ALU.mult)
    S.tt(s(T1), s(ABI), s(LR), ALU.mult)
    S.tt(s(T2), s(NR), s(LI), ALU.mult)
    S.tt(s(T1), s(T1), s(T2), ALU.subtract)
    S.tt(s(CIM), s(T1), s(DEN), ALU.mult)
    pw = k.s5pw
    S.memset(pw[:, 0, 0, :], 1.0)
    S.memset(pw[:, 1, 0, :], 0.0)
    for d in range(8):
        S.tt(s(T1), pw[:, 0, d, :], s(ABR), ALU.mult)
        S.tt(s(T2), pw[:, 1, d, :], s(ABI), ALU.mult)
        S.tt(pw[:, 0, d + 1, :], s(T1), s(T2), ALU.subtract)
        S.tt(s(T1), pw[:, 0, d, :], s(ABI), ALU.mult)
        S.tt(s(T2), pw[:, 1, d, :], s(ABR), ALU.mult)
        S.tt(pw[:, 1, d + 1, :], s(T1), s(T2), ALU.add)
    S.recip(s(RI), s(RH))
    D1R, D1I = 21, 22
    S.tt(s(D1R), pw[:, 0, 8, :], s(RI), ALU.mult)
    S.tt(s(D1I), pw[:, 1, 8, :], s(RI), ALU.mult)
    S.ts(s(D1I), s(D1I), -1.0, ALU.mult)
    Dre = [BIG[i].v().rr("p (q n) -> p q n", n=64) for i in range(4)]
    Dim = [BIG[4 + i].v().rr("p (q n) -> p q n", n=64) for i in range(4)]
    for i in range(4):
        qs = slice(i * 8, (i + 1) * 8)
        tr1 = R32.get()
        tr2 = R32.get()
        S.copy(Dre[i][:, :, 0], s(D1R)[:, qs], e="pool")
        S.copy(Dim[i][:, :, 0], s(D1I)[:, qs], e="pool")
        m = 1
        while m < 64:
            t1 = tr1.v()[:, 0:8 * m].rr("p (q n) -> p q n", n=m)
            t2 = tr2.v()[:, 0:8 * m].rr("p (q n) -> p q n", n=m)
            br = Dre[i][:, :, m - 1:m].bc([128, 8, m])
            bi = Dim[i][:, :, m - 1:m].bc([128, 8, m])
            ar = Dre[i][:, :, 0:m]
            ai = Dim[i][:, :, 0:m]
            S.tt(t1, ar, br, ALU.mult)
            S.tt(t2, ai, bi, ALU.mult, e="pool")
            S.tt(Dre[i][:, :, m:2 * m], t1, t2, ALU.subtract)
            S.tt(t1, ar, bi, ALU.mult)
            S.tt(t2, ai, br, ALU.mult, e="pool")
            S.tt(Dim[i][:, :, m:2 * m], t1, t2, ALU.add)
            m *= 2
        R32.put(tr1, tr2)
    for gt in range(8):
        i, o = gt // 2, (gt % 2) * 4
        dv = k.s5D[gt].rr("p (k c n) -> p k c n", k=4, c=2)
        S.dma(dv[:, :, 0, :], Dre[i][:, o:o + 4, :])
        S.dma(dv[:, :, 1, :], Dim[i][:, o:o + 4, :])
    bre = R32.get()
    bim = R32.get()
    S.dma(bre.v(), k.s5b[l, 0])
    S.dma(bim.v(), k.s5b[l, 1])
    v3 = lambda t: t.v().rr("p (q h) -> p q h", h=16)
    bcq = lambda view: view.rr("p (q o) -> p q o", o=1).bc([128, 32, 16])
    t1 = R32.get()
    t2 = R32.get()
    abr = R32.get()
    abi = R32.get()

    def cmul_bc(ore, oim, are, aim, sre, sim):
        S.tt(v3(t1), v3(are), bcq(sre), ALU.mult)
        S.tt(v3(t2), v3(aim), bcq(sim), ALU.mult, e="pool")
        S.tt(v3(t1), v3(t1), v3(t2), ALU.subtract)
        S.tt(v3(t2), v3(are), bcq(sim), ALU.mult, e="pool")
        S.tt(v3(oim), v3(aim), bcq(sre), ALU.mult)
        S.tt(v3(oim), v3(oim), v3(t2), ALU.add)
        S.copy(v3(ore), v3(t1), e="pool")
    cmul_bc(abr, abi, bre, bim, s(CRE), s(CIM))
    R32.put(bre, bim)
    cre_t = R32.get()
    cim_t = R32.get()
    S.dma(cre_t.v(), k.s5c[l, 0])
    S.dma(cim_t.v(), k.s5c[l, 1])
    pre, pim, pimn = k.s5pad
    for d in range(8):
        tau = 7 - d
        for gt in range(8):
            for (dst, src, sc) in ((pre, abr, None), (pim, abi, None), (pimn, abi, -1.0)):
                dv = dst.v().rr("p (k g h) -> p k g h", k=4, g=2)
                sv = v3(src)[:, gt * 4:(gt + 1) * 4, :]
                if sc is None:
                    S.copy(dv[0:64, :, 0, :], sv[0:64], e="pool")
                    S.copy(dv[64:128, :, 1, :], sv[64:128], e="pool")
                else:
                    S.ts(dv[0:64, :, 0, :], sv[0:64], sc, ALU.mult)
                    S.ts(dv[64:128, :, 1, :], sv[64:128], sc, ALU.mult)
            S.copy(k.s5pad3[0][:, 96:128], pre[:, 96:128], e="pool")
            S.copy(k.s5pad3[1][:, 96:128], pim[:, 96:128], e="pool")
            pt = ps()
            S.tr(pt[:, 0:128], pre.v(), k.ident.v())
            S.tr(pt[:, 128:256], pim.v(), k.ident.v())
            S.tr(pt[:, 256:384], k.s5pad3[0].v(), k.ident.v())
            S.tr(pt[:, 384:512], k.s5pad3[1].v(), k.ident.v())
            pb = k.R16.get()
            S.copy(pb.v(), pt.v(), e="act")
            S.dma(k.s5P[gt].rr("p (t c n) -> p t c n", t=8, c=2)[:, tau, :, :], pb[:, 0:256].rr("p (c n) -> p c n", c=2))
            S.dma(k.s5P3[gt].rr("p (t c n) -> p t c n", t=8, c=2)[:, tau, :, :], pb[:, 256:512].rr("p (c n) -> p c n", c=2))
            k.R16.put(pb)
            cp = R32.get()
            cpi = R32.get()
            S.dma(cp.v(), k.s5cpad[l, 0, gt])
            S.dma(cpi.v(), k.s5cpad[l, 1, gt])
            pbd = ps()
            for kk in range(4):
                S.mm(pbd[:, 32 * kk:32 * kk + 32], cp[:, 128 * kk:128 * kk + 128], pre[:, 32 * kk:32 * kk + 32], start=True, stop=False)
                S.mm(pbd[:, 32 * kk:32 * kk + 32], cpi[:, 128 * kk:128 * kk + 128], pimn[:, 32 * kk:32 * kk + 32], start=False, stop=True)
            R32.put(cp, cpi)
            bdT = R32.get()
            if d == 0:
                S.stt(bdT[:, 0:128], k.ident.v(), k.vec[:, VI["s5_d"], gt:gt + 1], pbd[:, 0:128], ALU.mult, ALU.add)
            else:
                S.copy(bdT[:, 0:128], pbd[:, 0:128], e="act")
            pt2 = ps()
            S.tr(pt2[:, 0:128], bdT[:, 0:128], k.ident.v())
            R32.put(bdT)
            bdb = k.R16.get()
            S.copy(bdb[:, 0:128], pt2[:, 0:128], e="act")
            S.dma(k.s5BD[gt].rr("p (d n) -> p d n", d=8)[:, d, :], bdb[:, 0:128])
            k.R16.put(bdb)
        if d < 7:
            cmul_bc(abr, abi, abr, abi, s(ABR), s(ABI))
    R32.put(abr, abi)
    qre = R32.get()
    qim = R32.get()
    s5qpad = [BIG[8 + i].v().bitcast(BF16).rr("p (q n) -> p q n", q=32) for i in range(2)]
    s5q3 = [BIG[10 + i].v().bitcast(BF16).rr("p (g n) -> p g n", g=8) for i in range(2)]
    for i in range(4):
        S.memset(BIG[8 + i].v(), 0.0)
    for tp in range(8):
        cmul_bc(qre, qim, cre_t, cim_t, pw[:, 0, tp + 1, :], pw[:, 1, tp + 1, :])
        for ci, (src, sc) in enumerate(((qre, 1.0), (qim, -1.0))):
            qp = s5qpad[ci]
            S.ts(qp[0:64, :, 0:16], v3(src)[0:64], sc, ALU.mult)
            S.ts(qp[64:128, :, 16:32], v3(src)[64:128], sc, ALU.mult)
            q3 = s5q3[ci]
            S.copy(q3[:, :, 96:128], qp.rr("p (g k) n -> p g k n", k=4)[:, :, 3, :], e="pool")
            for gt in range(8):
                dv = k.s5Q[gt].rr("p (t c k n) -> p t c k n", t=8, c=2, k=4)
                S.dma(dv[:, tp, ci, :, :], qp[:, gt * 4:(gt + 1) * 4, :])
                dv3 = k.s5Q3[gt].rr("p (t c n) -> p t c n", t=8, c=2)
                S.dma(dv3[:, tp, ci, :], q3[:, gt, :])
    R32.put(qre, qim, cre_t, cim_t, t1, t2)
    for gt in range(8):
        g = R32.get()
        S.dma(g[:, 0:128], k.s5glu[l, gt])
        S.copy(k.s5glub[:, gt, :], g[:, 0:128], e="pool")
        R32.put(g)
    S.memset(k.s5car.v(), 0.0)


def s5_tile(k, l, j):
    S = k.S
    R32, R16, BIG, xn, ps = k.R32, k.R16, k.BIG, k.xn, k.ps
    wib = k.w_in_b
    t0 = j * NTOK

    def tmp():
        t = k.s5t[k.s5ti % 12]
        k.s5ti += 1
        return t
    for gt in range(8):
        base = (gt % 2) * 10
        BDs = BIG[base + 0].v().bitcast(BF16).rr("p (d n) -> p d n", d=8)
        Pv = [BIG[base + 1 + i].v().bitcast(BF16).rr("p (t c n) -> p t c n", t=4, c=2) for i in range(2)]
        Qv = [BIG[base + 3 + i].v().bitcast(BF16).rr("p (t c k n) -> p t c k n", t=4, c=2, k=4) for i in range(2)]
        Dv = BIG[base + 5].v().rr("p (k c n) -> p k c n", k=4, c=2)
        P3v = [BIG[base + 6 + i].v().bitcast(BF16).rr("p (t c n) -> p t c n", t=4, c=2) for i in range(2)]
        Q3v = [BIG[base + 8 + i].v().bitcast(BF16).rr("p (t c n) -> p t c n", t=4, c=2) for i in range(2)]
        S.dma(BIG[base + 0].v().bitcast(BF16), k.s5BD[gt])
        for i in range(2):
            S.dma(BIG[base + 1 + i].v().bitcast(BF16), k.s5P[gt][:, i * 1024:(i + 1) * 1024])
            S.dma(BIG[base + 3 + i].v().bitcast(BF16), k.s5Q[gt][:, i * 1024:(i + 1) * 1024])
            S.dma(BIG[base + 6 + i].v().bitcast(BF16), k.s5P3[gt][:, i * 1024:(i + 1) * 1024])
            S.dma(BIG[base + 8 + i].v().bitcast(BF16), k.s5Q3[gt][:, i * 1024:(i + 1) * 1024])
        S.dma(BIG[base + 5].v(), k.s5D[gt])
        if gt % 4 == 0:
            w = k.load_w(wib[l, :, OFF_D + (gt // 4) * 512:OFF_D + (gt // 4 + 1) * 512], 8, 512)
        k.conv_tick()
        pu = ps()
        for c in range(8):
            S.mm(pu.v(), w[:, c, (gt % 4) * 128:(gt % 4 + 1) * 128], xn[:, c, :], start=(c == 0), stop=(c == 7))
        Ut = R16.get()
        Utv = Ut.v().rr("p (t n) -> p t n", t=8)
        S.copy(Utv, pu.v().rr("p (n t) -> p t n", t=8), e="act")
        for kk in range(4):
            q = gt * 4 + kk
            rows = slice(32 * kk, 32 * kk + 32)
            pwr = ps()
            pwi = ps()
            for (pw_, ci) in ((pwr, 0), (pwi, 1)):
                for tau in range(8):
                    if kk < 3:
                        S.mm(pw_[:, 0:64], Pv[tau // 4][rows, tau % 4, ci, :], Utv[rows, tau, :], start=(tau == 0), stop=(tau == 7))
                    else:
                        S.mm(pw_[:, 0:64], P3v[tau // 4][:, tau % 4, ci, :], Utv[:, tau, :], start=(tau == 0), stop=(tau == 7))
            dre = Dv[:, kk, 0, :]
            dim = Dv[:, kk, 1, :]
            a1, a2, a3, a4 = tmp(), tmp(), tmp(), tmp()
            S.tt(a1[:, 0:64], dre, pwr[:, 0:64], ALU.mult)
            S.tt(a2[:, 0:64], dim, pwi[:, 0:64], ALU.mult)
            S.tt(a1[:, 0:64], a1[:, 0:64], a2[:, 0:64], ALU.subtract, e="pool")
            S.tt(a3[:, 0:64], dre, pwi[:, 0:64], ALU.mult)
            S.tt(a4[:, 0:64], dim, pwr[:, 0:64], ALU.mult)
            S.tt(a3[:, 0:64], a3[:, 0:64], a4[:, 0:64], ALU.add, e="pool")
            rho = k.s5rho[:, q:q + 1].bc([128, 64])
            wre, wim = tmp(), tmp()
            S.scan(wre[:, 0:64], rho, a1[:, 0:64], k.s5car[:, 0, q:q + 1])
            S.scan(wim[:, 0:64], rho, a3[:, 0:64], k.s5car[:, 1, q:q + 1])
            xre, xim = tmp(), tmp()
            S.copy(xre[:, 0:1], k.s5car[:, 0, q:q + 1], e="pool")
            S.copy(xim[:, 0:1], k.s5car[:, 1, q:q + 1], e="pool")
            S.tt(a1[:, 0:64], dre, wre[:, 0:64], ALU.mult)
            S.tt(a2[:, 0:64], dim, wim[:, 0:64], ALU.mult, e="pool")
            S.tt(xre[:, 1:65], a1[:, 0:64], a2[:, 0:64], ALU.add)
            S.tt(a3[:, 0:64], dre, wim[:, 0:64], ALU.mult, e="pool")
            S.tt(a4[:, 0:64], dim, wre[:, 0:64], ALU.mult)
            S.tt(xim[:, 1:65], a3[:, 0:64], a4[:, 0:64], ALU.subtract)
            S.copy(k.s5car[:, 0, q:q + 1], xre[:, 64:65], e="pool")
            S.copy(k.s5car[:, 1, q:q + 1], xim[:, 64:65], e="pool")
            S.copy(k.s5x[kk][0].v(), xre[:, 0:64], e="act")
            S.copy(k.s5x[kk][1].v(), xim[:, 0:64], e="act")
        ysb = R32.get()
        yv = ysb.v().rr("p (n t) -> p t n", t=8)
        for tp in range(8):
            py = ps()
            nmm = (tp + 1)
            for tau in range(tp + 1):
                S.mm(py[:, 0:64], BDs[:, tp - tau, :], Utv[:, tau, :], start=(tau == 0), stop=False)
            for kk in range(3):
                S.mm(py[32 * kk:32 * kk + 32, 0:64], Qv[tp // 4][:, tp % 4, 0, kk, :], k.s5x[kk][0].v(), start=False, stop=False)
                S.mm(py[32 * kk:32 * kk + 32, 0:64], Qv[tp // 4][:, tp % 4, 1, kk, :], k.s5x[kk][1].v(), start=False, stop=False)
            S.mm(py[:, 0:64], Q3v[tp // 4][:, tp % 4, 0, :], k.s5x[3][0].v(), start=False, stop=False)
            S.mm(py[:, 0:64], Q3v[tp // 4][:, tp % 4, 1, :], k.s5x[3][1].v(), start=False, stop=True)
            S.copy(yv[:, tp, :], py[:, 0:64], e="act")
        R16.put(Ut)
        x2 = R32.get()
        S.act(x2.v(), ysb.v(), AF.Square)
        S.ts(x2.v(), x2.v(), 0.044715, ALU.mult, 1.0, ALU.add, e="pool")
        S.tt(x2.v(), x2.v(), ysb.v(), ALU.mult)
        S.act(x2.v(), x2.v(), AF.Tanh, scale=0.7978845608028654)
        S.stt(x2.v(), x2.v(), 1.0, ysb.v(), ALU.add, ALU.mult)
        zb = R16.get()
        S.act(zb.v(), x2.v(), AF.Copy, scale=0.5)
        pg = ps()
        S.mm(pg.v(), k.s5glub[:, gt, :], zb.v())
        R16.put(zb)
        sgl = ysb
        S.act(sgl.v(), pg.v(), AF.Sigmoid, bias=k.vec[:, VI["s5_glu_b"], gt:gt + 1])
        S.stt(k.ocur[:, gt, :], x2.v(), 0.5, sgl.v(), ALU.mult, ALU.mult)
        R32.put(x2, ysb)
    if "s5" in k.dbg_out:
        for h in range(8):
            tmp_ = R32.get()
            S.copy(tmp_.v(), k.ocur[:, h, :], e="pool")
            S.dma(k.dbg_out["s5"][h * 128:(h + 1) * 128, t0:t0 + NTOK], tmp_.v())
            R32.put(tmp_)


def build_full(L, T, dbg=()):
    k = build(L, T, dbg=dbg)
    k.stage = "z"
    hg_setup(k)
    rw_setup(k)
    s5_setup(k)
    S = k.S
    nt = T // NTOK
    for l in range(L):
        S.dma(k.vec.v().rr("p v c -> p (v c)"), k.vecs[l])
        hg_layer_init(k, l)
        rw_layer_init(k, l)
        rw_layer_vecs(k, l)
        s5_layer_init(k, l)
        items = k.conv_items(l + 1) if l + 1 < L else []
        per = (len(items) + nt - 1) // nt
        for j in range(nt):
            k.cvq = list(items[j * per:(j + 1) * per])
            k.rmsnorm_tile(k.hT, l, j, VI["mix_norm"])
            hgrn2_tile(k, l, j)
            k.merge_branch(l, j, 0, True)
            rwkv_tile(k, l, j)
            k.merge_branch(l, j, 1, False)
            s5_tile(k, l, j)
            k.merge_branch(l, j, 2, False)
            k.wout_tile(l, j)
            k.ffn_tile(l, j)
            k.do_conv(k.cvq)
            k.cvq = []
            if j == nt - 1:
                k.flush_conv()
    finalize(k)
    return k

from concourse.bass_utils import run_bass_kernel_spmd

L_FULL = 4
T_FULL = 4096


def _pack_vecs(inp, L):
    v = np.zeros((L, 128, NV, 8), np.float32)
    for n, i in VI.items():
        a = np.asarray(inp[n], np.float32)
        if n == "final_norm":
            a = np.broadcast_to(a[None], (4, D))
        elif n == "rw_v0":
            a = np.concatenate([np.zeros((1, D), np.float32), a], 0)
        elif n == "rw_r_k":
            a = a.reshape(a.shape[0], D)
        a = a[:L]
        v[:, :, i, :] = a.reshape(L, 8, 128).transpose(0, 2, 1)
    return v.reshape(L, 128, NV * 8)


def _rw_inmap(inp, L):
    mu_idx = list(range(2048, 3072))
    for p in range(8):
        mu_idx += list(range(p * 128, (p + 1) * 128)) + list(range(1024 + p * 128, 1024 + (p + 1) * 128))
    mu_idx += list(range(3072, 3360))
    mu = np.zeros((L, 27 * 128), np.float32)
    mu[:, :3360] = inp["rw_shift_mu"][:L][:, mu_idx]
    mu = np.ascontiguousarray(mu.reshape(L, 27, 128).transpose(0, 2, 1))
    v1 = np.concatenate([np.zeros((1, 1024, 32), np.float32), inp["rw_v1"]], 0)[:L]
    v2 = np.concatenate([np.zeros((1, 32, 1024), np.float32), inp["rw_v2"]], 0)[:L]
    return {"rw_w2": inp["rw_w2"][:L], "rw_a2": inp["rw_a2"][:L], "rw_g2": inp["rw_g2"][:L],
            "rw_v1": np.ascontiguousarray(v1), "rw_v2": np.ascontiguousarray(v2), "rwmu": mu}


def _s5_inmap(inp, L):
    def pairlay(a):
        Lh = a.shape[0]
        X = a.shape[3]
        return a.reshape(Lh, 32, 2, 64, X).transpose(0, 2, 3, 1, 4).reshape(Lh, 128, 32, X)
    lr = pairlay(inp["s5_lambda_re"][:L, :, :, None])[..., 0]
    li = pairlay(inp["s5_lambda_im"][:L, :, :, None])[..., 0]
    ls = pairlay(np.broadcast_to(inp["s5_log_step"][:L, :, None, None], (L, 64, 64, 1)))[..., 0]
    lam = np.ascontiguousarray(np.stack([lr, li, ls], 2))
    b = np.stack([pairlay(inp["s5_b_re"][:L]), pairlay(inp["s5_b_im"][:L])], 1).reshape(L, 2, 128, 512)
    cT = [inp["s5_c_re"][:L].transpose(0, 1, 3, 2), inp["s5_c_im"][:L].transpose(0, 1, 3, 2)]
    c = np.stack([pairlay(x) for x in cT], 1)
    cpad = np.zeros((L, 2, 8, 128, 4, 8, 16), np.float32)
    for gt in range(8):
        for kk in range(4):
            for g2 in range(2):
                cpad[:, :, gt, g2 * 64:(g2 + 1) * 64, kk, 2 * kk + g2, :] = c[:, :, g2 * 64:(g2 + 1) * 64, gt * 4 + kk, :]
    glu = np.zeros((L, 8, 128, 128), np.float32)
    for gt in range(8):
        for g8 in range(8):
            glu[:, gt, g8 * 16:(g8 + 1) * 16, g8 * 16:(g8 + 1) * 16] = inp["s5_glu_w"][:L, gt * 8 + g8]
    return {"s5lam": lam, "s5b": np.ascontiguousarray(b), "s5c": np.ascontiguousarray(c.reshape(L, 2, 128, 512)),
            "s5cpad": np.ascontiguousarray(cpad.reshape(L, 2, 8, 128, 512)), "s5glu": glu}


def kernel(**inputs):
    inp = {k_: np.asarray(v, dtype=np.float32) for k_, v in inputs.items()}
    L, T = L_FULL, T_FULL
    B = inp["x"].shape[0]
    perm = perm_cols()
    common = {"w_in": np.ascontiguousarray(inp["w_in"][:, :, perm]),
              "w_branch": inp["w_branch"], "w_out": inp["w_out"], "ffn_w_gate": inp["ffn_w_gate"],
              "ffn_w_up": inp["ffn_w_up"], "ffn_w_down": inp["ffn_w_down"], "vecs": _pack_vecs(inp, L)}
    common.update(_rw_inmap(inp, L))
    common.update(_s5_inmap(inp, L))
    k = build_full(L, T)
    in_maps = []
    for c in range(8):
        m = dict(common)
        m["xT"] = np.ascontiguousarray(inp["x"][c % B].T)
        in_maps.append(m)
    res = run_bass_kernel_spmd(k.nc, in_maps, core_ids=list(range(8)))
    out = np.stack([np.ascontiguousarray(res.results[b]["out"].T) for b in range(B)], 0)
    return out.astype(np.float32)
```

```python
import numpy as np
import concourse.bass as bass
import concourse.mybir as mybir

F32 = mybir.dt.float32
BF16 = mybir.dt.bfloat16
I32 = mybir.dt.int32
AF = mybir.ActivationFunctionType
ALU = mybir.AluOpType
AX = mybir.AxisListType


class Buf:
    __slots__ = ("t", "lw", "rd", "name", "pe_rg")

    def __init__(self, t, name=""):
        self.t = t
        self.lw = None
        self.rd = []
        self.name = name
        self.pe_rg = None

    def v(self):
        return View((self,), self.t.ap())

    def __getitem__(self, idx):
        return View((self,), self.t.ap()[idx])


class View:
    __slots__ = ("bufs", "ap")

    def __init__(self, bufs, ap):
        self.bufs = bufs
        self.ap = ap

    def __getitem__(self, idx):
        return View(self.bufs, self.ap[idx])

    def rr(self, pat, **kw):
        return View(self.bufs, self.ap.rearrange(pat, **kw))

    def bc(self, shape):
        return View(self.bufs, self.ap.to_broadcast(list(shape)))

    def bitcast(self, dt):
        return View(self.bufs, self.ap.bitcast(dt))

    @property
    def shape(self):
        return tuple(self.ap.shape)


def _bufs(*xs):
    out = []
    for x in xs:
        if isinstance(x, View):
            for b in x.bufs:
                if b not in out:
                    out.append(b)
    return out


def _ap(x):
    return x.ap if isinstance(x, View) else x


class Sched:
    ND = 8

    def __init__(self, nc):
        self.nc = nc
        self.eng = dict(pe=nc.tensor, dve=nc.vector, act=nc.scalar, pool=nc.gpsimd, sp=nc.sync)
        self.csem = {}
        self.cnt = {}
        for e in ("pe", "dve", "act", "pool"):
            self.csem[e] = nc.alloc_semaphore("cs_" + e)
            self.cnt[e] = 0
        self.dsem = {}
        self.dcnt = {}
        self.dk = {}
        for q in ("sp", "pool"):
            self.dsem[q] = [nc.alloc_semaphore("ds_%s%d" % (q, i)) for i in range(self.ND)]
            self.dcnt[q] = [0] * self.ND
            self.dk[q] = 0
        self.seen = {e: {} for e in self.eng}
        import os as _os
        self.nowait_same = set(_os.environ.get("NOWAIT", "pe").split(","))
        self.ninst = 0
        self.nwait = 0
        self.per = {e: 0 for e in self.eng}

    def sb(self, name, shape, dt=F32):
        return Buf(self.nc.alloc_sbuf_tensor(name, list(shape), dt), name)

    def ps(self, name, shape, dt=F32):
        return Buf(self.nc.alloc_psum_tensor(name, list(shape), dt), name)

    def dram(self, name, shape, dt=F32, kind="Internal"):
        return Buf(self.nc.dram_tensor(name, list(shape), dt, kind=kind), name)

    def _wait(self, e, tok, force=False):
        if tok is None:
            return
        sem, val, key, owner = tok
        if owner == e and e in self.nowait_same and not force:
            return
        if self.seen[e].get(key, 0) >= val:
            return
        self.eng[e].wait_ge(sem, val)
        self.seen[e][key] = val
        self.nwait += 1

    def _deps(self, e, reads, writes):
        for b in reads:
            self._wait(e, b.lw)
        for b in writes:
            self._wait(e, b.lw)
            for r in b.rd:
                self._wait(e, r)

    def _commit(self, tok, reads, writes):
        for b in reads:
            if b in writes:
                continue
            b.rd.append(tok)
            if len(b.rd) > 24:
                latest = {}
                for t in b.rd:
                    if t[2] not in latest or latest[t[2]][1] < t[1]:
                        latest[t[2]] = t
                b.rd = list(latest.values())
        for b in writes:
            b.lw = tok
            b.rd = []

    def op(self, e, fn, reads=(), writes=()):
        self._deps(e, reads, writes)
        inst = fn(self.eng[e])
        self.cnt[e] += 1
        inst.then_inc(self.csem[e], 1)
        tok = (self.csem[e], self.cnt[e], "c" + e, e)
        self._commit(tok, reads, writes)
        self.ninst += 1
        self.per[e] += 1
        return tok

    def dma(self, out, in_, q="sp", **kw):
        reads = _bufs(in_)
        writes = _bufs(out)
        k = self.dk[q]
        self.dk[q] += 1
        i = k % self.ND
        sem = self.dsem[q][i]
        key = "d%s%d" % (q, i)
        prev = self.dcnt[q][i]
        if prev > 0 and self.seen[q].get(key, 0) < 16 * prev:
            self.eng[q].wait_ge(sem, 16 * prev)
            self.seen[q][key] = 16 * prev
        self._deps(q, reads, writes)
        inst = self.eng[q].dma_start(out=_ap(out), in_=_ap(in_), **kw)
        self.dcnt[q][i] += 1
        inst.then_inc(sem, 16)
        tok = (sem, 16 * self.dcnt[q][i], key, "dma" + q)
        self._commit(tok, reads, writes)
        self.ninst += 1
        self.per[q] += 1
        return tok

    def finish(self, bufs):
        for b in bufs:
            self._wait("sp", b.lw)
        for e in ("pe", "dve", "act", "pool"):
            if self.cnt[e] > 0:
                self._wait("sp", (self.csem[e], self.cnt[e], "c" + e, e))
        for q in ("sp", "pool"):
            for i in range(self.ND):
                if self.dcnt[q][i] > 0:
                    self._wait("sp", (self.dsem[q][i], 16 * self.dcnt[q][i], "d%s%d" % (q, i), "dma" + q))

    def act(self, out, in_, func, bias=None, scale=None, accum=None, e="act"):
        kw = {}
        if bias is not None:
            kw["bias"] = _ap(bias)
        if scale is not None:
            kw["scale"] = _ap(scale)
        if accum is not None:
            kw["accum_out"] = _ap(accum)
        return self.op(e, lambda g: g.activation(out=_ap(out), in_=_ap(in_), func=func, **kw),
                       _bufs(in_, bias, scale), _bufs(out, accum))

    def tt(self, out, a, b, op, e="dve"):
        return self.op(e, lambda g: g.tensor_tensor(out=_ap(out), in0=_ap(a), in1=_ap(b), op=op),
                       _bufs(a, b), _bufs(out))

    def ts(self, out, a, s1, op0, s2=None, op1=None, e="dve"):
        if op1 is None:
            return self.op(e, lambda g: g.tensor_scalar(out=_ap(out), in0=_ap(a), scalar1=_ap(s1), scalar2=None, op0=op0),
                           _bufs(a, s1), _bufs(out))
        return self.op(e, lambda g: g.tensor_scalar(out=_ap(out), in0=_ap(a), scalar1=_ap(s1), scalar2=_ap(s2), op0=op0, op1=op1),
                       _bufs(a, s1, s2), _bufs(out))

    def stt(self, out, a, s, b, op0, op1):
        return self.op("dve", lambda g: g.scalar_tensor_tensor(out=_ap(out), in0=_ap(a), scalar=_ap(s), in1=_ap(b), op0=op0, op1=op1),
                       _bufs(a, s, b), _bufs(out))

    def copy(self, out, in_, e="dve"):
        if e == "act":
            return self.act(out, in_, AF.Copy)
        return self.op(e, lambda g: g.tensor_copy(out=_ap(out), in_=_ap(in_)), _bufs(in_), _bufs(out))

    def memset(self, out, val, e="pool"):
        return self.op(e, lambda g: g.memset(_ap(out), val), [], _bufs(out))

    def recip(self, out, in_, e="dve"):
        return self.op(e, lambda g: g.reciprocal(out=_ap(out), in_=_ap(in_)), _bufs(in_), _bufs(out))

    def scan(self, out, d0, d1, init, op0=ALU.mult, op1=ALU.add):
        return self.op("dve", lambda g: g.tensor_tensor_scan(out=_ap(out), data0=_ap(d0), data1=_ap(d1), initial=_ap(init), op0=op0, op1=op1),
                       _bufs(d0, d1, init), _bufs(out))

    def mm(self, out, lhsT, rhs, start=True, stop=True):
        la = _ap(lhsT)
        rg = (la.base_partition(), la.partition_size())
        for b in _bufs(out):
            if b.pe_rg is not None and b.pe_rg != rg and b.lw is not None and b.lw[3] == "pe" and not b.rd:
                self._wait("pe", b.lw, force=True)
            b.pe_rg = rg
        return self.op("pe", lambda g: g.matmul(_ap(out), lhsT=_ap(lhsT), rhs=_ap(rhs), start=start, stop=stop),
                       _bufs(lhsT, rhs), _bufs(out))

    def tr(self, out, in_, ident):
        return self.op("pe", lambda g: g.transpose(out=_ap(out), in_=_ap(in_), identity=_ap(ident)),
                       _bufs(in_, ident), _bufs(out))

    def asel(self, out, in_, pattern, cmp, fill, base, cm):
        return self.op("pool", lambda g: g.affine_select(out=_ap(out), in_=_ap(in_), pattern=pattern, compare_op=cmp, fill=fill, base=base, channel_multiplier=cm),
                       _bufs(in_), _bufs(out))


class Ring:
    def __init__(self, S, name, n, shape, dt=F32):
        self.tiles = [S.sb("%s%d" % (name, i), shape, dt) for i in range(n)]
        self.free = list(self.tiles)
        self.name = name

    def get(self):
        assert self.free, "ring %s exhausted" % self.name
        return self.free.pop(0)

    def put(self, *ts):
        for t in ts:
            assert t not in self.free
            self.free.append(t)

import numpy as np

D = 1024
NTOK = 512
FH = 2816
NHT = FH // 128
INW = 11552
EPS = 1e-6

OFF_A = 0
OFF_B = 1024
OFF_CV = OFF_B + 8 * 384
OFF_CP = OFF_CV + 1024
OFF_CL = OFF_CP + 8 * 256
OFF_D = OFF_CL + 288
OFF_E = OFF_D + 1024
assert OFF_E + 3072 == INW


def perm_cols():
    p = []
    HG = 0
    p += list(range(HG + 2048, HG + 3072))
    for h in range(8):
        p += list(range(HG + h * 128, HG + (h + 1) * 128))
        p += list(range(HG + 1024 + h * 128, HG + 1024 + (h + 1) * 128))
        p += list(range(HG + 3072 + h * 128, HG + 3072 + (h + 1) * 128))
    RW = 4096
    p += list(range(RW + 2048, RW + 3072))
    for q in range(8):
        p += list(range(RW + q * 128, RW + (q + 1) * 128))
        p += list(range(RW + 1024 + q * 128, RW + 1024 + (q + 1) * 128))
    p += list(range(RW + 3072, RW + 3360))
    S5 = RW + 3360
    p += list(range(S5, S5 + 1024))
    p += list(range(S5 + 1024, S5 + 1024 + 3072))
    p = np.array(p, dtype=np.int64)
    assert p.shape[0] == INW and len(set(p.tolist())) == INW
    return p


VEC_NAMES = ["mix_norm", "ffn_norm", "hg_lb_logits", "hg_onorm", "rw_w0", "rw_a0", "rw_v0", "rw_k_k", "rw_k_a",
             "rw_r_k", "rw_ln_w", "rw_ln_b", "s5_d", "s5_glu_b", "final_norm"]
NV = len(VEC_NAMES)
VI = {n: i for i, n in enumerate(VEC_NAMES)}


class K:
    pass


def build(L, T, dbg=(), stub=()):
    NTT = T // NTOK
    nc = bass.Bass("TRN2", target_bir_lowering=False)
    S = Sched(nc)
    k = K()
    k.S = S
    k.nc = nc
    k.L = L
    k.T = T

    def ext(name, shape):
        return Buf(nc.dram_tensor(name, list(shape), F32, kind="ExternalInput"), name)

    xT = ext("xT", [D, T])
    w_in = ext("w_in", [L, D, INW])
    w_branch = ext("w_branch", [L, 3 * D, D])
    w_out = ext("w_out", [L, D, D])
    w_gate = ext("ffn_w_gate", [L, D, FH])
    w_up = ext("ffn_w_up", [L, D, FH])
    w_down = ext("ffn_w_down", [L, FH, D])
    vecs = ext("vecs", [L, 128, NV * 8])
    out = Buf(nc.dram_tensor("out", [D, T], F32, kind="ExternalOutput"), "out")
    dbg_out = {}
    for name in dbg:
        dbg_out[name] = Buf(nc.dram_tensor("dbg_" + name, [D, T], F32, kind="ExternalOutput"), "dbg_" + name)

    class WT:
        def __init__(self, name, src, blocks):
            self.name = name
            self.src = src
            self.blocks = {}
            off = 0
            for (r0, kc, c0, n) in blocks:
                self.blocks[(r0, c0)] = (off, kc, n)
                off += kc * n
            self.total = off
            self.scr = S.dram(name + "_t", [L, 128, off], BF16)

        def __getitem__(self, idx):
            l, rs, cs = idx
            r0 = 0 if rs.start is None else rs.start
            return ("wt", self, l, r0, cs.start)

    in_blocks = [(0, 8, 0, 512), (0, 8, 512, 512)] + [(0, 8, OFF_B + 384 * h, 384) for h in range(8)] \
        + [(0, 8, OFF_CV + 512 * i, 512) for i in range(2)] + [(0, 8, OFF_CP + 512 * i, 512) for i in range(4)] \
        + [(0, 8, OFF_CL, 288)] + [(0, 8, OFF_D + 512 * i, 512) for i in range(2)] + [(0, 8, OFF_E + 512 * i, 512) for i in range(6)]
    w_in_b = WT("w_in_b", w_in, in_blocks)
    w_branch_b = WT("w_branch_b", w_branch, [(br * D, 8, c0, 512) for br in range(3) for c0 in (0, 512)])
    w_out_b = WT("w_out_b", w_out, [(0, 8, 0, 512), (0, 8, 512, 512)])
    fblocks = [(0, 8, 512 * i, 512) for i in range(5)] + [(0, 8, 2560, 256)]
    w_gate_b = WT("w_gate_b", w_gate, fblocks)
    w_up_b = WT("w_up_b", w_up, fblocks)
    w_down_b = WT("w_down_b", w_down, [(0, NHT, 128 * i, 128) for i in range(8)])
    hT = S.dram("hT", [D, T], F32)

    def conv_items(l):
        items = []
        for wt in (w_in_b, w_branch_b, w_out_b, w_gate_b, w_up_b, w_down_b):
            for (r0, c0), (off, kc, n) in wt.blocks.items():
                for c in range(kc):
                    items.append((wt, l, r0 + c * 128, c0, n, off + c * n))
        return items

    def do_conv(items):
        for it in items:
            (wt, l, r, c0, n, off) = it
            i = k.cvi % 3
            k.cvi += 1
            S.dma(cst32[i][:, 0:n], wt.src[l, r:r + 128, c0:c0 + n])
            S.copy(cst16[i][:, 0:n], cst32[i][:, 0:n], e="pool")
            k.cvpend.append((wt.scr[l, :, off:off + n], cst16[i][:, 0:n]))
            if len(k.cvpend) > 1:
                d, sv = k.cvpend.pop(0)
                S.dma(d, sv)

    def flush_conv():
        while k.cvpend:
            d, sv = k.cvpend.pop(0)
            S.dma(d, sv)

    k.cvpend = []
    k.mark = lambda name: None
    k.cvq = []

    def conv_tick(n=2):
        if k.cvq:
            do_conv(k.cvq[:n])
            del k.cvq[:n]
    k.conv_tick = conv_tick
    k.cvi = 0
    cst32 = [S.sb("cst32_%d" % i, [128, 512]) for i in range(3)]
    cst16 = [S.sb("cst16_%d" % i, [128, 512], BF16) for i in range(3)]

    ident = S.sb("ident", [128, 128])
    identb = S.sb("identb", [128, 128], BF16)
    onesb = S.sb("onesb", [128, 128], BF16)
    onesf = S.sb("onesf", [128, 128])
    vec = S.sb("vec", [128, NV, 8])
    xn = S.sb("xn", [128, 8, NTOK], BF16)
    merged = S.sb("merged", [128, 8, NTOK])
    ocur = S.sb("ocur", [128, 8, NTOK], BF16)
    wbufs = [S.sb("wbuf%d" % i, [128, 8 * 512], BF16) for i in range(3)]
    k.wi = 0
    R32 = Ring(S, "r32_", 14, [128, NTOK])
    R16 = Ring(S, "r16_", 8, [128, NTOK], BF16)
    BIG = [S.sb("big%d" % i, [128, NTOK]) for i in range(20)]
    psb = [S.ps("pb%d" % i, [128, 512]) for i in range(8)]
    k.pi = 0

    def ps():
        b = psb[k.pi % 7]
        k.pi += 1
        return b
    psheld = psb[7]
    k.ppi = {"a": 0, "b": 0}

    def ps_pool(name):
        if name == "a":
            b = psb[k.ppi["a"] % 4]
        else:
            b = psb[4 + k.ppi["b"] % 3]
        k.ppi[name] += 1
        return b

    def wbuf():
        b = wbufs[k.wi % 3]
        k.wi += 1
        return b

    def load_w(desc, kc, ncols):
        _, wt, l, r0, c0 = desc
        off, kc_, n_ = wt.blocks[(r0, c0)]
        assert kc_ == kc and n_ == ncols, (wt.name, r0, c0, kc, ncols, kc_, n_)
        b = wbuf()
        v = b.v()[:, 0:kc * ncols]
        S.dma(v, wt.scr[l, :, off:off + kc * ncols])
        return v.rr("p (c n) -> p c n", c=kc)

    S.memset(onesf.v(), 1.0)
    S.memset(onesb.v(), 1.0)
    S.asel(ident.v(), onesf.v(), [[-1, 128]], ALU.is_equal, 0.0, 0, 1)
    S.copy(identb.v(), ident.v(), e="pool")

    vecs_all = [vecs[l] for l in range(L)] + [vecs[L - 1]] * (4 - L)
    k.__dict__.update(locals())

    do_conv(conv_items(0))
    flush_conv()

    for c in range(8):
        S.dma(hT[c * 128:(c + 1) * 128, :], xT[c * 128:(c + 1) * 128, :])

    def rmsnorm_tile(src_dram, l, j, gidx):
        t0 = j * NTOK
        hts = []
        pss = ps()
        for c in range(8):
            ht = R32.get()
            S.dma(ht.v(), src_dram[c * 128:(c + 1) * 128, t0:t0 + NTOK])
            sq = R16.get()
            S.act(sq.v(), ht.v(), AF.Square)
            S.mm(pss.v(), onesb.v(), sq.v(), start=(c == 0), stop=(c == 7))
            R16.put(sq)
            hts.append(ht)
        rstd = R32.get()
        S.act(rstd.v(), pss.v(), AF.Sqrt, bias=k.epsb[:, 0:1], scale=1.0 / D)
        S.recip(rstd.v(), rstd.v())
        for c in range(8):
            S.stt(xn[:, c, :], hts[c].v(), vec[:, gidx, c:c + 1], rstd.v(), ALU.mult, ALU.mult)
            R32.put(hts[c])
        R32.put(rstd)

    k.rmsnorm_tile = rmsnorm_tile
    epsb = S.sb("epsb", [128, 4])
    S.memset(epsb[:, 0:1], EPS)
    S.memset(epsb[:, 1:2], 0.0)
    k.epsb = epsb

    def ffn_tile(l, j):
        t0 = j * NTOK
        rmsnorm_tile(hT, l, j, VI["ffn_norm"])
        def act_tile(ht):
            b = BIG[ht // 2]
            return b.v().bitcast(BF16)[:, (ht % 2) * NTOK:(ht % 2 + 1) * NTOK]
        for blk in range(6):
            c0 = blk * 512
            ncol = min(512, FH - c0)
            wg = load_w(w_gate_b[l, :, c0:c0 + ncol], 8, ncol)
            wu = load_w(w_up_b[l, :, c0:c0 + ncol], 8, ncol)
            conv_tick()
            for s in range(ncol // 128):
                ht = (c0 // 128) + s
                pg = ps()
                pu = ps()
                for c in range(8):
                    S.mm(pg.v(), wg[:, c, s * 128:(s + 1) * 128], xn[:, c, :], start=(c == 0), stop=(c == 7))
                for c in range(8):
                    S.mm(pu.v(), wu[:, c, s * 128:(s + 1) * 128], xn[:, c, :], start=(c == 0), stop=(c == 7))
                sg = R32.get()
                S.act(sg.v(), pg.v(), AF.Silu)
                S.tt(act_tile(ht), sg.v(), pu.v(), ALU.mult)
                R32.put(sg)
        for dt_ in range(8):
            wd = load_w(w_down_b[l, :, dt_ * 128:(dt_ + 1) * 128], NHT, 128)
            conv_tick()
            po = ps()
            for ht in range(NHT):
                S.mm(po.v(), wd[:, ht, :], act_tile(ht), start=(ht == 0), stop=(ht == NHT - 1))
            hres = R32.get()
            S.dma(hres.v(), hT[dt_ * 128:(dt_ + 1) * 128, t0:t0 + NTOK])
            S.tt(hres.v(), hres.v(), po.v(), ALU.add)
            S.dma(hT[dt_ * 128:(dt_ + 1) * 128, t0:t0 + NTOK], hres.v())
            R32.put(hres)

    k.ffn_tile = ffn_tile

    def merge_branch(l, j, br, first):
        for half in range(2):
            wb = load_w(w_branch_b[l, br * D:(br + 1) * D, half * 512:(half + 1) * 512], 8, 512)
            wg = load_w(w_in_b[l, :, OFF_E + br * D + half * 512: OFF_E + br * D + (half + 1) * 512], 8, 512)
            conv_tick()
            for s in range(4):
                dt_ = half * 4 + s
                pg = ps()
                pb = ps()
                for c in range(8):
                    S.mm(pg.v(), wg[:, c, s * 128:(s + 1) * 128], xn[:, c, :], start=(c == 0), stop=(c == 7))
                for c in range(8):
                    S.mm(pb.v(), wb[:, c, s * 128:(s + 1) * 128], ocur[:, c, :], start=(c == 0), stop=(c == 7))
                sg = R32.get()
                S.act(sg.v(), pg.v(), AF.Sigmoid)
                if first:
                    S.tt(merged[:, dt_, :], sg.v(), pb.v(), ALU.mult)
                else:
                    S.tt(sg.v(), sg.v(), pb.v(), ALU.mult)
                    S.tt(merged[:, dt_, :], merged[:, dt_, :], sg.v(), ALU.add, e="pool")
                R32.put(sg)

    k.merge_branch = merge_branch

    def wout_tile(l, j):
        t0 = j * NTOK
        for c in range(8):
            S.copy(ocur[:, c, :], merged[:, c, :], e=("act" if c % 2 else "dve"))
        for half in range(2):
            wo = load_w(w_out_b[l, :, half * 512:(half + 1) * 512], 8, 512)
            for s in range(4):
                dt_ = half * 4 + s
                po = ps()
                for c in range(8):
                    S.mm(po.v(), wo[:, c, s * 128:(s + 1) * 128], ocur[:, c, :], start=(c == 0), stop=(c == 7))
                hres = R32.get()
                S.dma(hres.v(), hT[dt_ * 128:(dt_ + 1) * 128, t0:t0 + NTOK])
                S.tt(hres.v(), hres.v(), po.v(), ALU.add)
                S.dma(hT[dt_ * 128:(dt_ + 1) * 128, t0:t0 + NTOK], hres.v())
                R32.put(hres)

    k.wout_tile = wout_tile
    return k


def finalize(k):
    S = k.S
    L, T = k.L, k.T
    for j in range(T // NTOK):
        t0 = j * NTOK
        hts = []
        pss = k.ps()
        for c in range(8):
            ht = k.R32.get()
            S.dma(ht.v(), k.hT[c * 128:(c + 1) * 128, t0:t0 + NTOK])
            sq = k.R16.get()
            S.act(sq.v(), ht.v(), AF.Square)
            S.mm(pss.v(), k.onesb.v(), sq.v(), start=(c == 0), stop=(c == 7))
            k.R16.put(sq)
            hts.append(ht)
        rstd = k.R32.get()
        S.act(rstd.v(), pss.v(), AF.Sqrt, bias=k.epsb[:, 0:1], scale=1.0 / D)
        S.recip(rstd.v(), rstd.v())
        for c in range(8):
            S.stt(hts[c].v(), hts[c].v(), k.vec[:, VI["final_norm"], c:c + 1], rstd.v(), ALU.mult, ALU.mult)
            S.dma(k.out[c * 128:(c + 1) * 128, t0:t0 + NTOK], hts[c].v())
            k.R32.put(hts[c])
        k.R32.put(rstd)
    S.finish([k.out] + list(k.dbg_out.values()))


def stub_mixer(k, l, j, col0):
    S = k.S
    for half in range(2):
        w = k.load_w(k.w_in_b[l, :, col0 + half * 512: col0 + (half + 1) * 512], 8, 512)
        for s in range(4):
            p = k.ps()
            for c in range(8):
                S.mm(p.v(), w[:, c, s * 128:(s + 1) * 128], k.xn[:, c, :], start=(c == 0), stop=(c == 7))
            S.copy(k.ocur[:, half * 4 + s, :], p.v(), e="act")


def run_layers(k, mixers):
    S = k.S
    for l in range(k.L):
        S.dma(k.vec.v().rr("p v c -> p (v c)"), k.vecs[l])
        nt = k.T // NTOK
        items = k.conv_items(l + 1) if l + 1 < k.L else []
        per = (len(items) + nt - 1) // nt
        for j in range(nt):
            k.do_conv(items[j * per:(j + 1) * per])
            k.rmsnorm_tile(k.hT, l, j, VI["mix_norm"])
            for bi, mx in enumerate(mixers):
                mx(k, l, j)
                k.merge_branch(l, j, bi, bi == 0)
            k.wout_tile(l, j)
            k.ffn_tile(l, j)
    finalize(k)


def hg_setup(k):
    S = k.S
    L = k.L
    k.resetmask = S.sb("resetmask", [128, NTOK])
    S.memset(k.resetmask.v(), 1.0)
    S.memset(k.resetmask.v().rr("p (c t) -> p c t", t=64)[:, :, 0:1], 0.0)
    k.maskincl = S.sb("maskincl", [128, 64])
    k.maskstr = S.sb("maskstr", [128, 64])
    for half in range(2):
        sl = slice(half * 64, half * 64 + 64)
        S.asel(k.maskincl[sl, :], k.onesf[sl, 0:64], [[1, 64]], ALU.is_ge, 0.0, 0, -1)
        S.asel(k.maskstr[sl, :], k.onesf[sl, 0:64], [[1, 64]], ALU.is_gt, 0.0, 0, -1)
    k.lball = S.sb("lball", [128, 4, 8])
    k.omlall = S.sb("omlall", [128, 4, 8])
    E = S.sb("lbE", [128, 4, 8])
    sm = S.sb("lbsum", [128, 8])
    c0 = VI["hg_lb_logits"] * 8
    S.memset(E.v(), 0.0)
    for l in range(4):
        S.dma(E[:, l, :], k.vecs_all[min(l, L - 1) if False else l][:, c0:c0 + 8])
    S.act(E.v(), E.v(), AF.Exp)
    S.tt(sm.v(), E[:, 0, :], E[:, 1, :], ALU.add)
    S.tt(sm.v(), sm.v(), E[:, 2, :], ALU.add)
    S.tt(sm.v(), sm.v(), E[:, 3, :], ALU.add)
    S.recip(sm.v(), sm.v())
    for l in range(4):
        S.tt(E[:, l, :], E[:, l, :], sm.v(), ALU.mult)
    S.memset(k.lball[:, 0, :], 0.0, e="dve")
    for l in range(1, 4):
        S.tt(k.lball[:, l, :], k.lball[:, l - 1, :], E[:, l, :], ALU.add)
    S.ts(k.lball.v(), k.lball.v(), 0.0, ALU.max)
    S.ts(k.omlall.v(), k.lball.v(), -1.0, ALU.mult, 1.0, ALU.add)
    k.hgS = S.sb("hgS", [128, 8, 128])
    k.hgSb = S.sb("hgSb", [128, 8, 128], BF16)
    k.sct = [S.sb("hgsct%d" % i, [128, 64], BF16) for i in range(4)]
    for t in k.sct:
        S.memset(t.v(), 0.0)
    k.scti = 0


def hg_layer_init(k, l):
    k.S.memset(k.hgS.v(), 0.0)
    k.S.memset(k.hgSb.v(), 0.0)


def hgrn2_tile(k, l, j):
    S = k.S
    R32, R16, BIG, xn, ps = k.R32, k.R16, k.BIG, k.xn, k.ps
    wib = k.w_in_b
    vtm = [BIG[tb].v().bitcast(BF16) for tb in range(4)]
    for half in range(2):
        w = k.load_w(wib[l, :, OFF_A + half * 512: OFF_A + (half + 1) * 512], 8, 512)
        for tb in range(4):
            p = ps()
            for c in range(8):
                S.mm(p.v(), xn[:, c, tb * 128:(tb + 1) * 128], w[:, c, :], start=(c == 0), stop=(c == 7))
            S.copy(vtm[tb][:, half * 512:(half + 1) * 512], p.v(), e="act")
    for h in range(8):
        w = k.load_w(wib[l, :, OFF_B + h * 384: OFF_B + (h + 1) * 384], 8, 384)
        k.conv_tick()
        lbv = k.lball[:, l, h:h + 1]
        omlv = k.omlall[:, l, h:h + 1]

        def proj(col0):
            p = ps()
            for c in range(8):
                S.mm(p.v(), w[:, c, col0:col0 + 128], xn[:, c, :], start=(c == 0), stop=(c == 7))
            return p
        pq = proj(0)
        q = R32.get()
        S.act(q.v(), pq.v(), AF.Silu)
        pf = proj(128)
        f = R32.get()
        S.act(f.v(), pf.v(), AF.Sigmoid)
        S.ts(f.v(), f.v(), omlv, ALU.mult, lbv, ALU.add)
        lf = R32.get()
        S.act(lf.v(), f.v(), AF.Ln)
        b = R32.get()
        S.scan(b.v(), k.resetmask.v(), lf.v(), 0.0)
        R32.put(lf)
        kk = R32.get()
        S.ts(kk.v(), f.v(), -1.0, ALU.mult, 1.0, ALU.add, e="pool")
        R32.put(f)
        eb = R32.get()
        S.act(eb.v(), b.v(), AF.Exp)
        enb = R32.get()
        S.act(enb.v(), b.v(), AF.Exp, scale=-1.0)
        R32.put(b)
        qt = R16.get()
        S.tt(qt.v(), q.v(), eb.v(), ALU.mult)
        R32.put(q)
        ktf = R32.get()
        S.tt(ktf.v(), kk.v(), enb.v(), ALU.mult, e="pool")
        R32.put(kk, enb)
        ktb = R16.get()
        S.copy(ktb.v(), ktf.v(), e="act")
        kdT = R16.get()
        eb3 = eb.v().rr("p (c t) -> p c t", t=64)
        S.tt(kdT.v().rr("p (c t) -> p c t", t=64), ktf.v().rr("p (c t) -> p c t", t=64),
             eb3[:, :, 63:64].bc([128, 8, 64]), ALU.mult)
        R32.put(ktf)
        ptr = ps()
        ptb = ptr.v().bitcast(BF16)
        for blk in range(4):
            S.tr(ptb[:, blk * 128:(blk + 1) * 128], kdT[:, blk * 128:(blk + 1) * 128], k.identb.v())
        kdtm = R16.get()
        S.copy(kdtm.v(), ptb[:, 0:512], e="act")
        R16.put(kdT)
        pog = proj(256)
        ogs = R32.get()
        S.act(ogs.v(), pog.v(), AF.Silu)
        osb = R32.get()
        for c in range(8):
            cs = slice(c * 64, (c + 1) * 64)
            pb = (c % 2) * 64
            rows = slice(pb, pb + 64)
            kdc = kdtm.v().rr("p (b n) -> p b n", b=4)[rows, c // 2, :]
            vch = vtm[c // 2][rows, h * 128:(h + 1) * 128]
            p1 = ps()
            S.mm(p1[rows, 0:64], ktb[:, cs], qt[:, cs])
            sct = k.sct[(c % 2) * 2 + (k.scti % 2)]
            k.scti += 1
            S.tt(sct[rows, :], p1[rows, 0:64], k.maskincl[rows, :], ALU.mult)
            p2 = ps()
            S.mm(p2[:, 0:64], vtm[c // 2][:, h * 128:(h + 1) * 128], sct.v(), start=True, stop=False)
            S.mm(p2[:, 0:64], k.hgSb[:, h, :], qt[:, cs], start=False, stop=True)
            S.copy(osb[:, cs], p2[:, 0:64], e="act")
            p3 = ps()
            S.mm(p3[:, 0:128], kdc, vch)
            S.stt(k.hgSb[:, h, :], k.hgS[:, h, :], eb[:, c * 64 + 63:c * 64 + 64], p3[:, 0:128], ALU.mult, ALU.add)
            S.stt(k.hgS[:, h, :], k.hgS[:, h, :], eb[:, c * 64 + 63:c * 64 + 64], p3[:, 0:128], ALU.mult, ALU.add)
        R32.put(eb)
        R16.put(qt, ktb, kdtm)
        sq = R16.get()
        S.act(sq.v(), osb.v(), AF.Square)
        pn = ps()
        S.mm(pn.v(), k.onesb.v(), sq.v())
        R16.put(sq)
        rstd = R32.get()
        S.act(rstd.v(), pn.v(), AF.Sqrt, bias=k.epsb[:, 0:1], scale=1.0 / 128)
        S.recip(rstd.v(), rstd.v())
        S.stt(osb.v(), osb.v(), k.vec[:, VI["hg_onorm"], h:h + 1], rstd.v(), ALU.mult, ALU.mult)
        S.tt(k.ocur[:, h, :], osb.v(), ogs.v(), ALU.mult, e="pool")
        R32.put(rstd, ogs, osb)
    if "hg" in k.dbg_out:
        t0 = j * NTOK
        for h in range(8):
            tmp = R32.get()
            S.copy(tmp.v(), k.ocur[:, h, :], e="pool")
            S.dma(k.dbg_out["hg"][h * 128:(h + 1) * 128, t0:t0 + NTOK], tmp.v())
            R32.put(tmp)


C0 = 0.6065306597126334
GN_EPS = 64e-5


def rw_setup(k):
    S = k.S
    nc = k.nc
    L, T = k.L, k.T

    def ext(name, shape):
        return Buf(nc.dram_tensor(name, list(shape), F32, kind="ExternalInput"), name)
    k.rw_w2 = ext("rw_w2", [L, 64, D])
    k.rw_a2 = ext("rw_a2", [L, 64, D])
    k.rw_g2 = ext("rw_g2", [L, 160, D])
    k.rw_v1 = ext("rw_v1", [L, D, 32])
    k.rw_v2 = ext("rw_v2", [L, 32, D])
    k.rwmu_d = ext("rwmu", [L, 128, 27])
    k.vfirst = S.dram("vfirst", [D, T])
    k.vdram = S.dram("vdram", [D, NTOK])
    k.w2b = S.sb("wa2b", [128, D], BF16)
    k.a2b = k.w2b
    k.g2b = S.sb("g2b", [128, 2, D], BF16)
    k.v1b = S.sb("v1b", [128, 8, 32], BF16)
    k.rwmu = S.sb("rwmu_s", [128, 27])
    k.omka = S.sb("omka", [128, 8])
    k.rwP = S.sb("rwP", [128, 8, 64])
    k.rwPb = S.sb("rwPb", [128, 8, 64], BF16)
    k.rwcarry = S.sb("rwcarry", [128, 27])
    k.zx = [S.sb("zx%d" % i, [128, NTOK + 8]) for i in range(2)]
    k.zxi = 0
    k.blockb = S.sb("blockb", [128, 128], BF16)
    k.blockf = S.sb("blockf", [128, 128])
    for t in (k.blockb, k.blockf):
        S.memset(t.v(), 0.0)
        S.memset(t[0:64, 0:64], 1.0)
        S.memset(t[64:128, 64:128], 1.0)
    k.mask4 = S.sb("mask4", [128, 128])
    S.copy(k.mask4[:, 0:64], k.maskstr.v(), e="pool")
    S.copy(k.mask4[:, 64:128], k.maskincl.v(), e="pool")
    k.msb = [S.sb("msb%d" % i, [128, 128], BF16) for i in range(32)]
    k.iwnr = [S.sb("iwnr%d" % i, [128, 256], BF16) for i in range(16)]
    k.iwl = [S.sb("iwl%d" % i, [128, 128], BF16) for i in range(16)]
    for t in k.iwnr + k.iwl:
        S.memset(t.v(), 0.0)
    k.ttb = [S.sb("ttb%d" % i, [128, 128], BF16) for i in range(16)]
    k.zsb = [S.sb("zsb%d" % i, [128, 128], BF16) for i in range(4)]
    k.gC = [S.sb("gC%d" % i, [128, 8]) for i in range(4)]
    k.tiny = S.sb("rwtiny", [128, 1])
    S.memset(k.tiny.v(), 1e-24)
    k.gneps = S.sb("gneps", [128, 1])
    S.memset(k.gneps.v(), GN_EPS)


def rw_layer_init(k, l):
    S = k.S
    st = k.cst32

    def ld(dst_view, src_view, rows, n, i, rbase=0):
        for h0 in range(0, n, 512):
            S.dma(st[i][rbase:rbase + rows, 0:512], src_view[:, h0:h0 + 512], q="pool")
            S.copy(dst_view[:, h0:h0 + 512], st[i][rbase:rbase + rows, 0:512], e="pool")
    ld(k.w2b[0:64, :], k.rw_w2[l], 64, D, 0)
    ld(k.a2b[64:128, :], k.rw_a2[l], 64, D, 1, rbase=64)
    ld(k.g2b[:, 0, :], k.rw_g2[l, 0:128, :], 128, D, 0)
    ld(k.g2b[0:32, 1, :], k.rw_g2[l, 128:160, :], 32, D, 1)
    ld(k.g2b[32:64, 1, :], k.rw_v2[l], 32, D, 0, rbase=32)
    S.dma(st[1][:, 0:256].rr("p (c n) -> p c n", c=8), k.rw_v1[l].rr("(c p) n -> p c n", p=128))
    S.copy(k.v1b.v(), st[1][:, 0:256].rr("p (c n) -> p c n", c=8), e="pool")
    S.dma(k.rwmu.v(), k.rwmu_d[l])
    S.memset(k.rwP.v(), 0.0)
    S.memset(k.rwPb.v(), 0.0)
    S.memset(k.rwcarry.v(), 0.0)


def rw_layer_vecs(k, l):
    k.S.ts(k.omka.v(), k.vec[:, VI["rw_k_a"], :], -1.0, ALU.mult, 1.0, ALU.add)


def rwkv_tile(k, l, j):
    S = k.S
    R32, R16, BIG, xn, ps = k.R32, k.R16, k.BIG, k.xn, k.ps
    wib = k.w_in_b
    t0 = j * NTOK
    V = lambda name, c: k.vec[:, VI[name], c:c + 1]

    def lerp(pview, ti, n=128):
        zx = k.zx[k.zxi % 2]
        k.zxi += 1
        S.copy(zx[0:n, 1:NTOK + 1], pview, e="act")
        S.copy(zx[0:n, 0:1], k.rwcarry[0:n, ti:ti + 1], e="act")
        S.copy(k.rwcarry[0:n, ti:ti + 1], zx[0:n, NTOK:NTOK + 1], e="act")
        d = R32.get()
        S.tt(d[0:n, :], zx[0:n, 0:NTOK], zx[0:n, 1:NTOK + 1], ALU.subtract)
        S.stt(d[0:n, :], d[0:n, :], k.rwmu[0:n, ti:ti + 1], zx[0:n, 1:NTOK + 1], ALU.mult, ALU.add)
        return d

    def projw(w, col0, n=128):
        p = ps()
        for c in range(8):
            S.mm(p[0:n, :], w[:, c, col0:col0 + n], xn[:, c, :], start=(c == 0), stop=(c == 7))
        return p

    w = k.load_w(wib[l, :, OFF_CL:OFF_CL + 288], 8, 288)
    z = lerp(projw(w, 0).v(), 24)
    lora1 = R16.get()
    S.act(lora1[0:64, :], z[0:64, :], AF.Tanh)
    S.copy(lora1[64:128, :], z[64:128, :], e="pool")
    R32.put(z)
    z = lerp(projw(w, 128).v(), 25)
    gsb0 = R16.get()
    S.act(gsb0.v(), z.v(), AF.Sigmoid)
    R32.put(z)
    z = lerp(projw(w, 256, 32)[0:32, :], 26, 32)
    gsb1 = R16.get()
    S.act(gsb1[0:32, :], z[0:32, :], AF.Sigmoid)
    R32.put(z)

    if k.stage == "a":
        R16.put(lora1, gsb0, gsb1)
        return
    k.mark('rw_a')
    vls = []
    for half in range(2):
        w = k.load_w(wib[l, :, OFF_CV + half * 512:OFF_CV + (half + 1) * 512], 8, 512)
        for s in range(4):
            vt = half * 4 + s
            vl = lerp(projw(w, s * 128).v(), vt)
            if l == 0:
                S.dma(k.vfirst[vt * 128:(vt + 1) * 128, t0:t0 + NTOK], vl.v())
                S.dma(k.vdram[vt * 128:(vt + 1) * 128, :], vl.v())
                R32.put(vl)
            else:
                vb = R16.get()
                S.copy(vb.v(), vl.v(), e="pool")
                S.mm(k.psheld[32:64, :], k.v1b[:, vt, :], vb.v(), start=(vt == 0), stop=(vt == 7))
                R16.put(vb)
                vls.append(vl)
    if l > 0:
        vv1 = R16.get()
        S.copy(vv1[32:64, :], k.psheld[32:64, :], e="act")
        for vt in range(8):
            vl = vls[vt]
            p2 = ps()
            S.mm(p2.v(), k.g2b[32:64, 1, vt * 128:(vt + 1) * 128], vv1[32:64, :])
            sv = R32.get()
            S.act(sv.v(), p2.v(), AF.Sigmoid, bias=V("rw_v0", vt))
            vf = R32.get()
            S.dma(vf.v(), k.vfirst[vt * 128:(vt + 1) * 128, t0:t0 + NTOK])
            S.tt(vf.v(), vf.v(), vl.v(), ALU.subtract)
            S.tt(vf.v(), vf.v(), sv.v(), ALU.mult, e="pool")
            S.tt(vl.v(), vl.v(), vf.v(), ALU.add)
            S.dma(k.vdram[vt * 128:(vt + 1) * 128, :], vl.v())
            R32.put(sv, vf, vl)
        R16.put(vv1)

    if k.stage == "b":
        R16.put(lora1, gsb0, gsb1)
        return
    k.mark('rw_b')
    for grp in range(2):
        prbs = []
        yTs = []
        for gi in range(4):
            p = grp * 4 + gi
            AR = BIG[gi].v().bitcast(BF16).rr("p (c s t) -> p c s t", c=8, s=2)
            BK = BIG[4 + gi].v().bitcast(BF16).rr("p (c s t) -> p c s t", c=8, s=2)
            BKtm = BIG[8 + gi].v().bitcast(BF16).rr("p (c n) -> p c n", c=8)
            VU = BIG[12 + gi].v().bitcast(BF16).rr("p (c h v) -> p c h v", c=8, h=2)
            yTs.append(BIG[16 + gi])
            if gi % 2 == 0:
                wp = k.load_w(wib[l, :, OFF_CP + (p // 2) * 512:OFF_CP + (p // 2 + 1) * 512], 8, 512)
            cb = (p % 2) * 256
            k.conv_tick()
            rl = lerp(projw(wp, cb).v(), 8 + 2 * p)
            kl = lerp(projw(wp, cb + 128).v(), 9 + 2 * p)
            pw = ps()
            S.mm(pw.v(), k.w2b[0:64, p * 128:(p + 1) * 128], lora1[0:64, :])
            sg = R32.get()
            S.act(sg.v(), pw.v(), AF.Sigmoid, bias=V("rw_w0", p))
            pa = ps()
            S.mm(pa.v(), k.a2b[64:128, p * 128:(p + 1) * 128], lora1[64:128, :])
            a = R32.get()
            S.act(a.v(), pa.v(), AF.Sigmoid, bias=V("rw_a0", p))
            kks = R16.get()
            S.act(kks.v(), kl.v(), AF.Square, scale=V("rw_k_k", p))
            pk = ps()
            S.mm(pk.v(), k.blockb.v(), kks.v())
            R16.put(kks)
            rn = R32.get()
            S.act(rn.v(), pk.v(), AF.Sqrt, bias=k.tiny[:, 0:1])
            S.recip(rn.v(), rn.v())
            kkn = R32.get()
            S.stt(kkn.v(), kl.v(), V("rw_k_k", p), rn.v(), ALU.mult, ALU.mult)
            R32.put(rn)
            kmod = R32.get()
            S.ts(kmod.v(), a.v(), V("rw_k_a", p), ALU.mult, k.omka[:, p:p + 1], ALU.add, e="pool")
            S.tt(kmod.v(), kmod.v(), kl.v(), ALU.mult, e="pool")
            R32.put(kl)
            prb = R16.get()
            S.stt(prb.v(), rl.v(), V("rw_r_k", p), kmod.v(), ALU.mult, ALU.mult)
            prbs.append(prb)
            cs = R32.get()
            S.scan(cs.v(), k.resetmask.v(), sg.v(), 0.0)
            csm = R32.get()
            S.tt(csm.v(), cs.v(), sg.v(), ALU.subtract, e="pool")
            R32.put(sg)
            G = R32.get()
            S.act(G.v(), cs.v(), AF.Exp, scale=-C0)
            G1 = csm
            S.act(G1.v(), csm.v(), AF.Exp, scale=-C0)
            Gi = cs
            S.act(Gi.v(), cs.v(), AF.Exp, scale=C0)
            S.copy(k.gC[gi].v(), G.v().rr("p (c t) -> p c t", t=64)[:, :, 63], e="pool")
            c3 = lambda t: t.v().rr("p (c t) -> p c t", t=64)
            S.stt(AR[:, :, 0, :], c3(kkn), -1.0, c3(G1), ALU.mult, ALU.mult)
            S.tt(AR[:, :, 1, :], c3(rl), c3(G), ALU.mult, e="pool")
            R32.put(rl, G, G1)
            bt = R32.get()
            S.tt(bt.v(), kkn.v(), a.v(), ALU.mult, e="pool")
            S.tt(bt.v(), bt.v(), Gi.v(), ALU.mult)
            kt = kmod
            S.tt(kt.v(), kmod.v(), Gi.v(), ALU.mult, e="pool")
            R32.put(kkn, a, Gi)
            BK2 = BK.rr("p (cb par) s t -> p cb par s t", par=2)
            bt4 = bt.v().rr("p (cb par t) -> p cb par t", par=2, t=64)
            kt4 = kt.v().rr("p (cb par t) -> p cb par t", par=2, t=64)
            S.copy(BK2[:, :, 0, 0, :], kt4[:, :, 0, :], e="act")
            S.copy(BK2[:, :, 0, 1, :], bt4[:, :, 0, :], e="dve")
            S.copy(BK2[:, :, 1, 0, :], bt4[:, :, 1, :], e="act")
            S.copy(BK2[:, :, 1, 1, :], kt4[:, :, 1, :], e="dve")
            R32.put(bt, kt)
            for half in range(2):
                pt = ps()
                ptb = pt.v().bitcast(BF16)
                for cc in range(4):
                    c = half * 4 + cc
                    S.tr(ptb[:, cc * 128:(cc + 1) * 128], BK[:, c, :, :].rr("p s t -> p (s t)"), k.identb.v())
                S.copy(BKtm[:, half * 4:(half + 1) * 4, :].rr("p c n -> p (c n)"), ptb[:, 0:512], e=("act" if half else "dve"))
            vl = R32.get()
            S.dma(vl.v(), k.vdram[p * 128:(p + 1) * 128, :])
            vb = R16.get()
            S.copy(vb.v(), vl.v(), e="pool")
            R32.put(vl)
            pt = ps()
            ptb = pt.v().bitcast(BF16)
            for blk in range(4):
                S.tr(ptb[:, blk * 128:(blk + 1) * 128], vb[:, blk * 128:(blk + 1) * 128], k.identb.v())
            R16.put(vb)
            VU2 = VU.rr("p (cb par) h v -> p cb par (h v)", par=2)
            pt4 = ptb[:, 0:512].rr("p (cb n) -> p cb n", cb=4)
            S.copy(VU2[0:64, :, 0, :], pt4[0:64, :, :], e="act")
            S.copy(VU2[64:128, :, 1, :], pt4[64:128, :, :], e="dve")
            k.rw_pair = getattr(k, "rw_pair", {})
            k.rw_pair[gi] = (AR, BK, BKtm, VU)

        k.mark('rw_pre%d' % grp)
        def scores(b):
            msb = {}
            mi = (b % 2) * 16
            for gi in range(4):
                AR, BK, BKtm, VU = k.rw_pair[gi]
                for hh in range(2):
                    hr = slice(hh * 64, hh * 64 + 64)
                    for par in range(2):
                        c = 2 * b + par
                        pS = k.ps_pool("b")
                        S.mm(pS[:, 0:128], BK[hr, c, :, :].rr("p s t -> p (s t)"), AR[hr, c, :, :].rr("p s t -> p (s t)"))
                        m = k.msb[mi]
                        mi += 1
                        S.tt(m.v(), pS[:, 0:128], k.mask4.v(), ALU.mult)
                        msb[(gi, hh, par)] = m
            return msb

        def inverse_gen(b, msb):
            units = [(gi, hh) for gi in range(4) for hh in range(2)]
            ws = {}
            for ui, (gi, hh) in enumerate(units):
                NRa, NRb = k.iwnr[ui * 2:ui * 2 + 2]
                La, Lb = k.iwl[ui * 2:ui * 2 + 2]
                S.copy(NRa[0:64, 0:64], msb[(gi, hh, 1)][0:64, 0:64], e="pool")
                S.copy(NRa[64:128, 64:128], msb[(gi, hh, 0)][64:128, 0:64], e="pool")
                ws[ui] = [NRa, NRb, La, Lb]
            yield
            for ui in range(8):
                NRa, NRb, La, Lb = ws[ui]
                pt = k.ps_pool("b")
                ptb = pt.v().bitcast(BF16)
                S.tr(ptb[:, 0:128], NRa[:, 0:128], k.identb.v())
                S.copy(La.v(), ptb[:, 0:128], e="act")
                S.tt(NRa[:, 128:256], NRa[:, 0:128], k.identb.v(), ALU.add, e="pool")
            yield
            for lev in range(1, 6):
                for half in range(2):
                    for ui in range(half * 4, half * 4 + 4):
                        NRa, NRb, La, Lb = ws[ui]
                        pc = k.ps_pool("b")
                        if lev > 1:
                            S.mm(pc[:, 0:256], La.v(), NRa[:, 0:256])
                        else:
                            S.mm(pc[:, 0:128], La.v(), NRa[:, 0:128])
                        pl = k.ps_pool("b")
                        S.mm(pl[:, 0:128], NRa[:, 0:128], La.v())
                        S.copy(Lb.v(), pl[:, 0:128], e="act")
                        S.copy(NRb[:, 0:128], pc[:, 0:128], e="dve")
                        if lev > 1:
                            S.tt(NRb[:, 128:256], pc[:, 128:256], NRa[:, 128:256], ALU.add)
                        else:
                            S.copy(NRb[:, 128:256], NRa[:, 128:256], e="pool")
                        ws[ui] = [NRb, NRa, Lb, La]
                    yield
            for half in range(2):
                for ui in range(half * 4, half * 4 + 4):
                    NRa, NRb, La, Lb = ws[ui]
                    pr = k.ps_pool("b")
                    S.mm(pr[:, 0:128], La.v(), NRa[:, 128:256])
                    S.tt(k.ttb[(b % 2) * 8 + ui].v(), pr[:, 0:128], NRa[:, 128:256], ALU.add)
                yield

        def steps_gen(b, msb):
            for par in range(2):
                c = 2 * b + par
                Vr = slice(0, 64) if par == 0 else slice(64, 128)
                Ur = slice(64, 128) if par == 0 else slice(0, 64)
                pZs, pUs, pYs, pPs = {}, {}, {}, {}
                for gi in range(4):
                    p = grp * 4 + gi
                    AR, BK, BKtm, VU = k.rw_pair[gi]
                    pZ = k.ps_pool("a")
                    h_same = 0 if par == 0 else 1
                    h_oth = 1 - h_same
                    hs = slice(h_same * 64, h_same * 64 + 64)
                    ho = slice(h_oth * 64, h_oth * 64 + 64)
                    S.mm(pZ[Ur, h_same * 64:(h_same + 1) * 64], AR[hs, c, 0, :], k.rwPb[hs, p, :], start=True, stop=False)
                    S.mm(pZ[Ur, h_same * 64:(h_same + 1) * 64], msb[(gi, h_same, par)][Vr, 0:64], VU[Vr, c, h_same, :], start=False, stop=True)
                    S.mm(pZ[Ur, h_oth * 64:(h_oth + 1) * 64], msb[(gi, h_oth, par)][Vr, 0:64], VU[Vr, c, h_oth, :], start=True, stop=False)
                    S.mm(pZ[Ur, h_oth * 64:(h_oth + 1) * 64], AR[ho, c, 0, :], k.rwPb[ho, p, :], start=False, stop=True)
                    pZs[gi] = pZ
                for gi in range(4):
                    S.copy(k.zsb[gi][Ur, :], pZs[gi][Ur, 0:128], e="act")
                yield
                for gi in range(4):
                    pU = k.ps_pool("a")
                    for hh in range(2):
                        S.mm(pU[Ur, hh * 64:(hh + 1) * 64], k.ttb[(b % 2) * 8 + gi * 2 + hh][Ur, Ur], k.zsb[gi][Ur, hh * 64:(hh + 1) * 64])
                    pUs[gi] = pU
                for gi in range(4):
                    AR, BK, BKtm, VU = k.rw_pair[gi]
                    S.copy(VU[Ur, c, :, :].rr("p h v -> p (h v)"), pUs[gi][Ur, 0:128], e="dve")
                yield
                for gi in range(4):
                    p = grp * 4 + gi
                    AR, BK, BKtm, VU = k.rw_pair[gi]
                    pY = k.ps_pool("a")
                    for hh in range(2):
                        hr = slice(hh * 64, hh * 64 + 64)
                        S.mm(pY[hr, 0:64], VU[:, c, hh, :], msb[(gi, hh, par)][:, 64:128], start=True, stop=False)
                    for hh in range(2):
                        hr = slice(hh * 64, hh * 64 + 64)
                        S.mm(pY[hr, 0:64], k.rwPb[hr, p, :], AR[hr, c, 1, :], start=False, stop=True)
                    pYs[gi] = pY
                for gi in range(4):
                    S.copy(yTs[gi][:, c * 64:(c + 1) * 64], pYs[gi][:, 0:64], e="act")
                yield
                for gi in range(4):
                    p = grp * 4 + gi
                    AR, BK, BKtm, VU = k.rw_pair[gi]
                    pP = k.ps_pool("a")
                    for hh in range(2):
                        hr = slice(hh * 64, hh * 64 + 64)
                        S.mm(pP[hr, 0:64], BKtm[:, c, hr], VU[:, c, hh, :])
                    pPs[gi] = pP
                    S.ts(k.rwP[:, p, :], k.rwP[:, p, :], k.gC[gi][:, c:c + 1], ALU.mult, e="pool")
                for gi in range(4):
                    p = grp * 4 + gi
                    S.stt(k.rwPb[:, p, :], pPs[gi][:, 0:64], k.gC[gi][:, c:c + 1], k.rwP[:, p, :], ALU.mult, ALU.add)
                    S.stt(k.rwP[:, p, :], pPs[gi][:, 0:64], k.gC[gi][:, c:c + 1], k.rwP[:, p, :], ALU.mult, ALU.add)
                yield

        import os as _os
        if _os.environ.get("RWIL", "1") == "1":
            msbs = {0: scores(0)}
            for _ in inverse_gen(0, msbs[0]):
                pass
            for b in range(4):
                g1 = steps_gen(b, msbs[b])
                k.conv_tick()
                g2 = None
                if b < 3:
                    msbs[b + 1] = scores(b + 1)
                    g2 = inverse_gen(b + 1, msbs[b + 1])
                d1 = d2 = False
                while not (d1 and (d2 or g2 is None)):
                    if not d1:
                        try:
                            next(g1)
                        except StopIteration:
                            d1 = True
                    if g2 is not None and not d2:
                        try:
                            next(g2)
                            next(g2)
                        except StopIteration:
                            d2 = True

        else:
            for b in range(4):
                msb_ = scores(b)
                for _ in inverse_gen(b, msb_):
                    pass
                for _ in steps_gen(b, msb_):
                    pass

        k.mark('rw_chunks%d' % grp)
        for gi in range(4):
            p = grp * 4 + gi
            y = yTs[gi]
            pm = ps()
            S.mm(pm.v(), k.blockf.v(), y.v())
            ysq = R32.get()
            S.act(ysq.v(), y.v(), AF.Square)
            pq = ps()
            S.mm(pq.v(), k.blockf.v(), ysq.v())
            m = R32.get()
            S.act(m.v(), pm.v(), AF.Copy, scale=1.0 / 64)
            S.act(ysq.v(), m.v(), AF.Square)
            var = R32.get()
            S.stt(var.v(), pq.v(), 1.0 / 64, ysq.v(), ALU.mult, ALU.subtract)
            S.act(var.v(), var.v(), AF.Sqrt, bias=k.gneps[:, 0:1])
            S.recip(var.v(), var.v())
            S.tt(y.v(), y.v(), m.v(), ALU.subtract, e="pool")
            S.tt(y.v(), y.v(), var.v(), ALU.mult)
            S.ts(y.v(), y.v(), V("rw_ln_w", p), ALU.mult, V("rw_ln_b", p), ALU.add, e="pool")
            R32.put(ysq, m, var)
            pbn = ps()
            S.mm(pbn.v(), k.blockb.v(), prbs[gi].v())
            R16.put(prbs[gi])
            vl = R32.get()
            S.dma(vl.v(), k.vdram[p * 128:(p + 1) * 128, :])
            S.tt(vl.v(), vl.v(), pbn.v(), ALU.mult)
            S.tt(y.v(), y.v(), vl.v(), ALU.add, e="pool")
            R32.put(vl)
            pg = ps()
            S.mm(pg.v(), k.g2b[:, 0, p * 128:(p + 1) * 128], gsb0.v(), start=True, stop=False)
            S.mm(pg.v(), k.g2b[0:32, 1, p * 128:(p + 1) * 128], gsb1[0:32, :], start=False, stop=True)
            S.tt(k.ocur[:, p, :], y.v(), pg.v(), ALU.mult)
    k.mark('rw_out')
    R16.put(lora1, gsb0, gsb1)
    if "rw" in k.dbg_out:
        for h in range(8):
            tmp = R32.get()
            S.copy(tmp.v(), k.ocur[:, h, :], e="pool")
            S.dma(k.dbg_out["rw"][h * 128:(h + 1) * 128, t0:t0 + NTOK], tmp.v())
            R32.put(tmp)

import math

TWO_PI = 2.0 * math.pi


def s5_setup(k):
    S = k.S
    nc = k.nc
    L, T = k.L, k.T

    def ext(name, shape):
        return Buf(nc.dram_tensor(name, list(shape), F32, kind="ExternalInput"), name)
    k.s5lam = ext("s5lam", [L, 128, 3, 32])
    k.s5b = ext("s5b", [L, 2, 128, 32 * 16])
    k.s5c = ext("s5c", [L, 2, 128, 32 * 16])
    k.s5cpad = ext("s5cpad", [L, 2, 8, 128, 4 * 128])
    k.s5glu = ext("s5glu", [L, 8, 128, 128])
    k.s5P = S.dram("s5P", [8, 128, 8 * 2 * 128], BF16)
    k.s5Q = S.dram("s5Q", [8, 128, 8 * 2 * 4 * 32], BF16)
    k.s5BD = S.dram("s5BD", [8, 128, 8 * 128], BF16)
    k.s5D = S.dram("s5D", [8, 128, 4 * 2 * 64], F32)
    k.s5P3 = S.dram("s5P3", [8, 128, 8 * 2 * 128], BF16)
    k.s5Q3 = S.dram("s5Q3", [8, 128, 8 * 2 * 128], BF16)
    k.s5pad3 = [S.sb("s5pad3_%d" % i, [128, 128]) for i in range(2)]
    for t in k.s5pad3:
        S.memset(t.v(), 0.0)
    k.s5small = S.sb("s5small", [128, 24, 32])
    k.s5pw = S.sb("s5pw", [128, 2, 9, 32])
    k.s5glub = S.sb("s5glub", [128, 8, 128], BF16)
    k.s5car = S.sb("s5car", [128, 2, 32])
    k.s5rho = S.sb("s5rho", [128, 32])
    k.s5pad = [S.sb("s5pad%d" % i, [128, 4 * 32]) for i in range(3)]
    for t in k.s5pad:
        S.memset(t.v(), 0.0)
    k.s5t = [S.sb("s5t%d" % i, [128, 72]) for i in range(12)]
    k.s5ti = 0
    k.s5x = [[S.sb("s5x%d_%d" % (i, c), [128, 64], BF16) for c in range(2)] for i in range(4)]


def s5_layer_init(k, l):
    S = k.S
    R32, BIG, ps = k.R32, k.BIG, k.ps
    sm = k.s5small

    def s(i):
        return sm[:, i, :]
    LR, LI, LS, DT, MAG, TH, R_, RF, M1, COS, SIN, ABR, ABI, DEN, NR, T1, T2, CRE, CIM, RH, RI = range(21)
    lamt = R32.get()
    lv = lamt.v()[:, 0:96].rr("p (a q) -> p a q", a=3)
    S.dma(lv, k.s5lam[l])
    S.copy(s(LR), lv[:, 0, :], e="pool")
    S.copy(s(LI), lv[:, 1, :], e="pool")
    S.act(s(DT), lv[:, 2, :], AF.Exp)
    R32.put(lamt)
    S.tt(s(T1), s(LR), s(DT), ALU.mult)
    S.act(s(MAG), s(T1), AF.Exp)
    S.act(s(RH), s(T1), AF.Exp, scale=8.0)
    S.copy(k.s5rho.v(), s(RH), e="pool")
    S.tt(s(TH), s(LI), s(DT), ALU.mult)

    def sincos(dst, shift):
        S.ts(s(R_), s(TH), 1.0 / TWO_PI, ALU.mult, shift, ALU.add)
        ri = sm[:, 23, :].bitcast(I32)
        S.copy(ri, s(R_), e="dve")
        S.copy(s(RF), ri, e="dve")
        S.tt(s(R_), s(R_), s(RF), ALU.subtract)
        S.ts(s(M1), s(R_), 0.5, ALU.is_gt)
        S.tt(s(R_), s(R_), s(M1), ALU.subtract)
        S.ts(s(M1), s(R_), -0.5, ALU.is_lt)
        S.tt(s(R_), s(R_), s(M1), ALU.add)
        S.act(dst, s(R_), AF.Sin, scale=6.28318)
    sincos(s(SIN), 0.0)
    sincos(s(COS), 0.25)
    S.tt(s(ABR), s(MAG), s(COS), ALU.mult)
    S.tt(s(ABI), s(MAG), s(SIN), ALU.mult)
    S.tt(s(DEN), s(LR), s(LR), ALU.mult)
    S.tt(s(T1), s(LI), s(LI), ALU.mult)
    S.tt(s(DEN), s(DEN), s(T1), ALU.add)
    S.recip(s(DEN), s(DEN))
    S.ts(s(NR), s(ABR), -1.0, ALU.add)
    S.tt(s(T1), s(NR), s(LR), ALU.mult)
    S.tt(s(T2), s(ABI), s(LI), ALU.mult)
    S.tt(s(T1), s(T1), s(T2), ALU.add)
    S.tt(s(CRE), s(T1), s(DEN), ALU.mult)
    S.tt(s(T1), s(ABI), s(LR), ALU.mult)
    S.tt(s(T2), s(NR), s(LI), ALU.mult)
    S.tt(s(T1), s(T1), s(T2), ALU.subtract)
    S.tt(s(CIM), s(T1), s(DEN), ALU.mult)
    pw = k.s5pw
    S.memset(pw[:, 0, 0, :], 1.0)
    S.memset(pw[:, 1, 0, :], 0.0)
    for d in range(8):
        S.tt(s(T1), pw[:, 0, d, :], s(ABR), ALU.mult)
        S.tt(s(T2), pw[:, 1, d, :], s(ABI), ALU.mult)
        S.tt(pw[:, 0, d + 1, :], s(T1), s(T2), ALU.subtract)
        S.tt(s(T1), pw[:, 0, d, :], s(ABI), ALU.mult)
        S.tt(s(T2), pw[:, 1, d, :], s(ABR), ALU.mult)
        S.tt(pw[:, 1, d + 1, :], s(T1), s(T2), ALU.add)
    S.recip(s(RI), s(RH))
    D1R, D1I = 21, 22
    S.tt(s(D1R), pw[:, 0, 8, :], s(RI), ALU.mult)
    S.tt(s(D1I), pw[:, 1, 8, :], s(RI), ALU.mult)
    S.ts(s(D1I), s(D1I), -1.0, ALU.mult)
    Dre = [BIG[i].v().rr("p (q n) -> p q n", n=64) for i in range(4)]
    Dim = [BIG[4 + i].v().rr("p (q n) -> p q n", n=64) for i in range(4)]
    for i in range(4):
        qs = slice(i * 8, (i + 1) * 8)
        tr1 = R32.get()
        tr2 = R32.get()
        S.copy(Dre[i][:, :, 0], s(D1R)[:, qs], e="pool")
        S.copy(Dim[i][:, :, 0], s(D1I)[:, qs], e="pool")
        m = 1
        while m < 64:
            t1 = tr1.v()[:, 0:8 * m].rr("p (q n) -> p q n", n=m)
            t2 = tr2.v()[:, 0:8 * m].rr("p (q n) -> p q n", n=m)
            br = Dre[i][:, :, m - 1:m].bc([128, 8, m])
            bi = Dim[i][:, :, m - 1:m].bc([128, 8, m])
            ar = Dre[i][:, :, 0:m]
            ai = Dim[i][:, :, 0:m]
            S.tt(t1, ar, br, ALU.mult)
            S.tt(t2, ai, bi, ALU.mult, e="pool")
            S.tt(Dre[i][:, :, m:2 * m], t1, t2, ALU.subtract)
            S.tt(t1, ar, bi, ALU.mult)
            S.tt(t2, ai, br, ALU.mult, e="pool")
            S.tt(Dim[i][:, :, m:2 * m], t1, t2, ALU.add)
            m *= 2
        R32.put(tr1, tr2)
    for gt in range(8):
        i, o = gt // 2, (gt % 2) * 4
        dv = k.s5D[gt].rr("p (k c n) -> p k c n", k=4, c=2)
        S.dma(dv[:, :, 0, :], Dre[i][:, o:o + 4, :])
        S.dma(dv[:, :, 1, :], Dim[i][:, o:o + 4, :])
    bre = R32.get()
    bim = R32.get()
    S.dma(bre.v(), k.s5b[l, 0])
    S.dma(bim.v(), k.s5b[l, 1])
    v3 = lambda t: t.v().rr("p (q h) -> p q h", h=16)
    bcq = lambda view: view.rr("p (q o) -> p q o", o=1).bc([128, 32, 16])
    t1 = R32.get()
    t2 = R32.get()
    abr = R32.get()
    abi = R32.get()

    def cmul_bc(ore, oim, are, aim, sre, sim):
        S.tt(v3(t1), v3(are), bcq(sre), ALU.mult)
        S.tt(v3(t2), v3(aim), bcq(sim), ALU.mult, e="pool")
        S.tt(v3(t1), v3(t1), v3(t2), ALU.subtract)
        S.tt(v3(t2), v3(are), bcq(sim), ALU.mult, e="pool")
        S.tt(v3(oim), v3(aim), bcq(sre), ALU.mult)
        S.tt(v3(oim), v3(oim), v3(t2), ALU.add)
        S.copy(v3(ore), v3(t1), e="pool")
    cmul_bc(abr, abi, bre, bim, s(CRE), s(CIM))
    R32.put(bre, bim)
    cre_t = R32.get()
    cim_t = R32.get()
    S.dma(cre_t.v(), k.s5c[l, 0])
    S.dma(cim_t.v(), k.s5c[l, 1])
    pre, pim, pimn = k.s5pad
    for d in range(8):
        tau = 7 - d
        for gt in range(8):
            for (dst, src, sc) in ((pre, abr, None), (pim, abi, None), (pimn, abi, -1.0)):
                dv = dst.v().rr("p (k g h) -> p k g h", k=4, g=2)
                sv = v3(src)[:, gt * 4:(gt + 1) * 4, :]
                if sc is None:
                    S.copy(dv[0:64, :, 0, :], sv[0:64], e="pool")
                    S.copy(dv[64:128, :, 1, :], sv[64:128], e="pool")
                else:
                    S.ts(dv[0:64, :, 0, :], sv[0:64], sc, ALU.mult)
                    S.ts(dv[64:128, :, 1, :], sv[64:128], sc, ALU.mult)
            S.copy(k.s5pad3[0][:, 96:128], pre[:, 96:128], e="pool")
            S.copy(k.s5pad3[1][:, 96:128], pim[:, 96:128], e="pool")
            pt = ps()
            S.tr(pt[:, 0:128], pre.v(), k.ident.v())
            S.tr(pt[:, 128:256], pim.v(), k.ident.v())
            S.tr(pt[:, 256:384], k.s5pad3[0].v(), k.ident.v())
            S.tr(pt[:, 384:512], k.s5pad3[1].v(), k.ident.v())
            pb = k.R16.get()
            S.copy(pb.v(), pt.v(), e="act")
            S.dma(k.s5P[gt].rr("p (t c n) -> p t c n", t=8, c=2)[:, tau, :, :], pb[:, 0:256].rr("p (c n) -> p c n", c=2))
            S.dma(k.s5P3[gt].rr("p (t c n) -> p t c n", t=8, c=2)[:, tau, :, :], pb[:, 256:512].rr("p (c n) -> p c n", c=2))
            k.R16.put(pb)
            cp = R32.get()
            cpi = R32.get()
            S.dma(cp.v(), k.s5cpad[l, 0, gt])
            S.dma(cpi.v(), k.s5cpad[l, 1, gt])
            pbd = ps()
            for kk in range(4):
                S.mm(pbd[:, 32 * kk:32 * kk + 32], cp[:, 128 * kk:128 * kk + 128], pre[:, 32 * kk:32 * kk + 32], start=True, stop=False)
                S.mm(pbd[:, 32 * kk:32 * kk + 32], cpi[:, 128 * kk:128 * kk + 128], pimn[:, 32 * kk:32 * kk + 32], start=False, stop=True)
            R32.put(cp, cpi)
            bdT = R32.get()
            if d == 0:
                S.stt(bdT[:, 0:128], k.ident.v(), k.vec[:, VI["s5_d"], gt:gt + 1], pbd[:, 0:128], ALU.mult, ALU.add)
            else:
                S.copy(bdT[:, 0:128], pbd[:, 0:128], e="act")
            pt2 = ps()
            S.tr(pt2[:, 0:128], bdT[:, 0:128], k.ident.v())
            R32.put(bdT)
            bdb = k.R16.get()
            S.copy(bdb[:, 0:128], pt2[:, 0:128], e="act")
            S.dma(k.s5BD[gt].rr("p (d n) -> p d n", d=8)[:, d, :], bdb[:, 0:128])
            k.R16.put(bdb)
        if d < 7:
            cmul_bc(abr, abi, abr, abi, s(ABR), s(ABI))
    R32.put(abr, abi)
    qre = R32.get()
    qim = R32.get()
    s5qpad = [BIG[8 + i].v().bitcast(BF16).rr("p (q n) -> p q n", q=32) for i in range(2)]
    s5q3 = [BIG[10 + i].v().bitcast(BF16).rr("p (g n) -> p g n", g=8) for i in range(2)]
    for i in range(4):
        S.memset(BIG[8 + i].v(), 0.0)
    for tp in range(8):
        cmul_bc(qre, qim, cre_t, cim_t, pw[:, 0, tp + 1, :], pw[:, 1, tp + 1, :])
        for ci, (src, sc) in enumerate(((qre, 1.0), (qim, -1.0))):
            qp = s5qpad[ci]
            S.ts(qp[0:64, :, 0:16], v3(src)[0:64], sc, ALU.mult)
            S.ts(qp[64:128, :, 16:32], v3(src)[64:128], sc, ALU.mult)
            q3 = s5q3[ci]
            S.copy(q3[:, :, 96:128], qp.rr("p (g k) n -> p g k n", k=4)[:, :, 3, :], e="pool")
            for gt in range(8):
                dv = k.s5Q[gt].rr("p (t c k n) -> p t c k n", t=8, c=2, k=4)
                S.dma(dv[:, tp, ci, :, :], qp[:, gt * 4:(gt + 1) * 4, :])
                dv3 = k.s5Q3[gt].rr("p (t c n) -> p t c n", t=8, c=2)
                S.dma(dv3[:, tp, ci, :], q3[:, gt, :])
    R32.put(qre, qim, cre_t, cim_t, t1, t2)
    for gt in range(8):
        g = R32.get()
        S.dma(g[:, 0:128], k.s5glu[l, gt])
        S.copy(k.s5glub[:, gt, :], g[:, 0:128], e="pool")
        R32.put(g)
    S.memset(k.s5car.v(), 0.0)


def s5_tile(k, l, j):
    S = k.S
    R32, R16, BIG, xn, ps = k.R32, k.R16, k.BIG, k.xn, k.ps
    wib = k.w_in_b
    t0 = j * NTOK

    def tmp():
        t = k.s5t[k.s5ti % 12]
        k.s5ti += 1
        return t
    for gt in range(8):
        base = (gt % 2) * 10
        BDs = BIG[base + 0].v().bitcast(BF16).rr("p (d n) -> p d n", d=8)
        Pv = [BIG[base + 1 + i].v().bitcast(BF16).rr("p (t c n) -> p t c n", t=4, c=2) for i in range(2)]
        Qv = [BIG[base + 3 + i].v().bitcast(BF16).rr("p (t c k n) -> p t c k n", t=4, c=2, k=4) for i in range(2)]
        Dv = BIG[base + 5].v().rr("p (k c n) -> p k c n", k=4, c=2)
        P3v = [BIG[base + 6 + i].v().bitcast(BF16).rr("p (t c n) -> p t c n", t=4, c=2) for i in range(2)]
        Q3v = [BIG[base + 8 + i].v().bitcast(BF16).rr("p (t c n) -> p t c n", t=4, c=2) for i in range(2)]
        S.dma(BIG[base + 0].v().bitcast(BF16), k.s5BD[gt])
        for i in range(2):
            S.dma(BIG[base + 1 + i].v().bitcast(BF16), k.s5P[gt][:, i * 1024:(i + 1) * 1024])
            S.dma(BIG[base + 3 + i].v().bitcast(BF16), k.s5Q[gt][:, i * 1024:(i + 1) * 1024])
            S.dma(BIG[base + 6 + i].v().bitcast(BF16), k.s5P3[gt][:, i * 1024:(i + 1) * 1024])
            S.dma(BIG[base + 8 + i].v().bitcast(BF16), k.s5Q3[gt][:, i * 1024:(i + 1) * 1024])
        S.dma(BIG[base + 5].v(), k.s5D[gt])
        if gt % 4 == 0:
            w = k.load_w(wib[l, :, OFF_D + (gt // 4) * 512:OFF_D + (gt // 4 + 1) * 512], 8, 512)
        k.conv_tick()
        pu = ps()
        for c in range(8):
            S.mm(pu.v(), w[:, c, (gt % 4) * 128:(gt % 4 + 1) * 128], xn[:, c, :], start=(c == 0), stop=(c == 7))
        Ut = R16.get()
        Utv = Ut.v().rr("p (t n) -> p t n", t=8)
        S.copy(Utv, pu.v().rr("p (n t) -> p t n", t=8), e="act")
        for kk in range(4):
            q = gt * 4 + kk
            rows = slice(32 * kk, 32 * kk + 32)
            pwr = ps()
            pwi = ps()
            for (pw_, ci) in ((pwr, 0), (pwi, 1)):
                for tau in range(8):
                    if kk < 3:
                        S.mm(pw_[:, 0:64], Pv[tau // 4][rows, tau % 4, ci, :], Utv[rows, tau, :], start=(tau == 0), stop=(tau == 7))
                    else:
                        S.mm(pw_[:, 0:64], P3v[tau // 4][:, tau % 4, ci, :], Utv[:, tau, :], start=(tau == 0), stop=(tau == 7))
            dre = Dv[:, kk, 0, :]
            dim = Dv[:, kk, 1, :]
            a1, a2, a3, a4 = tmp(), tmp(), tmp(), tmp()
            S.tt(a1[:, 0:64], dre, pwr[:, 0:64], ALU.mult)
            S.tt(a2[:, 0:64], dim, pwi[:, 0:64], ALU.mult)
            S.tt(a1[:, 0:64], a1[:, 0:64], a2[:, 0:64], ALU.subtract, e="pool")
            S.tt(a3[:, 0:64], dre, pwi[:, 0:64], ALU.mult)
            S.tt(a4[:, 0:64], dim, pwr[:, 0:64], ALU.mult)
            S.tt(a3[:, 0:64], a3[:, 0:64], a4[:, 0:64], ALU.add, e="pool")
            rho = k.s5rho[:, q:q + 1].bc([128, 64])
            wre, wim = tmp(), tmp()
            S.scan(wre[:, 0:64], rho, a1[:, 0:64], k.s5car[:, 0, q:q + 1])
            S.scan(wim[:, 0:64], rho, a3[:, 0:64], k.s5car[:, 1, q:q + 1])
            xre, xim = tmp(), tmp()
            S.copy(xre[:, 0:1], k.s5car[:, 0, q:q + 1], e="pool")
            S.copy(xim[:, 0:1], k.s5car[:, 1, q:q + 1], e="pool")
            S.tt(a1[:, 0:64], dre, wre[:, 0:64], ALU.mult)
            S.tt(a2[:, 0:64], dim, wim[:, 0:64], ALU.mult, e="pool")
            S.tt(xre[:, 1:65], a1[:, 0:64], a2[:, 0:64], ALU.add)
            S.tt(a3[:, 0:64], dre, wim[:, 0:64], ALU.mult, e="pool")
            S.tt(a4[:, 0:64], dim, wre[:, 0:64], ALU.mult)
            S.tt(xim[:, 1:65], a3[:, 0:64], a4[:, 0:64], ALU.subtract)
            S.copy(k.s5car[:, 0, q:q + 1], xre[:, 64:65], e="pool")
            S.copy(k.s5car[:, 1, q:q + 1], xim[:, 64:65], e="pool")
            S.copy(k.s5x[kk][0].v(), xre[:, 0:64], e="act")
            S.copy(k.s5x[kk][1].v(), xim[:, 0:64], e="act")
        ysb = R32.get()
        yv = ysb.v().rr("p (n t) -> p t n", t=8)
        for tp in range(8):
            py = ps()
            nmm = (tp + 1)
            for tau in range(tp + 1):
                S.mm(py[:, 0:64], BDs[:, tp - tau, :], Utv[:, tau, :], start=(tau == 0), stop=False)
            for kk in range(3):
                S.mm(py[32 * kk:32 * kk + 32, 0:64], Qv[tp // 4][:, tp % 4, 0, kk, :], k.s5x[kk][0].v(), start=False, stop=False)
                S.mm(py[32 * kk:32 * kk + 32, 0:64], Qv[tp // 4][:, tp % 4, 1, kk, :], k.s5x[kk][1].v(), start=False, stop=False)
            S.mm(py[:, 0:64], Q3v[tp // 4][:, tp % 4, 0, :], k.s5x[3][0].v(), start=False, stop=False)
            S.mm(py[:, 0:64], Q3v[tp // 4][:, tp % 4, 1, :], k.s5x[3][1].v(), start=False, stop=True)
            S.copy(yv[:, tp, :], py[:, 0:64], e="act")
        R16.put(Ut)
        x2 = R32.get()
        S.act(x2.v(), ysb.v(), AF.Square)
        S.ts(x2.v(), x2.v(), 0.044715, ALU.mult, 1.0, ALU.add, e="pool")
        S.tt(x2.v(), x2.v(), ysb.v(), ALU.mult)
        S.act(x2.v(), x2.v(), AF.Tanh, scale=0.7978845608028654)
        S.stt(x2.v(), x2.v(), 1.0, ysb.v(), ALU.add, ALU.mult)
        zb = R16.get()
        S.act(zb.v(), x2.v(), AF.Copy, scale=0.5)
        pg = ps()
        S.mm(pg.v(), k.s5glub[:, gt, :], zb.v())
        R16.put(zb)
        sgl = ysb
        S.act(sgl.v(), pg.v(), AF.Sigmoid, bias=k.vec[:, VI["s5_glu_b"], gt:gt + 1])
        S.stt(k.ocur[:, gt, :], x2.v(), 0.5, sgl.v(), ALU.mult, ALU.mult)
        R32.put(x2, ysb)
    if "s5" in k.dbg_out:
        for h in range(8):
            tmp_ = R32.get()
            S.copy(tmp_.v(), k.ocur[:, h, :], e="pool")
            S.dma(k.dbg_out["s5"][h * 128:(h + 1) * 128, t0:t0 + NTOK], tmp_.v())
            R32.put(tmp_)


def build_full(L, T, dbg=()):
    k = build(L, T, dbg=dbg)
    k.stage = "z"
    hg_setup(k)
    rw_setup(k)
    s5_setup(k)
    S = k.S
    nt = T // NTOK
    for l in range(L):
        S.dma(k.vec.v().rr("p v c -> p (v c)"), k.vecs[l])
        hg_layer_init(k, l)
        rw_layer_init(k, l)
        rw_layer_vecs(k, l)
        s5_layer_init(k, l)
        items = k.conv_items(l + 1) if l + 1 < L else []
        per = (len(items) + nt - 1) // nt
        for j in range(nt):
            k.cvq = list(items[j * per:(j + 1) * per])
            k.rmsnorm_tile(k.hT, l, j, VI["mix_norm"])
            hgrn2_tile(k, l, j)
            k.merge_branch(l, j, 0, True)
            rwkv_tile(k, l, j)
            k.merge_branch(l, j, 1, False)
            s5_tile(k, l, j)
            k.merge_branch(l, j, 2, False)
            k.wout_tile(l, j)
            k.ffn_tile(l, j)
            k.do_conv(k.cvq)
            k.cvq = []
            if j == nt - 1:
                k.flush_conv()
    finalize(k)
    return k

from concourse.bass_utils import run_bass_kernel_spmd

L_FULL = 4
T_FULL = 4096


def _pack_vecs(inp, L):
    v = np.zeros((L, 128, NV, 8), np.float32)
    for n, i in VI.items():
        a = np.asarray(inp[n], np.float32)
        if n == "final_norm":
            a = np.broadcast_to(a[None], (4, D))
        elif n == "rw_v0":
            a = np.concatenate([np.zeros((1, D), np.float32), a], 0)
        elif n == "rw_r_k":
            a = a.reshape(a.shape[0], D)
        a = a[:L]
        v[:, :, i, :] = a.reshape(L, 8, 128).transpose(0, 2, 1)
    return v.reshape(L, 128, NV * 8)


def _rw_inmap(inp, L):
    mu_idx = list(range(2048, 3072))
    for p in range(8):
        mu_idx += list(range(p * 128, (p + 1) * 128)) + list(range(1024 + p * 128, 1024 + (p + 1) * 128))
    mu_idx += list(range(3072, 3360))
    mu = np.zeros((L, 27 * 128), np.float32)
    mu[:, :3360] = inp["rw_shift_mu"][:L][:, mu_idx]
    mu = np.ascontiguousarray(mu.reshape(L, 27, 128).transpose(0, 2, 1))
    v1 = np.concatenate([np.zeros((1, 1024, 32), np.float32), inp["rw_v1"]], 0)[:L]
    v2 = np.concatenate([np.zeros((1, 32, 1024), np.float32), inp["rw_v2"]], 0)[:L]
    return {"rw_w2": inp["rw_w2"][:L], "rw_a2": inp["rw_a2"][:L], "rw_g2": inp["rw_g2"][:L],
            "rw_v1": np.ascontiguousarray(v1), "rw_v2": np.ascontiguousarray(v2), "rwmu": mu}


def _s5_inmap(inp, L):
    def pairlay(a):
        Lh = a.shape[0]
        X = a.shape[3]
        return a.reshape(Lh, 32, 2, 64, X).transpose(0, 2, 3, 1, 4).reshape(Lh, 128, 32, X)
    lr = pairlay(inp["s5_lambda_re"][:L, :, :, None])[..., 0]
    li = pairlay(inp["s5_lambda_im"][:L, :, :, None])[..., 0]
    ls = pairlay(np.broadcast_to(inp["s5_log_step"][:L, :, None, None], (L, 64, 64, 1)))[..., 0]
    lam = np.ascontiguousarray(np.stack([lr, li, ls], 2))
    b = np.stack([pairlay(inp["s5_b_re"][:L]), pairlay(inp["s5_b_im"][:L])], 1).reshape(L, 2, 128, 512)
    cT = [inp["s5_c_re"][:L].transpose(0, 1, 3, 2), inp["s5_c_im"][:L].transpose(0, 1, 3, 2)]
    c = np.stack([pairlay(x) for x in cT], 1)
    cpad = np.zeros((L, 2, 8, 128, 4, 8, 16), np.float32)
    for gt in range(8):
        for kk in range(4):
            for g2 in range(2):
                cpad[:, :, gt, g2 * 64:(g2 + 1) * 64, kk, 2 * kk + g2, :] = c[:, :, g2 * 64:(g2 + 1) * 64, gt * 4 + kk, :]
    glu = np.zeros((L, 8, 128, 128), np.float32)
    for gt in range(8):
        for g8 in range(8):
            glu[:, gt, g8 * 16:(g8 + 1) * 16, g8 * 16:(g8 + 1) * 16] = inp["s5_glu_w"][:L, gt * 8 + g8]
    return {"s5lam": lam, "s5b": np.ascontiguousarray(b), "s5c": np.ascontiguousarray(c.reshape(L, 2, 128, 512)),
            "s5cpad": np.ascontiguousarray(cpad.reshape(L, 2, 8, 128, 512)), "s5glu": glu}


def kernel(**inputs):
    inp = {k_: np.asarray(v, dtype=np.float32) for k_, v in inputs.items()}
    L, T = L_FULL, T_FULL
    B = inp["x"].shape[0]
    perm = perm_cols()
    common = {"w_in": np.ascontiguousarray(inp["w_in"][:, :, perm]),
              "w_branch": inp["w_branch"], "w_out": inp["w_out"], "ffn_w_gate": inp["ffn_w_gate"],
              "ffn_w_up": inp["ffn_w_up"], "ffn_w_down": inp["ffn_w_down"], "vecs": _pack_vecs(inp, L)}
    common.update(_rw_inmap(inp, L))
    common.update(_s5_inmap(inp, L))
    k = build_full(L, T)
    in_maps = []
    for c in range(8):
        m = dict(common)
        m["xT"] = np.ascontiguousarray(inp["x"][c % B].T)
        in_maps.append(m)
    res = run_bass_kernel_spmd(k.nc, in_maps, core_ids=list(range(8)))
    out = np.stack([np.ascontiguousarray(res.results[b]["out"].T) for b in range(B)], 0)
    return out.astype(np.float32)
```
